# Optimizing a Trainium2 kernel written in Bass

```python
import math
import jax
import jax.numpy as jnp
from jax import lax
import numpy as np

D_MODEL = 1024
BATCH = 8
SEQ = 2048
DEPTH = 4
DEC_BATCH = 128
DEC_SEQ = 1
PAST_LEN = 2048
PAGE_SIZE = 128

N_EVEN = (DEPTH + 1) // 2
N_ODD = DEPTH // 2
MIX_WIDTH = D_MODEL
BRANCH_WIDTH = MIX_WIDTH // 2
HEAD_DIM = 64
A_GROUPS = 4
A_CH = BRANCH_WIDTH // A_GROUPS
CHUNK = 128
B_HEADS = BRANCH_WIDTH // HEAD_DIM
C_HEADS = BRANCH_WIDTH // HEAD_DIM
C_KV_HEADS = 2
C_GROUP = C_HEADS // C_KV_HEADS
IDX_HEADS = 4
IDX_DIM = 32
TOPK_MAX = 256
D_WINDOWS = (2, 4, 8, 16)
D_GROUPS = len(D_WINDOWS)
D_CH = BRANCH_WIDTH // D_GROUPS
D_WMAX = max(D_WINDOWS)
D_BUF = D_WMAX - 1
REL_BUCKETS = 32
REL_MAX_DIST = 128
DN_ALPHA = (2.0 * DEPTH) ** 0.25
DN_BETA = (8.0 * DEPTH) ** -0.25
Q_BLOCK = 128
LN_EPS = 1e-5
EVEN_SPLITS = (BRANCH_WIDTH, BRANCH_WIDTH, BRANCH_WIDTH,
               B_HEADS * HEAD_DIM, B_HEADS * HEAD_DIM, B_HEADS * HEAD_DIM, BRANCH_WIDTH)
ODD_SPLITS = (C_HEADS * HEAD_DIM, 2 * C_KV_HEADS * HEAD_DIM, IDX_HEADS * IDX_DIM, IDX_DIM, IDX_HEADS,
              BRANCH_WIDTH, BRANCH_WIDTH, BRANCH_WIDTH)
EVEN_IN = sum(EVEN_SPLITS)
ODD_IN = sum(ODD_SPLITS)

kernel_name = 'hybrid_gmlp_stickbreak_dsa_pool_decode_step'


def _split(h, sizes):
    return jnp.split(h, [int(s) for s in np.cumsum(sizes)[:-1]], axis=-1)


def _standardize(x):
    xf = x.astype(jnp.float32)
    mu = jnp.mean(xf, axis=-1, keepdims=True)
    var = jnp.mean(jnp.square(xf - mu), axis=-1, keepdims=True)
    return (xf - mu) * lax.rsqrt(var + LN_EPS)


def _layer_norm(x, g, b):
    return (_standardize(x) * g.astype(jnp.float32) + b.astype(jnp.float32)).astype(x.dtype)


def gather_pages(cache, page_table):
    g = cache[page_table]
    return g.reshape(g.shape[0], g.shape[1] * g.shape[2], *g.shape[3:])


def sweep_query_blocks(fn, q_arrays, q_pos):
    t = q_pos.shape[0]
    if t % Q_BLOCK != 0:
        return fn(q_arrays, q_pos)
    nb = t // Q_BLOCK

    def to_blocks(a):
        return jnp.moveaxis(a.reshape(a.shape[0], nb, Q_BLOCK, *a.shape[2:]), 1, 0)

    out = lax.map(lambda xs: fn(xs[0], xs[1]),
                  (tuple(to_blocks(a) for a in q_arrays), q_pos.reshape(nb, Q_BLOCK)))
    out = jnp.moveaxis(out, 0, 1)
    return out.reshape(out.shape[0], t, *out.shape[3:])


def chunk_spatial_gate(u, v, w_sp, b_sp):
    b, t, _ = v.shape
    n = min(t, CHUNK)
    nc = t // n
    w = jnp.tril(w_sp[:, :n, :n])
    vc = v.reshape(b, nc, n, A_GROUPS, A_CH)
    s = jnp.einsum('gij,bcjgd->bcigd', w, vc) + jnp.transpose(b_sp[:, :n])[None, None, :, :, None]
    return u * s.reshape(b, t, BRANCH_WIDTH)


def stick_breaking(q, k, v, q_pos, k_pos):
    z = jnp.einsum('bqhd,bkhd->bhqk', q.astype(jnp.float32), k.astype(jnp.float32)) * HEAD_DIM ** -0.5
    mask = k_pos[None, :] < q_pos[:, None]
    log_keep = jnp.where(mask, jax.nn.log_sigmoid(-z), 0.0)
    later = lax.cumsum(log_keep, axis=3, reverse=True) - log_keep
    att = jnp.where(mask, jnp.exp(jax.nn.log_sigmoid(z) + later), 0.0)
    return jnp.einsum('bhqk,bkhd->bqhd', att, v.astype(jnp.float32)).astype(v.dtype)


def t5_bucket(n):
    max_exact = REL_BUCKETS // 2
    nf = jnp.maximum(n, 1).astype(jnp.float32)
    large = max_exact + (jnp.log(nf / max_exact) / math.log(REL_MAX_DIST / max_exact)
                         * (REL_BUCKETS - max_exact)).astype(jnp.int32)
    large = jnp.minimum(large, REL_BUCKETS - 1)
    return jnp.where(n < max_exact, n, large)


def dsa_attend(q, q_idx, w_idx, q_pos, k, v, k_idx, k_pos, rel_bias, topk):
    b, tq = q.shape[0], q.shape[1]
    score = jnp.einsum('bqhe,bke->bqhk', q_idx.astype(jnp.float32), k_idx.astype(jnp.float32)) * IDX_DIM ** -0.5
    index = jnp.einsum('bqhk,bqh->bqk', jax.nn.relu(score), w_idx.astype(jnp.float32))
    causal = k_pos[None, :] <= q_pos[:, None]
    index = jnp.where(causal[None], index, -jnp.inf)
    _, sel = lax.top_k(index, topk)
    sel_pos = k_pos[sel]
    valid = sel_pos <= q_pos[None, :, None]
    gather = jax.vmap(lambda a, i: a[i])
    k_sel = gather(k, sel).astype(jnp.float32)
    v_sel = gather(v, sel).astype(jnp.float32)
    qg = q.reshape(b, tq, C_KV_HEADS, C_GROUP, HEAD_DIM).astype(jnp.float32)
    logits = jnp.einsum('bqgrd,bqkgd->bqgrk', qg, k_sel) * HEAD_DIM ** -0.5
    dist = jnp.maximum(q_pos[None, :, None] - sel_pos, 0)
    bias = rel_bias.astype(jnp.float32)[t5_bucket(dist)]
    bias = jnp.moveaxis(bias, 2, 3).reshape(b, tq, C_KV_HEADS, C_GROUP, -1)
    logits = jnp.where(valid[:, :, None, None, :], logits + bias, -jnp.inf)
    p = jax.nn.softmax(logits, axis=-1)
    o = jnp.einsum('bqgrk,bqkgd->bqgrd', p, v_sel)
    return o.reshape(b, tq, C_HEADS, HEAD_DIM).astype(v.dtype)


def multiscale_pool(p_ext, n_pref, q_pos, w_grp, scale):
    b, n_ext, _ = p_ext.shape
    t = n_ext - n_pref
    cs = jnp.cumsum(p_ext.astype(jnp.float32), axis=1)
    cs = jnp.concatenate([jnp.zeros((b, D_WMAX, BRANCH_WIDTH), jnp.float32), cs], axis=1)
    start = D_WMAX + n_pref
    p_new = p_ext[:, n_pref:].astype(jnp.float32)
    pooled = []
    for g, w in enumerate(D_WINDOWS):
        sl = slice(g * D_CH, (g + 1) * D_CH)
        win_sum = cs[:, start:start + t, sl] - cs[:, start - w:start - w + t, sl]
        count = jnp.minimum(q_pos + 1, w).astype(jnp.float32)[None, :, None]
        pooled.append(win_sum / count - p_new[:, :, sl])
    pooled = jnp.stack(pooled, axis=2)
    mixed = jnp.einsum('btgc,gcd->btgd', pooled, w_grp.astype(jnp.float32))
    return (mixed.reshape(b, t, BRANCH_WIDTH) * scale.astype(jnp.float32)).astype(p_ext.dtype)


def even_mixer(x, past_kv, w_in, w_sp, b_sp):
    b, t, _ = x.shape
    h = jnp.einsum('btd,de->bte', x, w_in)
    u, v, gate_a, q, k, vb, gate_b = _split(h, EVEN_SPLITS)
    u = jax.nn.gelu(u)
    v = _standardize(jax.nn.gelu(v)).astype(x.dtype)
    out_a = chunk_spatial_gate(u, v, w_sp, b_sp) * jax.nn.silu(gate_a)
    q = q.reshape(b, t, B_HEADS, HEAD_DIM)
    new_kv = jnp.stack([k.reshape(b, t, B_HEADS, HEAD_DIM), vb.reshape(b, t, B_HEADS, HEAD_DIM)], axis=2)
    kv = new_kv if past_kv is None else jnp.concatenate([past_kv.astype(new_kv.dtype), new_kv], axis=1)
    n_past = kv.shape[1] - t
    q_pos = n_past + jnp.arange(t)
    k_pos = jnp.arange(n_past + t)
    keys, vals = kv[:, :, 0], kv[:, :, 1]
    o = sweep_query_blocks(lambda qs, pos: stick_breaking(qs[0], keys, vals, pos, k_pos), (q,), q_pos)
    out_b = o.reshape(b, t, BRANCH_WIDTH) * jax.nn.silu(gate_b)
    return jnp.concatenate([out_a, out_b], axis=-1), new_kv, v


def odd_mixer(x, past_kv, past_kidx, pool_buf, w_in, rel_bias, w_grp, scale):
    b, t, _ = x.shape
    h = jnp.einsum('btd,de->bte', x, w_in)
    q, kv, q_idx, k_idx, w_idx, gate_c, p, gate_d = _split(h, ODD_SPLITS)
    q = q.reshape(b, t, C_HEADS, HEAD_DIM)
    new_kv = kv.reshape(b, t, 2, C_KV_HEADS, HEAD_DIM)
    q_idx = q_idx.reshape(b, t, IDX_HEADS, IDX_DIM)
    if past_kv is None:
        kv_all, kidx_all = new_kv, k_idx
    else:
        kv_all = jnp.concatenate([past_kv.astype(new_kv.dtype), new_kv], axis=1)
        kidx_all = jnp.concatenate([past_kidx.astype(k_idx.dtype), k_idx], axis=1)
    n_keys = kv_all.shape[1]
    n_past = n_keys - t
    q_pos = n_past + jnp.arange(t)
    k_pos = jnp.arange(n_keys)
    topk = min(TOPK_MAX, n_keys // 4)
    keys, vals = kv_all[:, :, 0], kv_all[:, :, 1]
    o = sweep_query_blocks(
        lambda qs, pos: dsa_attend(qs[0], qs[1], qs[2], pos, keys, vals, kidx_all, k_pos, rel_bias, topk),
        (q, q_idx, w_idx), q_pos)
    out_c = o.reshape(b, t, BRANCH_WIDTH) * jax.nn.silu(gate_c)
    p_ext = p if pool_buf is None else jnp.concatenate([pool_buf.astype(p.dtype), p], axis=1)
    n_pref = p_ext.shape[1] - t
    out_d = multiscale_pool(p_ext, n_pref, q_pos, w_grp, scale) * jax.nn.silu(gate_d)
    new_buf = p_ext[:, -D_BUF:]
    return jnp.concatenate([out_c, out_d], axis=-1), new_kv, k_idx, new_buf


def post_norm(x, h, w_o, g, b):
    return _layer_norm(DN_ALPHA * x + jnp.einsum('bte,ed->btd', h, w_o), g, b)


def setup_inputs(seed: int = 0) -> dict:
    key = jax.random.key(seed)
    ks = jax.random.split(key, 20)
    n_pages = PAST_LEN // PAGE_SIZE
    n_used = DEC_BATCH * n_pages
    n_pool = n_used + (n_used + 3) // 4
    page_table = jax.random.permutation(ks[0], n_pool)[:n_used].reshape(DEC_BATCH, n_pages).astype(jnp.int32)

    def nrm(k, shape, s=1.0):
        return s * jax.random.normal(k, shape, jnp.float32)

    return {
        'x_prompt': nrm(ks[1], (BATCH, SEQ, D_MODEL)),
        'x_sample': nrm(ks[2], (DEC_BATCH, DEC_SEQ, D_MODEL)),
        'cache_b_kv': nrm(ks[3], (N_EVEN, n_pool, PAGE_SIZE, 2, B_HEADS, HEAD_DIM)),
        'cache_c_kv': nrm(ks[4], (N_ODD, n_pool, PAGE_SIZE, 2, C_KV_HEADS, HEAD_DIM)),
        'cache_c_kidx': nrm(ks[5], (N_ODD, n_pool, PAGE_SIZE, IDX_DIM)),
        'state_d_buf': nrm(ks[6], (N_ODD, DEC_BATCH, D_BUF, BRANCH_WIDTH)),
        'page_table': page_table,
        'w_in_even': nrm(ks[7], (N_EVEN, D_MODEL, EVEN_IN), D_MODEL ** -0.5),
        'w_in_odd': nrm(ks[8], (N_ODD, D_MODEL, ODD_IN), D_MODEL ** -0.5),
        'w_out': nrm(ks[9], (DEPTH, MIX_WIDTH, D_MODEL), DN_BETA * MIX_WIDTH ** -0.5),
        'ln_g': 1.0 + nrm(ks[10], (DEPTH, D_MODEL), 0.1),
        'ln_b': nrm(ks[11], (DEPTH, D_MODEL), 0.1),
        'a_w_sp': nrm(ks[12], (N_EVEN, A_GROUPS, CHUNK, CHUNK), CHUNK ** -0.5),
        'a_b_sp': 1.0 + nrm(ks[13], (N_EVEN, A_GROUPS, CHUNK), 0.1),
        'rel_bias': nrm(ks[14], (REL_BUCKETS, C_HEADS), 0.5),
        'd_w_grp': nrm(ks[15], (N_ODD, D_GROUPS, D_CH, D_CH), D_CH ** -0.5),
        'd_scale': 1.0 + nrm(ks[16], (N_ODD, BRANCH_WIDTH), 0.1),
    }


def reference(x_prompt, x_sample, cache_b_kv, cache_c_kv, cache_c_kidx, state_d_buf, page_table,
              w_in_even, w_in_odd, w_out, ln_g, ln_b, a_w_sp, a_b_sp, rel_bias, d_w_grp, d_scale):
    xp, xs = x_prompt, x_sample
    b_kv_p, b_kv_s, a_v_s = [], [], []
    c_kv_p, c_kv_s, c_ki_p, c_ki_s, d_buf_p, d_buf_s = [], [], [], [], [], []
    for layer in range(DEPTH):
        j = layer // 2
        if layer % 2 == 0:
            hp, kvp, _ = even_mixer(xp, None, w_in_even[j], a_w_sp[j], a_b_sp[j])
            past = gather_pages(cache_b_kv[j], page_table)
            hs, kvs, vs = even_mixer(xs, past, w_in_even[j], a_w_sp[j], a_b_sp[j])
            b_kv_p.append(kvp)
            b_kv_s.append(kvs)
            a_v_s.append(vs)
        else:
            hp, kvp, kip, bufp = odd_mixer(xp, None, None, None, w_in_odd[j], rel_bias, d_w_grp[j], d_scale[j])
            past_kv = gather_pages(cache_c_kv[j], page_table)
            past_ki = gather_pages(cache_c_kidx[j], page_table)
            hs, kvs, kis, bufs = odd_mixer(xs, past_kv, past_ki, state_d_buf[j], w_in_odd[j], rel_bias,
                                           d_w_grp[j], d_scale[j])
            c_kv_p.append(kvp)
            c_kv_s.append(kvs)
            c_ki_p.append(kip)
            c_ki_s.append(kis)
            d_buf_p.append(bufp)
            d_buf_s.append(bufs)
        xp = post_norm(xp, hp, w_out[layer], ln_g[layer], ln_b[layer])
        xs = post_norm(xs, hs, w_out[layer], ln_g[layer], ln_b[layer])
    return (xp, xs,
            jnp.stack(b_kv_p), jnp.stack(b_kv_s), jnp.stack(a_v_s),
            jnp.stack(c_kv_p), jnp.stack(c_kv_s), jnp.stack(c_ki_p), jnp.stack(c_ki_s),
            jnp.stack(d_buf_p), jnp.stack(d_buf_s))
```

```python
import math
import numpy as np
from contextlib import ExitStack
import concourse.bass as bass
import concourse.mybir as mybir
from concourse.bass_utils import run_bass_kernel_spmd

F32 = mybir.dt.float32
F32R = mybir.dt.float32r
BF16 = mybir.dt.bfloat16
I32 = mybir.dt.int32
AF = mybir.ActivationFunctionType
ALU = mybir.AluOpType
AX = mybir.AxisListType

ENGS = ("pe", "act", "dve", "pool", "sp")
DBG = {}
NCORES = 8
S = 2048
DM = 1024
NS = 16
NPG = 16
EVEN_IN = 3584
ODD_IN = 2468
DN_ALPHA = (2.0 * 4) ** 0.25
LN_EPS = 1e-5
NEG = -1.0e30


class Buf:
    __slots__ = ("t", "name", "last_w", "readers")

    def __init__(self, t, name):
        self.t = t
        self.name = name
        self.last_w = None
        self.readers = {}

    def __getitem__(self, idx):
        return self.t[idx]


class View:
    def __init__(self, parent, ap):
        self.parent = parent
        self.ap = ap
        self.name = parent.name + "_v"

    def __getitem__(self, idx):
        return self.ap[idx]

    @property
    def last_w(self):
        return self.parent.last_w

    @last_w.setter
    def last_w(self, v):
        self.parent.last_w = v

    @property
    def readers(self):
        return self.parent.readers

    @readers.setter
    def readers(self, v):
        self.parent.readers = v


def interleave(gens):
    gens = list(gens)
    while gens:
        for g in list(gens):
            try:
                next(g)
            except StopIteration:
                gens.remove(g)


class FW:
    def __init__(self, nc, stack, n_dma_sems=32):
        self.nc = nc
        self.stack = stack
        self.ops = {e: [] for e in ENGS}
        self.cnt = {e: 0 for e in ENGS}
        self.sems = {e: stack.enter_context(nc.semaphore("sem_" + e)) for e in ENGS}
        self.dma_sems = [stack.enter_context(nc.semaphore("sem_dma%d" % i)) for i in range(n_dma_sems)]
        self.dma_cnt = [0] * n_dma_sems
        self.dma_rr = 0
        self.seen = {e: {} for e in ENGS}
        self.nbuf = 0

    def sb(self, shape, dt, name=None):
        self.nbuf += 1
        name = (name or "sb") + "_%d" % self.nbuf
        t = self.stack.enter_context(self.nc.sbuf_tensor(name, list(shape), dt))
        return Buf(t, name)

    def ps(self, shape, dt=F32, name=None):
        self.nbuf += 1
        name = (name or "ps") + "_%d" % self.nbuf
        t = self.stack.enter_context(self.nc.psum_tensor(name, list(shape), dt))
        return Buf(t, name)

    def _collect(self, eng, reads, writes):
        waits = {}

        def add(ev):
            if ev is None:
                return
            k, v = ev
            if k == "pe" and eng == "pe":
                return
            if waits.get(k, 0) < v:
                waits[k] = v

        for b in reads:
            add(b.last_w)
        for b in writes:
            add(b.last_w)
            for k, v in b.readers.items():
                add((k, v))
        seen = self.seen[eng]
        out = []
        for k, v in waits.items():
            if seen.get(k, 0) >= v:
                continue
            seen[k] = v
            out.append((k, v))
        return out

    def _commit(self, ev, reads, writes):
        k, v = ev
        for b in reads:
            if b.readers.get(k, 0) < v:
                b.readers[k] = v
        for b in writes:
            b.last_w = ev
            b.readers = {}

    def op(self, eng, fn, reads=(), writes=()):
        waits = self._collect(eng, reads, writes)
        self.cnt[eng] += 1
        ev = (eng, self.cnt[eng])
        self.ops[eng].append((waits, fn, (eng, 1)))
        self._commit(ev, reads, writes)
        return ev

    def dma(self, eng, fn, reads=(), writes=()):
        i = self.dma_rr
        self.dma_rr = (self.dma_rr + 1) % len(self.dma_sems)
        key = "dma%d" % i
        waits = self._collect(eng, reads, writes)
        prev = self.dma_cnt[i] * 16
        if prev > 0 and self.seen[eng].get(key, 0) < prev:
            self.seen[eng][key] = prev
            waits.append((key, prev))
        self.dma_cnt[i] += 1
        ev = (key, self.dma_cnt[i] * 16)
        self.ops[eng].append((waits, fn, (key, 16)))
        self._commit(ev, reads, writes)
        return ev

    def _sem(self, key):
        if key in self.sems:
            return self.sems[key]
        return self.dma_sems[int(key[3:])]

    def emit(self):
        nc = self.nc
        with nc.Block() as block:
            def run(engname, e):
                for waits, fn, (ik, iv) in self.ops[engname]:
                    for k, v in waits:
                        e.wait_ge(self._sem(k), v)
                    fn(e).then_inc(self._sem(ik), iv)

            @block.tensor
            def _(e):
                run("pe", e)

            @block.scalar
            def _(e):
                run("act", e)

            @block.vector
            def _(e):
                run("dve", e)

            @block.gpsimd
            def _(e):
                run("pool", e)

            @block.sync
            def _(e):
                run("sp", e)
                for i, c in enumerate(self.dma_cnt):
                    if c:
                        e.wait_ge(self.dma_sems[i], c * 16)


def _consts():
    c = {}
    k = np.arange(128)
    c["c_ident"] = np.eye(128, dtype=np.float32)
    c["c_ntri"] = -(k[:, None] >= k[None, :]).astype(np.float32)
    c["c_nones"] = -np.ones((128, 128), np.float32)
    c["c_negm"] = np.where(k[:, None] >= k[None, :], -30000.0, 0.0).astype(np.float32)
    pg = k // 8
    h = k % 8
    c["c_m2"] = ((h[:, None] == h[None, :]) & (pg[:, None] > pg[None, :])).astype(np.float32)
    c["c_dsac"] = (k[None, :] >= k[:, None]).astype(np.float32)
    c["c_negq"] = np.where(k[None, :] > k[:, None], NEG, 0.0).astype(np.float32)
    esel = np.zeros((16, 2, 8, 16), np.float32)
    for b in range(16):
        esel[b, b // 8, b % 8, :] = 1.0
    c["c_esel"] = esel.reshape(16, 256)
    rowm = np.zeros((128, 4), np.float32)
    for ih in range(4):
        rowm[32 * ih:32 * ih + 32, ih] = 1.0
    c["c_rowm"] = rowm
    icnt = np.zeros((4, 16), np.float32)
    for g, w in enumerate((2, 4, 8, 16)):
        icnt[g] = 1.0 / np.minimum(np.arange(16) + 1, w)
    c["c_icnt"] = np.broadcast_to(icnt.reshape(1, 64), (128, 64)).copy()
    negrow = np.zeros((128, 1), np.float32)
    negrow[1:] = NEG
    c["c_negrow"] = negrow
    return c


def _t5_bucket(n):
    n = np.asarray(n)
    nf = np.maximum(n, 1).astype(np.float32)
    large = 16 + (np.log(nf / 16) / math.log(128 / 16) * 16).astype(np.int32)
    large = np.minimum(large, 31)
    return np.where(n < 16, n, large)


CONST_SHAPES = {"c_ident": [128, 128], "c_ntri": [128, 128], "c_nones": [128, 128], "c_negm": [128, 128],
                "c_m2": [128, 128], "c_dsac": [128, 128], "c_negq": [128, 128],
                "c_rowm": [128, 4], "c_esel": [16, 256], "c_icnt": [128, 64], "c_negrow": [128, 1]}


def build(n_layers=4, pool_rows=2560 * 128, debug=False):
    nc = bass.Bass("TRN2", target_bir_lowering=False)

    def din(name, shape, dt=F32):
        return nc.dram_tensor(name, list(shape), dt, kind="ExternalInput").ap()

    def dout(name, shape, dt=F32):
        return nc.dram_tensor(name, list(shape), dt, kind="ExternalOutput").ap()

    xp_d = din("xp", [S, DM])
    xs_d = din("xs", [NS, DM])
    pt_d = din("pt", [1, NS * NPG], I32)
    cb_d = din("cb", [2 * pool_rows, 1024])
    cc_d = din("cc", [2 * pool_rows, 256])
    ck_d = din("ck", [2 * pool_rows, 32])
    db_d = din("dbuf", [2, NS, 15, 512])
    wie_d = din("w_in_even", [2, DM, EVEN_IN])
    wio_d = din("w_in_odd", [2, DM, ODD_IN])
    wo_d = din("w_out", [4, DM, DM])
    lng_d = din("ln_g", [4, DM])
    lnb_d = din("ln_b", [4, DM])
    wsp_d = din("a_w_sp", [2, 4, 128, 128])
    bsp_d = din("a_b_sp", [2, 4, 128])
    wgrp_d = din("d_w_grp", [2, 4, 128, 128])
    dsc_d = din("d_scale", [2, 512])
    tz_d = din("t_toep", [128, 16 * 128])
    b31_d = din("t_b31", [1, 8])
    bs_d = din("t_bsamp", [128, 17 * 8])
    cst = {n: din(n, s) for n, s in CONST_SHAPES.items()}

    yp_d = dout("y_p", [S, DM])
    ys_d = dout("y_s", [NS, DM])
    bkvp_d = dout("bkv_p", [2, S, 1024])
    bkvs_d = dout("bkv_s", [2, NS, 1024])
    avs_d = dout("av_s", [2, NS, 512])
    ckvp_d = dout("ckv_p", [2, S, 256])
    ckvs_d = dout("ckv_s", [2, NS, 256])
    ckip_d = dout("cki_p", [2, S, 32])
    ckis_d = dout("cki_s", [2, NS, 32])
    dbp_d = dout("dbuf_p", [2, 15, 512])
    dbs_d = dout("dbuf_s", [2, NS, 15, 512])
    dbgh_d = dout("dbg_h", [8, 128, 512], BF16) if DBG.get("dbgh") else None
    scr_idx = nc.dram_tensor("scr_idx", [NS, 17 * 128], F32, kind="Internal").ap()
    scr_msk = nc.dram_tensor("scr_msk", [NS, 17 * 128], F32, kind="Internal").ap()
    scr_kv = nc.dram_tensor("scr_kv", [NS, 288], F32, kind="Internal").ap()

    with ExitStack() as st:
        fw = FW(nc, st)
        st.enter_context(nc.allow_non_contiguous_dma(reason="small strided parameter loads"))
        op, dma = fw.op, fw.dma

        C = {}
        for n, s in CONST_SHAPES.items():
            C[n] = fw.sb(s, F32, n)
            dma("sp", (lambda b, a: lambda e: e.dma_start(out=b[:], in_=a))(C[n], cst[n]), writes=[C[n]])
        ident_b = fw.sb([128, 128], BF16, "ident_b")
        negm_b = fw.sb([128, 128], BF16, "negm_b")
        op("dve", lambda e: e.tensor_copy(out=ident_b[:], in_=C["c_ident"][:]), [C["c_ident"]], [ident_b])
        op("dve", lambda e: e.tensor_copy(out=negm_b[:], in_=C["c_negm"][:]), [C["c_negm"]], [negm_b])
        ntri_r = fw.sb([128, 128], F32R, "ntri_r")
        nones_r = fw.sb([128, 128], F32R, "nones_r")
        op("dve", lambda e: e.tensor_copy(out=ntri_r[:], in_=C["c_ntri"][:]), [C["c_ntri"]], [ntri_r])
        op("dve", lambda e: e.tensor_copy(out=nones_r[:], in_=C["c_nones"][:]), [C["c_nones"]], [nones_r])
        zero_b = fw.sb([128, 128], BF16, "zero_b")
        op("pool", lambda e: e.memset(zero_b[:], 0.0), [], [zero_b])
        pts = fw.sb([128, NS * NPG], I32, "pts")
        dma("sp", lambda e: e.dma_start(out=pts[:], in_=pt_d.to_broadcast([128, NS * NPG])), writes=[pts])
        idx2 = fw.sb([128, 2], I32, "idx2")
        iota_p = fw.sb([128, 1], I32, "iota_p")
        op("pool", lambda e: e.iota(out=iota_p[:], pattern=[[0, 1]], base=0, channel_multiplier=1), [], [iota_p])
        idxC = pts
        idxB = fw.sb([128, NS * NPG], I32, "idxB")
        op("dve", lambda e: e.tensor_scalar(out=idxC[:], in0=pts[:], scalar1=128, scalar2=iota_p[:, 0:1], op0=ALU.mult, op1=ALU.add),
           [pts, iota_p], [idxC])
        op("dve", lambda e: e.tensor_scalar(out=idxB[:], in0=idxC[:], scalar1=2, scalar2=None, op0=ALU.mult), [idxC], [idxB])

        scrkv_b, scrkv_b2, scri_b, scrm_b = Buf(None, "scrkv"), Buf(None, "scrkv2"), Buf(None, "scri"), Buf(None, "scrm")
        xtok = [fw.sb([128, DM], F32, "xtok%d" % j) for j in range(17)]
        for j in range(16):
            dma("sp", (lambda j: lambda e: e.dma_start(out=xtok[j][:], in_=xp_d[j * 128:(j + 1) * 128, :]))(j),
                writes=[xtok[j]])
        op("pool", lambda e: e.memset(xtok[16][:], 0.0), [], [xtok[16]])
        dma("sp", lambda e: e.dma_start(out=xtok[16][0:NS, :], in_=xs_d), writes=[xtok[16]])

        xT = [fw.sb([128, 512], BF16, "xT%d" % c) for c in range(8)]
        hT = [fw.sb([128, 512], BF16, "hT%d" % c) for c in range(8)]
        lng_sb = fw.sb([128, DM], F32, "lng_sb")
        lnb_sb = fw.sb([128, DM], F32, "lnb_sb")
        wst = [fw.sb([128, 2, 128], F32, "wst%d" % i) for i in range(4)]
        wbf = [fw.sb([128, 8, 128], BF16, "wbf%d" % i) for i in range(3)]
        ring = {"st": 0, "bf": 0}
        psA = [fw.ps([128, 512], F32, "psA%d" % i) for i in range(2)]
        psZ = [fw.ps([128, 512], F32, "psZ%d" % i) for i in range(2)]
        psO = [fw.ps([128, 512], F32, "psO%d" % i) for i in range(2)]
        psY = [fw.ps([128, 512], F32, "psY%d" % i) for i in range(2)]
        rr = {"A": 0, "Z": 0, "O": 0}

        def nxt(kind):
            lst = {"A": psA, "Z": psZ, "O": psO}[kind]
            rr[kind] = (rr[kind] + 1) % len(lst)
            return lst[rr[kind]]

        kTt = [fw.sb([128, 4, 512], BF16, "kTt%d" % t) for t in range(4)]
        vbt = [fw.sb([128, 4, 512], BF16, "vbt%d" % t) for t in range(4)]
        qT = [fw.sb([128, 512], BF16, "qT%d" % c) for c in range(4)]
        t32 = [fw.sb([128, 512], F32, "t32_%d" % i) for i in range(6)]
        big32 = [fw.sb([128, 2304], F32, "big32_%d" % i) for i in range(2)]
        att_b = [fw.sb([128, 512], BF16, "att%d" % i) for i in range(2)]
        e32 = [fw.sb([128, 512], F32, "e32_%d" % i) for i in range(1)] * 2
        sp32 = [fw.sb([128, 512], F32, "sp32_%d" % i) for i in range(1)] * 2
        S32 = fw.sb([128, 512], F32, "S32")
        small = [fw.sb([128, 8], F32, "small%d" % i) for i in range(4)]
        wspT = fw.sb([128, 4, 128], BF16, "wspT")
        wsp_f = fw.sb([128, 4, 128], F32, "wsp_f")
        bsp_bc = fw.sb([128, 4, 128], F32, "bsp_bc")
        wsp00 = fw.sb([128, 4], F32, "wsp00")
        xTs = [fw.sb([128, NS], BF16, "xTs%d" % c) for c in range(8)]
        hTs = [fw.sb([128, NS], BF16, "hTs%d" % c) for c in range(8)]
        kpg = [fw.sb([128, 2, 512], F32, "kpg%d" % i) for i in range(2)]
        vpg = [fw.sb([128, 2, 512], BF16, "vpg%d" % i) for i in range(2)]
        qb_sb = fw.sb([128, 512], F32, "qb_sb")
        zs = fw.sb([128, 128], F32, "zs")
        sps = fw.sb([128, 128], F32, "sps")
        tts = fw.sb([128, 128], F32, "tts")
        atts = fw.sb([128, 128], BF16, "atts")
        oBT = fw.sb([128, 4, NS], F32, "oBT")
        gbTs = fw.sb([128, 4, NS], F32, "gbTs")
        s16 = t32

        def load_w(wap, col0, ncols, pieces=None):
            blk = wbf[ring["bf"] % 3]
            ring["bf"] += 1
            if pieces is None:
                pieces = [(col0, ncols, 0)]
            n_tot = max(d + n for (_, n, d) in pieces)
            for kq in range(4):
                stg = wst[ring["st"] % 4]
                ring["st"] += 1
                for (sc, n, dc) in pieces:
                    src = wap[:, sc:sc + n].rearrange("(k p) c -> p k c", p=128)
                    dma("sp", (lambda kq, stg, src, n, dc: lambda e: e.dma_start(out=stg[:, :, dc:dc + n],
                                                                                 in_=src[:, 2 * kq:2 * kq + 2, :]))(kq, stg, src, n, dc),
                        writes=[stg])
                if DBG.get("castact") and kq % 2 == 1:
                    op("act", (lambda kq, stg: lambda e: e.activation(out=blk[:, 2 * kq:2 * kq + 2, 0:n_tot], in_=stg[:, :, 0:n_tot],
                                                                      func=AF.Copy))(kq, stg), [stg], [blk])
                else:
                    op("dve", (lambda kq, stg: lambda e: e.tensor_copy(out=blk[:, 2 * kq:2 * kq + 2, 0:n_tot], in_=stg[:, :, 0:n_tot]))(kq, stg),
                       [stg], [blk])
            return blk

        def fm_mm(ps, blk, ncols, xTl, ntok):
            for k in range(8):
                op("pe", (lambda k: lambda e: e.matmul(ps[0:ncols, 0:ntok], lhsT=blk[:, k, 0:ncols],
                                                       rhs=xTl[k][:, 0:ntok], start=(k == 0), stop=(k == 7)))(k),
                   [blk, xTl[k]], [ps])

        def tm_mm(ps, pcol, blk, ncols, xTl, tok0, ntok):
            for k in range(8):
                op("pe", (lambda k: lambda e: e.matmul(ps[0:ntok, pcol:pcol + ncols], lhsT=xTl[k][:, tok0:tok0 + ntok],
                                                       rhs=blk[:, k, 0:ncols], start=(k == 0), stop=(k == 7)))(k),
                   [blk, xTl[k]], [ps])

        def make_xT(T):
            if T < 4:
                for jj in range(4):
                    src = xtok[4 * T + jj]
                    for half in range(2):
                        ps = nxt("A")
                        for cc in range(4):
                            c = half * 4 + cc
                            op("pe", (lambda c, cc, ps, src: lambda e: e.transpose(
                                out=ps[:, cc * 128:(cc + 1) * 128], in_=src[:, c * 128:(c + 1) * 128],
                                identity=C["c_ident"][:]))(c, cc, ps, src), [src, C["c_ident"]], [ps])
                        for cc in range(4):
                            c = half * 4 + cc
                            if DBG.get("xv", 0) == 2:
                                continue
                            ev_eng = "dve" if (cc % 2 == 0 or True) else "act"
                            op(ev_eng,
                               (lambda c, cc, ps, jj, ev_eng: lambda e: (e.tensor_copy(out=xT[c][:, jj * 128:(jj + 1) * 128],
                                                                               in_=ps[:, cc * 128:(cc + 1) * 128])
                                                                 if ev_eng == "dve" else
                                                                 e.activation(out=xT[c][:, jj * 128:(jj + 1) * 128],
                                                                              in_=ps[:, cc * 128:(cc + 1) * 128],
                                                                              func=AF.Copy)))(c, cc, ps, jj, ev_eng),
                               [ps], [xT[c]])
                return xT, 512
            src = xtok[16]
            for half in range(2):
                ps = nxt("A")
                for cc in range(4):
                    c = half * 4 + cc
                    op("pe", (lambda c, cc, ps: lambda e: e.transpose(
                        out=ps[:, cc * 128:cc * 128 + NS], in_=src[0:NS, c * 128:(c + 1) * 128],
                        identity=C["c_ident"][0:NS, 0:NS]))(c, cc, ps), [src, C["c_ident"]], [ps])
                for cc in range(4):
                    c = half * 4 + cc
                    op("dve", (lambda c, cc, ps: lambda e: e.tensor_copy(out=xTs[c][:, :], in_=ps[:, cc * 128:cc * 128 + NS]))(c, cc, ps),
                       [ps], [xTs[c]])
            return xTs, NS

        def gelu_inplace(x, n, p=128):
            tmp = t32[5] if n <= 512 else big32[1]
            op("dve", lambda e: e.tensor_tensor(out=tmp[0:p, 0:n], in0=x[0:p, 0:n], in1=x[0:p, 0:n], op=ALU.mult), [x], [tmp])
            op("dve", lambda e: e.tensor_scalar(out=tmp[0:p, 0:n], in0=tmp[0:p, 0:n], scalar1=0.044715, scalar2=1.0,
                                                op0=ALU.mult, op1=ALU.add), [tmp], [tmp])
            op("dve", lambda e: e.tensor_tensor(out=tmp[0:p, 0:n], in0=tmp[0:p, 0:n], in1=x[0:p, 0:n], op=ALU.mult), [tmp, x], [tmp])
            op("act", lambda e: e.activation(out=tmp[0:p, 0:n], in_=tmp[0:p, 0:n], func=AF.Sigmoid, scale=1.5957691216057308),
               [tmp], [tmp])
            op("dve", lambda e: e.tensor_tensor(out=x[0:p, 0:n], in0=x[0:p, 0:n], in1=tmp[0:p, 0:n], op=ALU.mult), [x, tmp], [x])

        def silu_from_psum(dst, ps, p0, p1, n):
            op("act", lambda e: e.activation(out=dst[p0:p1, 0:n], in_=ps[p0:p1, 0:n], func=AF.Sigmoid), [ps], [dst])
            op("dve", lambda e: e.tensor_tensor(out=dst[p0:p1, 0:n], in0=dst[p0:p1, 0:n], in1=ps[p0:p1, 0:n], op=ALU.mult),
               [dst, ps], [dst])

        def layer_norm_block(layer, j, hTl, tok0, ntok):
            xb = xtok[j]
            r = big32[0]
            for half in range(2):
                py = psY[half]
                for cbq in range(4):
                    cb = half * 4 + cbq
                    blk = load_w(wo_d[layer], cb * 128, 128)
                    for f in range(8):
                        op("pe", (lambda f, py, cbq, blk: lambda e: e.matmul(py[0:ntok, cbq * 128:(cbq + 1) * 128],
                                                                             lhsT=hTl[f][:, tok0:tok0 + ntok], rhs=blk[:, f, :],
                                                                             start=(f == 0), stop=(f == 7)))(f, py, cbq, blk),
                           [hTl[f], blk], [py])
                op("dve", (lambda py, half: lambda e: e.scalar_tensor_tensor(
                    out=r[0:ntok, half * 512:(half + 1) * 512], in0=xb[0:ntok, half * 512:(half + 1) * 512],
                    scalar=DN_ALPHA, in1=py[0:ntok, :], op0=ALU.mult, op1=ALU.add))(py, half), [xb, py], [r])
            sq = big32[1]
            st_ = small[0]
            op("dve", lambda e: e.tensor_reduce(out=st_[0:ntok, 0:1], in_=r[0:ntok, 0:DM], axis=AX.X, op=ALU.add), [r], [st_])
            op("pool", lambda e: e.tensor_tensor(out=sq[0:ntok, 0:DM], in0=r[0:ntok, 0:DM], in1=r[0:ntok, 0:DM], op=ALU.mult), [r], [sq])
            op("dve", lambda e: e.tensor_reduce(out=st_[0:ntok, 1:2], in_=sq[0:ntok, 0:DM], axis=AX.X, op=ALU.add), [sq], [st_])
            op("dve", lambda e: e.tensor_scalar(out=st_[0:ntok, 0:2], in0=st_[0:ntok, 0:2], scalar1=1.0 / DM, scalar2=None,
                                                op0=ALU.mult), [st_], [st_])
            op("dve", lambda e: e.tensor_tensor(out=st_[0:ntok, 2:3], in0=st_[0:ntok, 0:1], in1=st_[0:ntok, 0:1], op=ALU.mult), [st_], [st_])
            op("dve", lambda e: e.tensor_tensor(out=st_[0:ntok, 3:4], in0=st_[0:ntok, 1:2], in1=st_[0:ntok, 2:3], op=ALU.subtract), [st_], [st_])
            op("dve", lambda e: e.tensor_scalar(out=st_[0:ntok, 4:5], in0=st_[0:ntok, 3:4], scalar1=LN_EPS, scalar2=None, op0=ALU.add), [st_], [st_])
            op("act", lambda e: e.activation(out=st_[0:ntok, 4:5], in_=st_[0:ntok, 4:5], func=AF.Sqrt), [st_], [st_])
            op("dve", lambda e: e.reciprocal(out=st_[0:ntok, 4:5], in_=st_[0:ntok, 4:5]), [st_], [st_])
            op("dve", lambda e: e.tensor_scalar(out=r[0:ntok, 0:DM], in0=r[0:ntok, 0:DM], scalar1=st_[0:ntok, 0:1],
                                                scalar2=st_[0:ntok, 4:5], op0=ALU.subtract, op1=ALU.mult), [r, st_], [r])
            op("pool", lambda e: e.tensor_tensor(out=r[0:ntok, 0:DM], in0=r[0:ntok, 0:DM], in1=lng_sb[0:ntok, :], op=ALU.mult),
               [r, lng_sb], [r])
            op("dve", lambda e: e.tensor_tensor(out=xb[0:ntok, :], in0=r[0:ntok, 0:DM], in1=lnb_sb[0:ntok, :], op=ALU.add),
               [r, lnb_sb], [xb])

        def layer_setup(layer):
            dma("sp", lambda e: e.dma_start(out=lng_sb[:], in_=lng_d[layer:layer + 1, :].to_broadcast([128, DM])), writes=[lng_sb])
            dma("sp", lambda e: e.dma_start(out=lnb_sb[:], in_=lnb_d[layer:layer + 1, :].to_broadcast([128, DM])), writes=[lnb_sb])

        mpos = C["c_m2"]
        def even_setup(j):
            dma("sp", lambda e: e.dma_start(out=wsp_f[:], in_=wsp_d[j].rearrange("g i k -> i g k")), writes=[wsp_f])
            for g in range(4):
                op("pool", (lambda g: lambda e: e.affine_select(out=wsp_f[:, g, :], in_=wsp_f[:, g, :], pattern=[[-1, 128]],
                                                                compare_op=ALU.is_ge, fill=0.0, base=0,
                                                                channel_multiplier=1))(g), [wsp_f], [wsp_f])
            ps = nxt("A")
            for g in range(4):
                op("pe", (lambda g: lambda e: e.transpose(out=ps[:, g * 128:(g + 1) * 128], in_=wsp_f[:, g, :],
                                                          identity=C["c_ident"][:]))(g), [wsp_f, C["c_ident"]], [ps])
            op("dve", lambda e: e.tensor_copy(out=wspT[:].rearrange("p g k -> p (g k)"), in_=ps[:, :]), [ps], [wspT])
            dma("sp", lambda e: e.dma_start(out=bsp_bc[:].rearrange("p g k -> p (g k)"),
                                            in_=bsp_d[j:j + 1].rearrange("o g k -> o (g k)").to_broadcast([128, 512])),
                writes=[bsp_bc])
            dma("sp", lambda e: e.dma_start(out=wsp00[:], in_=wsp_d[j, :, 0:1, 0:1].rearrange("g a b -> (a b) g").to_broadcast([128, 4])),
                writes=[wsp00])

        RR = (lambda ap: ap.bitcast(F32R)) if DBG.get("f32r", 0) else (lambda ap: ap)

        def stick_break_head(T, c, hh, po, hb):
            base = 64 * hh
            kbs = list(range(4 * T + 3, -1, -1))
            op("pe", lambda e: e.matmul(po[:, 0:512], lhsT=zero_b[:, :], rhs=xT[0][:, 0:512], start=True, stop=False), [zero_b, xT[0]], [po])
            def step(bi, kb):
                m = kb - 4 * T
                c0 = max(0, m) * 128
                Tk, kk = kb // 4, kb % 4
                pz, ee, sp, at, S32 = hb["pz"], hb["ee"], hb["sp"], hb["at"], hb["S"]
                ksrc = kTt[Tk]
                op("pe", lambda e: e.matmul(pz[:, c0:512], lhsT=ksrc[base:base + 64, c, kk * 128:(kk + 1) * 128],
                                            rhs=qT[c][base:base + 64, c0:512], start=True, stop=True), [ksrc, qT[c]], [pz])
                if m >= 0:
                    op("pe", lambda e: e.matmul(pz[:, c0:c0 + 128], lhsT=ident_b[:], rhs=negm_b[:], start=False, stop=True),
                       [ident_b, negm_b], [pz])
                op("act", lambda e: e.activation(out=ee[:, c0:512], in_=pz[:, c0:512], func=AF.Exp), [pz], [ee])
                op("act", lambda e: e.activation(out=RR(sp[:, c0:512]), in_=ee[:, c0:512], func=AF.Ln, bias=1.0), [ee], [sp])
                if DBG.get("f32r", 0):
                    op("pe", lambda e: e.matmul(pz[:, c0:512], lhsT=ntri_r[:], rhs=sp[:, c0:512].bitcast(F32R),
                                                start=False, stop=True), [ntri_r, sp], [pz])
                    if bi > 0:
                        op("pe", lambda e: e.matmul(pz[:, c0:512], lhsT=nones_r[:], rhs=S32[:, c0:512].bitcast(F32R),
                                                    start=False, stop=True), [nones_r, S32], [pz])
                else:
                    op("pe", lambda e: e.matmul(pz[:, c0:512], lhsT=C["c_ntri"][:], rhs=sp[:, c0:512], start=False, stop=True),
                       [C["c_ntri"], sp], [pz])
                    if bi > 0:
                        op("pe", lambda e: e.matmul(pz[:, c0:512], lhsT=C["c_nones"][:], rhs=S32[:, c0:512], start=False, stop=True),
                           [C["c_nones"], S32], [pz])
                op("act", lambda e: e.activation(out=at[:, c0:512], in_=pz[:, c0:512], func=AF.Exp), [pz], [at])
                if bi == 0:
                    if c0 > 0:
                        op("pool", lambda e: e.memset(S32[:, 0:c0], 0.0), [], [S32])
                    op("pool", lambda e: e.tensor_copy(out=RR(S32[:, c0:512]), in_=sp[:, c0:512]), [sp], [S32])
                elif bi < len(kbs) - 1:
                    op("pool", lambda e: e.tensor_tensor(out=RR(S32[:, c0:512]), in0=S32[:, c0:512], in1=sp[:, c0:512], op=ALU.add),
                       [S32, sp], [S32])
                vsrc = vbt[Tk]
                last = (kb == 0)
                if m >= 0:
                    op("pe", lambda e: e.matmul(po[:, c0:c0 + 128], lhsT=vsrc[:, kk, c * 128:(c + 1) * 128], rhs=at[:, c0:c0 + 128],
                                                start=False, stop=last), [vsrc, at], [po])
                    if c0 + 128 < 512:
                        op("pe", lambda e: e.matmul(po[:, c0 + 128:512], lhsT=vsrc[:, kk, c * 128:(c + 1) * 128],
                                                    rhs=at[:, c0 + 128:512], start=False, stop=last), [vsrc, at], [po])
                else:
                    op("pe", lambda e: e.matmul(po[:, 0:512], lhsT=vsrc[:, kk, c * 128:(c + 1) * 128], rhs=at[:, 0:512],
                                                start=False, stop=last), [vsrc, at], [po])

            for bi, kb in enumerate(kbs):
                step(bi, kb)
                yield

        def even_prompt_tile(layer, j, T):
            W = wie_d[j]
            t0 = T * 512
            if DBG.get("stage", 9) < 1:
                return
            xTl, ntok = make_xT(T)
            sub = DBG.get("sub", 9)
            if sub < 1:
                return
            for c in range(4):
                blk = load_w(W, 2048 + 128 * c, 128)
                if sub < 2:
                    continue
                ps = nxt("A")
                fm_mm(ps, blk, 128, xTl, 512)
                if sub < 3:
                    continue
                op("dve", (lambda c, ps: lambda e: e.tensor_copy(out=kTt[T][:, c, :], in_=ps[:, :]))(c, ps), [ps], [kTt[T]])
                if sub < 4:
                    continue
                ps2 = nxt("A")
                for jj in range(4):
                    tm_mm(ps2, jj * 128, blk, 128, xTl, jj * 128, 128)
                for hf in range(2):
                    op("dve", (lambda c, ps2, hf: lambda e: e.tensor_copy(
                        out=big32[hf][:, 0:2048].rearrange("p (j f) -> p j f", j=2)[:, :, 128 * c:128 * (c + 1)],
                        in_=ps2[:, hf * 256:(hf + 1) * 256].rearrange("p (j f) -> p j f", j=2)))(c, ps2, hf), [ps2], [big32[hf]])
            if sub < 5:
                return
            for c in range(4):
                blk = load_w(W, 2560 + 128 * c, 128)
                ps2 = nxt("A")
                for jj in range(4):
                    tm_mm(ps2, jj * 128, blk, 128, xTl, jj * 128, 128)
                for hf in range(2):
                    op("dve", (lambda c, ps2, hf: lambda e: e.tensor_copy(
                        out=big32[hf][:, 0:2048].rearrange("p (j f) -> p j f", j=2)[:, :, 512 + 128 * c:512 + 128 * (c + 1)],
                        in_=ps2[:, hf * 256:(hf + 1) * 256].rearrange("p (j f) -> p j f", j=2)))(c, ps2, hf), [ps2], [big32[hf]])
                op("dve", (lambda c, ps2: lambda e: e.tensor_copy(out=vbt[T][:, :, 128 * c:128 * (c + 1)],
                                                                  in_=ps2[:, :].rearrange("p (j f) -> p j f", j=4)))(c, ps2), [ps2], [vbt[T]])
            for hf in range(2):
                dma("pool", (lambda hf: lambda e: e.dma_start(
                    out=bkvp_d[j, t0 + hf * 256:t0 + (hf + 1) * 256, :].rearrange("(j p) f -> p j f", p=128),
                    in_=big32[hf][:, 0:2048].rearrange("p (j f) -> p j f", j=2)))(hf), reads=[big32[hf]])
            if DBG.get("stage", 9) < 2:
                return
            for c in range(4):
                blk = load_w(W, 1536 + 128 * c, 128)
                ps = nxt("A")
                fm_mm(ps, blk, 128, xTl, 512)
                op("dve", (lambda c, ps: lambda e: e.tensor_scalar(out=qT[c][:, :], in0=ps[:, :], scalar1=0.125, scalar2=None,
                                                                   op0=ALU.mult))(c, ps), [ps], [qT[c]])
            if DBG.get("stage", 9) < 3:
                return
            vraw = big32[0]
            for c in range(4):
                blk = load_w(W, 512 + 128 * c, 128)
                ps2 = nxt("A")
                for jj in range(4):
                    tm_mm(ps2, jj * 128, blk, 128, xTl, jj * 128, 128)
                op("dve", (lambda c, ps2: lambda e: e.tensor_copy(
                    out=vraw[:, 0:2048].rearrange("p (j f) -> p j f", j=4)[:, :, 128 * c:128 * (c + 1)],
                    in_=ps2[:, :].rearrange("p (j f) -> p j f", j=4)))(c, ps2), [ps2], [vraw])
            gelu_inplace(vraw, 2048)
            standardize(vraw, 4, 128)
            for jj in range(4):
                op("pool", (lambda jj: lambda e: e.tensor_copy(out=hT[4 + jj][:, :], in_=vraw[:, jj * 512:(jj + 1) * 512]))(jj),
                   [vraw], [hT[4 + jj]])
            for g in range(4):
                blk = load_w(W, 128 * g, 128)
                ps = nxt("A")
                fm_mm(ps, blk, 128, xTl, 512)
                ug = t32[0]
                op("dve", (lambda ps: lambda e: e.tensor_copy(out=ug[:, :], in_=ps[:, :]))(ps), [ps], [ug])
                gelu_inplace(ug, 512)
                blk = load_w(W, 1024 + 128 * g, 128)
                ps = nxt("A")
                fm_mm(ps, blk, 128, xTl, 512)
                sg = t32[1]
                silu_from_psum(sg, ps, 0, 128, 512)
                ps = nxt("A")
                for jj in range(4):
                    op("pe", (lambda g, jj, ps: lambda e: e.matmul(ps[:, jj * 128:(jj + 1) * 128], lhsT=hT[4 + jj][:, 128 * g:128 * (g + 1)],
                                                                   rhs=wspT[:, g, :], start=True, stop=True))(g, jj, ps),
                       [hT[4 + jj], wspT], [ps])
                sb_ = t32[2]
                op("dve", (lambda g, ps: lambda e: e.tensor_tensor(
                    out=sb_[:, :].rearrange("p (j i) -> p j i", j=4), in0=ps[:, :].rearrange("p (j i) -> p j i", j=4),
                    in1=bsp_bc[:, g:g + 1, :].to_broadcast([128, 4, 128]), op=ALU.add))(g, ps), [ps, bsp_bc], [sb_])
                op("pool", lambda e: e.tensor_tensor(out=sb_[:, :], in0=sb_[:, :], in1=ug[:, :], op=ALU.mult), [sb_, ug], [sb_])
                op("dve", (lambda g: lambda e: e.tensor_tensor(out=hT[g][:, :], in0=sb_[:, :], in1=sg[:, :], op=ALU.mult))(g),
                   [sb_, sg], [hT[g]])
            if DBG.get("stage", 9) < 4:
                return
            for c in range(4):
                blk = load_w(W, 3072 + 128 * c, 128)
                ps = nxt("A")
                fm_mm(ps, blk, 128, xTl, 512)
                sgb = t32[3]
                silu_from_psum(sgb, ps, 0, 128, 512)
                hbufs = [dict(pz=psZ[0], ee=e32[0], sp=sp32[0], at=att_b[0], S=S32),
                         dict(pz=psZ[1], ee=View(kpg[0], kpg[0][:, 0, :]), sp=View(kpg[1], kpg[1][:, 0, :]), at=att_b[1], S=qb_sb)]
                interleave([stick_break_head(T, c, hh, psO[hh], hbufs[hh]) for hh in range(2)])
                for hh in range(2):
                    po = psO[hh]
                    b0 = 64 * hh
                    op("dve", (lambda c, po, b0: lambda e: e.tensor_tensor(out=hT[4 + c][b0:b0 + 64, :], in0=po[b0:b0 + 64, :],
                                                                           in1=sgb[b0:b0 + 64, :], op=ALU.mult))(c, po, b0),
                       [po, sgb], [hT[4 + c]])
            if dbgh_d is not None and T == DBG.get("dbgT", 0) and DBG.get("dbgL", 0) == 0:
                for f in range(8):
                    dma("pool", (lambda f: lambda e: e.dma_start(out=dbgh_d[f], in_=hT[f][:, :]))(f), reads=[hT[f]])
            if DBG.get("stage", 9) < 5:
                return
            for jj in range(4):
                layer_norm_block(layer, 4 * T + jj, hT, jj * 128, 128)

        def standardize(x, nj, p):
            sq = big32[1]
            n = nj * 512
            st_ = small[1]
            xv = x[0:p, 0:n].rearrange("p (j f) -> p j f", j=nj)
            op("dve", lambda e: e.tensor_reduce(out=st_[0:p, 0:nj], in_=xv, axis=AX.X, op=ALU.add), [x], [st_])
            op("pool", lambda e: e.tensor_tensor(out=sq[0:p, 0:n], in0=x[0:p, 0:n], in1=x[0:p, 0:n], op=ALU.mult), [x], [sq])
            st2 = small[2]
            op("dve", lambda e: e.tensor_reduce(out=st2[0:p, 0:nj], in_=sq[0:p, 0:n].rearrange("p (j f) -> p j f", j=nj),
                                                axis=AX.X, op=ALU.add), [sq], [st2])
            op("dve", lambda e: e.tensor_scalar(out=st_[0:p, 0:nj], in0=st_[0:p, 0:nj], scalar1=1.0 / 512, scalar2=None, op0=ALU.mult),
               [st_], [st_])
            op("dve", lambda e: e.tensor_scalar(out=st2[0:p, 0:nj], in0=st2[0:p, 0:nj], scalar1=1.0 / 512, scalar2=None, op0=ALU.mult),
               [st2], [st2])
            st3 = small[3]
            op("dve", lambda e: e.tensor_tensor(out=st3[0:p, 0:nj], in0=st_[0:p, 0:nj], in1=st_[0:p, 0:nj], op=ALU.mult), [st_], [st3])
            op("dve", lambda e: e.tensor_tensor(out=st2[0:p, 0:nj], in0=st2[0:p, 0:nj], in1=st3[0:p, 0:nj], op=ALU.subtract),
               [st2, st3], [st2])
            op("dve", lambda e: e.tensor_scalar(out=st2[0:p, 0:nj], in0=st2[0:p, 0:nj], scalar1=LN_EPS, scalar2=None, op0=ALU.add), [st2], [st2])
            op("act", lambda e: e.activation(out=st2[0:p, 0:nj], in_=st2[0:p, 0:nj], func=AF.Sqrt), [st2], [st2])
            op("dve", lambda e: e.reciprocal(out=st2[0:p, 0:nj], in_=st2[0:p, 0:nj]), [st2], [st2])
            for jx in range(nj):
                op("dve", (lambda jx: lambda e: e.tensor_scalar(out=x[0:p, jx * 512:(jx + 1) * 512], in0=x[0:p, jx * 512:(jx + 1) * 512],
                                                                scalar1=st_[0:p, jx:jx + 1], scalar2=st2[0:p, jx:jx + 1],
                                                                op0=ALU.subtract, op1=ALU.mult))(jx), [x, st_, st2], [x])

        def sample_inproj(W, col0, ncols, xTl, ps):
            c0 = 0
            while c0 < ncols:
                n = min(128, ncols - c0)
                blk = load_w(W, col0 + c0, n)
                tm_mm(ps, c0, blk, n, xTl, 0, NS)
                c0 += n

        def page_val(e, b, pg):
            return e.value_load(pts[0:1, b * NPG + pg:b * NPG + pg + 1])

        def even_sample_tile(layer, j):
            W = wie_d[j]
            xTl, ntok = make_xT(4)
            for c in range(4):
                blk = load_w(W, 3072 + 128 * c, 128)
                ps = nxt("A")
                fm_mm(ps, blk, 128, xTl, NS)
                op("act", (lambda c, ps: lambda e: e.activation(out=gbTs[:, c, :], in_=ps[:, 0:NS], func=AF.Sigmoid))(c, ps), [ps], [gbTs])
                op("dve", (lambda c, ps: lambda e: e.tensor_tensor(out=gbTs[:, c, :], in0=gbTs[:, c, :], in1=ps[:, 0:NS], op=ALU.mult))(c, ps),
                   [gbTs, ps], [gbTs])
            for hf in range(2):
                psk = nxt("A")
                sample_inproj(W, 2048 + 512 * hf, 512, xTl, psk)
                op("dve", (lambda hf, psk: lambda e: e.tensor_copy(out=big32[0][0:NS, hf * 512:(hf + 1) * 512], in_=psk[0:NS, :]))(hf, psk),
                   [psk], [big32[0]])
            dma("pool", lambda e: e.dma_start(out=bkvs_d[j], in_=big32[0][0:NS, 0:1024]), reads=[big32[0]])
            v = s16[0]
            psv = nxt("A")
            sample_inproj(W, 512, 512, xTl, psv)
            op("dve", lambda e: e.tensor_copy(out=v[0:NS, :], in_=psv[0:NS, :]), [psv], [v])
            gelu_inplace(v, 512, NS)
            standardize(v, 1, NS)
            dma("pool", lambda e: e.dma_start(out=avs_d[j], in_=v[0:NS, :]), reads=[v])
            u = s16[1]
            psu = nxt("A")
            sample_inproj(W, 0, 512, xTl, psu)
            op("dve", lambda e: e.tensor_copy(out=u[0:NS, :], in_=psu[0:NS, :]), [psu], [u])
            gelu_inplace(u, 512, NS)
            ga = s16[2]
            psg = nxt("A")
            sample_inproj(W, 1024, 512, xTl, psg)
            silu_from_psum(ga, psg, 0, NS, 512)
            sv = s16[3]
            op("dve", lambda e: e.tensor_tensor(out=sv[0:NS, :].rearrange("p (g f) -> p g f", g=4),
                                                in0=v[0:NS, :].rearrange("p (g f) -> p g f", g=4),
                                                in1=wsp00[0:NS, :].unsqueeze(2).to_broadcast([NS, 4, 128]), op=ALU.mult), [v, wsp00], [sv])
            op("dve", lambda e: e.tensor_tensor(out=sv[0:NS, :].rearrange("p (g f) -> p g f", g=4),
                                                in0=sv[0:NS, :].rearrange("p (g f) -> p g f", g=4),
                                                in1=bsp_bc[0:NS, :, 0:1].to_broadcast([NS, 4, 128]), op=ALU.add), [sv, bsp_bc], [sv])
            op("dve", lambda e: e.tensor_tensor(out=sv[0:NS, :], in0=sv[0:NS, :], in1=u[0:NS, :], op=ALU.mult), [sv, u], [sv])
            op("dve", lambda e: e.tensor_tensor(out=sv[0:NS, :], in0=sv[0:NS, :], in1=ga[0:NS, :], op=ALU.mult), [sv, ga], [sv])
            ps = nxt("A")
            for g in range(4):
                op("pe", (lambda g, ps: lambda e: e.transpose(out=ps[:, g * 128:g * 128 + NS], in_=sv[0:NS, g * 128:(g + 1) * 128],
                                                              identity=C["c_ident"][0:NS, 0:NS]))(g, ps), [sv, C["c_ident"]], [ps])
            for g in range(4):
                op("dve", (lambda g, ps: lambda e: e.tensor_copy(out=hTs[g][:, :], in_=ps[:, g * 128:g * 128 + NS]))(g, ps), [ps], [hTs[g]])
            qsc = s16[4]
            psq = nxt("A")
            sample_inproj(W, 1536, 512, xTl, psq)
            op("dve", lambda e: e.tensor_scalar(out=qsc[0:NS, :], in0=psq[0:NS, :], scalar1=0.125, scalar2=None, op0=ALU.mult), [psq], [qsc])
            pso = psO[0]
            op("pe", lambda e: e.matmul(pso[:, 0:512], lhsT=zero_b[:, :], rhs=xT[0][:, 0:512], start=True, stop=False), [zero_b, xT[0]], [pso])
            for b in range(NS):
                pq = nxt("A")
                qm = t32[5]
                op("dve", (lambda b: lambda e: e.tensor_scalar(out=qm[0:NS, :], in0=qsc[0:NS, :], scalar1=C["c_ident"][0:NS, b:b + 1],
                                                               scalar2=-1.0, op0=ALU.mult, op1=ALU.mult))(b), [qsc, C["c_ident"]], [qm])
                op("pe", (lambda b, pq: lambda e: e.matmul(pq[:, :], lhsT=C["c_nones"][0:NS, :], rhs=qm[0:NS, :],
                                                           start=True, stop=True))(b, pq), [C["c_nones"], qm], [pq])
                op("dve", (lambda pq: lambda e: e.tensor_copy(out=qb_sb[:, :], in_=pq[:, :]))(pq), [pq], [qb_sb])
                for pgq in range(8):
                    kb_ = kpg[pgq % 2]
                    for i in range(2):
                        pg = pgq * 2 + i
                        dma("pool", (lambda b, pg, i, kb_: lambda e: e.indirect_dma_start(
                            out=kb_[:, i, :], out_offset=None, in_=cb_d[:, 0:512], element_offset=j * pool_rows * 1024,
                            in_offset=bass.IndirectOffsetOnAxis(ap=idxB[:, b * NPG + pg:b * NPG + pg + 1], axis=0)))(b, pg, i, kb_),
                            reads=[idxB], writes=[kb_])
                    prod = big32[1]
                    op("dve", (lambda kb_: lambda e: e.tensor_tensor(
                        out=prod[:, 0:1024].rearrange("p (i f) -> p i f", i=2), in0=kb_[:, :, :],
                        in1=qb_sb[:, :].unsqueeze(1).to_broadcast([128, 2, 512]), op=ALU.mult))(kb_), [kb_, qb_sb], [prod])
                    op("dve", (lambda pgq: lambda e: e.tensor_reduce(
                        out=zs[:, pgq * 16:(pgq + 1) * 16], in_=prod[:, 0:1024].rearrange("p (a d) -> p a d", d=64), axis=AX.X,
                        op=ALU.add))(pgq), [prod], [zs])
                op("act", lambda e: e.activation(out=sps[:, :], in_=zs[:, :], func=AF.Exp), [zs], [sps])
                op("act", lambda e: e.activation(out=sps[:, :], in_=sps[:, :], func=AF.Ln, bias=1.0), [sps], [sps])
                pt_ = nxt("Z")
                op("pe", (lambda pt_: lambda e: e.matmul(pt_[:, 0:128], lhsT=sps[:, :], rhs=C["c_nones"][:, :], start=True, stop=True))(pt_),
                   [sps, C["c_nones"]], [pt_])
                op("dve", (lambda pt_: lambda e: e.tensor_copy(out=tts[:, :], in_=pt_[:, 0:128]))(pt_), [pt_], [tts])
                pz = nxt("Z")
                op("pe", (lambda pz: lambda e: e.matmul(pz[:, 0:128], lhsT=C["c_ident"][:, :], rhs=zs[:, :], start=True, stop=True))(pz),
                   [C["c_ident"], zs], [pz])
                op("pe", (lambda pz: lambda e: e.matmul(pz[:, 0:128], lhsT=C["c_ntri"][:, :], rhs=sps[:, :], start=False, stop=True))(pz),
                   [C["c_ntri"], sps], [pz])
                op("pe", (lambda pz: lambda e: e.matmul(pz[:, 0:128], lhsT=tts[:, :], rhs=mpos[:, :], start=False, stop=True))(pz),
                   [tts, mpos], [pz])
                op("act", (lambda pz: lambda e: e.activation(out=atts[:, :], in_=pz[:, 0:128], func=AF.Exp))(pz), [pz], [atts])
                for pgq in range(8):
                    vb_ = vpg[pgq % 2]
                    for i in range(2):
                        pg = pgq * 2 + i
                        dma("pool", (lambda b, pg, i, vb_: lambda e: e.indirect_dma_start(
                            out=vb_[:, i, :], out_offset=None, in_=cb_d[:, 0:512], element_offset=j * pool_rows * 1024 + 512,
                            in_offset=bass.IndirectOffsetOnAxis(ap=idxB[:, b * NPG + pg:b * NPG + pg + 1], axis=0)))(b, pg, i, vb_),
                            reads=[idxB], writes=[vb_])
                    for i in range(2):
                        pg = pgq * 2 + i
                        for c in range(4):
                            op("pe", (lambda b, pg, i, c, vb_: lambda e: e.matmul(
                                pso[:, c * 128 + b * 8:c * 128 + b * 8 + 8], lhsT=vb_[:, i, c * 128:(c + 1) * 128],
                                rhs=atts[:, pg * 8:(pg + 1) * 8], start=False, stop=(pg == 15)))(b, pg, i, c, vb_),
                               [vb_, atts], [pso])
            for c in range(4):
                src = pso[:, c * 128:(c + 1) * 128].rearrange("p (b h) -> p b h", h=8)
                op("dve", (lambda c, src: lambda e: e.tensor_copy(out=oBT[0:64, c, :], in_=src[0:64, :, 2 * c]))(c, src), [pso], [oBT])
                op("dve", (lambda c, src: lambda e: e.tensor_copy(out=oBT[64:128, c, :], in_=src[64:128, :, 2 * c + 1]))(c, src), [pso], [oBT])
            for c in range(4):
                op("dve", (lambda c: lambda e: e.tensor_tensor(out=hTs[4 + c][:, :], in0=oBT[:, c, :], in1=gbTs[:, c, :], op=ALU.mult))(c),
                   [oBT, gbTs], [hTs[4 + c]])
            layer_norm_block(layer, 16, hTs, 0, NS)

        mpos = C["c_m2"]


        ISQ = 1.0 / math.sqrt(32.0)
        maskT = kTt[3]
        b31_sb = small[3]

        def odd_setup(j):
            for i in range(4):
                op("pool", (lambda i: lambda e: e.memset(vbt[i][:, :, :], 1.0))(i), [], [vbt[i]])
            dma("sp", lambda e: e.dma_start(out=qb_sb[:, :].rearrange("p (g d) -> p g d", g=4), in_=wgrp_d[j].rearrange("g c d -> c g d")),
                writes=[qb_sb])
            op("dve", lambda e: e.tensor_copy(out=wspT[:].rearrange("p g d -> p (g d)"), in_=qb_sb[:, :]), [qb_sb], [wspT])
            dma("sp", lambda e: e.dma_start(out=wsp00[:, :], in_=dsc_d[j:j + 1, :].rearrange("o (g p) -> p (o g)", p=128)), writes=[wsp00])
            dma("sp", lambda e: e.dma_start(out=b31_sb[:, 0:8], in_=b31_d.to_broadcast([128, 8])), writes=[b31_sb])
            op("pool", lambda e: e.memset(wsp_f[:, :, :], 0.0), [], [wsp_f])

        def odd_qblock(T, jj, idx, wk):
            qb = 4 * T + jj
            Wk = (qb + 1) * 128
            wq = bsp_bc
            nkc = (Wk + 511) // 512
            for kc in range(nkc):
                n = min(512, Wk - kc * 512)
                for ih in range(4):
                    pz = nxt("Z")
                    op("pe", (lambda kc, n, ih, pz: lambda e: e.matmul(pz[:, 0:n], lhsT=vpg[ih // 2][:, ih % 2, jj * 128:(jj + 1) * 128],
                                                                       rhs=kTt[2][:, kc, 0:n], start=True, stop=True))(kc, n, ih, pz),
                       [vpg[ih // 2], kTt[2]], [pz])
                    rl = e32[0]
                    op("act", (lambda n, pz: lambda e: e.activation(out=rl[:, 0:n], in_=pz[:, 0:n], func=AF.Relu))(n, pz), [pz], [rl])
                    if ih == 0:
                        op("dve", (lambda kc, n: lambda e: e.tensor_scalar(out=idx[:, kc * 512:kc * 512 + n], in0=rl[:, 0:n],
                                                                           scalar1=wq[:, jj, 0:1], scalar2=None, op0=ALU.mult))(kc, n),
                           [rl, wq], [idx])
                    else:
                        op("dve", (lambda kc, n, ih: lambda e: e.scalar_tensor_tensor(
                            out=idx[:, kc * 512:kc * 512 + n], in0=rl[:, 0:n], scalar=wq[:, jj, ih:ih + 1],
                            in1=idx[:, kc * 512:kc * 512 + n], op0=ALU.mult, op1=ALU.add))(kc, n, ih), [rl, wq, idx], [idx])
            mT = maskT[:, :, :].rearrange("p a (b q) -> p (a b) q", q=128)
            if qb < 2:
                for kb in range(qb):
                    op("pool", (lambda kb: lambda e: e.memset(mT[:, kb, :], 1.0))(kb), [], [maskT])
                op("dve", lambda e: e.tensor_copy(out=mT[:, qb, :], in_=C["c_dsac"][:, :]), [C["c_dsac"]], [maskT])
            else:
                op("dve", lambda e: e.tensor_tensor(out=idx[:, qb * 128:Wk], in0=idx[:, qb * 128:Wk], in1=C["c_negq"][:, :], op=ALU.add),
                   [idx, C["c_negq"]], [idx])
                op("pool", lambda e: e.tensor_copy(out=wk[:, 0:Wk], in_=idx[:, 0:Wk]), [idx], [wk])
                mx = small[1]
                for r in range(32):
                    op("dve", lambda e: e.max(out=mx[:, 0:8], in_=wk[:, 0:Wk]), [wk], [mx])
                    op("dve", lambda e: e.match_replace(out=wk[:, 0:Wk], in_to_replace=mx[:, 0:8], in_values=wk[:, 0:Wk], imm_value=2.0 * NEG),
                       [wk, mx], [wk])
                op("dve", lambda e: e.tensor_tensor(out=wk[:, 0:Wk], in0=wk[:, 0:Wk], in1=idx[:, 0:Wk], op=ALU.not_equal), [wk, idx], [wk])
                for k4 in range(0, qb + 1, 4):
                    nb = min(4, qb + 1 - k4)
                    pt_ = nxt("A")
                    for i in range(nb):
                        op("pe", (lambda k4, i, pt_: lambda e: e.transpose(out=pt_[:, i * 128:(i + 1) * 128],
                                                                           in_=wk[:, (k4 + i) * 128:(k4 + i + 1) * 128],
                                                                           identity=C["c_ident"][:]))(k4, i, pt_), [wk, C["c_ident"]], [pt_])
                    op("dve", (lambda k4, nb, pt_: lambda e: e.tensor_copy(out=mT[:, k4:k4 + nb, :],
                                                                           in_=pt_[:, 0:nb * 128].rearrange("p (b q) -> p b q", q=128)))(k4, nb, pt_),
                       [pt_], [maskT])
            obufs = [dict(po=psO[0], pz=psZ[0], at=att_b[0], tz=[sps, tts], lt=sp32[0], rz=zs),
                     dict(po=psO[1], pz=psZ[1], at=att_b[1], tz=[View(kpg[0], kpg[0][:, 0, 0:128]), View(kpg[0], kpg[0][:, 0, 128:256])],
                          lt=View(kpg[1], kpg[1][:, 0, 0:128]), rz=View(qb_sb, qb_sb[:, 0:128]))]
            for h2 in range(0, 8, 2):
                interleave([odd_head(T, jj, qb, h2 + i, mT, obufs[i]) for i in range(2)])

        def odd_head(T, jj, qb, h, mT, ob_):
            g, c, base = h // 4, h // 2, 64 * (h % 2)
            po = ob_["po"]
            op("pe", lambda e: e.matmul(po[:, 0:128], lhsT=zero_b[:, :], rhs=xT[0][:, 0:128], start=True, stop=False), [zero_b, xT[0]], [po])
            vA = vbt[g] if base == 0 else vbt[2 + g]
            tz = ob_["tz"]
            zs_ = ob_["rz"]
            if True:
                for dl in range(2):
                    dma("sp", (lambda dl: lambda e: e.dma_start(out=tz[dl][:, 0:128], in_=tz_d[:, (dl * 8 + h) * 128:(dl * 8 + h + 1) * 128]))(dl),
                        writes=[tz[dl]])

            def step(kb):
                Tk, kk = kb // 4, kb % 4
                pz = ob_["pz"]
                at = ob_["at"]
                op("pe", lambda e: e.matmul(pz[:, 0:128], lhsT=kTt[g][base:base + 64, Tk, kk * 128:(kk + 1) * 128],
                                            rhs=qT[c][base:base + 64, jj * 128:(jj + 1) * 128], start=True, stop=True), [kTt[g], qT[c]], [pz])
                delta = qb - kb
                if delta >= 2:
                    op("act", lambda e: e.activation(out=at[:, 0:128], in_=pz[:, 0:128], func=AF.Exp, bias=b31_sb[:, h:h + 1]), [pz, b31_sb], [at])
                else:
                    lt = ob_["lt"]
                    op("dve", lambda e: e.tensor_tensor(out=lt[:, 0:128], in0=pz[:, 0:128], in1=tz[delta][:, 0:128], op=ALU.add), [pz, tz[delta]], [lt])
                    op("act", lambda e: e.activation(out=at[:, 0:128], in_=lt[:, 0:128], func=AF.Exp), [lt], [at])
                op("dve", lambda e: e.tensor_tensor(out=at[:, 0:128], in0=at[:, 0:128], in1=mT[:, kb, :], op=ALU.mult), [at, maskT], [at])
                op("pe", lambda e: e.matmul(po[:, 0:128], lhsT=vA[:, Tk, kk * 128:(kk + 1) * 128], rhs=at[:, 0:128], start=False, stop=(kb == qb)),
                   [vA, at], [po])

            for kb in range(qb + 1):
                step(kb)
                yield
            ob = 64 - base
            op("dve", lambda e: e.reciprocal(out=zs_[base:base + 64, 0:128], in_=po[ob:ob + 64, 0:128]), [po], [zs_])
            op("dve", lambda e: e.tensor_tensor(out=t32[c][base:base + 64, jj * 128:(jj + 1) * 128], in0=po[base:base + 64, 0:128],
                                                in1=zs_[base:base + 64, 0:128], op=ALU.mult), [po, zs_], [t32[c]])

        def odd_prompt_tile(layer, j, T):
            W = wio_d[j]
            xTl, _ = make_xT(T)
            for c in range(4):
                blk = load_w(W, 128 * c, 128)
                ps = nxt("A")
                fm_mm(ps, blk, 128, xTl, 512)
                op("dve", (lambda c, ps: lambda e: e.tensor_scalar(out=qT[c][:, :], in0=ps[:, :], scalar1=0.125, scalar2=None,
                                                                   op0=ALU.mult))(c, ps), [ps], [qT[c]])
            for g in range(2):
                blk = load_w(W, 0, 0, pieces=[(512 + 64 * g, 64, 0), (512 + 64 * g, 64, 64)])
                ps = nxt("A")
                fm_mm(ps, blk, 128, xTl, 512)
                op("dve", (lambda g, ps: lambda e: e.tensor_copy(out=kTt[g][:, T, :], in_=ps[:, :]))(g, ps), [ps], [kTt[g]])
            blk = load_w(W, 0, 0, pieces=[(896, 32, 0), (896, 32, 32), (896, 32, 64), (896, 32, 96)])
            ps = nxt("A")
            fm_mm(ps, blk, 128, xTl, 512)
            op("dve", (lambda ps: lambda e: e.tensor_copy(out=kTt[2][:, T, :], in_=ps[:, :]))(ps), [ps], [kTt[2]])
            blk = load_w(W, 768, 128)
            ps = nxt("A")
            fm_mm(ps, blk, 128, xTl, 512)
            for ih in range(4):
                op("dve", (lambda ih, ps: lambda e: e.tensor_scalar(out=vpg[ih // 2][:, ih % 2, :], in0=ps[:, :],
                                                                    scalar1=C["c_rowm"][:, ih:ih + 1], scalar2=None, op0=ALU.mult))(ih, ps),
                   [ps, C["c_rowm"]], [vpg[ih // 2]])
            pstm = [psA[0], psA[1], psY[0], psY[1]]
            for (c0, n, pc) in ((512, 128, 0), (640, 128, 128), (896, 36, 256)):
                blk = load_w(W, c0, n)
                for jj in range(4):
                    tm_mm(pstm[jj], pc, blk, n, xTl, jj * 128, 128)
            rowbuf = t32[5]
            for jj in range(4):
                tok0 = T * 512 + jj * 128
                op("dve", (lambda jj: lambda e: e.tensor_copy(out=rowbuf[:, 0:292], in_=pstm[jj][:, 0:292]))(jj), [pstm[jj]], [rowbuf])
                dma("pool", (lambda tok0: lambda e: e.dma_start(out=ckvp_d[j, tok0:tok0 + 128, :], in_=rowbuf[:, 0:256]))(tok0), reads=[rowbuf])
                dma("pool", (lambda tok0: lambda e: e.dma_start(out=ckip_d[j, tok0:tok0 + 128, :], in_=rowbuf[:, 256:288]))(tok0), reads=[rowbuf])
                for g in range(2):
                    op("dve", (lambda g, jj: lambda e: e.tensor_copy(out=vbt[g][:, T, jj * 128:jj * 128 + 64],
                                                                    in_=rowbuf[:, 128 + 64 * g:192 + 64 * g]))(g, jj), [rowbuf], [vbt[g]])
                    op("pool", (lambda g, jj: lambda e: e.tensor_copy(out=vbt[2 + g][:, T, jj * 128 + 64:jj * 128 + 128],
                                                                     in_=rowbuf[:, 128 + 64 * g:192 + 64 * g]))(g, jj), [rowbuf], [vbt[2 + g]])
                op("dve", (lambda jj: lambda e: e.tensor_scalar(out=bsp_bc[:, jj, 0:4], in0=rowbuf[:, 288:292], scalar1=ISQ, scalar2=None,
                                                                op0=ALU.mult))(jj), [rowbuf], [bsp_bc])
            idx, wk = big32[0], big32[1]
            if DBG.get("ostage", 9) >= 1:
                for jj in range(4):
                    odd_qblock(T, jj, idx, wk)
            else:
                for c in range(4):
                    op("pool", (lambda c: lambda e: e.memset(t32[c][:, :], 0.0))(c), [], [t32[c]])
            for c in range(4):
                blk = load_w(W, 932 + 128 * c, 128)
                ps = nxt("A")
                fm_mm(ps, blk, 128, xTl, 512)
                sg = t32[4]
                silu_from_psum(sg, ps, 0, 128, 512)
                op("dve", (lambda c: lambda e: e.tensor_tensor(out=hT[c][:, :], in0=t32[c][:, :], in1=sg[:, :], op=ALU.mult))(c),
                   [t32[c], sg], [hT[c]])
            Pv = kpg[0][:, :, :].rearrange("p a b -> p (a b)")
            Av = kpg[1][:, :, :].rearrange("p a b -> p (a b)")
            Bv = big32[1]
            for g in range(4):
                wwin = 2 ** (g + 1)
                blk = load_w(W, 1444 + 128 * g, 128)
                ps = nxt("A")
                fm_mm(ps, blk, 128, xTl, 512)
                op("dve", (lambda g: lambda e: e.tensor_copy(out=Pv[:, 0:16], in_=wsp_f[:, g, 0:16]))(g), [wsp_f], [kpg[0]])
                op("dve", (lambda ps: lambda e: e.tensor_copy(out=Pv[:, 16:528], in_=ps[:, :]))(ps), [ps], [kpg[0]])
                op("pool", (lambda g: lambda e: e.tensor_copy(out=wsp_f[:, g, 0:16], in_=Pv[:, 512:528]))(g), [kpg[0]], [wsp_f])
                op("dve", lambda e: e.tensor_tensor(out=Av[:, 1:528], in0=Pv[:, 1:528], in1=Pv[:, 0:527], op=ALU.add), [kpg[0]], [kpg[1]])
                Sv, Sobj = Av, kpg[1]
                if g >= 1:
                    op("dve", lambda e: e.tensor_tensor(out=Bv[:, 3:528], in0=Av[:, 3:528], in1=Av[:, 1:526], op=ALU.add), [kpg[1]], [big32[1]])
                    Sv, Sobj = Bv, big32[1]
                if g >= 2:
                    op("dve", lambda e: e.tensor_tensor(out=Av[:, 7:528], in0=Bv[:, 7:528], in1=Bv[:, 3:524], op=ALU.add), [big32[1]], [kpg[1]])
                    Sv, Sobj = Av, kpg[1]
                if g >= 3:
                    op("dve", lambda e: e.tensor_tensor(out=Bv[:, 15:528], in0=Av[:, 15:528], in1=Av[:, 7:520], op=ALU.add), [kpg[1]], [big32[1]])
                    Sv, Sobj = Bv, big32[1]
                pl = att_b[0]
                op("dve", (lambda Sv, wwin: lambda e: e.scalar_tensor_tensor(out=pl[:, :], in0=Sv[:, 16:528], scalar=1.0 / wwin, in1=Pv[:, 16:528],
                                                                             op0=ALU.mult, op1=ALU.subtract))(Sv, wwin), [Sobj, kpg[0]], [pl])
                if T == 0:
                    op("dve", (lambda Sv, g: lambda e: e.tensor_tensor(out=zs[:, 0:16], in0=Sv[:, 16:32], in1=C["c_icnt"][:, g * 16:(g + 1) * 16],
                                                                       op=ALU.mult))(Sv, g), [Sobj, C["c_icnt"]], [zs])
                    op("dve", lambda e: e.tensor_tensor(out=pl[:, 0:16], in0=zs[:, 0:16], in1=Pv[:, 16:32], op=ALU.subtract), [zs, kpg[0]], [pl])
                pm = nxt("A")
                op("pe", (lambda g, pm: lambda e: e.matmul(pm[:, :], lhsT=wspT[:, g, :], rhs=pl[:, :], start=True, stop=True))(g, pm), [wspT, pl], [pm])
                blk = load_w(W, 1956 + 128 * g, 128)
                ps = nxt("A")
                fm_mm(ps, blk, 128, xTl, 512)
                sg = t32[4]
                silu_from_psum(sg, ps, 0, 128, 512)
                op("dve", (lambda g, pm: lambda e: e.scalar_tensor_tensor(out=hT[4 + g][:, :], in0=pm[:, :], scalar=wsp00[:, g:g + 1], in1=sg[:, :],
                                                                          op0=ALU.mult, op1=ALU.mult))(g, pm), [pm, wsp00, sg], [hT[4 + g]])
            if T == 3:
                pp = nxt("A")
                for cb in range(4):
                    blk = load_w(W, 1444 + 128 * cb, 128)
                    tm_mm(pp, cb * 128, blk, 128, xTl, 384, 128)
                op("dve", (lambda pp: lambda e: e.tensor_copy(out=e32[0][:, :], in_=pp[:, :]))(pp), [pp], [e32[0]])
                dma("pool", lambda e: e.dma_start(out=dbp_d[j], in_=e32[0][113:128, :]), reads=[e32[0]])
            if dbgh_d is not None and T == DBG.get("dbgT", 0):
                for f in range(8):
                    dma("pool", (lambda f: lambda e: e.dma_start(out=dbgh_d[f], in_=hT[f][:, :]))(f), reads=[hT[f]])
            for jj in range(4):
                layer_norm_block(layer, 4 * T + jj, hT, jj * 128, 128)

        def odd_sample_tile(layer, j):
            W = wio_d[j]
            xTl, _ = make_xT(4)
            qs, rows, sgc_unused, pp_, sgd = t32[0], t32[1], t32[2], t32[3], t32[4]
            ps = nxt("A")
            sample_inproj(W, 0, 512, xTl, ps)
            op("dve", (lambda ps: lambda e: e.tensor_scalar(out=qs[0:NS, :], in0=ps[0:NS, :], scalar1=0.125, scalar2=None, op0=ALU.mult))(ps), [ps], [qs])
            ps = nxt("A")
            sample_inproj(W, 512, 420, xTl, ps)
            op("dve", (lambda ps: lambda e: e.tensor_copy(out=rows[0:NS, 0:420], in_=ps[0:NS, 0:420]))(ps), [ps], [rows])
            ps = nxt("A")
            sample_inproj(W, 1444, 512, xTl, ps)
            op("dve", (lambda ps: lambda e: e.tensor_copy(out=pp_[0:NS, :], in_=ps[0:NS, :]))(ps), [ps], [pp_])
            ps = nxt("A")
            sample_inproj(W, 1956, 512, xTl, ps)
            silu_from_psum(sgd, ps, 0, NS, 512)
            for c in range(4):
                blk = load_w(W, 932 + 128 * c, 128)
                ps = nxt("A")
                fm_mm(ps, blk, 128, xTl, NS)
                op("act", (lambda c, ps: lambda e: e.activation(out=gbTs[:, c, :], in_=ps[:, 0:NS], func=AF.Sigmoid))(c, ps), [ps], [gbTs])
                op("dve", (lambda c, ps: lambda e: e.tensor_tensor(out=gbTs[:, c, :], in0=gbTs[:, c, :], in1=ps[:, 0:NS], op=ALU.mult))(c, ps),
                   [gbTs, ps], [gbTs])
            dma("pool", lambda e: e.dma_start(out=ckvs_d[j], in_=rows[0:NS, 0:256]), reads=[rows])
            dma("pool", lambda e: e.dma_start(out=ckis_d[j], in_=rows[0:NS, 384:416]), reads=[rows])
            dma("pool", lambda e: e.dma_start(out=dbs_d[j, :, 0:14, :], in_=db_d[j, :, 1:15, :]))
            dma("pool", lambda e: e.dma_start(out=dbs_d[j, :, 14, :], in_=pp_[0:NS, :]), reads=[pp_])
            ev1 = dma("pool", lambda e: e.dma_start(out=scr_kv[:, 0:256], in_=rows[0:NS, 0:256]), reads=[rows], writes=[scrkv_b])
            ev2 = dma("pool", lambda e: e.dma_start(out=scr_kv[:, 256:288], in_=rows[0:NS, 384:416]), reads=[rows], writes=[scrkv_b2])
            G = big32[1]
            offs = [0, 128, 512, 1408]
            rs = t32[5]
            for g in range(4):
                wwin = 2 ** (g + 1)
                nr = wwin - 1
                Gb = big32[1] if g < 3 else big32[0]
                o0 = offs[g] if g < 3 else 0
                dma("sp", (lambda g, nr, Gb, o0: lambda e: e.dma_start(
                    out=Gb[0:NS, o0:o0 + nr * 128].rearrange("p (r c) -> p r c", c=128),
                    in_=db_d[j, :, 15 - nr:15, g * 128:(g + 1) * 128]))(g, nr, Gb, o0), writes=[Gb])
                op("dve", (lambda g, nr, Gb, o0: lambda e: e.tensor_reduce(
                    out=rs[0:NS, g * 128:(g + 1) * 128], in_=Gb[0:NS, o0:o0 + nr * 128].rearrange("p (r c) -> p c r", c=128),
                    axis=AX.X, op=ALU.add))(g, nr, Gb, o0), [Gb], [rs])
                op("dve", (lambda g: lambda e: e.tensor_tensor(out=rs[0:NS, g * 128:(g + 1) * 128], in0=rs[0:NS, g * 128:(g + 1) * 128],
                                                               in1=pp_[0:NS, g * 128:(g + 1) * 128], op=ALU.add))(g), [rs, pp_], [rs])
                op("dve", (lambda g, wwin: lambda e: e.scalar_tensor_tensor(
                    out=rs[0:NS, g * 128:(g + 1) * 128], in0=rs[0:NS, g * 128:(g + 1) * 128], scalar=1.0 / wwin,
                    in1=pp_[0:NS, g * 128:(g + 1) * 128], op0=ALU.mult, op1=ALU.subtract))(g, wwin), [rs, pp_], [rs])
            pt_ = nxt("A")
            for g in range(4):
                op("pe", (lambda g, pt_: lambda e: e.transpose(out=pt_[:, g * 128:g * 128 + NS], in_=rs[0:NS, g * 128:(g + 1) * 128],
                                                               identity=C["c_ident"][0:NS, 0:NS]))(g, pt_), [rs, C["c_ident"]], [pt_])
            plT = att_b[1]
            op("dve", (lambda pt_: lambda e: e.tensor_copy(out=plT[:, :], in_=pt_[:, :]))(pt_), [pt_], [plT])
            pm = nxt("A")
            for g in range(4):
                op("pe", (lambda g, pm: lambda e: e.matmul(pm[0:NS, g * 128:(g + 1) * 128], lhsT=plT[:, g * 128:g * 128 + NS], rhs=wspT[:, g, :],
                                                           start=True, stop=True))(g, pm), [plT, wspT], [pm])
            dscb = t32[2]
            dma("sp", lambda e: e.dma_start(out=dscb[0:NS, :], in_=dsc_d[j:j + 1, :].to_broadcast([NS, 512])), writes=[dscb])
            od = rs
            op("dve", (lambda pm: lambda e: e.tensor_tensor(out=od[0:NS, :], in0=pm[0:NS, :], in1=dscb[0:NS, :], op=ALU.mult))(pm), [pm, dscb], [od])
            op("dve", lambda e: e.tensor_tensor(out=od[0:NS, :], in0=od[0:NS, :], in1=sgd[0:NS, :], op=ALU.mult), [od, sgd], [od])
            pt_ = nxt("A")
            for g in range(4):
                op("pe", (lambda g, pt_: lambda e: e.transpose(out=pt_[:, g * 128:g * 128 + NS], in_=od[0:NS, g * 128:(g + 1) * 128],
                                                               identity=C["c_ident"][0:NS, 0:NS]))(g, pt_), [od, C["c_ident"]], [pt_])
            for g in range(4):
                op("dve", (lambda g, pt_: lambda e: e.tensor_copy(out=hTs[4 + g][:, :], in_=pt_[:, g * 128:g * 128 + NS]))(g, pt_), [pt_], [hTs[4 + g]])
            bsT = sps
            dma("sp", lambda e: e.dma_start(out=bsT[:, :], in_=bs_d[:, 0:128]), writes=[bsT])
            dma("sp", lambda e: e.dma_start(out=tts[:, 0:8], in_=bs_d[:, 128:136]), writes=[tts])
            idxall = S32
            KI, KIs = e32[0], sp32[0]
            qib = t32[2]
            op("pool", lambda e: e.memset(KIs[:, 0:32], 0.0), [], [KIs])

            def bcast_rows(b, src, c0, n, dst):
                qm = t32[5]
                op("dve", lambda e: e.tensor_scalar(out=qm[0:NS, 0:n], in0=src[0:NS, c0:c0 + n], scalar1=C["c_ident"][0:NS, b:b + 1],
                                                    scalar2=-1.0, op0=ALU.mult, op1=ALU.mult), [src, C["c_ident"]], [qm])
                pq = nxt("A")
                op("pe", lambda e: e.matmul(pq[:, 0:n], lhsT=C["c_nones"][0:NS, :], rhs=qm[0:NS, 0:n], start=True, stop=True),
                   [C["c_nones"], qm], [pq])
                op("dve", lambda e: e.tensor_copy(out=dst[:, 0:n], in_=pq[:, 0:n]), [pq], [dst])

            idx, wk = big32[0], big32[1]
            ckh = ck_d.rearrange("(n r) e -> n (r e)", r=64)
            qexp = t32[2]
            idxp = zs
            sc_ = t32[5]
            op("dve", lambda e: e.tensor_scalar(out=rows[0:NS, 416:420], in0=rows[0:NS, 416:420], scalar1=ISQ, scalar2=None, op0=ALU.mult),
               [rows], [rows])
            for bt in range(2):
                dma("sp", (lambda bt: lambda e: e.dma_start(out=idx2[:, bt:bt + 1],
                                                            in_=pt_d.rearrange("o (p c) -> (o p) c", c=1)[128 * bt:128 * (bt + 1), :]))(bt),
                    writes=[idx2])
            op("dve", lambda e: e.tensor_scalar(out=idx2[:, 0:2], in0=idx2[:, 0:2], scalar1=2, scalar2=None, op0=ALU.mult), [idx2], [idx2])
            for bt in range(2):
                pq = nxt("A")
                op("pe", (lambda bt, pq: lambda e: e.matmul(pq[:, 0:164], lhsT=C["c_esel"][:, 128 * bt:128 * (bt + 1)], rhs=rows[0:NS, 256:420],
                                                            start=True, stop=True))(bt, pq), [C["c_esel"], rows], [pq])
                op("dve", (lambda pq: lambda e: e.tensor_copy(out=qexp[:, 0:164], in_=pq[:, 0:164]))(pq), [pq], [qexp])
                for half in range(2):
                    dma("pool", (lambda bt, half: lambda e: e.indirect_dma_start(
                        out=idx[:, 0:2048], out_offset=None, in_=ckh[:, 0:2048], element_offset=j * pool_rows * 32 + half * 2048,
                        in_offset=bass.IndirectOffsetOnAxis(ap=idx2[:, bt:bt + 1], axis=0)))(bt, half), reads=[idx2], writes=[idx])
                    for ih in range(4):
                        op("dve", (lambda ih: lambda e: e.tensor_tensor(
                            out=wk[:, 0:2048].rearrange("p (k e) -> p k e", e=32), in0=idx[:, 0:2048].rearrange("p (k e) -> p k e", e=32),
                            in1=qexp[:, ih * 32:(ih + 1) * 32].unsqueeze(1).to_broadcast([128, 64, 32]), op=ALU.mult))(ih), [idx, qexp], [wk])
                        op("dve", lambda e: e.tensor_reduce(out=sc_[:, 0:64], in_=wk[:, 0:2048].rearrange("p (k e) -> p k e", e=32), axis=AX.X,
                                                            op=ALU.add), [wk], [sc_])
                        op("dve", lambda e: e.tensor_scalar(out=sc_[:, 0:64], in0=sc_[:, 0:64], scalar1=0.0, scalar2=None, op0=ALU.max), [sc_], [sc_])
                        if ih == 0:
                            op("dve", (lambda half: lambda e: e.tensor_scalar(out=idxp[:, half * 64:(half + 1) * 64], in0=sc_[:, 0:64],
                                                                              scalar1=qexp[:, 160:161], scalar2=None, op0=ALU.mult))(half),
                               [sc_, qexp], [idxp])
                        else:
                            op("dve", (lambda half, ih: lambda e: e.scalar_tensor_tensor(
                                out=idxp[:, half * 64:(half + 1) * 64], in0=sc_[:, 0:64], scalar=qexp[:, 160 + ih:161 + ih],
                                in1=idxp[:, half * 64:(half + 1) * 64], op0=ALU.mult, op1=ALU.add))(half, ih), [sc_, qexp, idxp], [idxp])
                for b8 in range(8):
                    bb = 8 * bt + b8
                    dma("pool", (lambda b8, bb: lambda e: e.dma_start(
                        out=scr_idx[bb:bb + 1, 0:2048].rearrange("o (g k) -> (o g) k", k=128), in_=idxp[b8 * 16:(b8 + 1) * 16, :]))(b8, bb),
                        reads=[idxp], writes=[scri_b])
            sfp = t32[5]
            op("dve", lambda e: e.tensor_tensor(out=sfp[0:NS, 0:128].rearrange("p (i e) -> p i e", e=32),
                                                in0=rows[0:NS, 256:384].rearrange("p (i e) -> p i e", e=32),
                                                in1=rows[0:NS, 384:416].unsqueeze(1).to_broadcast([NS, 4, 32]), op=ALU.mult), [rows], [sfp])
            op("dve", lambda e: e.tensor_reduce(out=sfp[0:NS, 128:132], in_=sfp[0:NS, 0:128].rearrange("p (i e) -> p i e", e=32), axis=AX.X,
                                                op=ALU.add), [sfp], [sfp])
            op("dve", lambda e: e.tensor_scalar(out=sfp[0:NS, 128:132], in0=sfp[0:NS, 128:132], scalar1=0.0, scalar2=None, op0=ALU.max), [sfp], [sfp])
            op("dve", lambda e: e.tensor_tensor(out=sfp[0:NS, 128:132], in0=sfp[0:NS, 128:132], in1=rows[0:NS, 416:420], op=ALU.mult), [sfp, rows], [sfp])
            op("pool", lambda e: e.memset(sfp[0:NS, 256:384], NEG), [], [sfp])
            op("dve", lambda e: e.tensor_reduce(out=sfp[0:NS, 256:257], in_=sfp[0:NS, 128:132], axis=AX.X, op=ALU.add), [sfp], [sfp])
            dma("pool", lambda e: e.dma_start(out=scr_idx[:, 2048:2176], in_=sfp[0:NS, 256:384]), reads=[sfp], writes=[scri_b])
            dma("sp", lambda e: e.dma_start(out=idx[0:NS, 0:2176], in_=scr_idx), reads=[scri_b], writes=[idx])
            op("pool", lambda e: e.tensor_copy(out=wk[0:NS, 0:2176], in_=idx[0:NS, 0:2176]), [idx], [wk])
            mx = small[1]
            for r in range(32):
                op("dve", lambda e: e.max(out=mx[0:NS, 0:8], in_=wk[0:NS, 0:2176]), [wk], [mx])
                op("dve", lambda e: e.match_replace(out=wk[0:NS, 0:2176], in_to_replace=mx[0:NS, 0:8], in_values=wk[0:NS, 0:2176],
                                                    imm_value=2.0 * NEG), [wk, mx], [wk])
            op("dve", lambda e: e.tensor_tensor(out=wk[0:NS, 0:2176], in0=wk[0:NS, 0:2176], in1=idx[0:NS, 0:2176], op=ALU.not_equal), [wk, idx], [wk])
            dma("pool", lambda e: e.dma_start(out=scr_msk, in_=wk[0:NS, 0:2176]), reads=[wk], writes=[scrm_b])
            mS = idxall
            for (r0, nr) in ((0, 128), (128, 128), (256, 16)):
                tb = t32[5]
                dma("sp", (lambda r0, nr: lambda e: e.dma_start(out=tb[0:nr, 0:128],
                                                                in_=scr_msk.rearrange("b (g k) -> (b g) k", k=128)[r0:r0 + nr, :]))(r0, nr),
                    reads=[scrm_b], writes=[tb])
                pt_ = nxt("A")
                op("pe", (lambda nr, pt_: lambda e: e.transpose(out=pt_[:, 0:nr], in_=tb[0:nr, 0:128], identity=C["c_ident"][0:nr, 0:nr]))(nr, pt_),
                   [tb, C["c_ident"]], [pt_])
                op("dve", (lambda r0, nr, pt_: lambda e: e.tensor_copy(out=mS[:, r0:r0 + nr], in_=pt_[:, 0:nr]))(r0, nr, pt_), [pt_], [mS])
            pn, pd = psO[0], psO[1]
            op("pe", lambda e: e.matmul(pn[:, 0:128], lhsT=zero_b[:, :], rhs=xT[0][:, 0:128], start=True, stop=False), [zero_b, xT[0]], [pn])
            op("pe", lambda e: e.matmul(pd[:, 0:128], lhsT=zero_b[:, :], rhs=xT[0][:, 0:128], start=True, stop=False), [zero_b, xT[0]], [pd])
            Kself, Vself = qb_sb, atts
            op("pool", lambda e: e.memset(Kself[:, 0:128], 0.0), [], [Kself])
            op("pool", lambda e: e.memset(Vself[:, :], 0.0), [], [Vself])
            ones64 = vbt[0][:, 0, 64:128]

            def attend_sample(b):
                qbq = t32[2]
                bcast_rows(b, qs, 0, 512, qbq)
                Kb = [kpg[0][:, :, :].rearrange("p a b -> p (a b)"), kpg[1][:, :, :].rearrange("p a b -> p (a b)")]
                Vb = [vpg[0][:, :, :].rearrange("p a b -> p (a b)"), vpg[1][:, :, :].rearrange("p a b -> p (a b)")]
                for pg in range(NPG):
                    hb, i = pg // 8, pg % 8
                    dma("pool", (lambda pg, hb, i: lambda e: e.indirect_dma_start(
                        out=Kb[hb][:, i * 128:(i + 1) * 128], out_offset=None, in_=cc_d[:, 0:128], element_offset=j * pool_rows * 256,
                        in_offset=bass.IndirectOffsetOnAxis(ap=idxB[:, b * NPG + pg:b * NPG + pg + 1], axis=0)))(pg, hb, i),
                        reads=[idxB], writes=[kpg[hb]])
                    dma("pool", (lambda pg, hb, i: lambda e: e.indirect_dma_start(
                        out=Vb[hb][:, i * 128:(i + 1) * 128], out_offset=None, in_=cc_d[:, 0:128], element_offset=j * pool_rows * 256 + 128,
                        in_offset=bass.IndirectOffsetOnAxis(ap=idxB[:, b * NPG + pg:b * NPG + pg + 1], axis=0)))(pg, hb, i),
                        reads=[idxB], writes=[vpg[hb]])
                dma("sp", lambda e: e.dma_start(out=Kself[0:1, 0:128], in_=scr_kv[b:b + 1, 0:128]), reads=[scrkv_b], writes=[Kself])
                vst = small[0]
                dma("sp", lambda e: e.dma_start(out=sp32[0][0:1, 128:256], in_=scr_kv[b:b + 1, 128:256]), reads=[scrkv_b], writes=[sp32[0]])
                op("dve", lambda e: e.tensor_copy(out=Vself[0:1, :], in_=sp32[0][0:1, 128:256]), [sp32[0]], [Vself])
                L = zs
                Ls = small[2]
                prod = big32[1]
                qv = qbq[:, 0:512].rearrange("p (g r d) -> p g r d", g=2, r=4)
                for pq4 in range(4):
                    hb, i0 = pq4 // 2, (pq4 % 2) * 4
                    Kv = Kb[hb][:, i0 * 128:(i0 + 4) * 128].rearrange("p (a g d) -> p a g d", g=2, d=64)
                    for g in range(2):
                        op("dve", (lambda Kv, g: lambda e: e.tensor_tensor(
                            out=prod[:, g * 1024:(g + 1) * 1024].rearrange("p (a r d) -> p a r d", r=4, d=64),
                            in0=Kv[:, :, g, :].unsqueeze(2).to_broadcast([128, 4, 4, 64]),
                            in1=qv[:, g, :, :].unsqueeze(1).to_broadcast([128, 4, 4, 64]), op=ALU.mult))(Kv, g), [kpg[hb], qbq], [prod])
                    for g in range(2):
                        op("dve", (lambda pq4, g: lambda e: e.tensor_reduce(
                            out=L[:, pq4 * 32:(pq4 + 1) * 32].rearrange("p (a h) -> p a h", h=8)[:, :, g * 4:(g + 1) * 4],
                            in_=prod[:, g * 1024:(g + 1) * 1024].rearrange("p (a r d) -> p a r d", r=4, d=64), axis=AX.X, op=ALU.add))(pq4, g),
                           [prod], [L])
                Ksv = Kself[:, 0:128].rearrange("p (g d) -> p g d", d=64)
                op("dve", lambda e: e.tensor_tensor(out=prod[:, 0:512].rearrange("p (g r d) -> p g r d", g=2, r=4),
                                                    in0=Ksv.unsqueeze(2).to_broadcast([128, 2, 4, 64]), in1=qv, op=ALU.mult), [Kself, qbq], [prod])
                op("dve", lambda e: e.tensor_reduce(out=Ls[:, 0:8], in_=prod[:, 0:512].rearrange("p (h d) -> p h d", d=64), axis=AX.X, op=ALU.add),
                   [prod], [Ls])
                op("dve", lambda e: e.tensor_tensor(out=L[:, :], in0=L[:, :], in1=bsT[:, :], op=ALU.add), [L, bsT], [L])
                op("dve", lambda e: e.tensor_tensor(out=Ls[:, 0:8], in0=Ls[:, 0:8], in1=tts[:, 0:8], op=ALU.add), [Ls, tts], [Ls])
                op("act", lambda e: e.activation(out=L[:, :], in_=L[:, :], func=AF.Exp), [L], [L])
                op("act", lambda e: e.activation(out=Ls[:, 0:8], in_=Ls[:, 0:8], func=AF.Exp), [Ls], [Ls])
                pb = att_b[0]
                op("dve", lambda e: e.tensor_tensor(out=pb[:, 0:128].rearrange("p (a h) -> p a h", h=8), in0=L[:, :].rearrange("p (a h) -> p a h", h=8),
                                                    in1=mS[:, b * 17:b * 17 + 16].unsqueeze(2).to_broadcast([128, 16, 8]), op=ALU.mult), [L, mS], [pb])
                op("dve", lambda e: e.tensor_scalar(out=pb[:, 128:136], in0=Ls[:, 0:8], scalar1=mS[:, b * 17 + 16:b * 17 + 17], scalar2=None,
                                                    op0=ALU.mult), [Ls, mS], [pb])
                for pg in range(NPG):
                    hb, i = pg // 8, pg % 8
                    op("pe", (lambda pg, hb, i: lambda e: e.matmul(pn[:, b * 8:(b + 1) * 8], lhsT=Vb[hb][:, i * 128:(i + 1) * 128],
                                                                   rhs=pb[:, pg * 8:(pg + 1) * 8], start=False, stop=False))(pg, hb, i), [vpg[hb], pb], [pn])
                    op("pe", (lambda pg: lambda e: e.matmul(pd[0:64, b * 8:(b + 1) * 8], lhsT=ones64, rhs=pb[:, pg * 8:(pg + 1) * 8],
                                                            start=False, stop=False))(pg), [vbt[0], pb], [pd])
                op("pe", lambda e: e.matmul(pn[:, b * 8:(b + 1) * 8], lhsT=Vself[:, :], rhs=pb[:, 128:136], start=False, stop=True), [Vself, pb], [pn])
                op("pe", lambda e: e.matmul(pd[0:64, b * 8:(b + 1) * 8], lhsT=ones64, rhs=pb[:, 128:136], start=False, stop=True), [vbt[0], pb], [pd])

            for b in range(NS):
                attend_sample(b)
            rd = zs
            op("dve", lambda e: e.reciprocal(out=rd[0:64, 0:128], in_=pd[0:64, 0:128]), [pd], [rd])
            op("dve", lambda e: e.reciprocal(out=rd[64:128, 0:128], in_=pd[0:64, 0:128]), [pd], [rd])
            oT = t32[5]
            op("dve", lambda e: e.tensor_tensor(out=oT[:, 0:128], in0=pn[:, 0:128], in1=rd[:, 0:128], op=ALU.mult), [pn, rd], [oT])
            for h in range(8):
                g, c, hb_ = h // 4, h // 2, 64 * (h % 2)
                tmpo = t32[4]
                op("dve", (lambda h, g, c, hb_: lambda e: e.tensor_copy(
                    out=tmpo[hb_:hb_ + 64, 0:NS], in_=oT[64 * g:64 * g + 64, 0:128].rearrange("p (b h) -> p b h", h=8)[:, :, h]))(h, g, c, hb_),
                   [oT], [tmpo])
                op("dve", (lambda h, g, c, hb_: lambda e: e.tensor_tensor(
                    out=hTs[c][hb_:hb_ + 64, :], in0=tmpo[hb_:hb_ + 64, 0:NS],
                    in1=gbTs[hb_:hb_ + 64, c, :], op=ALU.mult))(h, g, c, hb_), [tmpo, gbTs], [hTs[c]])
            layer_norm_block(layer, 16, hTs, 0, NS)

        for layer in range(n_layers):
            j = layer // 2
            layer_setup(layer)
            if layer % 2 == 0:
                even_setup(j)
                for T in range(DBG.get("ntiles", 4)):
                    even_prompt_tile(layer, j, T)
                if DBG.get("sample", 1):
                    even_sample_tile(layer, j)
            else:
                odd_setup(j)
                for T in range(DBG.get("ntiles", 4)):
                    odd_prompt_tile(layer, j, T)
                if DBG.get("sample", 1):
                    odd_sample_tile(layer, j)

        for jb in range(16):
            dma("pool", (lambda jb: lambda e: e.dma_start(out=yp_d[jb * 128:(jb + 1) * 128, :], in_=xtok[jb][:, :]))(jb), reads=[xtok[jb]])
        dma("pool", lambda e: e.dma_start(out=ys_d, in_=xtok[16][0:NS, :]), reads=[xtok[16]])
        fw.emit()
    return nc


def _bias_tables(rel_bias):
    k = np.arange(128)
    d0 = np.maximum(k[None, :] - k[:, None], 0)
    d1 = 128 + k[None, :] - k[:, None]
    t = np.zeros((128, 2, 8, 128), np.float32)
    for h in range(8):
        t[:, 0, h, :] = rel_bias[_t5_bucket(d0), h]
        t[:, 1, h, :] = rel_bias[_t5_bucket(d1), h]
    toep = t.reshape(128, 16 * 128)
    b31 = np.ascontiguousarray(rel_bias[31:32, :]).astype(np.float32)
    bs = np.zeros((128, 17, 8), np.float32)
    for pg in range(16):
        dist = 2048 - (pg * 128 + k)
        bs[:, pg, :] = rel_bias[_t5_bucket(dist), :]
    bs[:, 16, :] = rel_bias[0:1, :]
    return toep, b31, bs.reshape(128, 17 * 8)


def _core_inputs(inp, c, consts, tabs, pool_views, pt_rows):
    d = {
        "xp": np.ascontiguousarray(inp["x_prompt"][c]),
        "xs": np.ascontiguousarray(inp["x_sample"][c * NS:(c + 1) * NS, 0, :]),
        "pt": np.ascontiguousarray(pt_rows.reshape(1, NS * NPG)).astype(np.int32),
        "cb": pool_views[0], "cc": pool_views[1], "ck": pool_views[2],
        "dbuf": np.ascontiguousarray(inp["state_d_buf"][:, c * NS:(c + 1) * NS]),
        "w_in_even": inp["w_in_even"], "w_in_odd": inp["w_in_odd"], "w_out": inp["w_out"],
        "ln_g": inp["ln_g"], "ln_b": inp["ln_b"], "a_w_sp": inp["a_w_sp"], "a_b_sp": inp["a_b_sp"],
        "d_w_grp": inp["d_w_grp"], "d_scale": inp["d_scale"],
        "t_toep": tabs[0], "t_b31": tabs[1], "t_bsamp": tabs[2],
    }
    d.update(consts)
    return d


def kernel(**inputs):
    inp = {k: np.asarray(v) for k, v in inputs.items()}
    n_pool = inp["cache_b_kv"].shape[1]
    nc = build(4, n_pool * 128)
    consts = _consts()
    tabs = _bias_tables(inp["rel_bias"].astype(np.float32))
    views = (inp["cache_b_kv"].reshape(2 * n_pool * 128, 1024), inp["cache_c_kv"].reshape(2 * n_pool * 128, 256),
             inp["cache_c_kidx"].reshape(2 * n_pool * 128, 32))
    in_maps = [_core_inputs(inp, c, consts, tabs, views, inp["page_table"][c * NS:(c + 1) * NS]) for c in range(NCORES)]
    res = run_bass_kernel_spmd(nc, in_maps, core_ids=list(range(NCORES))).results
    return _assemble(res)


def _assemble(res):
    n = len(res)
    f = np.float32
    y_p = np.stack([res[c]["y_p"] for c in range(n)]).astype(f)
    y_s = np.concatenate([res[c]["y_s"] for c in range(n)])[:, None, :].astype(f)
    bkv_p = np.stack([res[c]["bkv_p"] for c in range(n)], axis=1).reshape(2, n, S, 2, 8, 64)
    bkv_s = np.concatenate([res[c]["bkv_s"] for c in range(n)], axis=1).reshape(2, n * NS, 1, 2, 8, 64)
    av_s = np.concatenate([res[c]["av_s"] for c in range(n)], axis=1).reshape(2, n * NS, 1, 512)
    ckv_p = np.stack([res[c]["ckv_p"] for c in range(n)], axis=1).reshape(2, n, S, 2, 2, 64)
    ckv_s = np.concatenate([res[c]["ckv_s"] for c in range(n)], axis=1).reshape(2, n * NS, 1, 2, 2, 64)
    cki_p = np.stack([res[c]["cki_p"] for c in range(n)], axis=1).reshape(2, n, S, 32)
    cki_s = np.concatenate([res[c]["cki_s"] for c in range(n)], axis=1).reshape(2, n * NS, 1, 32)
    db_p = np.stack([res[c]["dbuf_p"] for c in range(n)], axis=1).reshape(2, n, 15, 512)
    db_s = np.concatenate([res[c]["dbuf_s"] for c in range(n)], axis=1).reshape(2, n * NS, 15, 512)
    return tuple(np.ascontiguousarray(a, dtype=f) for a in
                 (y_p, y_s, bkv_p, bkv_s, av_s, ckv_p, ckv_s, cki_p, cki_s, db_p, db_s))
```

```python
import math
import numpy as np
from contextlib import ExitStack
import concourse.bass as bass
import concourse.mybir as mybir
from concourse.bass_utils import run_bass_kernel_spmd

F32 = mybir.dt.float32
F32R = mybir.dt.float32r
BF16 = mybir.dt.bfloat16
I32 = mybir.dt.int32
AF = mybir.ActivationFunctionType
ALU = mybir.AluOpType
AX = mybir.AxisListType

ENGS = ("pe", "act", "dve", "pool", "sp")
DBG = {}
NCORES = 8
S = 2048
DM = 1024
NS = 16
NPG = 16
EVEN_IN = 3584
ODD_IN = 2468
DN_ALPHA = (2.0 * 4) ** 0.25
LN_EPS = 1e-5
NEG = -1.0e30


class Buf:
    __slots__ = ("t", "name", "last_w", "readers")

    def __init__(self, t, name):
        self.t = t
        self.name = name
        self.last_w = None
        self.readers = {}

    def __getitem__(self, idx):
        return self.t[idx]


class View:
    def __init__(self, parent, ap):
        self.parent = parent
        self.ap = ap
        self.name = parent.name + "_v"

    def __getitem__(self, idx):
        return self.ap[idx]

    @property
    def last_w(self):
        return self.parent.last_w

    @last_w.setter
    def last_w(self, v):
        self.parent.last_w = v

    @property
    def readers(self):
        return self.parent.readers

    @readers.setter
    def readers(self, v):
        self.parent.readers = v


def interleave(gens):
    gens = list(gens)
    while gens:
        for g in list(gens):
            try:
                next(g)
            except StopIteration:
                gens.remove(g)


class FW:
    def __init__(self, nc, stack, n_dma_sems=32):
        self.nc = nc
        self.stack = stack
        self.ops = {e: [] for e in ENGS}
        self.cnt = {e: 0 for e in ENGS}
        self.sems = {e: stack.enter_context(nc.semaphore("sem_" + e)) for e in ENGS}
        self.dma_sems = [stack.enter_context(nc.semaphore("sem_dma%d" % i)) for i in range(n_dma_sems)]
        self.dma_cnt = [0] * n_dma_sems
        self.dma_rr = 0
        self.seen = {e: {} for e in ENGS}
        self.nbuf = 0

    def sb(self, shape, dt, name=None):
        self.nbuf += 1
        name = (name or "sb") + "_%d" % self.nbuf
        t = self.stack.enter_context(self.nc.sbuf_tensor(name, list(shape), dt))
        return Buf(t, name)

    def ps(self, shape, dt=F32, name=None):
        self.nbuf += 1
        name = (name or "ps") + "_%d" % self.nbuf
        t = self.stack.enter_context(self.nc.psum_tensor(name, list(shape), dt))
        return Buf(t, name)

    def _collect(self, eng, reads, writes):
        waits = {}

        def add(ev):
            if ev is None:
                return
            k, v = ev
            if k == "pe" and eng == "pe":
                return
            if waits.get(k, 0) < v:
                waits[k] = v

        for b in reads:
            add(b.last_w)
        for b in writes:
            add(b.last_w)
            for k, v in b.readers.items():
                add((k, v))
        seen = self.seen[eng]
        out = []
        for k, v in waits.items():
            if seen.get(k, 0) >= v:
                continue
            seen[k] = v
            out.append((k, v))
        return out

    def _commit(self, ev, reads, writes):
        k, v = ev
        for b in reads:
            if b.readers.get(k, 0) < v:
                b.readers[k] = v
        for b in writes:
            b.last_w = ev
            b.readers = {}

    def op(self, eng, fn, reads=(), writes=()):
        waits = self._collect(eng, reads, writes)
        self.cnt[eng] += 1
        ev = (eng, self.cnt[eng])
        self.ops[eng].append((waits, fn, (eng, 1)))
        self._commit(ev, reads, writes)
        return ev

    def dma(self, eng, fn, reads=(), writes=()):
        i = self.dma_rr
        self.dma_rr = (self.dma_rr + 1) % len(self.dma_sems)
        key = "dma%d" % i
        waits = self._collect(eng, reads, writes)
        prev = self.dma_cnt[i] * 16
        if prev > 0 and self.seen[eng].get(key, 0) < prev:
            self.seen[eng][key] = prev
            waits.append((key, prev))
        self.dma_cnt[i] += 1
        ev = (key, self.dma_cnt[i] * 16)
        self.ops[eng].append((waits, fn, (key, 16)))
        self._commit(ev, reads, writes)
        return ev

    def _sem(self, key):
        if key in self.sems:
            return self.sems[key]
        return self.dma_sems[int(key[3:])]

    def emit(self):
        nc = self.nc
        with nc.Block() as block:
            def run(engname, e):
                for waits, fn, (ik, iv) in self.ops[engname]:
                    for k, v in waits:
                        e.wait_ge(self._sem(k), v)
                    fn(e).then_inc(self._sem(ik), iv)

            @block.tensor
            def _(e):
                run("pe", e)

            @block.scalar
            def _(e):
                run("act", e)

            @block.vector
            def _(e):
                run("dve", e)

            @block.gpsimd
            def _(e):
                run("pool", e)

            @block.sync
            def _(e):
                run("sp", e)
                for i, c in enumerate(self.dma_cnt):
                    if c:
                        e.wait_ge(self.dma_sems[i], c * 16)


def _consts():
    c = {}
    k = np.arange(128)
    c["c_ident"] = np.eye(128, dtype=np.float32)
    c["c_ntri"] = -(k[:, None] >= k[None, :]).astype(np.float32)
    c["c_nones"] = -np.ones((128, 128), np.float32)
    c["c_negm"] = np.where(k[:, None] >= k[None, :], -30000.0, 0.0).astype(np.float32)
    pg = k // 8
    h = k % 8
    c["c_m2"] = ((h[:, None] == h[None, :]) & (pg[:, None] > pg[None, :])).astype(np.float32)
    c["c_dsac"] = (k[None, :] >= k[:, None]).astype(np.float32)
    c["c_negq"] = np.where(k[None, :] > k[:, None], NEG, 0.0).astype(np.float32)
    esel = np.zeros((16, 2, 8, 16), np.float32)
    for b in range(16):
        esel[b, b // 8, b % 8, :] = 1.0
    c["c_esel"] = esel.reshape(16, 256)
    rowm = np.zeros((128, 4), np.float32)
    for ih in range(4):
        rowm[32 * ih:32 * ih + 32, ih] = 1.0
    c["c_rowm"] = rowm
    icnt = np.zeros((4, 16), np.float32)
    for g, w in enumerate((2, 4, 8, 16)):
        icnt[g] = 1.0 / np.minimum(np.arange(16) + 1, w)
    c["c_icnt"] = np.broadcast_to(icnt.reshape(1, 64), (128, 64)).copy()
    negrow = np.zeros((128, 1), np.float32)
    negrow[1:] = NEG
    c["c_negrow"] = negrow
    return c


def _t5_bucket(n):
    n = np.asarray(n)
    nf = np.maximum(n, 1).astype(np.float32)
    large = 16 + (np.log(nf / 16) / math.log(128 / 16) * 16).astype(np.int32)
    large = np.minimum(large, 31)
    return np.where(n < 16, n, large)


CONST_SHAPES = {"c_ident": [128, 128], "c_ntri": [128, 128], "c_nones": [128, 128], "c_negm": [128, 128],
                "c_m2": [128, 128], "c_dsac": [128, 128], "c_negq": [128, 128],
                "c_rowm": [128, 4], "c_esel": [16, 256], "c_icnt": [128, 64], "c_negrow": [128, 1]}


def build(n_layers=4, pool_rows=2560 * 128, debug=False):
    nc = bass.Bass("TRN2", target_bir_lowering=False)

    def din(name, shape, dt=F32):
        return nc.dram_tensor(name, list(shape), dt, kind="ExternalInput").ap()

    def dout(name, shape, dt=F32):
        return nc.dram_tensor(name, list(shape), dt, kind="ExternalOutput").ap()

    xp_d = din("xp", [S, DM])
    xs_d = din("xs", [NS, DM])
    pt_d = din("pt", [1, NS * NPG], I32)
    cb_d = din("cb", [2 * pool_rows, 1024])
    cc_d = din("cc", [2 * pool_rows, 256])
    ck_d = din("ck", [2 * pool_rows, 32])
    db_d = din("dbuf", [2, NS, 15, 512])
    wie_d = din("w_in_even", [2, DM, EVEN_IN])
    wio_d = din("w_in_odd", [2, DM, ODD_IN])
    wo_d = din("w_out", [4, DM, DM])
    lng_d = din("ln_g", [4, DM])
    lnb_d = din("ln_b", [4, DM])
    wsp_d = din("a_w_sp", [2, 4, 128, 128])
    bsp_d = din("a_b_sp", [2, 4, 128])
    wgrp_d = din("d_w_grp", [2, 4, 128, 128])
    dsc_d = din("d_scale", [2, 512])
    tz_d = din("t_toep", [128, 16 * 128])
    b31_d = din("t_b31", [1, 8])
    bs_d = din("t_bsamp", [128, 17 * 8])
    cst = {n: din(n, s) for n, s in CONST_SHAPES.items()}

    yp_d = dout("y_p", [S, DM])
    ys_d = dout("y_s", [NS, DM])
    bkvp_d = dout("bkv_p", [2, S, 1024])
    bkvs_d = dout("bkv_s", [2, NS, 1024])
    avs_d = dout("av_s", [2, NS, 512])
    ckvp_d = dout("ckv_p", [2, S, 256])
    ckvs_d = dout("ckv_s", [2, NS, 256])
    ckip_d = dout("cki_p", [2, S, 32])
    ckis_d = dout("cki_s", [2, NS, 32])
    dbp_d = dout("dbuf_p", [2, 15, 512])
    dbs_d = dout("dbuf_s", [2, NS, 15, 512])
    dbgh_d = dout("dbg_h", [8, 128, 512], BF16) if DBG.get("dbgh") else None
    scr_idx = nc.dram_tensor("scr_idx", [NS, 17 * 128], F32, kind="Internal").ap()
    scr_msk = nc.dram_tensor("scr_msk", [NS, 17 * 128], F32, kind="Internal").ap()
    scr_kv = nc.dram_tensor("scr_kv", [NS, 288], F32, kind="Internal").ap()

    with ExitStack() as st:
        fw = FW(nc, st)
        st.enter_context(nc.allow_non_contiguous_dma(reason="small strided parameter loads"))
        op, dma = fw.op, fw.dma

        C = {}
        for n, s in CONST_SHAPES.items():
            C[n] = fw.sb(s, F32, n)
            dma("sp", (lambda b, a: lambda e: e.dma_start(out=b[:], in_=a))(C[n], cst[n]), writes=[C[n]])
        ident_b = fw.sb([128, 128], BF16, "ident_b")
        negm_b = fw.sb([128, 128], BF16, "negm_b")
        op("dve", lambda e: e.tensor_copy(out=ident_b[:], in_=C["c_ident"][:]), [C["c_ident"]], [ident_b])
        op("dve", lambda e: e.tensor_copy(out=negm_b[:], in_=C["c_negm"][:]), [C["c_negm"]], [negm_b])
        ntri_r = fw.sb([128, 128], F32R, "ntri_r")
        nones_r = fw.sb([128, 128], F32R, "nones_r")
        op("dve", lambda e: e.tensor_copy(out=ntri_r[:], in_=C["c_ntri"][:]), [C["c_ntri"]], [ntri_r])
        op("dve", lambda e: e.tensor_copy(out=nones_r[:], in_=C["c_nones"][:]), [C["c_nones"]], [nones_r])
        zero_b = fw.sb([128, 128], BF16, "zero_b")
        op("pool", lambda e: e.memset(zero_b[:], 0.0), [], [zero_b])
        pts = fw.sb([128, NS * NPG], I32, "pts")
        dma("sp", lambda e: e.dma_start(out=pts[:], in_=pt_d.to_broadcast([128, NS * NPG])), writes=[pts])
        idx2 = fw.sb([128, 2], I32, "idx2")
        iota_p = fw.sb([128, 1], I32, "iota_p")
        op("pool", lambda e: e.iota(out=iota_p[:], pattern=[[0, 1]], base=0, channel_multiplier=1), [], [iota_p])
        idxC = pts
        idxB = fw.sb([128, NS * NPG], I32, "idxB")
        op("dve", lambda e: e.tensor_scalar(out=idxC[:], in0=pts[:], scalar1=128, scalar2=iota_p[:, 0:1], op0=ALU.mult, op1=ALU.add),
           [pts, iota_p], [idxC])
        op("dve", lambda e: e.tensor_scalar(out=idxB[:], in0=idxC[:], scalar1=2, scalar2=None, op0=ALU.mult), [idxC], [idxB])

        scrkv_b, scrkv_b2, scri_b, scrm_b = Buf(None, "scrkv"), Buf(None, "scrkv2"), Buf(None, "scri"), Buf(None, "scrm")
        xtok = [fw.sb([128, DM], F32, "xtok%d" % j) for j in range(17)]
        for j in range(16):
            dma("sp", (lambda j: lambda e: e.dma_start(out=xtok[j][:], in_=xp_d[j * 128:(j + 1) * 128, :]))(j),
                writes=[xtok[j]])
        op("pool", lambda e: e.memset(xtok[16][:], 0.0), [], [xtok[16]])
        dma("sp", lambda e: e.dma_start(out=xtok[16][0:NS, :], in_=xs_d), writes=[xtok[16]])

        xT = [fw.sb([128, 512], BF16, "xT%d" % c) for c in range(8)]
        hT = [fw.sb([128, 512], BF16, "hT%d" % c) for c in range(8)]
        lng_sb = fw.sb([128, DM], F32, "lng_sb")
        lnb_sb = fw.sb([128, DM], F32, "lnb_sb")
        wst = [fw.sb([128, 2, 128], F32, "wst%d" % i) for i in range(4)]
        wbf = [fw.sb([128, 8, 128], BF16, "wbf%d" % i) for i in range(3)]
        ring = {"st": 0, "bf": 0}
        psA = [fw.ps([128, 512], F32, "psA%d" % i) for i in range(2)]
        psZ = [fw.ps([128, 512], F32, "psZ%d" % i) for i in range(2)]
        psO = [fw.ps([128, 512], F32, "psO%d" % i) for i in range(2)]
        psY = [fw.ps([128, 512], F32, "psY%d" % i) for i in range(2)]
        rr = {"A": 0, "Z": 0, "O": 0}

        def nxt(kind):
            lst = {"A": psA, "Z": psZ, "O": psO}[kind]
            rr[kind] = (rr[kind] + 1) % len(lst)
            return lst[rr[kind]]

        kTt = [fw.sb([128, 4, 512], BF16, "kTt%d" % t) for t in range(4)]
        vbt = [fw.sb([128, 4, 512], BF16, "vbt%d" % t) for t in range(4)]
        qT = [fw.sb([128, 512], BF16, "qT%d" % c) for c in range(4)]
        t32 = [fw.sb([128, 512], F32, "t32_%d" % i) for i in range(6)]
        big32 = [fw.sb([128, 2304], F32, "big32_%d" % i) for i in range(2)]
        att_b = [fw.sb([128, 512], BF16, "att%d" % i) for i in range(2)]
        e32 = [fw.sb([128, 512], F32, "e32_%d" % i) for i in range(1)] * 2
        sp32 = [fw.sb([128, 512], F32, "sp32_%d" % i) for i in range(1)] * 2
        S32 = fw.sb([128, 512], F32, "S32")
        small = [fw.sb([128, 8], F32, "small%d" % i) for i in range(4)]
        wspT = fw.sb([128, 4, 128], BF16, "wspT")
        wsp_f = fw.sb([128, 4, 128], F32, "wsp_f")
        bsp_bc = fw.sb([128, 4, 128], F32, "bsp_bc")
        wsp00 = fw.sb([128, 4], F32, "wsp00")
        xTs = [fw.sb([128, NS], BF16, "xTs%d" % c) for c in range(8)]
        hTs = [fw.sb([128, NS], BF16, "hTs%d" % c) for c in range(8)]
        kpg = [fw.sb([128, 2, 512], F32, "kpg%d" % i) for i in range(2)]
        vpg = [fw.sb([128, 2, 512], BF16, "vpg%d" % i) for i in range(2)]
        qb_sb = fw.sb([128, 512], F32, "qb_sb")
        zs = fw.sb([128, 128], F32, "zs")
        sps = fw.sb([128, 128], F32, "sps")
        tts = fw.sb([128, 128], F32, "tts")
        atts = fw.sb([128, 128], BF16, "atts")
        oBT = fw.sb([128, 4, NS], F32, "oBT")
        gbTs = fw.sb([128, 4, NS], F32, "gbTs")
        s16 = t32

        def load_w(wap, col0, ncols, pieces=None):
            blk = wbf[ring["bf"] % 3]
            ring["bf"] += 1
            if pieces is None:
                pieces = [(col0, ncols, 0)]
            n_tot = max(d + n for (_, n, d) in pieces)
            if DBG.get("wdma", 0):
                for (sc, n, dc) in pieces:
                    src = wap[:, sc:sc + n].rearrange("(k p) c -> p k c", p=128)
                    for kh in range(2):
                        dma("pool", (lambda kh, src, n, dc: lambda e: e.dma_start(out=blk[:, 4 * kh:4 * kh + 4, dc:dc + n],
                                                                                  in_=src[:, 4 * kh:4 * kh + 4, :]))(kh, src, n, dc),
                            writes=[blk])
                return blk
            for kq in range(4):
                stg = wst[ring["st"] % 4]
                ring["st"] += 1
                for (sc, n, dc) in pieces:
                    src = wap[:, sc:sc + n].rearrange("(k p) c -> p k c", p=128)
                    dma("sp", (lambda kq, stg, src, n, dc: lambda e: e.dma_start(out=stg[:, :, dc:dc + n],
                                                                                 in_=src[:, 2 * kq:2 * kq + 2, :]))(kq, stg, src, n, dc),
                        writes=[stg])
                if DBG.get("castact") and kq % 2 == 1:
                    op("act", (lambda kq, stg: lambda e: e.activation(out=blk[:, 2 * kq:2 * kq + 2, 0:n_tot], in_=stg[:, :, 0:n_tot],
                                                                      func=AF.Copy))(kq, stg), [stg], [blk])
                else:
                    op("dve", (lambda kq, stg: lambda e: e.tensor_copy(out=blk[:, 2 * kq:2 * kq + 2, 0:n_tot], in_=stg[:, :, 0:n_tot]))(kq, stg),
                       [stg], [blk])
            return blk

        def fm_mm(ps, blk, ncols, xTl, ntok):
            for k in range(8):
                op("pe", (lambda k: lambda e: e.matmul(ps[0:ncols, 0:ntok], lhsT=blk[:, k, 0:ncols],
                                                       rhs=xTl[k][:, 0:ntok], start=(k == 0), stop=(k == 7)))(k),
                   [blk, xTl[k]], [ps])

        def tm_mm(ps, pcol, blk, ncols, xTl, tok0, ntok):
            for k in range(8):
                op("pe", (lambda k: lambda e: e.matmul(ps[0:ntok, pcol:pcol + ncols], lhsT=xTl[k][:, tok0:tok0 + ntok],
                                                       rhs=blk[:, k, 0:ncols], start=(k == 0), stop=(k == 7)))(k),
                   [blk, xTl[k]], [ps])

        def make_xT(T):
            if T < 4:
                for jj in range(4):
                    src = xtok[4 * T + jj]
                    for half in range(2):
                        ps = nxt("A")
                        for cc in range(4):
                            c = half * 4 + cc
                            op("pe", (lambda c, cc, ps, src: lambda e: e.transpose(
                                out=ps[:, cc * 128:(cc + 1) * 128], in_=src[:, c * 128:(c + 1) * 128],
                                identity=C["c_ident"][:]))(c, cc, ps, src), [src, C["c_ident"]], [ps])
                        for cc in range(4):
                            c = half * 4 + cc
                            if DBG.get("xv", 0) == 2:
                                continue
                            ev_eng = "dve" if (cc % 2 == 0 or True) else "act"
                            op(ev_eng,
                               (lambda c, cc, ps, jj, ev_eng: lambda e: (e.tensor_copy(out=xT[c][:, jj * 128:(jj + 1) * 128],
                                                                               in_=ps[:, cc * 128:(cc + 1) * 128])
                                                                 if ev_eng == "dve" else
                                                                 e.activation(out=xT[c][:, jj * 128:(jj + 1) * 128],
                                                                              in_=ps[:, cc * 128:(cc + 1) * 128],
                                                                              func=AF.Copy)))(c, cc, ps, jj, ev_eng),
                               [ps], [xT[c]])
                return xT, 512
            src = xtok[16]
            for half in range(2):
                ps = nxt("A")
                for cc in range(4):
                    c = half * 4 + cc
                    op("pe", (lambda c, cc, ps: lambda e: e.transpose(
                        out=ps[:, cc * 128:cc * 128 + NS], in_=src[0:NS, c * 128:(c + 1) * 128],
                        identity=C["c_ident"][0:NS, 0:NS]))(c, cc, ps), [src, C["c_ident"]], [ps])
                for cc in range(4):
                    c = half * 4 + cc
                    op("dve", (lambda c, cc, ps: lambda e: e.tensor_copy(out=xTs[c][:, :], in_=ps[:, cc * 128:cc * 128 + NS]))(c, cc, ps),
                       [ps], [xTs[c]])
            return xTs, NS

        def gelu_inplace(x, n, p=128):
            tmp = t32[5] if n <= 512 else big32[1]
            op("dve", lambda e: e.tensor_tensor(out=tmp[0:p, 0:n], in0=x[0:p, 0:n], in1=x[0:p, 0:n], op=ALU.mult), [x], [tmp])
            op("dve", lambda e: e.tensor_scalar(out=tmp[0:p, 0:n], in0=tmp[0:p, 0:n], scalar1=0.044715, scalar2=1.0,
                                                op0=ALU.mult, op1=ALU.add), [tmp], [tmp])
            op("dve", lambda e: e.tensor_tensor(out=tmp[0:p, 0:n], in0=tmp[0:p, 0:n], in1=x[0:p, 0:n], op=ALU.mult), [tmp, x], [tmp])
            op("act", lambda e: e.activation(out=tmp[0:p, 0:n], in_=tmp[0:p, 0:n], func=AF.Sigmoid, scale=1.5957691216057308),
               [tmp], [tmp])
            op("dve", lambda e: e.tensor_tensor(out=x[0:p, 0:n], in0=x[0:p, 0:n], in1=tmp[0:p, 0:n], op=ALU.mult), [x, tmp], [x])

        def silu_from_psum(dst, ps, p0, p1, n):
            op("act", lambda e: e.activation(out=dst[p0:p1, 0:n], in_=ps[p0:p1, 0:n], func=AF.Sigmoid), [ps], [dst])
            op("dve", lambda e: e.tensor_tensor(out=dst[p0:p1, 0:n], in0=dst[p0:p1, 0:n], in1=ps[p0:p1, 0:n], op=ALU.mult),
               [dst, ps], [dst])

        def layer_norm_block(layer, j, hTl, tok0, ntok):
            layer_norm_tile(layer, [(j, tok0, ntok)], hTl)

        def layer_norm_tile(layer, blocks, hTl):
            banks = [psA[0], psA[1], psY[0], psY[1]]
            rbuf = [(big32[bi // 2], (bi % 2) * 1024) for bi in range(4)]
            for half in range(2):
                for cbq in range(4):
                    cb = half * 4 + cbq
                    blk = load_w(wo_d[layer], cb * 128, 128)
                    for bi, (j, tok0, ntok) in enumerate(blocks):
                        py = banks[bi]
                        for f in range(8):
                            op("pe", (lambda f, py, cbq, blk, tok0, ntok: lambda e: e.matmul(
                                py[0:ntok, cbq * 128:(cbq + 1) * 128], lhsT=hTl[f][:, tok0:tok0 + ntok], rhs=blk[:, f, :],
                                start=(f == 0), stop=(f == 7)))(f, py, cbq, blk, tok0, ntok), [hTl[f], blk], [py])
                for bi, (j, tok0, ntok) in enumerate(blocks):
                    py = banks[bi]
                    xb = xtok[j]
                    rb, ro = rbuf[bi]
                    op("dve", (lambda py, half, xb, rb, ro, ntok: lambda e: e.scalar_tensor_tensor(
                        out=rb[0:ntok, ro + half * 512:ro + (half + 1) * 512], in0=xb[0:ntok, half * 512:(half + 1) * 512],
                        scalar=DN_ALPHA, in1=py[0:ntok, :], op0=ALU.mult, op1=ALU.add))(py, half, xb, rb, ro, ntok), [xb, py], [rb])
            for bi, (j, tok0, ntok) in enumerate(blocks):
                ln_finish(xtok[j], rbuf[bi][0], rbuf[bi][1], ntok)

        def ln_finish(xb, rb, ro, ntok):
            st_ = small[0]
            sq = t32[0]
            op("dve", lambda e: e.tensor_reduce(out=st_[0:ntok, 0:1], in_=rb[0:ntok, ro:ro + DM], axis=AX.X, op=ALU.add), [rb], [st_])
            for hf in range(2):
                op("pool", (lambda hf: lambda e: e.tensor_tensor(out=sq[0:ntok, 0:512], in0=rb[0:ntok, ro + hf * 512:ro + (hf + 1) * 512],
                                                                 in1=rb[0:ntok, ro + hf * 512:ro + (hf + 1) * 512], op=ALU.mult))(hf), [rb], [sq])
                op("dve", (lambda hf: lambda e: e.tensor_reduce(out=st_[0:ntok, 5 + hf:6 + hf], in_=sq[0:ntok, 0:512], axis=AX.X, op=ALU.add))(hf),
                   [sq], [st_])
            op("dve", lambda e: e.tensor_tensor(out=st_[0:ntok, 1:2], in0=st_[0:ntok, 5:6], in1=st_[0:ntok, 6:7], op=ALU.add), [st_], [st_])
            op("dve", lambda e: e.tensor_scalar(out=st_[0:ntok, 0:2], in0=st_[0:ntok, 0:2], scalar1=1.0 / DM, scalar2=None,
                                                op0=ALU.mult), [st_], [st_])
            op("dve", lambda e: e.tensor_tensor(out=st_[0:ntok, 2:3], in0=st_[0:ntok, 0:1], in1=st_[0:ntok, 0:1], op=ALU.mult), [st_], [st_])
            op("dve", lambda e: e.tensor_tensor(out=st_[0:ntok, 3:4], in0=st_[0:ntok, 1:2], in1=st_[0:ntok, 2:3], op=ALU.subtract), [st_], [st_])
            op("dve", lambda e: e.tensor_scalar(out=st_[0:ntok, 4:5], in0=st_[0:ntok, 3:4], scalar1=LN_EPS, scalar2=None, op0=ALU.add), [st_], [st_])
            op("act", lambda e: e.activation(out=st_[0:ntok, 4:5], in_=st_[0:ntok, 4:5], func=AF.Sqrt), [st_], [st_])
            op("dve", lambda e: e.reciprocal(out=st_[0:ntok, 4:5], in_=st_[0:ntok, 4:5]), [st_], [st_])
            op("dve", lambda e: e.tensor_scalar(out=rb[0:ntok, ro:ro + DM], in0=rb[0:ntok, ro:ro + DM], scalar1=st_[0:ntok, 0:1],
                                                scalar2=st_[0:ntok, 4:5], op0=ALU.subtract, op1=ALU.mult), [rb, st_], [rb])
            op("pool", lambda e: e.tensor_tensor(out=rb[0:ntok, ro:ro + DM], in0=rb[0:ntok, ro:ro + DM], in1=lng_sb[0:ntok, :], op=ALU.mult),
               [rb, lng_sb], [rb])
            op("dve", lambda e: e.tensor_tensor(out=xb[0:ntok, :], in0=rb[0:ntok, ro:ro + DM], in1=lnb_sb[0:ntok, :], op=ALU.add),
               [rb, lnb_sb], [xb])

        def layer_setup(layer):
            dma("sp", lambda e: e.dma_start(out=lng_sb[:], in_=lng_d[layer:layer + 1, :].to_broadcast([128, DM])), writes=[lng_sb])
            dma("sp", lambda e: e.dma_start(out=lnb_sb[:], in_=lnb_d[layer:layer + 1, :].to_broadcast([128, DM])), writes=[lnb_sb])

        mpos = C["c_m2"]
        def even_setup(j):
            dma("sp", lambda e: e.dma_start(out=wsp_f[:], in_=wsp_d[j].rearrange("g i k -> i g k")), writes=[wsp_f])
            for g in range(4):
                op("pool", (lambda g: lambda e: e.affine_select(out=wsp_f[:, g, :], in_=wsp_f[:, g, :], pattern=[[-1, 128]],
                                                                compare_op=ALU.is_ge, fill=0.0, base=0,
                                                                channel_multiplier=1))(g), [wsp_f], [wsp_f])
            ps = nxt("A")
            for g in range(4):
                op("pe", (lambda g: lambda e: e.transpose(out=ps[:, g * 128:(g + 1) * 128], in_=wsp_f[:, g, :],
                                                          identity=C["c_ident"][:]))(g), [wsp_f, C["c_ident"]], [ps])
            op("dve", lambda e: e.tensor_copy(out=wspT[:].rearrange("p g k -> p (g k)"), in_=ps[:, :]), [ps], [wspT])
            dma("sp", lambda e: e.dma_start(out=bsp_bc[:].rearrange("p g k -> p (g k)"),
                                            in_=bsp_d[j:j + 1].rearrange("o g k -> o (g k)").to_broadcast([128, 512])),
                writes=[bsp_bc])
            dma("sp", lambda e: e.dma_start(out=wsp00[:], in_=wsp_d[j, :, 0:1, 0:1].rearrange("g a b -> (a b) g").to_broadcast([128, 4])),
                writes=[wsp00])

        RR = (lambda ap: ap.bitcast(F32R)) if DBG.get("f32r", 0) else (lambda ap: ap)

        def stick_break_head(T, c, hh, po, hb):
            base = 64 * hh
            kbs = list(range(4 * T + 3, -1, -1))
            op("pe", lambda e: e.matmul(po[:, 0:512], lhsT=zero_b[:, :], rhs=xT[0][:, 0:512], start=True, stop=False), [zero_b, xT[0]], [po])
            def step(bi, kb):
                m = kb - 4 * T
                c0 = max(0, m) * 128
                Tk, kk = kb // 4, kb % 4
                pz, ee, sp, at, S32 = hb["pz"], hb["ee"], hb["sp"], hb["at"], hb["S"]
                ksrc = kTt[Tk]
                op("pe", lambda e: e.matmul(pz[:, c0:512], lhsT=ksrc[base:base + 64, c, kk * 128:(kk + 1) * 128],
                                            rhs=qT[c][base:base + 64, c0:512], start=True, stop=True), [ksrc, qT[c]], [pz])
                if m >= 0:
                    op("pe", lambda e: e.matmul(pz[:, c0:c0 + 128], lhsT=ident_b[:], rhs=negm_b[:], start=False, stop=True),
                       [ident_b, negm_b], [pz])
                op("act", lambda e: e.activation(out=ee[:, c0:512], in_=pz[:, c0:512], func=AF.Exp), [pz], [ee])
                op("act", lambda e: e.activation(out=RR(sp[:, c0:512]), in_=ee[:, c0:512], func=AF.Ln, bias=1.0), [ee], [sp])
                if DBG.get("f32r", 0):
                    op("pe", lambda e: e.matmul(pz[:, c0:512], lhsT=ntri_r[:], rhs=sp[:, c0:512].bitcast(F32R),
                                                start=False, stop=True), [ntri_r, sp], [pz])
                    if bi > 0:
                        op("pe", lambda e: e.matmul(pz[:, c0:512], lhsT=nones_r[:], rhs=S32[:, c0:512].bitcast(F32R),
                                                    start=False, stop=True), [nones_r, S32], [pz])
                else:
                    op("pe", lambda e: e.matmul(pz[:, c0:512], lhsT=C["c_ntri"][:], rhs=sp[:, c0:512], start=False, stop=True),
                       [C["c_ntri"], sp], [pz])
                    if bi > 0:
                        op("pe", lambda e: e.matmul(pz[:, c0:512], lhsT=C["c_nones"][:], rhs=S32[:, c0:512], start=False, stop=True),
                           [C["c_nones"], S32], [pz])
                op("act", lambda e: e.activation(out=at[:, c0:512], in_=pz[:, c0:512], func=AF.Exp), [pz], [at])
                if bi == 0:
                    if c0 > 0:
                        op("pool", lambda e: e.memset(S32[:, 0:c0], 0.0), [], [S32])
                    op("pool", lambda e: e.tensor_copy(out=RR(S32[:, c0:512]), in_=sp[:, c0:512]), [sp], [S32])
                elif bi < len(kbs) - 1:
                    op("pool", lambda e: e.tensor_tensor(out=RR(S32[:, c0:512]), in0=S32[:, c0:512], in1=sp[:, c0:512], op=ALU.add),
                       [S32, sp], [S32])
                vsrc = vbt[Tk]
                last = (kb == 0)
                if m >= 0:
                    op("pe", lambda e: e.matmul(po[:, c0:c0 + 128], lhsT=vsrc[:, kk, c * 128:(c + 1) * 128], rhs=at[:, c0:c0 + 128],
                                                start=False, stop=last), [vsrc, at], [po])
                    if c0 + 128 < 512:
                        op("pe", lambda e: e.matmul(po[:, c0 + 128:512], lhsT=vsrc[:, kk, c * 128:(c + 1) * 128],
                                                    rhs=at[:, c0 + 128:512], start=False, stop=last), [vsrc, at], [po])
                else:
                    op("pe", lambda e: e.matmul(po[:, 0:512], lhsT=vsrc[:, kk, c * 128:(c + 1) * 128], rhs=at[:, 0:512],
                                                start=False, stop=last), [vsrc, at], [po])

            for bi, kb in enumerate(kbs):
                step(bi, kb)
                yield

        def even_prompt_tile(layer, j, T):
            W = wie_d[j]
            t0 = T * 512
            if DBG.get("stage", 9) < 1:
                return
            xTl, ntok = make_xT(T)
            sub = DBG.get("sub", 9)
            if sub < 1:
                return
            for c in range(4):
                blk = load_w(W, 2048 + 128 * c, 128)
                if sub < 2:
                    continue
                ps = nxt("A")
                fm_mm(ps, blk, 128, xTl, 512)
                if sub < 3:
                    continue
                op("dve", (lambda c, ps: lambda e: e.tensor_copy(out=kTt[T][:, c, :], in_=ps[:, :]))(c, ps), [ps], [kTt[T]])
                if sub < 4:
                    continue
                ps2 = nxt("A")
                for jj in range(4):
                    tm_mm(ps2, jj * 128, blk, 128, xTl, jj * 128, 128)
                for hf in range(2):
                    op("dve", (lambda c, ps2, hf: lambda e: e.tensor_copy(
                        out=big32[hf][:, 0:2048].rearrange("p (j f) -> p j f", j=2)[:, :, 128 * c:128 * (c + 1)],
                        in_=ps2[:, hf * 256:(hf + 1) * 256].rearrange("p (j f) -> p j f", j=2)))(c, ps2, hf), [ps2], [big32[hf]])
            if sub < 5:
                return
            for c in range(4):
                blk = load_w(W, 2560 + 128 * c, 128)
                ps2 = nxt("A")
                for jj in range(4):
                    tm_mm(ps2, jj * 128, blk, 128, xTl, jj * 128, 128)
                for hf in range(2):
                    op("dve", (lambda c, ps2, hf: lambda e: e.tensor_copy(
                        out=big32[hf][:, 0:2048].rearrange("p (j f) -> p j f", j=2)[:, :, 512 + 128 * c:512 + 128 * (c + 1)],
                        in_=ps2[:, hf * 256:(hf + 1) * 256].rearrange("p (j f) -> p j f", j=2)))(c, ps2, hf), [ps2], [big32[hf]])
                op("dve", (lambda c, ps2: lambda e: e.tensor_copy(out=vbt[T][:, :, 128 * c:128 * (c + 1)],
                                                                  in_=ps2[:, :].rearrange("p (j f) -> p j f", j=4)))(c, ps2), [ps2], [vbt[T]])
            for hf in range(2):
                dma("pool", (lambda hf: lambda e: e.dma_start(
                    out=bkvp_d[j, t0 + hf * 256:t0 + (hf + 1) * 256, :].rearrange("(j p) f -> p j f", p=128),
                    in_=big32[hf][:, 0:2048].rearrange("p (j f) -> p j f", j=2)))(hf), reads=[big32[hf]])
            if DBG.get("stage", 9) < 2:
                return
            for c in range(4):
                blk = load_w(W, 1536 + 128 * c, 128)
                ps = nxt("A")
                fm_mm(ps, blk, 128, xTl, 512)
                op("dve", (lambda c, ps: lambda e: e.tensor_scalar(out=qT[c][:, :], in0=ps[:, :], scalar1=0.125, scalar2=None,
                                                                   op0=ALU.mult))(c, ps), [ps], [qT[c]])
            if DBG.get("stage", 9) < 3:
                return
            vraw = big32[0]
            for c in range(4):
                blk = load_w(W, 512 + 128 * c, 128)
                ps2 = nxt("A")
                for jj in range(4):
                    tm_mm(ps2, jj * 128, blk, 128, xTl, jj * 128, 128)
                op("dve", (lambda c, ps2: lambda e: e.tensor_copy(
                    out=vraw[:, 0:2048].rearrange("p (j f) -> p j f", j=4)[:, :, 128 * c:128 * (c + 1)],
                    in_=ps2[:, :].rearrange("p (j f) -> p j f", j=4)))(c, ps2), [ps2], [vraw])
            gelu_inplace(vraw, 2048)
            standardize(vraw, 4, 128)
            for jj in range(4):
                op("pool", (lambda jj: lambda e: e.tensor_copy(out=hT[4 + jj][:, :], in_=vraw[:, jj * 512:(jj + 1) * 512]))(jj),
                   [vraw], [hT[4 + jj]])
            for g in range(4):
                blk = load_w(W, 128 * g, 128)
                ps = nxt("A")
                fm_mm(ps, blk, 128, xTl, 512)
                ug = t32[0]
                op("dve", (lambda ps: lambda e: e.tensor_copy(out=ug[:, :], in_=ps[:, :]))(ps), [ps], [ug])
                gelu_inplace(ug, 512)
                blk = load_w(W, 1024 + 128 * g, 128)
                ps = nxt("A")
                fm_mm(ps, blk, 128, xTl, 512)
                sg = t32[1]
                silu_from_psum(sg, ps, 0, 128, 512)
                ps = nxt("A")
                for jj in range(4):
                    op("pe", (lambda g, jj, ps: lambda e: e.matmul(ps[:, jj * 128:(jj + 1) * 128], lhsT=hT[4 + jj][:, 128 * g:128 * (g + 1)],
                                                                   rhs=wspT[:, g, :], start=True, stop=True))(g, jj, ps),
                       [hT[4 + jj], wspT], [ps])
                sb_ = t32[2]
                op("dve", (lambda g, ps: lambda e: e.tensor_tensor(
                    out=sb_[:, :].rearrange("p (j i) -> p j i", j=4), in0=ps[:, :].rearrange("p (j i) -> p j i", j=4),
                    in1=bsp_bc[:, g:g + 1, :].to_broadcast([128, 4, 128]), op=ALU.add))(g, ps), [ps, bsp_bc], [sb_])
                op("pool", lambda e: e.tensor_tensor(out=sb_[:, :], in0=sb_[:, :], in1=ug[:, :], op=ALU.mult), [sb_, ug], [sb_])
                op("dve", (lambda g: lambda e: e.tensor_tensor(out=hT[g][:, :], in0=sb_[:, :], in1=sg[:, :], op=ALU.mult))(g),
                   [sb_, sg], [hT[g]])
            if DBG.get("stage", 9) < 4:
                return
            for c in range(4):
                blk = load_w(W, 3072 + 128 * c, 128)
                ps = nxt("A")
                fm_mm(ps, blk, 128, xTl, 512)
                sgb = t32[3]
                silu_from_psum(sgb, ps, 0, 128, 512)
                hbufs = [dict(pz=psZ[0], ee=e32[0], sp=sp32[0], at=att_b[0], S=S32),
                         dict(pz=psZ[1], ee=View(kpg[0], kpg[0][:, 0, :]), sp=View(kpg[1], kpg[1][:, 0, :]), at=att_b[1], S=qb_sb)]
                interleave([stick_break_head(T, c, hh, psO[hh], hbufs[hh]) for hh in range(2)])
                for hh in range(2):
                    po = psO[hh]
                    b0 = 64 * hh
                    op("dve", (lambda c, po, b0: lambda e: e.tensor_tensor(out=hT[4 + c][b0:b0 + 64, :], in0=po[b0:b0 + 64, :],
                                                                           in1=sgb[b0:b0 + 64, :], op=ALU.mult))(c, po, b0),
                       [po, sgb], [hT[4 + c]])
            if dbgh_d is not None and T == DBG.get("dbgT", 0) and DBG.get("dbgL", 0) == 0:
                for f in range(8):
                    dma("pool", (lambda f: lambda e: e.dma_start(out=dbgh_d[f], in_=hT[f][:, :]))(f), reads=[hT[f]])
            if DBG.get("stage", 9) < 5:
                return
            layer_norm_tile(layer, [(4 * T + jj, jj * 128, 128) for jj in range(4)], hT)

        def standardize(x, nj, p):
            sq = big32[1]
            n = nj * 512
            st_ = small[1]
            xv = x[0:p, 0:n].rearrange("p (j f) -> p j f", j=nj)
            op("dve", lambda e: e.tensor_reduce(out=st_[0:p, 0:nj], in_=xv, axis=AX.X, op=ALU.add), [x], [st_])
            op("pool", lambda e: e.tensor_tensor(out=sq[0:p, 0:n], in0=x[0:p, 0:n], in1=x[0:p, 0:n], op=ALU.mult), [x], [sq])
            st2 = small[2]
            op("dve", lambda e: e.tensor_reduce(out=st2[0:p, 0:nj], in_=sq[0:p, 0:n].rearrange("p (j f) -> p j f", j=nj),
                                                axis=AX.X, op=ALU.add), [sq], [st2])
            op("dve", lambda e: e.tensor_scalar(out=st_[0:p, 0:nj], in0=st_[0:p, 0:nj], scalar1=1.0 / 512, scalar2=None, op0=ALU.mult),
               [st_], [st_])
            op("dve", lambda e: e.tensor_scalar(out=st2[0:p, 0:nj], in0=st2[0:p, 0:nj], scalar1=1.0 / 512, scalar2=None, op0=ALU.mult),
               [st2], [st2])
            st3 = small[3]
            op("dve", lambda e: e.tensor_tensor(out=st3[0:p, 0:nj], in0=st_[0:p, 0:nj], in1=st_[0:p, 0:nj], op=ALU.mult), [st_], [st3])
            op("dve", lambda e: e.tensor_tensor(out=st2[0:p, 0:nj], in0=st2[0:p, 0:nj], in1=st3[0:p, 0:nj], op=ALU.subtract),
               [st2, st3], [st2])
            op("dve", lambda e: e.tensor_scalar(out=st2[0:p, 0:nj], in0=st2[0:p, 0:nj], scalar1=LN_EPS, scalar2=None, op0=ALU.add), [st2], [st2])
            op("act", lambda e: e.activation(out=st2[0:p, 0:nj], in_=st2[0:p, 0:nj], func=AF.Sqrt), [st2], [st2])
            op("dve", lambda e: e.reciprocal(out=st2[0:p, 0:nj], in_=st2[0:p, 0:nj]), [st2], [st2])
            for jx in range(nj):
                op("dve", (lambda jx: lambda e: e.tensor_scalar(out=x[0:p, jx * 512:(jx + 1) * 512], in0=x[0:p, jx * 512:(jx + 1) * 512],
                                                                scalar1=st_[0:p, jx:jx + 1], scalar2=st2[0:p, jx:jx + 1],
                                                                op0=ALU.subtract, op1=ALU.mult))(jx), [x, st_, st2], [x])

        def sample_inproj(W, col0, ncols, xTl, ps):
            c0 = 0
            while c0 < ncols:
                n = min(128, ncols - c0)
                blk = load_w(W, col0 + c0, n)
                tm_mm(ps, c0, blk, n, xTl, 0, NS)
                c0 += n

        def page_val(e, b, pg):
            return e.value_load(pts[0:1, b * NPG + pg:b * NPG + pg + 1])

        def even_sample_tile(layer, j):
            W = wie_d[j]
            xTl, ntok = make_xT(4)
            for c in range(4):
                blk = load_w(W, 3072 + 128 * c, 128)
                ps = nxt("A")
                fm_mm(ps, blk, 128, xTl, NS)
                op("act", (lambda c, ps: lambda e: e.activation(out=gbTs[:, c, :], in_=ps[:, 0:NS], func=AF.Sigmoid))(c, ps), [ps], [gbTs])
                op("dve", (lambda c, ps: lambda e: e.tensor_tensor(out=gbTs[:, c, :], in0=gbTs[:, c, :], in1=ps[:, 0:NS], op=ALU.mult))(c, ps),
                   [gbTs, ps], [gbTs])
            for hf in range(2):
                psk = nxt("A")
                sample_inproj(W, 2048 + 512 * hf, 512, xTl, psk)
                op("dve", (lambda hf, psk: lambda e: e.tensor_copy(out=big32[0][0:NS, hf * 512:(hf + 1) * 512], in_=psk[0:NS, :]))(hf, psk),
                   [psk], [big32[0]])
            dma("pool", lambda e: e.dma_start(out=bkvs_d[j], in_=big32[0][0:NS, 0:1024]), reads=[big32[0]])
            v = s16[0]
            psv = nxt("A")
            sample_inproj(W, 512, 512, xTl, psv)
            op("dve", lambda e: e.tensor_copy(out=v[0:NS, :], in_=psv[0:NS, :]), [psv], [v])
            gelu_inplace(v, 512, NS)
            standardize(v, 1, NS)
            dma("pool", lambda e: e.dma_start(out=avs_d[j], in_=v[0:NS, :]), reads=[v])
            u = s16[1]
            psu = nxt("A")
            sample_inproj(W, 0, 512, xTl, psu)
            op("dve", lambda e: e.tensor_copy(out=u[0:NS, :], in_=psu[0:NS, :]), [psu], [u])
            gelu_inplace(u, 512, NS)
            ga = s16[2]
            psg = nxt("A")
            sample_inproj(W, 1024, 512, xTl, psg)
            silu_from_psum(ga, psg, 0, NS, 512)
            sv = s16[3]
            op("dve", lambda e: e.tensor_tensor(out=sv[0:NS, :].rearrange("p (g f) -> p g f", g=4),
                                                in0=v[0:NS, :].rearrange("p (g f) -> p g f", g=4),
                                                in1=wsp00[0:NS, :].unsqueeze(2).to_broadcast([NS, 4, 128]), op=ALU.mult), [v, wsp00], [sv])
            op("dve", lambda e: e.tensor_tensor(out=sv[0:NS, :].rearrange("p (g f) -> p g f", g=4),
                                                in0=sv[0:NS, :].rearrange("p (g f) -> p g f", g=4),
                                                in1=bsp_bc[0:NS, :, 0:1].to_broadcast([NS, 4, 128]), op=ALU.add), [sv, bsp_bc], [sv])
            op("dve", lambda e: e.tensor_tensor(out=sv[0:NS, :], in0=sv[0:NS, :], in1=u[0:NS, :], op=ALU.mult), [sv, u], [sv])
            op("dve", lambda e: e.tensor_tensor(out=sv[0:NS, :], in0=sv[0:NS, :], in1=ga[0:NS, :], op=ALU.mult), [sv, ga], [sv])
            ps = nxt("A")
            for g in range(4):
                op("pe", (lambda g, ps: lambda e: e.transpose(out=ps[:, g * 128:g * 128 + NS], in_=sv[0:NS, g * 128:(g + 1) * 128],
                                                              identity=C["c_ident"][0:NS, 0:NS]))(g, ps), [sv, C["c_ident"]], [ps])
            for g in range(4):
                op("dve", (lambda g, ps: lambda e: e.tensor_copy(out=hTs[g][:, :], in_=ps[:, g * 128:g * 128 + NS]))(g, ps), [ps], [hTs[g]])
            qsc = s16[4]
            psq = nxt("A")
            sample_inproj(W, 1536, 512, xTl, psq)
            op("dve", lambda e: e.tensor_scalar(out=qsc[0:NS, :], in0=psq[0:NS, :], scalar1=0.125, scalar2=None, op0=ALU.mult), [psq], [qsc])
            VB = [View(big32[i], big32[i][:, 0:2304].bitcast(BF16)) for i in range(2)]
            pso = psO[0]
            op("pe", lambda e: e.matmul(pso[:, 0:512], lhsT=zero_b[:, :], rhs=xT[0][:, 0:512], start=True, stop=False), [zero_b, xT[0]], [pso])
            for b in range(NS):
                pq = nxt("A")
                qm = t32[5]
                op("dve", (lambda b: lambda e: e.tensor_scalar(out=qm[0:NS, :], in0=qsc[0:NS, :], scalar1=C["c_ident"][0:NS, b:b + 1],
                                                               scalar2=-1.0, op0=ALU.mult, op1=ALU.mult))(b), [qsc, C["c_ident"]], [qm])
                op("pe", (lambda b, pq: lambda e: e.matmul(pq[:, :], lhsT=C["c_nones"][0:NS, :], rhs=qm[0:NS, :],
                                                           start=True, stop=True))(b, pq), [C["c_nones"], qm], [pq])
                op("dve", (lambda pq: lambda e: e.tensor_copy(out=qb_sb[:, :], in_=pq[:, :]))(pq), [pq], [qb_sb])
                for pg in range(NPG):
                    kb_ = kpg[pg % 2]
                    dma("pool", (lambda b, pg, kb_: lambda e: e.indirect_dma_start(
                        out=kb_[:, :, :].rearrange("p a f -> p (a f)"), out_offset=None, in_=cb_d[:, 0:1024], element_offset=j * pool_rows * 1024,
                        in_offset=bass.IndirectOffsetOnAxis(ap=idxC[:, b * NPG + pg:b * NPG + pg + 1], axis=0)))(b, pg, kb_),
                        reads=[idxC], writes=[kb_])
                    prod = t32[0]
                    op("dve", (lambda kb_: lambda e: e.tensor_tensor(out=prod[:, 0:512], in0=kb_[:, 0, :], in1=qb_sb[:, :], op=ALU.mult))(kb_),
                       [kb_, qb_sb], [prod])
                    op("dve", (lambda pg: lambda e: e.tensor_reduce(
                        out=zs[:, pg * 8:(pg + 1) * 8], in_=prod[:, 0:512].rearrange("p (a d) -> p a d", d=64), axis=AX.X,
                        op=ALU.add))(pg), [prod], [zs])
                    vdst = VB[0 if pg < 9 else 1]
                    vo = (pg if pg < 9 else pg - 9) * 512
                    op("dve", (lambda kb_, vdst, vo: lambda e: e.tensor_copy(out=vdst[:, vo:vo + 512], in_=kb_[:, 1, :]))(kb_, vdst, vo),
                       [kb_], [vdst])
                op("act", lambda e: e.activation(out=sps[:, :], in_=zs[:, :], func=AF.Exp), [zs], [sps])
                op("act", lambda e: e.activation(out=sps[:, :], in_=sps[:, :], func=AF.Ln, bias=1.0), [sps], [sps])
                pt_ = nxt("Z")
                op("pe", (lambda pt_: lambda e: e.matmul(pt_[:, 0:128], lhsT=sps[:, :], rhs=C["c_nones"][:, :], start=True, stop=True))(pt_),
                   [sps, C["c_nones"]], [pt_])
                op("dve", (lambda pt_: lambda e: e.tensor_copy(out=tts[:, :], in_=pt_[:, 0:128]))(pt_), [pt_], [tts])
                pz = nxt("Z")
                op("pe", (lambda pz: lambda e: e.matmul(pz[:, 0:128], lhsT=C["c_ident"][:, :], rhs=zs[:, :], start=True, stop=True))(pz),
                   [C["c_ident"], zs], [pz])
                op("pe", (lambda pz: lambda e: e.matmul(pz[:, 0:128], lhsT=C["c_ntri"][:, :], rhs=sps[:, :], start=False, stop=True))(pz),
                   [C["c_ntri"], sps], [pz])
                op("pe", (lambda pz: lambda e: e.matmul(pz[:, 0:128], lhsT=tts[:, :], rhs=mpos[:, :], start=False, stop=True))(pz),
                   [tts, mpos], [pz])
                op("act", (lambda pz: lambda e: e.activation(out=atts[:, :], in_=pz[:, 0:128], func=AF.Exp))(pz), [pz], [atts])
                for pg in range(NPG):
                    vsrc_ = VB[0 if pg < 9 else 1]
                    vo = (pg if pg < 9 else pg - 9) * 512
                    for c in range(4):
                        op("pe", (lambda b, pg, c, vsrc_, vo: lambda e: e.matmul(
                            pso[:, c * 128 + b * 8:c * 128 + b * 8 + 8], lhsT=vsrc_[:, vo + c * 128:vo + (c + 1) * 128],
                            rhs=atts[:, pg * 8:(pg + 1) * 8], start=False, stop=(pg == 15)))(b, pg, c, vsrc_, vo),
                           [vsrc_, atts], [pso])
            for c in range(4):
                src = pso[:, c * 128:(c + 1) * 128].rearrange("p (b h) -> p b h", h=8)
                op("dve", (lambda c, src: lambda e: e.tensor_copy(out=oBT[0:64, c, :], in_=src[0:64, :, 2 * c]))(c, src), [pso], [oBT])
                op("dve", (lambda c, src: lambda e: e.tensor_copy(out=oBT[64:128, c, :], in_=src[64:128, :, 2 * c + 1]))(c, src), [pso], [oBT])
            for c in range(4):
                op("dve", (lambda c: lambda e: e.tensor_tensor(out=hTs[4 + c][:, :], in0=oBT[:, c, :], in1=gbTs[:, c, :], op=ALU.mult))(c),
                   [oBT, gbTs], [hTs[4 + c]])
            layer_norm_block(layer, 16, hTs, 0, NS)

        mpos = C["c_m2"]


        ISQ = 1.0 / math.sqrt(32.0)
        maskT = kTt[3]
        b31_sb = small[3]

        def odd_setup(j):
            for i in range(4):
                op("pool", (lambda i: lambda e: e.memset(vbt[i][:, :, :], 1.0))(i), [], [vbt[i]])
            dma("sp", lambda e: e.dma_start(out=qb_sb[:, :].rearrange("p (g d) -> p g d", g=4), in_=wgrp_d[j].rearrange("g c d -> c g d")),
                writes=[qb_sb])
            op("dve", lambda e: e.tensor_copy(out=wspT[:].rearrange("p g d -> p (g d)"), in_=qb_sb[:, :]), [qb_sb], [wspT])
            dma("sp", lambda e: e.dma_start(out=wsp00[:, :], in_=dsc_d[j:j + 1, :].rearrange("o (g p) -> p (o g)", p=128)), writes=[wsp00])
            dma("sp", lambda e: e.dma_start(out=b31_sb[:, 0:8], in_=b31_d.to_broadcast([128, 8])), writes=[b31_sb])
            op("pool", lambda e: e.memset(wsp_f[:, :, :], 0.0), [], [wsp_f])

        def odd_select(T, jj, idx, wk):
            qb = 4 * T + jj
            Wk = (qb + 1) * 128
            wq = bsp_bc
            if qb < 2:
                return
            nkc = (Wk + 511) // 512
            for kc in range(nkc):
                n = min(512, Wk - kc * 512)
                for ih in range(4):
                    pz = nxt("A")
                    op("pe", (lambda kc, n, ih, pz: lambda e: e.matmul(pz[:, 0:n], lhsT=vpg[ih // 2][:, ih % 2, jj * 128:(jj + 1) * 128],
                                                                       rhs=kTt[2][:, kc, 0:n], start=True, stop=True))(kc, n, ih, pz),
                       [vpg[ih // 2], kTt[2]], [pz])
                    rl = e32[0]
                    op("act", (lambda n, pz: lambda e: e.activation(out=rl[:, 0:n], in_=pz[:, 0:n], func=AF.Relu))(n, pz), [pz], [rl])
                    if ih == 0:
                        op("dve", (lambda kc, n: lambda e: e.tensor_scalar(out=idx[:, kc * 512:kc * 512 + n], in0=rl[:, 0:n],
                                                                           scalar1=wq[:, jj, 0:1], scalar2=None, op0=ALU.mult))(kc, n),
                           [rl, wq], [idx])
                    else:
                        op("dve", (lambda kc, n, ih: lambda e: e.scalar_tensor_tensor(
                            out=idx[:, kc * 512:kc * 512 + n], in0=rl[:, 0:n], scalar=wq[:, jj, ih:ih + 1],
                            in1=idx[:, kc * 512:kc * 512 + n], op0=ALU.mult, op1=ALU.add))(kc, n, ih), [rl, wq, idx], [idx])
                yield
            op("dve", lambda e: e.tensor_tensor(out=idx[:, qb * 128:Wk], in0=idx[:, qb * 128:Wk], in1=C["c_negq"][:, :], op=ALU.add),
               [idx, C["c_negq"]], [idx])
            op("pool", lambda e: e.tensor_copy(out=wk[:, 0:Wk], in_=idx[:, 0:Wk]), [idx], [wk])
            mx = small[1]
            for r in range(32):
                op("dve", lambda e: e.max(out=mx[:, 0:8], in_=wk[:, 0:Wk]), [wk], [mx])
                op("dve", lambda e: e.match_replace(out=wk[:, 0:Wk], in_to_replace=mx[:, 0:8], in_values=wk[:, 0:Wk], imm_value=2.0 * NEG),
                   [wk, mx], [wk])
                yield
            op("dve", lambda e: e.tensor_tensor(out=wk[:, 0:Wk], in0=wk[:, 0:Wk], in1=idx[:, 0:Wk], op=ALU.not_equal), [wk, idx], [wk])

        def odd_finish_mask(T, jj, wk):
            qb = 4 * T + jj
            mT = maskT[:, :, :].rearrange("p a (b q) -> p (a b) q", q=128)
            if qb < 2:
                for kb in range(qb):
                    op("pool", (lambda kb: lambda e: e.memset(mT[:, kb, :], 1.0))(kb), [], [maskT])
                op("dve", lambda e: e.tensor_copy(out=mT[:, qb, :], in_=C["c_dsac"][:, :]), [C["c_dsac"]], [maskT])
            else:
                for k4 in range(0, qb + 1, 4):
                    nb = min(4, qb + 1 - k4)
                    pt_ = nxt("A")
                    for i in range(nb):
                        op("pe", (lambda k4, i, pt_: lambda e: e.transpose(out=pt_[:, i * 128:(i + 1) * 128],
                                                                           in_=wk[:, (k4 + i) * 128:(k4 + i + 1) * 128],
                                                                           identity=C["c_ident"][:]))(k4, i, pt_), [wk, C["c_ident"]], [pt_])
                    op("dve", (lambda k4, nb, pt_: lambda e: e.tensor_copy(out=mT[:, k4:k4 + nb, :],
                                                                           in_=pt_[:, 0:nb * 128].rearrange("p (b q) -> p b q", q=128)))(k4, nb, pt_),
                       [pt_], [maskT])
            return mT

        def odd_attend(T, jj, mT):
            qb = 4 * T + jj
            obufs = [dict(po=psO[0], pz=psZ[0], at=att_b[0], tz=[sps, tts], lt=sp32[0], rz=zs),
                     dict(po=psO[1], pz=psZ[1], at=att_b[1], tz=[View(kpg[0], kpg[0][:, 0, 0:128]), View(kpg[0], kpg[0][:, 0, 128:256])],
                          lt=View(kpg[1], kpg[1][:, 0, 0:128]), rz=View(qb_sb, qb_sb[:, 0:128]))]
            for h2 in range(0, 8, 2):
                gens = [odd_head(T, jj, qb, h2 + i, mT, obufs[i]) for i in range(2)]
                while gens:
                    for g_ in list(gens):
                        try:
                            next(g_)
                        except StopIteration:
                            gens.remove(g_)
                    yield

        def odd_head(T, jj, qb, h, mT, ob_):
            g, c, base = h // 4, h // 2, 64 * (h % 2)
            po = ob_["po"]
            op("pe", lambda e: e.matmul(po[:, 0:128], lhsT=zero_b[:, :], rhs=xT[0][:, 0:128], start=True, stop=False), [zero_b, xT[0]], [po])
            vA = vbt[g] if base == 0 else vbt[2 + g]
            tz = ob_["tz"]
            zs_ = ob_["rz"]
            if True:
                for dl in range(2):
                    dma("sp", (lambda dl: lambda e: e.dma_start(out=tz[dl][:, 0:128], in_=tz_d[:, (dl * 8 + h) * 128:(dl * 8 + h + 1) * 128]))(dl),
                        writes=[tz[dl]])

            def step(kb):
                Tk, kk = kb // 4, kb % 4
                pz = ob_["pz"]
                at = ob_["at"]
                op("pe", lambda e: e.matmul(pz[:, 0:128], lhsT=kTt[g][base:base + 64, Tk, kk * 128:(kk + 1) * 128],
                                            rhs=qT[c][base:base + 64, jj * 128:(jj + 1) * 128], start=True, stop=True), [kTt[g], qT[c]], [pz])
                delta = qb - kb
                if delta >= 2:
                    op("act", lambda e: e.activation(out=at[:, 0:128], in_=pz[:, 0:128], func=AF.Exp, bias=b31_sb[:, h:h + 1]), [pz, b31_sb], [at])
                else:
                    lt = ob_["lt"]
                    op("dve", lambda e: e.tensor_tensor(out=lt[:, 0:128], in0=pz[:, 0:128], in1=tz[delta][:, 0:128], op=ALU.add), [pz, tz[delta]], [lt])
                    op("act", lambda e: e.activation(out=at[:, 0:128], in_=lt[:, 0:128], func=AF.Exp), [lt], [at])
                op("dve", lambda e: e.tensor_tensor(out=at[:, 0:128], in0=at[:, 0:128], in1=mT[:, kb, :], op=ALU.mult), [at, maskT], [at])
                op("pe", lambda e: e.matmul(po[:, 0:128], lhsT=vA[:, Tk, kk * 128:(kk + 1) * 128], rhs=at[:, 0:128], start=False, stop=(kb == qb)),
                   [vA, at], [po])

            for kb in range(qb + 1):
                step(kb)
                yield
            ob = 64 - base
            op("dve", lambda e: e.reciprocal(out=zs_[base:base + 64, 0:128], in_=po[ob:ob + 64, 0:128]), [po], [zs_])
            op("dve", lambda e: e.tensor_tensor(out=t32[c][base:base + 64, jj * 128:(jj + 1) * 128], in0=po[base:base + 64, 0:128],
                                                in1=zs_[base:base + 64, 0:128], op=ALU.mult), [po, zs_], [t32[c]])

        def odd_prompt_tile(layer, j, T):
            W = wio_d[j]
            xTl, _ = make_xT(T)
            for c in range(4):
                blk = load_w(W, 128 * c, 128)
                ps = nxt("A")
                fm_mm(ps, blk, 128, xTl, 512)
                op("dve", (lambda c, ps: lambda e: e.tensor_scalar(out=qT[c][:, :], in0=ps[:, :], scalar1=0.125, scalar2=None,
                                                                   op0=ALU.mult))(c, ps), [ps], [qT[c]])
            for g in range(2):
                blk = load_w(W, 0, 0, pieces=[(512 + 64 * g, 64, 0), (512 + 64 * g, 64, 64)])
                ps = nxt("A")
                fm_mm(ps, blk, 128, xTl, 512)
                op("dve", (lambda g, ps: lambda e: e.tensor_copy(out=kTt[g][:, T, :], in_=ps[:, :]))(g, ps), [ps], [kTt[g]])
            blk = load_w(W, 0, 0, pieces=[(896, 32, 0), (896, 32, 32), (896, 32, 64), (896, 32, 96)])
            ps = nxt("A")
            fm_mm(ps, blk, 128, xTl, 512)
            op("dve", (lambda ps: lambda e: e.tensor_copy(out=kTt[2][:, T, :], in_=ps[:, :]))(ps), [ps], [kTt[2]])
            blk = load_w(W, 768, 128)
            ps = nxt("A")
            fm_mm(ps, blk, 128, xTl, 512)
            for ih in range(4):
                op("dve", (lambda ih, ps: lambda e: e.tensor_scalar(out=vpg[ih // 2][:, ih % 2, :], in0=ps[:, :],
                                                                    scalar1=C["c_rowm"][:, ih:ih + 1], scalar2=None, op0=ALU.mult))(ih, ps),
                   [ps, C["c_rowm"]], [vpg[ih // 2]])
            pstm = [psA[0], psA[1], psY[0], psY[1]]
            for (c0, n, pc) in ((512, 128, 0), (640, 128, 128), (896, 36, 256)):
                blk = load_w(W, c0, n)
                for jj in range(4):
                    tm_mm(pstm[jj], pc, blk, n, xTl, jj * 128, 128)
            rowbuf = t32[5]
            for jj in range(4):
                tok0 = T * 512 + jj * 128
                op("dve", (lambda jj: lambda e: e.tensor_copy(out=rowbuf[:, 0:292], in_=pstm[jj][:, 0:292]))(jj), [pstm[jj]], [rowbuf])
                dma("pool", (lambda tok0: lambda e: e.dma_start(out=ckvp_d[j, tok0:tok0 + 128, :], in_=rowbuf[:, 0:256]))(tok0), reads=[rowbuf])
                dma("pool", (lambda tok0: lambda e: e.dma_start(out=ckip_d[j, tok0:tok0 + 128, :], in_=rowbuf[:, 256:288]))(tok0), reads=[rowbuf])
                for g in range(2):
                    op("dve", (lambda g, jj: lambda e: e.tensor_copy(out=vbt[g][:, T, jj * 128:jj * 128 + 64],
                                                                    in_=rowbuf[:, 128 + 64 * g:192 + 64 * g]))(g, jj), [rowbuf], [vbt[g]])
                    op("pool", (lambda g, jj: lambda e: e.tensor_copy(out=vbt[2 + g][:, T, jj * 128 + 64:jj * 128 + 128],
                                                                     in_=rowbuf[:, 128 + 64 * g:192 + 64 * g]))(g, jj), [rowbuf], [vbt[2 + g]])
                op("dve", (lambda jj: lambda e: e.tensor_scalar(out=bsp_bc[:, jj, 0:4], in0=rowbuf[:, 288:292], scalar1=ISQ, scalar2=None,
                                                                op0=ALU.mult))(jj), [rowbuf], [bsp_bc])
            idx, wk = big32[0], big32[1]
            if DBG.get("ostage", 9) >= 1:
                interleave([odd_select(T, 0, idx, wk)])
                for jj in range(4):
                    mT = odd_finish_mask(T, jj, wk)
                    gens = [odd_attend(T, jj, mT)]
                    if jj < 3:
                        gens.append(odd_select(T, jj + 1, idx, wk))
                    interleave(gens)
            else:
                for c in range(4):
                    op("pool", (lambda c: lambda e: e.memset(t32[c][:, :], 0.0))(c), [], [t32[c]])
            for c in range(4):
                blk = load_w(W, 932 + 128 * c, 128)
                ps = nxt("A")
                fm_mm(ps, blk, 128, xTl, 512)
                sg = t32[4]
                silu_from_psum(sg, ps, 0, 128, 512)
                op("dve", (lambda c: lambda e: e.tensor_tensor(out=hT[c][:, :], in0=t32[c][:, :], in1=sg[:, :], op=ALU.mult))(c),
                   [t32[c], sg], [hT[c]])
            Pv = kpg[0][:, :, :].rearrange("p a b -> p (a b)")
            Av = kpg[1][:, :, :].rearrange("p a b -> p (a b)")
            Bv = big32[1]
            for g in range(4):
                wwin = 2 ** (g + 1)
                blk = load_w(W, 1444 + 128 * g, 128)
                ps = nxt("A")
                fm_mm(ps, blk, 128, xTl, 512)
                op("dve", (lambda g: lambda e: e.tensor_copy(out=Pv[:, 0:16], in_=wsp_f[:, g, 0:16]))(g), [wsp_f], [kpg[0]])
                op("dve", (lambda ps: lambda e: e.tensor_copy(out=Pv[:, 16:528], in_=ps[:, :]))(ps), [ps], [kpg[0]])
                op("pool", (lambda g: lambda e: e.tensor_copy(out=wsp_f[:, g, 0:16], in_=Pv[:, 512:528]))(g), [kpg[0]], [wsp_f])
                op("dve", lambda e: e.tensor_tensor(out=Av[:, 1:528], in0=Pv[:, 1:528], in1=Pv[:, 0:527], op=ALU.add), [kpg[0]], [kpg[1]])
                Sv, Sobj = Av, kpg[1]
                if g >= 1:
                    op("dve", lambda e: e.tensor_tensor(out=Bv[:, 3:528], in0=Av[:, 3:528], in1=Av[:, 1:526], op=ALU.add), [kpg[1]], [big32[1]])
                    Sv, Sobj = Bv, big32[1]
                if g >= 2:
                    op("dve", lambda e: e.tensor_tensor(out=Av[:, 7:528], in0=Bv[:, 7:528], in1=Bv[:, 3:524], op=ALU.add), [big32[1]], [kpg[1]])
                    Sv, Sobj = Av, kpg[1]
                if g >= 3:
                    op("dve", lambda e: e.tensor_tensor(out=Bv[:, 15:528], in0=Av[:, 15:528], in1=Av[:, 7:520], op=ALU.add), [kpg[1]], [big32[1]])
                    Sv, Sobj = Bv, big32[1]
                pl = att_b[0]
                op("dve", (lambda Sv, wwin: lambda e: e.scalar_tensor_tensor(out=pl[:, :], in0=Sv[:, 16:528], scalar=1.0 / wwin, in1=Pv[:, 16:528],
                                                                             op0=ALU.mult, op1=ALU.subtract))(Sv, wwin), [Sobj, kpg[0]], [pl])
                if T == 0:
                    op("dve", (lambda Sv, g: lambda e: e.tensor_tensor(out=zs[:, 0:16], in0=Sv[:, 16:32], in1=C["c_icnt"][:, g * 16:(g + 1) * 16],
                                                                       op=ALU.mult))(Sv, g), [Sobj, C["c_icnt"]], [zs])
                    op("dve", lambda e: e.tensor_tensor(out=pl[:, 0:16], in0=zs[:, 0:16], in1=Pv[:, 16:32], op=ALU.subtract), [zs, kpg[0]], [pl])
                pm = nxt("A")
                op("pe", (lambda g, pm: lambda e: e.matmul(pm[:, :], lhsT=wspT[:, g, :], rhs=pl[:, :], start=True, stop=True))(g, pm), [wspT, pl], [pm])
                blk = load_w(W, 1956 + 128 * g, 128)
                ps = nxt("A")
                fm_mm(ps, blk, 128, xTl, 512)
                sg = t32[4]
                silu_from_psum(sg, ps, 0, 128, 512)
                op("dve", (lambda g, pm: lambda e: e.scalar_tensor_tensor(out=hT[4 + g][:, :], in0=pm[:, :], scalar=wsp00[:, g:g + 1], in1=sg[:, :],
                                                                          op0=ALU.mult, op1=ALU.mult))(g, pm), [pm, wsp00, sg], [hT[4 + g]])
            if T == 3:
                pp = nxt("A")
                for cb in range(4):
                    blk = load_w(W, 1444 + 128 * cb, 128)
                    tm_mm(pp, cb * 128, blk, 128, xTl, 384, 128)
                op("dve", (lambda pp: lambda e: e.tensor_copy(out=e32[0][:, :], in_=pp[:, :]))(pp), [pp], [e32[0]])
                dma("pool", lambda e: e.dma_start(out=dbp_d[j], in_=e32[0][113:128, :]), reads=[e32[0]])
            if dbgh_d is not None and T == DBG.get("dbgT", 0):
                for f in range(8):
                    dma("pool", (lambda f: lambda e: e.dma_start(out=dbgh_d[f], in_=hT[f][:, :]))(f), reads=[hT[f]])
            layer_norm_tile(layer, [(4 * T + jj, jj * 128, 128) for jj in range(4)], hT)

        def odd_sample_tile(layer, j):
            W = wio_d[j]
            xTl, _ = make_xT(4)
            qs, rows, sgc_unused, pp_, sgd = t32[0], t32[1], t32[2], t32[3], t32[4]
            ps = nxt("A")
            sample_inproj(W, 0, 512, xTl, ps)
            op("dve", (lambda ps: lambda e: e.tensor_scalar(out=qs[0:NS, :], in0=ps[0:NS, :], scalar1=0.125, scalar2=None, op0=ALU.mult))(ps), [ps], [qs])
            ps = nxt("A")
            sample_inproj(W, 512, 420, xTl, ps)
            op("dve", (lambda ps: lambda e: e.tensor_copy(out=rows[0:NS, 0:420], in_=ps[0:NS, 0:420]))(ps), [ps], [rows])
            ps = nxt("A")
            sample_inproj(W, 1444, 512, xTl, ps)
            op("dve", (lambda ps: lambda e: e.tensor_copy(out=pp_[0:NS, :], in_=ps[0:NS, :]))(ps), [ps], [pp_])
            ps = nxt("A")
            sample_inproj(W, 1956, 512, xTl, ps)
            silu_from_psum(sgd, ps, 0, NS, 512)
            for c in range(4):
                blk = load_w(W, 932 + 128 * c, 128)
                ps = nxt("A")
                fm_mm(ps, blk, 128, xTl, NS)
                op("act", (lambda c, ps: lambda e: e.activation(out=gbTs[:, c, :], in_=ps[:, 0:NS], func=AF.Sigmoid))(c, ps), [ps], [gbTs])
                op("dve", (lambda c, ps: lambda e: e.tensor_tensor(out=gbTs[:, c, :], in0=gbTs[:, c, :], in1=ps[:, 0:NS], op=ALU.mult))(c, ps),
                   [gbTs, ps], [gbTs])
            dma("pool", lambda e: e.dma_start(out=ckvs_d[j], in_=rows[0:NS, 0:256]), reads=[rows])
            dma("pool", lambda e: e.dma_start(out=ckis_d[j], in_=rows[0:NS, 384:416]), reads=[rows])
            dma("pool", lambda e: e.dma_start(out=dbs_d[j, :, 0:14, :], in_=db_d[j, :, 1:15, :]))
            dma("pool", lambda e: e.dma_start(out=dbs_d[j, :, 14, :], in_=pp_[0:NS, :]), reads=[pp_])
            ev1 = dma("pool", lambda e: e.dma_start(out=scr_kv[:, 0:256], in_=rows[0:NS, 0:256]), reads=[rows], writes=[scrkv_b])
            ev2 = dma("pool", lambda e: e.dma_start(out=scr_kv[:, 256:288], in_=rows[0:NS, 384:416]), reads=[rows], writes=[scrkv_b2])
            G = big32[1]
            offs = [0, 128, 512, 1408]
            rs = t32[5]
            for g in range(4):
                wwin = 2 ** (g + 1)
                nr = wwin - 1
                Gb = big32[1] if g < 3 else big32[0]
                o0 = offs[g] if g < 3 else 0
                dma("sp", (lambda g, nr, Gb, o0: lambda e: e.dma_start(
                    out=Gb[0:NS, o0:o0 + nr * 128].rearrange("p (r c) -> p r c", c=128),
                    in_=db_d[j, :, 15 - nr:15, g * 128:(g + 1) * 128]))(g, nr, Gb, o0), writes=[Gb])
                op("dve", (lambda g, nr, Gb, o0: lambda e: e.tensor_reduce(
                    out=rs[0:NS, g * 128:(g + 1) * 128], in_=Gb[0:NS, o0:o0 + nr * 128].rearrange("p (r c) -> p c r", c=128),
                    axis=AX.X, op=ALU.add))(g, nr, Gb, o0), [Gb], [rs])
                op("dve", (lambda g: lambda e: e.tensor_tensor(out=rs[0:NS, g * 128:(g + 1) * 128], in0=rs[0:NS, g * 128:(g + 1) * 128],
                                                               in1=pp_[0:NS, g * 128:(g + 1) * 128], op=ALU.add))(g), [rs, pp_], [rs])
                op("dve", (lambda g, wwin: lambda e: e.scalar_tensor_tensor(
                    out=rs[0:NS, g * 128:(g + 1) * 128], in0=rs[0:NS, g * 128:(g + 1) * 128], scalar=1.0 / wwin,
                    in1=pp_[0:NS, g * 128:(g + 1) * 128], op0=ALU.mult, op1=ALU.subtract))(g, wwin), [rs, pp_], [rs])
            pt_ = nxt("A")
            for g in range(4):
                op("pe", (lambda g, pt_: lambda e: e.transpose(out=pt_[:, g * 128:g * 128 + NS], in_=rs[0:NS, g * 128:(g + 1) * 128],
                                                               identity=C["c_ident"][0:NS, 0:NS]))(g, pt_), [rs, C["c_ident"]], [pt_])
            plT = att_b[1]
            op("dve", (lambda pt_: lambda e: e.tensor_copy(out=plT[:, :], in_=pt_[:, :]))(pt_), [pt_], [plT])
            pm = nxt("A")
            for g in range(4):
                op("pe", (lambda g, pm: lambda e: e.matmul(pm[0:NS, g * 128:(g + 1) * 128], lhsT=plT[:, g * 128:g * 128 + NS], rhs=wspT[:, g, :],
                                                           start=True, stop=True))(g, pm), [plT, wspT], [pm])
            dscb = t32[2]
            dma("sp", lambda e: e.dma_start(out=dscb[0:NS, :], in_=dsc_d[j:j + 1, :].to_broadcast([NS, 512])), writes=[dscb])
            od = rs
            op("dve", (lambda pm: lambda e: e.tensor_tensor(out=od[0:NS, :], in0=pm[0:NS, :], in1=dscb[0:NS, :], op=ALU.mult))(pm), [pm, dscb], [od])
            op("dve", lambda e: e.tensor_tensor(out=od[0:NS, :], in0=od[0:NS, :], in1=sgd[0:NS, :], op=ALU.mult), [od, sgd], [od])
            pt_ = nxt("A")
            for g in range(4):
                op("pe", (lambda g, pt_: lambda e: e.transpose(out=pt_[:, g * 128:g * 128 + NS], in_=od[0:NS, g * 128:(g + 1) * 128],
                                                               identity=C["c_ident"][0:NS, 0:NS]))(g, pt_), [od, C["c_ident"]], [pt_])
            for g in range(4):
                op("dve", (lambda g, pt_: lambda e: e.tensor_copy(out=hTs[4 + g][:, :], in_=pt_[:, g * 128:g * 128 + NS]))(g, pt_), [pt_], [hTs[4 + g]])
            bsT = sps
            dma("sp", lambda e: e.dma_start(out=bsT[:, :], in_=bs_d[:, 0:128]), writes=[bsT])
            dma("sp", lambda e: e.dma_start(out=tts[:, 0:8], in_=bs_d[:, 128:136]), writes=[tts])
            idxall = S32
            KI, KIs = e32[0], sp32[0]
            qib = t32[2]
            op("pool", lambda e: e.memset(KIs[:, 0:32], 0.0), [], [KIs])

            def bcast_rows(b, src, c0, n, dst):
                qm = t32[5]
                op("dve", lambda e: e.tensor_scalar(out=qm[0:NS, 0:n], in0=src[0:NS, c0:c0 + n], scalar1=C["c_ident"][0:NS, b:b + 1],
                                                    scalar2=-1.0, op0=ALU.mult, op1=ALU.mult), [src, C["c_ident"]], [qm])
                pq = nxt("A")
                op("pe", lambda e: e.matmul(pq[:, 0:n], lhsT=C["c_nones"][0:NS, :], rhs=qm[0:NS, 0:n], start=True, stop=True),
                   [C["c_nones"], qm], [pq])
                op("dve", lambda e: e.tensor_copy(out=dst[:, 0:n], in_=pq[:, 0:n]), [pq], [dst])

            idx, wk = big32[0], big32[1]
            ckh = ck_d.rearrange("(n r) e -> n (r e)", r=64)
            qexp = t32[2]
            idxp = zs
            sc_ = t32[5]
            op("dve", lambda e: e.tensor_scalar(out=rows[0:NS, 416:420], in0=rows[0:NS, 416:420], scalar1=ISQ, scalar2=None, op0=ALU.mult),
               [rows], [rows])
            for bt in range(2):
                dma("sp", (lambda bt: lambda e: e.dma_start(out=idx2[:, bt:bt + 1],
                                                            in_=pt_d.rearrange("o (p c) -> (o p) c", c=1)[128 * bt:128 * (bt + 1), :]))(bt),
                    writes=[idx2])
            op("dve", lambda e: e.tensor_scalar(out=idx2[:, 0:2], in0=idx2[:, 0:2], scalar1=2, scalar2=None, op0=ALU.mult), [idx2], [idx2])
            for bt in range(2):
                pq = nxt("A")
                op("pe", (lambda bt, pq: lambda e: e.matmul(pq[:, 0:164], lhsT=C["c_esel"][:, 128 * bt:128 * (bt + 1)], rhs=rows[0:NS, 256:420],
                                                            start=True, stop=True))(bt, pq), [C["c_esel"], rows], [pq])
                op("dve", (lambda pq: lambda e: e.tensor_copy(out=qexp[:, 0:164], in_=pq[:, 0:164]))(pq), [pq], [qexp])
                for half in range(2):
                    dma("pool", (lambda bt, half: lambda e: e.indirect_dma_start(
                        out=idx[:, 0:2048], out_offset=None, in_=ckh[:, 0:2048], element_offset=j * pool_rows * 32 + half * 2048,
                        in_offset=bass.IndirectOffsetOnAxis(ap=idx2[:, bt:bt + 1], axis=0)))(bt, half), reads=[idx2], writes=[idx])
                    for ih in range(4):
                        op("dve", (lambda ih: lambda e: e.tensor_tensor(
                            out=wk[:, 0:2048].rearrange("p (k e) -> p k e", e=32), in0=idx[:, 0:2048].rearrange("p (k e) -> p k e", e=32),
                            in1=qexp[:, ih * 32:(ih + 1) * 32].unsqueeze(1).to_broadcast([128, 64, 32]), op=ALU.mult))(ih), [idx, qexp], [wk])
                        op("dve", lambda e: e.tensor_reduce(out=sc_[:, 0:64], in_=wk[:, 0:2048].rearrange("p (k e) -> p k e", e=32), axis=AX.X,
                                                            op=ALU.add), [wk], [sc_])
                        op("dve", lambda e: e.tensor_scalar(out=sc_[:, 0:64], in0=sc_[:, 0:64], scalar1=0.0, scalar2=None, op0=ALU.max), [sc_], [sc_])
                        if ih == 0:
                            op("dve", (lambda half: lambda e: e.tensor_scalar(out=idxp[:, half * 64:(half + 1) * 64], in0=sc_[:, 0:64],
                                                                              scalar1=qexp[:, 160:161], scalar2=None, op0=ALU.mult))(half),
                               [sc_, qexp], [idxp])
                        else:
                            op("dve", (lambda half, ih: lambda e: e.scalar_tensor_tensor(
                                out=idxp[:, half * 64:(half + 1) * 64], in0=sc_[:, 0:64], scalar=qexp[:, 160 + ih:161 + ih],
                                in1=idxp[:, half * 64:(half + 1) * 64], op0=ALU.mult, op1=ALU.add))(half, ih), [sc_, qexp, idxp], [idxp])
                for b8 in range(8):
                    bb = 8 * bt + b8
                    dma("pool", (lambda b8, bb: lambda e: e.dma_start(
                        out=scr_idx[bb:bb + 1, 0:2048].rearrange("o (g k) -> (o g) k", k=128), in_=idxp[b8 * 16:(b8 + 1) * 16, :]))(b8, bb),
                        reads=[idxp], writes=[scri_b])
            sfp = t32[5]
            op("dve", lambda e: e.tensor_tensor(out=sfp[0:NS, 0:128].rearrange("p (i e) -> p i e", e=32),
                                                in0=rows[0:NS, 256:384].rearrange("p (i e) -> p i e", e=32),
                                                in1=rows[0:NS, 384:416].unsqueeze(1).to_broadcast([NS, 4, 32]), op=ALU.mult), [rows], [sfp])
            op("dve", lambda e: e.tensor_reduce(out=sfp[0:NS, 128:132], in_=sfp[0:NS, 0:128].rearrange("p (i e) -> p i e", e=32), axis=AX.X,
                                                op=ALU.add), [sfp], [sfp])
            op("dve", lambda e: e.tensor_scalar(out=sfp[0:NS, 128:132], in0=sfp[0:NS, 128:132], scalar1=0.0, scalar2=None, op0=ALU.max), [sfp], [sfp])
            op("dve", lambda e: e.tensor_tensor(out=sfp[0:NS, 128:132], in0=sfp[0:NS, 128:132], in1=rows[0:NS, 416:420], op=ALU.mult), [sfp, rows], [sfp])
            op("pool", lambda e: e.memset(sfp[0:NS, 256:384], NEG), [], [sfp])
            op("dve", lambda e: e.tensor_reduce(out=sfp[0:NS, 256:257], in_=sfp[0:NS, 128:132], axis=AX.X, op=ALU.add), [sfp], [sfp])
            dma("pool", lambda e: e.dma_start(out=scr_idx[:, 2048:2176], in_=sfp[0:NS, 256:384]), reads=[sfp], writes=[scri_b])
            dma("sp", lambda e: e.dma_start(out=idx[0:NS, 0:2176], in_=scr_idx), reads=[scri_b], writes=[idx])
            op("pool", lambda e: e.tensor_copy(out=wk[0:NS, 0:2176], in_=idx[0:NS, 0:2176]), [idx], [wk])
            mx = small[1]
            for r in range(32):
                op("dve", lambda e: e.max(out=mx[0:NS, 0:8], in_=wk[0:NS, 0:2176]), [wk], [mx])
                op("dve", lambda e: e.match_replace(out=wk[0:NS, 0:2176], in_to_replace=mx[0:NS, 0:8], in_values=wk[0:NS, 0:2176],
                                                    imm_value=2.0 * NEG), [wk, mx], [wk])
            op("dve", lambda e: e.tensor_tensor(out=wk[0:NS, 0:2176], in0=wk[0:NS, 0:2176], in1=idx[0:NS, 0:2176], op=ALU.not_equal), [wk, idx], [wk])
            dma("pool", lambda e: e.dma_start(out=scr_msk, in_=wk[0:NS, 0:2176]), reads=[wk], writes=[scrm_b])
            mS = idxall
            for (r0, nr) in ((0, 128), (128, 128), (256, 16)):
                tb = t32[5]
                dma("sp", (lambda r0, nr: lambda e: e.dma_start(out=tb[0:nr, 0:128],
                                                                in_=scr_msk.rearrange("b (g k) -> (b g) k", k=128)[r0:r0 + nr, :]))(r0, nr),
                    reads=[scrm_b], writes=[tb])
                pt_ = nxt("A")
                op("pe", (lambda nr, pt_: lambda e: e.transpose(out=pt_[:, 0:nr], in_=tb[0:nr, 0:128], identity=C["c_ident"][0:nr, 0:nr]))(nr, pt_),
                   [tb, C["c_ident"]], [pt_])
                op("dve", (lambda r0, nr, pt_: lambda e: e.tensor_copy(out=mS[:, r0:r0 + nr], in_=pt_[:, 0:nr]))(r0, nr, pt_), [pt_], [mS])
            pn, pd = psO[0], psO[1]
            op("pe", lambda e: e.matmul(pn[:, 0:128], lhsT=zero_b[:, :], rhs=xT[0][:, 0:128], start=True, stop=False), [zero_b, xT[0]], [pn])
            op("pe", lambda e: e.matmul(pd[:, 0:128], lhsT=zero_b[:, :], rhs=xT[0][:, 0:128], start=True, stop=False), [zero_b, xT[0]], [pd])
            Kself, Vself = qb_sb, atts
            op("pool", lambda e: e.memset(Kself[:, 0:128], 0.0), [], [Kself])
            op("pool", lambda e: e.memset(Vself[:, :], 0.0), [], [Vself])
            ones64 = vbt[0][:, 0, 64:128]

            def attend_sample(b):
                qbq = t32[2]
                bcast_rows(b, qs, 0, 512, qbq)
                Kb = [kpg[0][:, :, :].rearrange("p a b -> p (a b)"), kpg[1][:, :, :].rearrange("p a b -> p (a b)")]
                Vb = [vpg[0][:, :, :].rearrange("p a b -> p (a b)"), vpg[1][:, :, :].rearrange("p a b -> p (a b)")]
                for pg in range(NPG):
                    hb, i = pg // 8, pg % 8
                    stg_ = t32[3] if pg % 2 == 0 else t32[4]
                    dma("pool", (lambda pg, stg_: lambda e: e.indirect_dma_start(
                        out=stg_[:, 0:256], out_offset=None, in_=cc_d[:, 0:256], element_offset=j * pool_rows * 256,
                        in_offset=bass.IndirectOffsetOnAxis(ap=idxC[:, b * NPG + pg:b * NPG + pg + 1], axis=0)))(pg, stg_),
                        reads=[idxC], writes=[stg_])
                    op("dve", (lambda hb, i, stg_: lambda e: e.tensor_copy(out=Kb[hb][:, i * 128:(i + 1) * 128], in_=stg_[:, 0:128]))(hb, i, stg_),
                       [stg_], [kpg[hb]])
                    op("dve", (lambda hb, i, stg_: lambda e: e.tensor_copy(out=Vb[hb][:, i * 128:(i + 1) * 128], in_=stg_[:, 128:256]))(hb, i, stg_),
                       [stg_], [vpg[hb]])
                dma("sp", lambda e: e.dma_start(out=Kself[0:1, 0:128], in_=scr_kv[b:b + 1, 0:128]), reads=[scrkv_b], writes=[Kself])
                vst = small[0]
                dma("sp", lambda e: e.dma_start(out=sp32[0][0:1, 128:256], in_=scr_kv[b:b + 1, 128:256]), reads=[scrkv_b], writes=[sp32[0]])
                op("dve", lambda e: e.tensor_copy(out=Vself[0:1, :], in_=sp32[0][0:1, 128:256]), [sp32[0]], [Vself])
                L = zs
                Ls = small[2]
                prod = big32[1]
                qv = qbq[:, 0:512].rearrange("p (g r d) -> p g r d", g=2, r=4)
                for pq4 in range(4):
                    hb, i0 = pq4 // 2, (pq4 % 2) * 4
                    Kv = Kb[hb][:, i0 * 128:(i0 + 4) * 128].rearrange("p (a g d) -> p a g d", g=2, d=64)
                    for g in range(2):
                        op("dve", (lambda Kv, g: lambda e: e.tensor_tensor(
                            out=prod[:, g * 1024:(g + 1) * 1024].rearrange("p (a r d) -> p a r d", r=4, d=64),
                            in0=Kv[:, :, g, :].unsqueeze(2).to_broadcast([128, 4, 4, 64]),
                            in1=qv[:, g, :, :].unsqueeze(1).to_broadcast([128, 4, 4, 64]), op=ALU.mult))(Kv, g), [kpg[hb], qbq], [prod])
                    for g in range(2):
                        op("dve", (lambda pq4, g: lambda e: e.tensor_reduce(
                            out=L[:, pq4 * 32:(pq4 + 1) * 32].rearrange("p (a h) -> p a h", h=8)[:, :, g * 4:(g + 1) * 4],
                            in_=prod[:, g * 1024:(g + 1) * 1024].rearrange("p (a r d) -> p a r d", r=4, d=64), axis=AX.X, op=ALU.add))(pq4, g),
                           [prod], [L])
                Ksv = Kself[:, 0:128].rearrange("p (g d) -> p g d", d=64)
                op("dve", lambda e: e.tensor_tensor(out=prod[:, 0:512].rearrange("p (g r d) -> p g r d", g=2, r=4),
                                                    in0=Ksv.unsqueeze(2).to_broadcast([128, 2, 4, 64]), in1=qv, op=ALU.mult), [Kself, qbq], [prod])
                op("dve", lambda e: e.tensor_reduce(out=Ls[:, 0:8], in_=prod[:, 0:512].rearrange("p (h d) -> p h d", d=64), axis=AX.X, op=ALU.add),
                   [prod], [Ls])
                op("dve", lambda e: e.tensor_tensor(out=L[:, :], in0=L[:, :], in1=bsT[:, :], op=ALU.add), [L, bsT], [L])
                op("dve", lambda e: e.tensor_tensor(out=Ls[:, 0:8], in0=Ls[:, 0:8], in1=tts[:, 0:8], op=ALU.add), [Ls, tts], [Ls])
                op("act", lambda e: e.activation(out=L[:, :], in_=L[:, :], func=AF.Exp), [L], [L])
                op("act", lambda e: e.activation(out=Ls[:, 0:8], in_=Ls[:, 0:8], func=AF.Exp), [Ls], [Ls])
                pb = att_b[0]
                op("dve", lambda e: e.tensor_tensor(out=pb[:, 0:128].rearrange("p (a h) -> p a h", h=8), in0=L[:, :].rearrange("p (a h) -> p a h", h=8),
                                                    in1=mS[:, b * 17:b * 17 + 16].unsqueeze(2).to_broadcast([128, 16, 8]), op=ALU.mult), [L, mS], [pb])
                op("dve", lambda e: e.tensor_scalar(out=pb[:, 128:136], in0=Ls[:, 0:8], scalar1=mS[:, b * 17 + 16:b * 17 + 17], scalar2=None,
                                                    op0=ALU.mult), [Ls, mS], [pb])
                for pg in range(NPG):
                    hb, i = pg // 8, pg % 8
                    op("pe", (lambda pg, hb, i: lambda e: e.matmul(pn[:, b * 8:(b + 1) * 8], lhsT=Vb[hb][:, i * 128:(i + 1) * 128],
                                                                   rhs=pb[:, pg * 8:(pg + 1) * 8], start=False, stop=False))(pg, hb, i), [vpg[hb], pb], [pn])
                    op("pe", (lambda pg: lambda e: e.matmul(pd[0:64, b * 8:(b + 1) * 8], lhsT=ones64, rhs=pb[:, pg * 8:(pg + 1) * 8],
                                                            start=False, stop=False))(pg), [vbt[0], pb], [pd])
                op("pe", lambda e: e.matmul(pn[:, b * 8:(b + 1) * 8], lhsT=Vself[:, :], rhs=pb[:, 128:136], start=False, stop=True), [Vself, pb], [pn])
                op("pe", lambda e: e.matmul(pd[0:64, b * 8:(b + 1) * 8], lhsT=ones64, rhs=pb[:, 128:136], start=False, stop=True), [vbt[0], pb], [pd])

            for b in range(NS):
                attend_sample(b)
            rd = zs
            op("dve", lambda e: e.reciprocal(out=rd[0:64, 0:128], in_=pd[0:64, 0:128]), [pd], [rd])
            op("dve", lambda e: e.reciprocal(out=rd[64:128, 0:128], in_=pd[0:64, 0:128]), [pd], [rd])
            oT = t32[5]
            op("dve", lambda e: e.tensor_tensor(out=oT[:, 0:128], in0=pn[:, 0:128], in1=rd[:, 0:128], op=ALU.mult), [pn, rd], [oT])
            for h in range(8):
                g, c, hb_ = h // 4, h // 2, 64 * (h % 2)
                tmpo = t32[4]
                op("dve", (lambda h, g, c, hb_: lambda e: e.tensor_copy(
                    out=tmpo[hb_:hb_ + 64, 0:NS], in_=oT[64 * g:64 * g + 64, 0:128].rearrange("p (b h) -> p b h", h=8)[:, :, h]))(h, g, c, hb_),
                   [oT], [tmpo])
                op("dve", (lambda h, g, c, hb_: lambda e: e.tensor_tensor(
                    out=hTs[c][hb_:hb_ + 64, :], in0=tmpo[hb_:hb_ + 64, 0:NS],
                    in1=gbTs[hb_:hb_ + 64, c, :], op=ALU.mult))(h, g, c, hb_), [tmpo, gbTs], [hTs[c]])
            layer_norm_block(layer, 16, hTs, 0, NS)

        for layer in range(n_layers):
            j = layer // 2
            layer_setup(layer)
            if layer % 2 == 0:
                even_setup(j)
                for T in range(DBG.get("ntiles", 4)):
                    even_prompt_tile(layer, j, T)
                if DBG.get("sample", 1):
                    even_sample_tile(layer, j)
            else:
                odd_setup(j)
                for T in range(DBG.get("ntiles", 4)):
                    odd_prompt_tile(layer, j, T)
                if DBG.get("sample", 1):
                    odd_sample_tile(layer, j)

        for jb in range(16):
            dma("pool", (lambda jb: lambda e: e.dma_start(out=yp_d[jb * 128:(jb + 1) * 128, :], in_=xtok[jb][:, :]))(jb), reads=[xtok[jb]])
        dma("pool", lambda e: e.dma_start(out=ys_d, in_=xtok[16][0:NS, :]), reads=[xtok[16]])
        fw.emit()
    return nc


def _bias_tables(rel_bias):
    k = np.arange(128)
    d0 = np.maximum(k[None, :] - k[:, None], 0)
    d1 = 128 + k[None, :] - k[:, None]
    t = np.zeros((128, 2, 8, 128), np.float32)
    for h in range(8):
        t[:, 0, h, :] = rel_bias[_t5_bucket(d0), h]
        t[:, 1, h, :] = rel_bias[_t5_bucket(d1), h]
    toep = t.reshape(128, 16 * 128)
    b31 = np.ascontiguousarray(rel_bias[31:32, :]).astype(np.float32)
    bs = np.zeros((128, 17, 8), np.float32)
    for pg in range(16):
        dist = 2048 - (pg * 128 + k)
        bs[:, pg, :] = rel_bias[_t5_bucket(dist), :]
    bs[:, 16, :] = rel_bias[0:1, :]
    return toep, b31, bs.reshape(128, 17 * 8)


def _core_inputs(inp, c, consts, tabs, pool_views, pt_rows):
    d = {
        "xp": np.ascontiguousarray(inp["x_prompt"][c]),
        "xs": np.ascontiguousarray(inp["x_sample"][c * NS:(c + 1) * NS, 0, :]),
        "pt": np.ascontiguousarray(pt_rows.reshape(1, NS * NPG)).astype(np.int32),
        "cb": pool_views[0], "cc": pool_views[1], "ck": pool_views[2],
        "dbuf": np.ascontiguousarray(inp["state_d_buf"][:, c * NS:(c + 1) * NS]),
        "w_in_even": inp["w_in_even"], "w_in_odd": inp["w_in_odd"], "w_out": inp["w_out"],
        "ln_g": inp["ln_g"], "ln_b": inp["ln_b"], "a_w_sp": inp["a_w_sp"], "a_b_sp": inp["a_b_sp"],
        "d_w_grp": inp["d_w_grp"], "d_scale": inp["d_scale"],
        "t_toep": tabs[0], "t_b31": tabs[1], "t_bsamp": tabs[2],
    }
    d.update(consts)
    return d


def kernel(**inputs):
    inp = {k: np.asarray(v) for k, v in inputs.items()}
    n_pool = inp["cache_b_kv"].shape[1]
    nc = build(4, n_pool * 128)
    consts = _consts()
    tabs = _bias_tables(inp["rel_bias"].astype(np.float32))
    views = (inp["cache_b_kv"].reshape(2 * n_pool * 128, 1024), inp["cache_c_kv"].reshape(2 * n_pool * 128, 256),
             inp["cache_c_kidx"].reshape(2 * n_pool * 128, 32))
    in_maps = [_core_inputs(inp, c, consts, tabs, views, inp["page_table"][c * NS:(c + 1) * NS]) for c in range(NCORES)]
    res = run_bass_kernel_spmd(nc, in_maps, core_ids=list(range(NCORES))).results
    return _assemble(res)


def _assemble(res):
    n = len(res)
    f = np.float32
    y_p = np.stack([res[c]["y_p"] for c in range(n)]).astype(f)
    y_s = np.concatenate([res[c]["y_s"] for c in range(n)])[:, None, :].astype(f)
    bkv_p = np.stack([res[c]["bkv_p"] for c in range(n)], axis=1).reshape(2, n, S, 2, 8, 64)
    bkv_s = np.concatenate([res[c]["bkv_s"] for c in range(n)], axis=1).reshape(2, n * NS, 1, 2, 8, 64)
    av_s = np.concatenate([res[c]["av_s"] for c in range(n)], axis=1).reshape(2, n * NS, 1, 512)
    ckv_p = np.stack([res[c]["ckv_p"] for c in range(n)], axis=1).reshape(2, n, S, 2, 2, 64)
    ckv_s = np.concatenate([res[c]["ckv_s"] for c in range(n)], axis=1).reshape(2, n * NS, 1, 2, 2, 64)
    cki_p = np.stack([res[c]["cki_p"] for c in range(n)], axis=1).reshape(2, n, S, 32)
    cki_s = np.concatenate([res[c]["cki_s"] for c in range(n)], axis=1).reshape(2, n * NS, 1, 32)
    db_p = np.stack([res[c]["dbuf_p"] for c in range(n)], axis=1).reshape(2, n, 15, 512)
    db_s = np.concatenate([res[c]["dbuf_s"] for c in range(n)], axis=1).reshape(2, n * NS, 15, 512)
    return tuple(np.ascontiguousarray(a, dtype=f) for a in
                 (y_p, y_s, bkv_p, bkv_s, av_s, ckv_p, ckv_s, cki_p, cki_s, db_p, db_s))
```

```python
import math
import numpy as np
from contextlib import ExitStack
import concourse.bass as bass
import concourse.mybir as mybir
from concourse.bass_utils import run_bass_kernel_spmd

F32 = mybir.dt.float32
F32R = mybir.dt.float32r
BF16 = mybir.dt.bfloat16
I32 = mybir.dt.int32
AF = mybir.ActivationFunctionType
ALU = mybir.AluOpType
AX = mybir.AxisListType

ENGS = ("pe", "act", "dve", "pool", "sp")
DBG = {}
NCORES = 8
S = 2048
DM = 1024
NS = 16
NPG = 16
EVEN_IN = 3584
ODD_IN = 2468
DN_ALPHA = (2.0 * 4) ** 0.25
LN_EPS = 1e-5
NEG = -1.0e30


class Buf:
    __slots__ = ("t", "name", "last_w", "readers")

    def __init__(self, t, name):
        self.t = t
        self.name = name
        self.last_w = None
        self.readers = {}

    def __getitem__(self, idx):
        return self.t[idx]


class View:
    def __init__(self, parent, ap):
        self.parent = parent
        self.ap = ap
        self.name = parent.name + "_v"

    def __getitem__(self, idx):
        return self.ap[idx]

    @property
    def last_w(self):
        return self.parent.last_w

    @last_w.setter
    def last_w(self, v):
        self.parent.last_w = v

    @property
    def readers(self):
        return self.parent.readers

    @readers.setter
    def readers(self, v):
        self.parent.readers = v


def interleave(gens):
    gens = list(gens)
    while gens:
        for g in list(gens):
            try:
                next(g)
            except StopIteration:
                gens.remove(g)


class FW:
    def __init__(self, nc, stack, n_dma_sems=32):
        self.nc = nc
        self.stack = stack
        self.ops = {e: [] for e in ENGS}
        self.cnt = {e: 0 for e in ENGS}
        self.sems = {e: stack.enter_context(nc.semaphore("sem_" + e)) for e in ENGS}
        self.dma_sems = [stack.enter_context(nc.semaphore("sem_dma%d" % i)) for i in range(n_dma_sems)]
        self.dma_cnt = [0] * n_dma_sems
        self.dma_rr = 0
        self.seen = {e: {} for e in ENGS}
        self.nbuf = 0

    def sb(self, shape, dt, name=None):
        self.nbuf += 1
        name = (name or "sb") + "_%d" % self.nbuf
        t = self.stack.enter_context(self.nc.sbuf_tensor(name, list(shape), dt))
        return Buf(t, name)

    def ps(self, shape, dt=F32, name=None):
        self.nbuf += 1
        name = (name or "ps") + "_%d" % self.nbuf
        t = self.stack.enter_context(self.nc.psum_tensor(name, list(shape), dt))
        return Buf(t, name)

    def _collect(self, eng, reads, writes):
        waits = {}

        def add(ev):
            if ev is None:
                return
            k, v = ev
            if k == "pe" and eng == "pe":
                return
            if waits.get(k, 0) < v:
                waits[k] = v

        for b in reads:
            add(b.last_w)
        for b in writes:
            add(b.last_w)
            for k, v in b.readers.items():
                add((k, v))
        seen = self.seen[eng]
        out = []
        for k, v in waits.items():
            if seen.get(k, 0) >= v:
                continue
            seen[k] = v
            out.append((k, v))
        return out

    def _commit(self, ev, reads, writes):
        k, v = ev
        for b in reads:
            if b.readers.get(k, 0) < v:
                b.readers[k] = v
        for b in writes:
            b.last_w = ev
            b.readers = {}

    def op(self, eng, fn, reads=(), writes=()):
        waits = self._collect(eng, reads, writes)
        self.cnt[eng] += 1
        ev = (eng, self.cnt[eng])
        self.ops[eng].append((waits, fn, (eng, 1)))
        self._commit(ev, reads, writes)
        return ev

    def dma(self, eng, fn, reads=(), writes=()):
        i = self.dma_rr
        self.dma_rr = (self.dma_rr + 1) % len(self.dma_sems)
        key = "dma%d" % i
        waits = self._collect(eng, reads, writes)
        prev = self.dma_cnt[i] * 16
        if prev > 0 and self.seen[eng].get(key, 0) < prev:
            self.seen[eng][key] = prev
            waits.append((key, prev))
        self.dma_cnt[i] += 1
        ev = (key, self.dma_cnt[i] * 16)
        self.ops[eng].append((waits, fn, (key, 16)))
        self._commit(ev, reads, writes)
        return ev

    def _sem(self, key):
        if key in self.sems:
            return self.sems[key]
        return self.dma_sems[int(key[3:])]

    def emit(self):
        nc = self.nc
        with nc.Block() as block:
            def run(engname, e):
                for waits, fn, (ik, iv) in self.ops[engname]:
                    for k, v in waits:
                        e.wait_ge(self._sem(k), v)
                    fn(e).then_inc(self._sem(ik), iv)

            @block.tensor
            def _(e):
                run("pe", e)

            @block.scalar
            def _(e):
                run("act", e)

            @block.vector
            def _(e):
                run("dve", e)

            @block.gpsimd
            def _(e):
                run("pool", e)

            @block.sync
            def _(e):
                run("sp", e)
                for i, c in enumerate(self.dma_cnt):
                    if c:
                        e.wait_ge(self.dma_sems[i], c * 16)


def _consts():
    c = {}
    k = np.arange(128)
    c["c_ident"] = np.eye(128, dtype=np.float32)
    c["c_ntri"] = -(k[:, None] >= k[None, :]).astype(np.float32)
    c["c_nones"] = -np.ones((128, 128), np.float32)
    c["c_negm"] = np.where(k[:, None] >= k[None, :], -30000.0, 0.0).astype(np.float32)
    pg = k // 8
    h = k % 8
    c["c_m2"] = ((h[:, None] == h[None, :]) & (pg[:, None] > pg[None, :])).astype(np.float32)
    c["c_dsac"] = (k[None, :] >= k[:, None]).astype(np.float32)
    c["c_negq"] = np.where(k[None, :] > k[:, None], NEG, 0.0).astype(np.float32)
    esel = np.zeros((16, 2, 8, 16), np.float32)
    for b in range(16):
        esel[b, b // 8, b % 8, :] = 1.0
    c["c_esel"] = esel.reshape(16, 256)
    rowm = np.zeros((128, 4), np.float32)
    for ih in range(4):
        rowm[32 * ih:32 * ih + 32, ih] = 1.0
    c["c_rowm"] = rowm
    icnt = np.zeros((4, 16), np.float32)
    for g, w in enumerate((2, 4, 8, 16)):
        icnt[g] = 1.0 / np.minimum(np.arange(16) + 1, w)
    c["c_icnt"] = np.broadcast_to(icnt.reshape(1, 64), (128, 64)).copy()
    negrow = np.zeros((128, 1), np.float32)
    negrow[1:] = NEG
    c["c_negrow"] = negrow
    return c


def _t5_bucket(n):
    n = np.asarray(n)
    nf = np.maximum(n, 1).astype(np.float32)
    large = 16 + (np.log(nf / 16) / math.log(128 / 16) * 16).astype(np.int32)
    large = np.minimum(large, 31)
    return np.where(n < 16, n, large)


CONST_SHAPES = {"c_ident": [128, 128], "c_ntri": [128, 128], "c_nones": [128, 128], "c_negm": [128, 128],
                "c_m2": [128, 128], "c_dsac": [128, 128], "c_negq": [128, 128],
                "c_rowm": [128, 4], "c_esel": [16, 256], "c_icnt": [128, 64], "c_negrow": [128, 1]}


def build(n_layers=4, pool_rows=2560 * 128, debug=False):
    nc = bass.Bass("TRN2", target_bir_lowering=False)

    def din(name, shape, dt=F32):
        return nc.dram_tensor(name, list(shape), dt, kind="ExternalInput").ap()

    def dout(name, shape, dt=F32):
        return nc.dram_tensor(name, list(shape), dt, kind="ExternalOutput").ap()

    xp_d = din("xp", [S, DM])
    xs_d = din("xs", [NS, DM])
    pt_d = din("pt", [1, NS * NPG], I32)
    cb_d = din("cb", [2 * pool_rows, 1024])
    cc_d = din("cc", [2 * pool_rows, 256])
    ck_d = din("ck", [2 * pool_rows, 32])
    db_d = din("dbuf", [2, NS, 15, 512])
    wie_d = din("w_in_even", [2, DM, EVEN_IN])
    wio_d = din("w_in_odd", [2, DM, ODD_IN])
    wo_d = din("w_out", [4, DM, DM])
    lng_d = din("ln_g", [4, DM])
    lnb_d = din("ln_b", [4, DM])
    wsp_d = din("a_w_sp", [2, 4, 128, 128])
    bsp_d = din("a_b_sp", [2, 4, 128])
    wgrp_d = din("d_w_grp", [2, 4, 128, 128])
    dsc_d = din("d_scale", [2, 512])
    tz_d = din("t_toep", [128, 16 * 128])
    b31_d = din("t_b31", [1, 8])
    bs_d = din("t_bsamp", [128, 17 * 8])
    cst = {n: din(n, s) for n, s in CONST_SHAPES.items()}

    yp_d = dout("y_p", [S, DM])
    ys_d = dout("y_s", [NS, DM])
    bkvp_d = dout("bkv_p", [2, S, 1024])
    bkvs_d = dout("bkv_s", [2, NS, 1024])
    avs_d = dout("av_s", [2, NS, 512])
    ckvp_d = dout("ckv_p", [2, S, 256])
    ckvs_d = dout("ckv_s", [2, NS, 256])
    ckip_d = dout("cki_p", [2, S, 32])
    ckis_d = dout("cki_s", [2, NS, 32])
    dbp_d = dout("dbuf_p", [2, 15, 512])
    dbs_d = dout("dbuf_s", [2, NS, 15, 512])
    dbgh_d = dout("dbg_h", [8, 128, 512], BF16) if DBG.get("dbgh") else None
    scr_idx = nc.dram_tensor("scr_idx", [NS, 17 * 128], F32, kind="Internal").ap()
    scr_msk = nc.dram_tensor("scr_msk", [NS, 17 * 128], F32, kind="Internal").ap()
    scr_kv = nc.dram_tensor("scr_kv", [NS, 288], F32, kind="Internal").ap()

    with ExitStack() as st:
        fw = FW(nc, st)
        st.enter_context(nc.allow_non_contiguous_dma(reason="small strided parameter loads"))
        op, dma = fw.op, fw.dma

        C = {}
        for n, s in CONST_SHAPES.items():
            C[n] = fw.sb(s, F32, n)
            dma("sp", (lambda b, a: lambda e: e.dma_start(out=b[:], in_=a))(C[n], cst[n]), writes=[C[n]])
        ident_b = fw.sb([128, 128], BF16, "ident_b")
        negm_b = fw.sb([128, 128], BF16, "negm_b")
        op("dve", lambda e: e.tensor_copy(out=ident_b[:], in_=C["c_ident"][:]), [C["c_ident"]], [ident_b])
        op("dve", lambda e: e.tensor_copy(out=negm_b[:], in_=C["c_negm"][:]), [C["c_negm"]], [negm_b])
        ntri_r = fw.sb([128, 128], F32R, "ntri_r")
        nones_r = fw.sb([128, 128], F32R, "nones_r")
        op("dve", lambda e: e.tensor_copy(out=ntri_r[:], in_=C["c_ntri"][:]), [C["c_ntri"]], [ntri_r])
        op("dve", lambda e: e.tensor_copy(out=nones_r[:], in_=C["c_nones"][:]), [C["c_nones"]], [nones_r])
        zero_b = fw.sb([128, 128], BF16, "zero_b")
        op("pool", lambda e: e.memset(zero_b[:], 0.0), [], [zero_b])
        pts = fw.sb([128, NS * NPG], I32, "pts")
        dma("sp", lambda e: e.dma_start(out=pts[:], in_=pt_d.to_broadcast([128, NS * NPG])), writes=[pts])
        idx2 = fw.sb([128, 2], I32, "idx2")
        iota_p = fw.sb([128, 1], I32, "iota_p")
        op("pool", lambda e: e.iota(out=iota_p[:], pattern=[[0, 1]], base=0, channel_multiplier=1), [], [iota_p])
        idxC = pts
        idxB = fw.sb([128, NS * NPG], I32, "idxB")
        op("dve", lambda e: e.tensor_scalar(out=idxC[:], in0=pts[:], scalar1=128, scalar2=iota_p[:, 0:1], op0=ALU.mult, op1=ALU.add),
           [pts, iota_p], [idxC])
        op("dve", lambda e: e.tensor_scalar(out=idxB[:], in0=idxC[:], scalar1=2, scalar2=None, op0=ALU.mult), [idxC], [idxB])

        scrkv_b, scrkv_b2, scri_b, scrm_b = Buf(None, "scrkv"), Buf(None, "scrkv2"), Buf(None, "scri"), Buf(None, "scrm")
        xtok = [fw.sb([128, DM], F32, "xtok%d" % j) for j in range(17)]
        for j in range(16):
            dma("sp", (lambda j: lambda e: e.dma_start(out=xtok[j][:], in_=xp_d[j * 128:(j + 1) * 128, :]))(j),
                writes=[xtok[j]])
        op("pool", lambda e: e.memset(xtok[16][:], 0.0), [], [xtok[16]])
        dma("sp", lambda e: e.dma_start(out=xtok[16][0:NS, :], in_=xs_d), writes=[xtok[16]])

        xT = [fw.sb([128, 512], BF16, "xT%d" % c) for c in range(8)]
        hT = [fw.sb([128, 512], BF16, "hT%d" % c) for c in range(8)]
        lng_sb = fw.sb([128, DM], F32, "lng_sb")
        lnb_sb = fw.sb([128, DM], F32, "lnb_sb")
        wst = [fw.sb([128, 2, 128], F32, "wst%d" % i) for i in range(4)]
        wbf = [fw.sb([128, 8, 128], BF16, "wbf%d" % i) for i in range(3)]
        ring = {"st": 0, "bf": 0}
        psA = [fw.ps([128, 512], F32, "psA%d" % i) for i in range(2)]
        psZ = [fw.ps([128, 512], F32, "psZ%d" % i) for i in range(2)]
        psO = [fw.ps([128, 512], F32, "psO%d" % i) for i in range(2)]
        psY = [fw.ps([128, 512], F32, "psY%d" % i) for i in range(2)]
        rr = {"A": 0, "Z": 0, "O": 0}

        def nxt(kind):
            lst = {"A": psA, "Z": psZ, "O": psO}[kind]
            rr[kind] = (rr[kind] + 1) % len(lst)
            return lst[rr[kind]]

        kTt = [fw.sb([128, 4, 512], BF16, "kTt%d" % t) for t in range(4)]
        vbt = [fw.sb([128, 4, 512], BF16, "vbt%d" % t) for t in range(4)]
        qT = [fw.sb([128, 512], BF16, "qT%d" % c) for c in range(4)]
        t32 = [fw.sb([128, 512], F32, "t32_%d" % i) for i in range(6)]
        big32 = [fw.sb([128, 2304], F32, "big32_%d" % i) for i in range(2)]
        att_b = [fw.sb([128, 512], BF16, "att%d" % i) for i in range(2)]
        e32 = [fw.sb([128, 512], F32, "e32_%d" % i) for i in range(1)] * 2
        sp32 = [fw.sb([128, 512], F32, "sp32_%d" % i) for i in range(1)] * 2
        S32 = fw.sb([128, 512], F32, "S32")
        small = [fw.sb([128, 8], F32, "small%d" % i) for i in range(4)]
        if DBG.get("wstx", 0):
            for xb_ in (e32[0], sp32[0], S32):
                for hv in range(2):
                    wst.append(View(xb_, xb_[:, hv * 256:(hv + 1) * 256].rearrange("p (a b) -> p a b", a=2)))
        wspT = fw.sb([128, 4, 128], BF16, "wspT")
        wsp_f = fw.sb([128, 4, 128], F32, "wsp_f")
        bsp_bc = fw.sb([128, 4, 128], F32, "bsp_bc")
        wsp00 = fw.sb([128, 4], F32, "wsp00")
        xTs = [fw.sb([128, NS], BF16, "xTs%d" % c) for c in range(8)]
        hTs = [fw.sb([128, NS], BF16, "hTs%d" % c) for c in range(8)]
        kpg = [fw.sb([128, 2, 512], F32, "kpg%d" % i) for i in range(2)]
        vpg = [fw.sb([128, 2, 512], BF16, "vpg%d" % i) for i in range(2)]
        qb_sb = fw.sb([128, 512], F32, "qb_sb")
        zs = fw.sb([128, 128], F32, "zs")
        sps = fw.sb([128, 128], F32, "sps")
        tts = fw.sb([128, 128], F32, "tts")
        atts = fw.sb([128, 128], BF16, "atts")
        oBT = fw.sb([128, 4, NS], F32, "oBT")
        gbTs = fw.sb([128, 4, NS], F32, "gbTs")
        s16 = t32

        def load_w(wap, col0, ncols, pieces=None):
            blk = wbf[ring["bf"] % 3]
            ring["bf"] += 1
            if pieces is None:
                pieces = [(col0, ncols, 0)]
            n_tot = max(d + n for (_, n, d) in pieces)
            if DBG.get("wdma", 0):
                for (sc, n, dc) in pieces:
                    src = wap[:, sc:sc + n].rearrange("(k p) c -> p k c", p=128)
                    for kh in range(2):
                        dma("pool", (lambda kh, src, n, dc: lambda e: e.dma_start(out=blk[:, 4 * kh:4 * kh + 4, dc:dc + n],
                                                                                  in_=src[:, 4 * kh:4 * kh + 4, :]))(kh, src, n, dc),
                            writes=[blk])
                return blk
            for kq in range(4):
                stg = wst[ring["st"] % len(wst)]
                ring["st"] += 1
                for (sc, n, dc) in pieces:
                    src = wap[:, sc:sc + n].rearrange("(k p) c -> p k c", p=128)
                    dma("sp", (lambda kq, stg, src, n, dc: lambda e: e.dma_start(out=stg[:, :, dc:dc + n],
                                                                                 in_=src[:, 2 * kq:2 * kq + 2, :]))(kq, stg, src, n, dc),
                        writes=[stg])
                if DBG.get("castact") and kq % 2 == 1:
                    op("act", (lambda kq, stg: lambda e: e.activation(out=blk[:, 2 * kq:2 * kq + 2, 0:n_tot], in_=stg[:, :, 0:n_tot],
                                                                      func=AF.Copy))(kq, stg), [stg], [blk])
                else:
                    op("dve", (lambda kq, stg: lambda e: e.tensor_copy(out=blk[:, 2 * kq:2 * kq + 2, 0:n_tot], in_=stg[:, :, 0:n_tot]))(kq, stg),
                       [stg], [blk])
            return blk

        def fm_mm(ps, blk, ncols, xTl, ntok):
            for k in range(8):
                op("pe", (lambda k: lambda e: e.matmul(ps[0:ncols, 0:ntok], lhsT=blk[:, k, 0:ncols],
                                                       rhs=xTl[k][:, 0:ntok], start=(k == 0), stop=(k == 7)))(k),
                   [blk, xTl[k]], [ps])

        def tm_mm(ps, pcol, blk, ncols, xTl, tok0, ntok):
            for k in range(8):
                op("pe", (lambda k: lambda e: e.matmul(ps[0:ntok, pcol:pcol + ncols], lhsT=xTl[k][:, tok0:tok0 + ntok],
                                                       rhs=blk[:, k, 0:ncols], start=(k == 0), stop=(k == 7)))(k),
                   [blk, xTl[k]], [ps])

        def make_xT(T):
            if T < 4:
                for jj in range(4):
                    src = xtok[4 * T + jj]
                    for half in range(2):
                        ps = nxt("A")
                        for cc in range(4):
                            c = half * 4 + cc
                            op("pe", (lambda c, cc, ps, src: lambda e: e.transpose(
                                out=ps[:, cc * 128:(cc + 1) * 128], in_=src[:, c * 128:(c + 1) * 128],
                                identity=C["c_ident"][:]))(c, cc, ps, src), [src, C["c_ident"]], [ps])
                        for cc in range(4):
                            c = half * 4 + cc
                            if DBG.get("xv", 0) == 2:
                                continue
                            ev_eng = "dve" if (cc % 2 == 0 or True) else "act"
                            op(ev_eng,
                               (lambda c, cc, ps, jj, ev_eng: lambda e: (e.tensor_copy(out=xT[c][:, jj * 128:(jj + 1) * 128],
                                                                               in_=ps[:, cc * 128:(cc + 1) * 128])
                                                                 if ev_eng == "dve" else
                                                                 e.activation(out=xT[c][:, jj * 128:(jj + 1) * 128],
                                                                              in_=ps[:, cc * 128:(cc + 1) * 128],
                                                                              func=AF.Copy)))(c, cc, ps, jj, ev_eng),
                               [ps], [xT[c]])
                return xT, 512
            src = xtok[16]
            for half in range(2):
                ps = nxt("A")
                for cc in range(4):
                    c = half * 4 + cc
                    op("pe", (lambda c, cc, ps: lambda e: e.transpose(
                        out=ps[:, cc * 128:cc * 128 + NS], in_=src[0:NS, c * 128:(c + 1) * 128],
                        identity=C["c_ident"][0:NS, 0:NS]))(c, cc, ps), [src, C["c_ident"]], [ps])
                for cc in range(4):
                    c = half * 4 + cc
                    op("dve", (lambda c, cc, ps: lambda e: e.tensor_copy(out=xTs[c][:, :], in_=ps[:, cc * 128:cc * 128 + NS]))(c, cc, ps),
                       [ps], [xTs[c]])
            return xTs, NS

        def gelu_inplace(x, n, p=128):
            tmp = t32[5] if n <= 512 else big32[1]
            op("dve", lambda e: e.tensor_tensor(out=tmp[0:p, 0:n], in0=x[0:p, 0:n], in1=x[0:p, 0:n], op=ALU.mult), [x], [tmp])
            op("dve", lambda e: e.tensor_scalar(out=tmp[0:p, 0:n], in0=tmp[0:p, 0:n], scalar1=0.044715, scalar2=1.0,
                                                op0=ALU.mult, op1=ALU.add), [tmp], [tmp])
            op("dve", lambda e: e.tensor_tensor(out=tmp[0:p, 0:n], in0=tmp[0:p, 0:n], in1=x[0:p, 0:n], op=ALU.mult), [tmp, x], [tmp])
            op("act", lambda e: e.activation(out=tmp[0:p, 0:n], in_=tmp[0:p, 0:n], func=AF.Sigmoid, scale=1.5957691216057308),
               [tmp], [tmp])
            op("dve", lambda e: e.tensor_tensor(out=x[0:p, 0:n], in0=x[0:p, 0:n], in1=tmp[0:p, 0:n], op=ALU.mult), [x, tmp], [x])

        def silu_from_psum(dst, ps, p0, p1, n):
            op("act", lambda e: e.activation(out=dst[p0:p1, 0:n], in_=ps[p0:p1, 0:n], func=AF.Sigmoid), [ps], [dst])
            op("dve", lambda e: e.tensor_tensor(out=dst[p0:p1, 0:n], in0=dst[p0:p1, 0:n], in1=ps[p0:p1, 0:n], op=ALU.mult),
               [dst, ps], [dst])

        def layer_norm_block(layer, j, hTl, tok0, ntok):
            layer_norm_tile(layer, [(j, tok0, ntok)], hTl)

        def layer_norm_tile(layer, blocks, hTl):
            banks = [psA[0], psA[1], psY[0], psY[1]]
            rbuf = [(big32[bi // 2], (bi % 2) * 1024) for bi in range(4)]
            for half in range(2):
                for cbq in range(4):
                    cb = half * 4 + cbq
                    blk = load_w(wo_d[layer], cb * 128, 128)
                    for bi, (j, tok0, ntok) in enumerate(blocks):
                        py = banks[bi]
                        for f in range(8):
                            op("pe", (lambda f, py, cbq, blk, tok0, ntok: lambda e: e.matmul(
                                py[0:ntok, cbq * 128:(cbq + 1) * 128], lhsT=hTl[f][:, tok0:tok0 + ntok], rhs=blk[:, f, :],
                                start=(f == 0), stop=(f == 7)))(f, py, cbq, blk, tok0, ntok), [hTl[f], blk], [py])
                for bi, (j, tok0, ntok) in enumerate(blocks):
                    py = banks[bi]
                    xb = xtok[j]
                    rb, ro = rbuf[bi]
                    op("dve", (lambda py, half, xb, rb, ro, ntok: lambda e: e.scalar_tensor_tensor(
                        out=rb[0:ntok, ro + half * 512:ro + (half + 1) * 512], in0=xb[0:ntok, half * 512:(half + 1) * 512],
                        scalar=DN_ALPHA, in1=py[0:ntok, :], op0=ALU.mult, op1=ALU.add))(py, half, xb, rb, ro, ntok), [xb, py], [rb])
            for bi, (j, tok0, ntok) in enumerate(blocks):
                ln_finish(xtok[j], rbuf[bi][0], rbuf[bi][1], ntok)

        def ln_finish(xb, rb, ro, ntok):
            st_ = small[0]
            sq = t32[0]
            op("dve", lambda e: e.tensor_reduce(out=st_[0:ntok, 0:1], in_=rb[0:ntok, ro:ro + DM], axis=AX.X, op=ALU.add), [rb], [st_])
            for hf in range(2):
                op("pool", (lambda hf: lambda e: e.tensor_tensor(out=sq[0:ntok, 0:512], in0=rb[0:ntok, ro + hf * 512:ro + (hf + 1) * 512],
                                                                 in1=rb[0:ntok, ro + hf * 512:ro + (hf + 1) * 512], op=ALU.mult))(hf), [rb], [sq])
                op("dve", (lambda hf: lambda e: e.tensor_reduce(out=st_[0:ntok, 5 + hf:6 + hf], in_=sq[0:ntok, 0:512], axis=AX.X, op=ALU.add))(hf),
                   [sq], [st_])
            op("dve", lambda e: e.tensor_tensor(out=st_[0:ntok, 1:2], in0=st_[0:ntok, 5:6], in1=st_[0:ntok, 6:7], op=ALU.add), [st_], [st_])
            op("dve", lambda e: e.tensor_scalar(out=st_[0:ntok, 0:2], in0=st_[0:ntok, 0:2], scalar1=1.0 / DM, scalar2=None,
                                                op0=ALU.mult), [st_], [st_])
            op("dve", lambda e: e.tensor_tensor(out=st_[0:ntok, 2:3], in0=st_[0:ntok, 0:1], in1=st_[0:ntok, 0:1], op=ALU.mult), [st_], [st_])
            op("dve", lambda e: e.tensor_tensor(out=st_[0:ntok, 3:4], in0=st_[0:ntok, 1:2], in1=st_[0:ntok, 2:3], op=ALU.subtract), [st_], [st_])
            op("dve", lambda e: e.tensor_scalar(out=st_[0:ntok, 4:5], in0=st_[0:ntok, 3:4], scalar1=LN_EPS, scalar2=None, op0=ALU.add), [st_], [st_])
            op("act", lambda e: e.activation(out=st_[0:ntok, 4:5], in_=st_[0:ntok, 4:5], func=AF.Sqrt), [st_], [st_])
            op("dve", lambda e: e.reciprocal(out=st_[0:ntok, 4:5], in_=st_[0:ntok, 4:5]), [st_], [st_])
            op("dve", lambda e: e.tensor_scalar(out=rb[0:ntok, ro:ro + DM], in0=rb[0:ntok, ro:ro + DM], scalar1=st_[0:ntok, 0:1],
                                                scalar2=st_[0:ntok, 4:5], op0=ALU.subtract, op1=ALU.mult), [rb, st_], [rb])
            op("pool", lambda e: e.tensor_tensor(out=rb[0:ntok, ro:ro + DM], in0=rb[0:ntok, ro:ro + DM], in1=lng_sb[0:ntok, :], op=ALU.mult),
               [rb, lng_sb], [rb])
            op("dve", lambda e: e.tensor_tensor(out=xb[0:ntok, :], in0=rb[0:ntok, ro:ro + DM], in1=lnb_sb[0:ntok, :], op=ALU.add),
               [rb, lnb_sb], [xb])

        def layer_setup(layer):
            dma("sp", lambda e: e.dma_start(out=lng_sb[:], in_=lng_d[layer:layer + 1, :].to_broadcast([128, DM])), writes=[lng_sb])
            dma("sp", lambda e: e.dma_start(out=lnb_sb[:], in_=lnb_d[layer:layer + 1, :].to_broadcast([128, DM])), writes=[lnb_sb])

        mpos = C["c_m2"]
        def even_setup(j):
            dma("sp", lambda e: e.dma_start(out=wsp_f[:], in_=wsp_d[j].rearrange("g i k -> i g k")), writes=[wsp_f])
            for g in range(4):
                op("pool", (lambda g: lambda e: e.affine_select(out=wsp_f[:, g, :], in_=wsp_f[:, g, :], pattern=[[-1, 128]],
                                                                compare_op=ALU.is_ge, fill=0.0, base=0,
                                                                channel_multiplier=1))(g), [wsp_f], [wsp_f])
            ps = nxt("A")
            for g in range(4):
                op("pe", (lambda g: lambda e: e.transpose(out=ps[:, g * 128:(g + 1) * 128], in_=wsp_f[:, g, :],
                                                          identity=C["c_ident"][:]))(g), [wsp_f, C["c_ident"]], [ps])
            op("dve", lambda e: e.tensor_copy(out=wspT[:].rearrange("p g k -> p (g k)"), in_=ps[:, :]), [ps], [wspT])
            dma("sp", lambda e: e.dma_start(out=bsp_bc[:].rearrange("p g k -> p (g k)"),
                                            in_=bsp_d[j:j + 1].rearrange("o g k -> o (g k)").to_broadcast([128, 512])),
                writes=[bsp_bc])
            dma("sp", lambda e: e.dma_start(out=wsp00[:], in_=wsp_d[j, :, 0:1, 0:1].rearrange("g a b -> (a b) g").to_broadcast([128, 4])),
                writes=[wsp00])

        RR = (lambda ap: ap.bitcast(F32R)) if DBG.get("f32r", 0) else (lambda ap: ap)

        def stick_break_head(T, c, hh, po, hb):
            base = 64 * hh
            kbs = list(range(4 * T + 3, -1, -1))
            op("pe", lambda e: e.matmul(po[:, 0:512], lhsT=zero_b[:, :], rhs=xT[0][:, 0:512], start=True, stop=False), [zero_b, xT[0]], [po])
            def step(bi, kb):
                m = kb - 4 * T
                c0 = max(0, m) * 128
                Tk, kk = kb // 4, kb % 4
                pz, ee, sp, at, S32 = hb["pz"], hb["ee"], hb["sp"], hb["at"], hb["S"]
                ksrc = kTt[Tk]
                op("pe", lambda e: e.matmul(pz[:, c0:512], lhsT=ksrc[base:base + 64, c, kk * 128:(kk + 1) * 128],
                                            rhs=qT[c][base:base + 64, c0:512], start=True, stop=True), [ksrc, qT[c]], [pz])
                if m >= 0:
                    op("pe", lambda e: e.matmul(pz[:, c0:c0 + 128], lhsT=ident_b[:], rhs=negm_b[:], start=False, stop=True),
                       [ident_b, negm_b], [pz])
                op("act", lambda e: e.activation(out=ee[:, c0:512], in_=pz[:, c0:512], func=AF.Exp), [pz], [ee])
                op("act", lambda e: e.activation(out=RR(sp[:, c0:512]), in_=ee[:, c0:512], func=AF.Ln, bias=1.0), [ee], [sp])
                if DBG.get("f32r", 0):
                    op("pe", lambda e: e.matmul(pz[:, c0:512], lhsT=ntri_r[:], rhs=sp[:, c0:512].bitcast(F32R),
                                                start=False, stop=True), [ntri_r, sp], [pz])
                    if bi > 0:
                        op("pe", lambda e: e.matmul(pz[:, c0:512], lhsT=nones_r[:], rhs=S32[:, c0:512].bitcast(F32R),
                                                    start=False, stop=True), [nones_r, S32], [pz])
                else:
                    op("pe", lambda e: e.matmul(pz[:, c0:512], lhsT=C["c_ntri"][:], rhs=sp[:, c0:512], start=False, stop=True),
                       [C["c_ntri"], sp], [pz])
                    if bi > 0:
                        op("pe", lambda e: e.matmul(pz[:, c0:512], lhsT=C["c_nones"][:], rhs=S32[:, c0:512], start=False, stop=True),
                           [C["c_nones"], S32], [pz])
                op("act", lambda e: e.activation(out=at[:, c0:512], in_=pz[:, c0:512], func=AF.Exp), [pz], [at])
                if bi == 0:
                    if c0 > 0:
                        op("pool", lambda e: e.memset(S32[:, 0:c0], 0.0), [], [S32])
                    op("pool", lambda e: e.tensor_copy(out=RR(S32[:, c0:512]), in_=sp[:, c0:512]), [sp], [S32])
                elif bi < len(kbs) - 1:
                    op("pool", lambda e: e.tensor_tensor(out=RR(S32[:, c0:512]), in0=S32[:, c0:512], in1=sp[:, c0:512], op=ALU.add),
                       [S32, sp], [S32])
                vsrc = vbt[Tk]
                last = (kb == 0)
                if m >= 0:
                    op("pe", lambda e: e.matmul(po[:, c0:c0 + 128], lhsT=vsrc[:, kk, c * 128:(c + 1) * 128], rhs=at[:, c0:c0 + 128],
                                                start=False, stop=last), [vsrc, at], [po])
                    if c0 + 128 < 512:
                        op("pe", lambda e: e.matmul(po[:, c0 + 128:512], lhsT=vsrc[:, kk, c * 128:(c + 1) * 128],
                                                    rhs=at[:, c0 + 128:512], start=False, stop=last), [vsrc, at], [po])
                else:
                    op("pe", lambda e: e.matmul(po[:, 0:512], lhsT=vsrc[:, kk, c * 128:(c + 1) * 128], rhs=at[:, 0:512],
                                                start=False, stop=last), [vsrc, at], [po])

            for bi, kb in enumerate(kbs):
                step(bi, kb)
                yield

        def even_prompt_tile(layer, j, T):
            W = wie_d[j]
            t0 = T * 512
            if DBG.get("stage", 9) < 1:
                return
            xTl, ntok = make_xT(T)
            sub = DBG.get("sub", 9)
            if sub < 1:
                return
            for c in range(4):
                blk = load_w(W, 2048 + 128 * c, 128)
                if sub < 2:
                    continue
                ps = nxt("A")
                fm_mm(ps, blk, 128, xTl, 512)
                if sub < 3:
                    continue
                op("dve", (lambda c, ps: lambda e: e.tensor_copy(out=kTt[T][:, c, :], in_=ps[:, :]))(c, ps), [ps], [kTt[T]])
                if sub < 4:
                    continue
                ps2 = nxt("A")
                for jj in range(4):
                    tm_mm(ps2, jj * 128, blk, 128, xTl, jj * 128, 128)
                for hf in range(2):
                    op("dve", (lambda c, ps2, hf: lambda e: e.tensor_copy(
                        out=big32[hf][:, 0:2048].rearrange("p (j f) -> p j f", j=2)[:, :, 128 * c:128 * (c + 1)],
                        in_=ps2[:, hf * 256:(hf + 1) * 256].rearrange("p (j f) -> p j f", j=2)))(c, ps2, hf), [ps2], [big32[hf]])
            if sub < 5:
                return
            for c in range(4):
                blk = load_w(W, 2560 + 128 * c, 128)
                ps2 = nxt("A")
                for jj in range(4):
                    tm_mm(ps2, jj * 128, blk, 128, xTl, jj * 128, 128)
                for hf in range(2):
                    op("dve", (lambda c, ps2, hf: lambda e: e.tensor_copy(
                        out=big32[hf][:, 0:2048].rearrange("p (j f) -> p j f", j=2)[:, :, 512 + 128 * c:512 + 128 * (c + 1)],
                        in_=ps2[:, hf * 256:(hf + 1) * 256].rearrange("p (j f) -> p j f", j=2)))(c, ps2, hf), [ps2], [big32[hf]])
                op("dve", (lambda c, ps2: lambda e: e.tensor_copy(out=vbt[T][:, :, 128 * c:128 * (c + 1)],
                                                                  in_=ps2[:, :].rearrange("p (j f) -> p j f", j=4)))(c, ps2), [ps2], [vbt[T]])
            for hf in range(2):
                dma("pool", (lambda hf: lambda e: e.dma_start(
                    out=bkvp_d[j, t0 + hf * 256:t0 + (hf + 1) * 256, :].rearrange("(j p) f -> p j f", p=128),
                    in_=big32[hf][:, 0:2048].rearrange("p (j f) -> p j f", j=2)))(hf), reads=[big32[hf]])
            if DBG.get("stage", 9) < 2:
                return
            for c in range(4):
                blk = load_w(W, 1536 + 128 * c, 128)
                ps = nxt("A")
                fm_mm(ps, blk, 128, xTl, 512)
                op("dve", (lambda c, ps: lambda e: e.tensor_scalar(out=qT[c][:, :], in0=ps[:, :], scalar1=0.125, scalar2=None,
                                                                   op0=ALU.mult))(c, ps), [ps], [qT[c]])
            if DBG.get("stage", 9) < 3:
                return
            vraw = big32[0]
            for c in range(4):
                blk = load_w(W, 512 + 128 * c, 128)
                ps2 = nxt("A")
                for jj in range(4):
                    tm_mm(ps2, jj * 128, blk, 128, xTl, jj * 128, 128)
                op("dve", (lambda c, ps2: lambda e: e.tensor_copy(
                    out=vraw[:, 0:2048].rearrange("p (j f) -> p j f", j=4)[:, :, 128 * c:128 * (c + 1)],
                    in_=ps2[:, :].rearrange("p (j f) -> p j f", j=4)))(c, ps2), [ps2], [vraw])
            gelu_inplace(vraw, 2048)
            standardize(vraw, 4, 128)
            for jj in range(4):
                op("pool", (lambda jj: lambda e: e.tensor_copy(out=hT[4 + jj][:, :], in_=vraw[:, jj * 512:(jj + 1) * 512]))(jj),
                   [vraw], [hT[4 + jj]])
            for g in range(4):
                blk = load_w(W, 128 * g, 128)
                ps = nxt("A")
                fm_mm(ps, blk, 128, xTl, 512)
                ug = t32[0]
                op("dve", (lambda ps: lambda e: e.tensor_copy(out=ug[:, :], in_=ps[:, :]))(ps), [ps], [ug])
                gelu_inplace(ug, 512)
                blk = load_w(W, 1024 + 128 * g, 128)
                ps = nxt("A")
                fm_mm(ps, blk, 128, xTl, 512)
                sg = t32[1]
                silu_from_psum(sg, ps, 0, 128, 512)
                ps = nxt("A")
                for jj in range(4):
                    op("pe", (lambda g, jj, ps: lambda e: e.matmul(ps[:, jj * 128:(jj + 1) * 128], lhsT=hT[4 + jj][:, 128 * g:128 * (g + 1)],
                                                                   rhs=wspT[:, g, :], start=True, stop=True))(g, jj, ps),
                       [hT[4 + jj], wspT], [ps])
                sb_ = t32[2]
                op("dve", (lambda g, ps: lambda e: e.tensor_tensor(
                    out=sb_[:, :].rearrange("p (j i) -> p j i", j=4), in0=ps[:, :].rearrange("p (j i) -> p j i", j=4),
                    in1=bsp_bc[:, g:g + 1, :].to_broadcast([128, 4, 128]), op=ALU.add))(g, ps), [ps, bsp_bc], [sb_])
                op("pool", lambda e: e.tensor_tensor(out=sb_[:, :], in0=sb_[:, :], in1=ug[:, :], op=ALU.mult), [sb_, ug], [sb_])
                op("dve", (lambda g: lambda e: e.tensor_tensor(out=hT[g][:, :], in0=sb_[:, :], in1=sg[:, :], op=ALU.mult))(g),
                   [sb_, sg], [hT[g]])
            if DBG.get("stage", 9) < 4:
                return
            for c in range(4):
                blk = load_w(W, 3072 + 128 * c, 128)
                ps = nxt("A")
                fm_mm(ps, blk, 128, xTl, 512)
                sgb = t32[3]
                silu_from_psum(sgb, ps, 0, 128, 512)
                hbufs = [dict(pz=psZ[0], ee=e32[0], sp=sp32[0], at=att_b[0], S=S32),
                         dict(pz=psZ[1], ee=View(kpg[0], kpg[0][:, 0, :]), sp=View(kpg[1], kpg[1][:, 0, :]), at=att_b[1], S=qb_sb)]
                interleave([stick_break_head(T, c, hh, psO[hh], hbufs[hh]) for hh in range(2)])
                for hh in range(2):
                    po = psO[hh]
                    b0 = 64 * hh
                    op("dve", (lambda c, po, b0: lambda e: e.tensor_tensor(out=hT[4 + c][b0:b0 + 64, :], in0=po[b0:b0 + 64, :],
                                                                           in1=sgb[b0:b0 + 64, :], op=ALU.mult))(c, po, b0),
                       [po, sgb], [hT[4 + c]])
            if dbgh_d is not None and T == DBG.get("dbgT", 0) and DBG.get("dbgL", 0) == 0:
                for f in range(8):
                    dma("pool", (lambda f: lambda e: e.dma_start(out=dbgh_d[f], in_=hT[f][:, :]))(f), reads=[hT[f]])
            if DBG.get("stage", 9) < 5:
                return
            layer_norm_tile(layer, [(4 * T + jj, jj * 128, 128) for jj in range(4)], hT)

        def standardize(x, nj, p):
            sq = big32[1]
            n = nj * 512
            st_ = small[1]
            xv = x[0:p, 0:n].rearrange("p (j f) -> p j f", j=nj)
            op("dve", lambda e: e.tensor_reduce(out=st_[0:p, 0:nj], in_=xv, axis=AX.X, op=ALU.add), [x], [st_])
            op("pool", lambda e: e.tensor_tensor(out=sq[0:p, 0:n], in0=x[0:p, 0:n], in1=x[0:p, 0:n], op=ALU.mult), [x], [sq])
            st2 = small[2]
            op("dve", lambda e: e.tensor_reduce(out=st2[0:p, 0:nj], in_=sq[0:p, 0:n].rearrange("p (j f) -> p j f", j=nj),
                                                axis=AX.X, op=ALU.add), [sq], [st2])
            op("dve", lambda e: e.tensor_scalar(out=st_[0:p, 0:nj], in0=st_[0:p, 0:nj], scalar1=1.0 / 512, scalar2=None, op0=ALU.mult),
               [st_], [st_])
            op("dve", lambda e: e.tensor_scalar(out=st2[0:p, 0:nj], in0=st2[0:p, 0:nj], scalar1=1.0 / 512, scalar2=None, op0=ALU.mult),
               [st2], [st2])
            st3 = small[3]
            op("dve", lambda e: e.tensor_tensor(out=st3[0:p, 0:nj], in0=st_[0:p, 0:nj], in1=st_[0:p, 0:nj], op=ALU.mult), [st_], [st3])
            op("dve", lambda e: e.tensor_tensor(out=st2[0:p, 0:nj], in0=st2[0:p, 0:nj], in1=st3[0:p, 0:nj], op=ALU.subtract),
               [st2, st3], [st2])
            op("dve", lambda e: e.tensor_scalar(out=st2[0:p, 0:nj], in0=st2[0:p, 0:nj], scalar1=LN_EPS, scalar2=None, op0=ALU.add), [st2], [st2])
            op("act", lambda e: e.activation(out=st2[0:p, 0:nj], in_=st2[0:p, 0:nj], func=AF.Sqrt), [st2], [st2])
            op("dve", lambda e: e.reciprocal(out=st2[0:p, 0:nj], in_=st2[0:p, 0:nj]), [st2], [st2])
            for jx in range(nj):
                op("dve", (lambda jx: lambda e: e.tensor_scalar(out=x[0:p, jx * 512:(jx + 1) * 512], in0=x[0:p, jx * 512:(jx + 1) * 512],
                                                                scalar1=st_[0:p, jx:jx + 1], scalar2=st2[0:p, jx:jx + 1],
                                                                op0=ALU.subtract, op1=ALU.mult))(jx), [x, st_, st2], [x])

        def sample_inproj(W, col0, ncols, xTl, ps):
            c0 = 0
            while c0 < ncols:
                n = min(128, ncols - c0)
                blk = load_w(W, col0 + c0, n)
                tm_mm(ps, c0, blk, n, xTl, 0, NS)
                c0 += n

        def page_val(e, b, pg):
            return e.value_load(pts[0:1, b * NPG + pg:b * NPG + pg + 1])

        def even_sample_tile(layer, j):
            W = wie_d[j]
            xTl, ntok = make_xT(4)
            for c in range(4):
                blk = load_w(W, 3072 + 128 * c, 128)
                ps = nxt("A")
                fm_mm(ps, blk, 128, xTl, NS)
                op("act", (lambda c, ps: lambda e: e.activation(out=gbTs[:, c, :], in_=ps[:, 0:NS], func=AF.Sigmoid))(c, ps), [ps], [gbTs])
                op("dve", (lambda c, ps: lambda e: e.tensor_tensor(out=gbTs[:, c, :], in0=gbTs[:, c, :], in1=ps[:, 0:NS], op=ALU.mult))(c, ps),
                   [gbTs, ps], [gbTs])
            for hf in range(2):
                psk = nxt("A")
                sample_inproj(W, 2048 + 512 * hf, 512, xTl, psk)
                op("dve", (lambda hf, psk: lambda e: e.tensor_copy(out=big32[0][0:NS, hf * 512:(hf + 1) * 512], in_=psk[0:NS, :]))(hf, psk),
                   [psk], [big32[0]])
            dma("pool", lambda e: e.dma_start(out=bkvs_d[j], in_=big32[0][0:NS, 0:1024]), reads=[big32[0]])
            v = s16[0]
            psv = nxt("A")
            sample_inproj(W, 512, 512, xTl, psv)
            op("dve", lambda e: e.tensor_copy(out=v[0:NS, :], in_=psv[0:NS, :]), [psv], [v])
            gelu_inplace(v, 512, NS)
            standardize(v, 1, NS)
            dma("pool", lambda e: e.dma_start(out=avs_d[j], in_=v[0:NS, :]), reads=[v])
            u = s16[1]
            psu = nxt("A")
            sample_inproj(W, 0, 512, xTl, psu)
            op("dve", lambda e: e.tensor_copy(out=u[0:NS, :], in_=psu[0:NS, :]), [psu], [u])
            gelu_inplace(u, 512, NS)
            ga = s16[2]
            psg = nxt("A")
            sample_inproj(W, 1024, 512, xTl, psg)
            silu_from_psum(ga, psg, 0, NS, 512)
            sv = s16[3]
            op("dve", lambda e: e.tensor_tensor(out=sv[0:NS, :].rearrange("p (g f) -> p g f", g=4),
                                                in0=v[0:NS, :].rearrange("p (g f) -> p g f", g=4),
                                                in1=wsp00[0:NS, :].unsqueeze(2).to_broadcast([NS, 4, 128]), op=ALU.mult), [v, wsp00], [sv])
            op("dve", lambda e: e.tensor_tensor(out=sv[0:NS, :].rearrange("p (g f) -> p g f", g=4),
                                                in0=sv[0:NS, :].rearrange("p (g f) -> p g f", g=4),
                                                in1=bsp_bc[0:NS, :, 0:1].to_broadcast([NS, 4, 128]), op=ALU.add), [sv, bsp_bc], [sv])
            op("dve", lambda e: e.tensor_tensor(out=sv[0:NS, :], in0=sv[0:NS, :], in1=u[0:NS, :], op=ALU.mult), [sv, u], [sv])
            op("dve", lambda e: e.tensor_tensor(out=sv[0:NS, :], in0=sv[0:NS, :], in1=ga[0:NS, :], op=ALU.mult), [sv, ga], [sv])
            ps = nxt("A")
            for g in range(4):
                op("pe", (lambda g, ps: lambda e: e.transpose(out=ps[:, g * 128:g * 128 + NS], in_=sv[0:NS, g * 128:(g + 1) * 128],
                                                              identity=C["c_ident"][0:NS, 0:NS]))(g, ps), [sv, C["c_ident"]], [ps])
            for g in range(4):
                op("dve", (lambda g, ps: lambda e: e.tensor_copy(out=hTs[g][:, :], in_=ps[:, g * 128:g * 128 + NS]))(g, ps), [ps], [hTs[g]])
            qsc = s16[4]
            psq = nxt("A")
            sample_inproj(W, 1536, 512, xTl, psq)
            op("dve", lambda e: e.tensor_scalar(out=qsc[0:NS, :], in0=psq[0:NS, :], scalar1=0.125, scalar2=None, op0=ALU.mult), [psq], [qsc])
            VB = [View(big32[i], big32[i][:, 0:2304].bitcast(BF16)) for i in range(2)]
            pso = psO[0]
            op("pe", lambda e: e.matmul(pso[:, 0:512], lhsT=zero_b[:, :], rhs=xT[0][:, 0:512], start=True, stop=False), [zero_b, xT[0]], [pso])
            for b in range(NS):
                pq = nxt("A")
                qm = t32[5]
                op("dve", (lambda b: lambda e: e.tensor_scalar(out=qm[0:NS, :], in0=qsc[0:NS, :], scalar1=C["c_ident"][0:NS, b:b + 1],
                                                               scalar2=-1.0, op0=ALU.mult, op1=ALU.mult))(b), [qsc, C["c_ident"]], [qm])
                op("pe", (lambda b, pq: lambda e: e.matmul(pq[:, :], lhsT=C["c_nones"][0:NS, :], rhs=qm[0:NS, :],
                                                           start=True, stop=True))(b, pq), [C["c_nones"], qm], [pq])
                op("dve", (lambda pq: lambda e: e.tensor_copy(out=qb_sb[:, :], in_=pq[:, :]))(pq), [pq], [qb_sb])
                for pg in range(NPG):
                    kb_ = kpg[pg % 2]
                    dma("pool", (lambda b, pg, kb_: lambda e: e.indirect_dma_start(
                        out=kb_[:, :, :].rearrange("p a f -> p (a f)"), out_offset=None, in_=cb_d[:, 0:1024], element_offset=j * pool_rows * 1024,
                        in_offset=bass.IndirectOffsetOnAxis(ap=idxC[:, b * NPG + pg:b * NPG + pg + 1], axis=0)))(b, pg, kb_),
                        reads=[idxC], writes=[kb_])
                    prod = t32[0]
                    op("dve", (lambda kb_: lambda e: e.tensor_tensor(out=prod[:, 0:512], in0=kb_[:, 0, :], in1=qb_sb[:, :], op=ALU.mult))(kb_),
                       [kb_, qb_sb], [prod])
                    op("dve", (lambda pg: lambda e: e.tensor_reduce(
                        out=zs[:, pg * 8:(pg + 1) * 8], in_=prod[:, 0:512].rearrange("p (a d) -> p a d", d=64), axis=AX.X,
                        op=ALU.add))(pg), [prod], [zs])
                    vdst = VB[0 if pg < 9 else 1]
                    vo = (pg if pg < 9 else pg - 9) * 512
                    op("dve", (lambda kb_, vdst, vo: lambda e: e.tensor_copy(out=vdst[:, vo:vo + 512], in_=kb_[:, 1, :]))(kb_, vdst, vo),
                       [kb_], [vdst])
                op("act", lambda e: e.activation(out=sps[:, :], in_=zs[:, :], func=AF.Exp), [zs], [sps])
                op("act", lambda e: e.activation(out=sps[:, :], in_=sps[:, :], func=AF.Ln, bias=1.0), [sps], [sps])
                pt_ = nxt("Z")
                op("pe", (lambda pt_: lambda e: e.matmul(pt_[:, 0:128], lhsT=sps[:, :], rhs=C["c_nones"][:, :], start=True, stop=True))(pt_),
                   [sps, C["c_nones"]], [pt_])
                op("dve", (lambda pt_: lambda e: e.tensor_copy(out=tts[:, :], in_=pt_[:, 0:128]))(pt_), [pt_], [tts])
                pz = nxt("Z")
                op("pe", (lambda pz: lambda e: e.matmul(pz[:, 0:128], lhsT=C["c_ident"][:, :], rhs=zs[:, :], start=True, stop=True))(pz),
                   [C["c_ident"], zs], [pz])
                op("pe", (lambda pz: lambda e: e.matmul(pz[:, 0:128], lhsT=C["c_ntri"][:, :], rhs=sps[:, :], start=False, stop=True))(pz),
                   [C["c_ntri"], sps], [pz])
                op("pe", (lambda pz: lambda e: e.matmul(pz[:, 0:128], lhsT=tts[:, :], rhs=mpos[:, :], start=False, stop=True))(pz),
                   [tts, mpos], [pz])
                op("act", (lambda pz: lambda e: e.activation(out=atts[:, :], in_=pz[:, 0:128], func=AF.Exp))(pz), [pz], [atts])
                for pg in range(NPG):
                    vsrc_ = VB[0 if pg < 9 else 1]
                    vo = (pg if pg < 9 else pg - 9) * 512
                    for c in range(4):
                        op("pe", (lambda b, pg, c, vsrc_, vo: lambda e: e.matmul(
                            pso[:, c * 128 + b * 8:c * 128 + b * 8 + 8], lhsT=vsrc_[:, vo + c * 128:vo + (c + 1) * 128],
                            rhs=atts[:, pg * 8:(pg + 1) * 8], start=False, stop=(pg == 15)))(b, pg, c, vsrc_, vo),
                           [vsrc_, atts], [pso])
            for c in range(4):
                src = pso[:, c * 128:(c + 1) * 128].rearrange("p (b h) -> p b h", h=8)
                op("dve", (lambda c, src: lambda e: e.tensor_copy(out=oBT[0:64, c, :], in_=src[0:64, :, 2 * c]))(c, src), [pso], [oBT])
                op("dve", (lambda c, src: lambda e: e.tensor_copy(out=oBT[64:128, c, :], in_=src[64:128, :, 2 * c + 1]))(c, src), [pso], [oBT])
            for c in range(4):
                op("dve", (lambda c: lambda e: e.tensor_tensor(out=hTs[4 + c][:, :], in0=oBT[:, c, :], in1=gbTs[:, c, :], op=ALU.mult))(c),
                   [oBT, gbTs], [hTs[4 + c]])
            layer_norm_block(layer, 16, hTs, 0, NS)

        mpos = C["c_m2"]


        ISQ = 1.0 / math.sqrt(32.0)
        maskT = kTt[3]
        b31_sb = small[3]

        def odd_setup(j):
            for i in range(4):
                op("pool", (lambda i: lambda e: e.memset(vbt[i][:, :, :], 1.0))(i), [], [vbt[i]])
            dma("sp", lambda e: e.dma_start(out=qb_sb[:, :].rearrange("p (g d) -> p g d", g=4), in_=wgrp_d[j].rearrange("g c d -> c g d")),
                writes=[qb_sb])
            op("dve", lambda e: e.tensor_copy(out=wspT[:].rearrange("p g d -> p (g d)"), in_=qb_sb[:, :]), [qb_sb], [wspT])
            dma("sp", lambda e: e.dma_start(out=wsp00[:, :], in_=dsc_d[j:j + 1, :].rearrange("o (g p) -> p (o g)", p=128)), writes=[wsp00])
            dma("sp", lambda e: e.dma_start(out=b31_sb[:, 0:8], in_=b31_d.to_broadcast([128, 8])), writes=[b31_sb])
            op("pool", lambda e: e.memset(wsp_f[:, :, :], 0.0), [], [wsp_f])

        def odd_select(T, jj, idx, wk):
            qb = 4 * T + jj
            Wk = (qb + 1) * 128
            wq = bsp_bc
            if qb < 2:
                return
            nkc = (Wk + 511) // 512
            for kc in range(nkc):
                n = min(512, Wk - kc * 512)
                for ih in range(4):
                    pz = nxt("A")
                    op("pe", (lambda kc, n, ih, pz: lambda e: e.matmul(pz[:, 0:n], lhsT=vpg[ih // 2][:, ih % 2, jj * 128:(jj + 1) * 128],
                                                                       rhs=kTt[2][:, kc, 0:n], start=True, stop=True))(kc, n, ih, pz),
                       [vpg[ih // 2], kTt[2]], [pz])
                    rl = e32[0]
                    op("act", (lambda n, pz: lambda e: e.activation(out=rl[:, 0:n], in_=pz[:, 0:n], func=AF.Relu))(n, pz), [pz], [rl])
                    if ih == 0:
                        op("dve", (lambda kc, n: lambda e: e.tensor_scalar(out=idx[:, kc * 512:kc * 512 + n], in0=rl[:, 0:n],
                                                                           scalar1=wq[:, jj, 0:1], scalar2=None, op0=ALU.mult))(kc, n),
                           [rl, wq], [idx])
                    else:
                        op("dve", (lambda kc, n, ih: lambda e: e.scalar_tensor_tensor(
                            out=idx[:, kc * 512:kc * 512 + n], in0=rl[:, 0:n], scalar=wq[:, jj, ih:ih + 1],
                            in1=idx[:, kc * 512:kc * 512 + n], op0=ALU.mult, op1=ALU.add))(kc, n, ih), [rl, wq, idx], [idx])
                yield
            op("dve", lambda e: e.tensor_tensor(out=idx[:, qb * 128:Wk], in0=idx[:, qb * 128:Wk], in1=C["c_negq"][:, :], op=ALU.add),
               [idx, C["c_negq"]], [idx])
            op("pool", lambda e: e.tensor_copy(out=wk[:, 0:Wk], in_=idx[:, 0:Wk]), [idx], [wk])
            mx = small[1]
            for r in range(32):
                op("dve", lambda e: e.max(out=mx[:, 0:8], in_=wk[:, 0:Wk]), [wk], [mx])
                op("dve", lambda e: e.match_replace(out=wk[:, 0:Wk], in_to_replace=mx[:, 0:8], in_values=wk[:, 0:Wk], imm_value=2.0 * NEG),
                   [wk, mx], [wk])
                yield
            op("dve", lambda e: e.tensor_tensor(out=wk[:, 0:Wk], in0=wk[:, 0:Wk], in1=idx[:, 0:Wk], op=ALU.not_equal), [wk, idx], [wk])

        def odd_finish_mask(T, jj, wk):
            qb = 4 * T + jj
            mT = maskT[:, :, :].rearrange("p a (b q) -> p (a b) q", q=128)
            if qb < 2:
                for kb in range(qb):
                    op("pool", (lambda kb: lambda e: e.memset(mT[:, kb, :], 1.0))(kb), [], [maskT])
                op("dve", lambda e: e.tensor_copy(out=mT[:, qb, :], in_=C["c_dsac"][:, :]), [C["c_dsac"]], [maskT])
            else:
                for k4 in range(0, qb + 1, 4):
                    nb = min(4, qb + 1 - k4)
                    pt_ = nxt("A")
                    for i in range(nb):
                        op("pe", (lambda k4, i, pt_: lambda e: e.transpose(out=pt_[:, i * 128:(i + 1) * 128],
                                                                           in_=wk[:, (k4 + i) * 128:(k4 + i + 1) * 128],
                                                                           identity=C["c_ident"][:]))(k4, i, pt_), [wk, C["c_ident"]], [pt_])
                    op("dve", (lambda k4, nb, pt_: lambda e: e.tensor_copy(out=mT[:, k4:k4 + nb, :],
                                                                           in_=pt_[:, 0:nb * 128].rearrange("p (b q) -> p b q", q=128)))(k4, nb, pt_),
                       [pt_], [maskT])
            return mT

        def odd_attend(T, jj, mT):
            qb = 4 * T + jj
            obufs = [dict(po=psO[0], pz=psZ[0], at=att_b[0], tz=[sps, tts], lt=sp32[0], rz=zs),
                     dict(po=psO[1], pz=psZ[1], at=att_b[1], tz=[View(kpg[0], kpg[0][:, 0, 0:128]), View(kpg[0], kpg[0][:, 0, 128:256])],
                          lt=View(kpg[1], kpg[1][:, 0, 0:128]), rz=View(qb_sb, qb_sb[:, 0:128]))]
            for h2 in range(0, 8, 2):
                gens = [odd_head(T, jj, qb, h2 + i, mT, obufs[i]) for i in range(2)]
                while gens:
                    for g_ in list(gens):
                        try:
                            next(g_)
                        except StopIteration:
                            gens.remove(g_)
                    yield

        def odd_head(T, jj, qb, h, mT, ob_):
            g, c, base = h // 4, h // 2, 64 * (h % 2)
            po = ob_["po"]
            op("pe", lambda e: e.matmul(po[:, 0:128], lhsT=zero_b[:, :], rhs=xT[0][:, 0:128], start=True, stop=False), [zero_b, xT[0]], [po])
            vA = vbt[g] if base == 0 else vbt[2 + g]
            tz = ob_["tz"]
            zs_ = ob_["rz"]
            if True:
                for dl in range(2):
                    dma("sp", (lambda dl: lambda e: e.dma_start(out=tz[dl][:, 0:128], in_=tz_d[:, (dl * 8 + h) * 128:(dl * 8 + h + 1) * 128]))(dl),
                        writes=[tz[dl]])

            def step(kb):
                Tk, kk = kb // 4, kb % 4
                pz = ob_["pz"]
                at = ob_["at"]
                op("pe", lambda e: e.matmul(pz[:, 0:128], lhsT=kTt[g][base:base + 64, Tk, kk * 128:(kk + 1) * 128],
                                            rhs=qT[c][base:base + 64, jj * 128:(jj + 1) * 128], start=True, stop=True), [kTt[g], qT[c]], [pz])
                delta = qb - kb
                if delta >= 2:
                    op("act", lambda e: e.activation(out=at[:, 0:128], in_=pz[:, 0:128], func=AF.Exp, bias=b31_sb[:, h:h + 1]), [pz, b31_sb], [at])
                else:
                    lt = ob_["lt"]
                    op("dve", lambda e: e.tensor_tensor(out=lt[:, 0:128], in0=pz[:, 0:128], in1=tz[delta][:, 0:128], op=ALU.add), [pz, tz[delta]], [lt])
                    op("act", lambda e: e.activation(out=at[:, 0:128], in_=lt[:, 0:128], func=AF.Exp), [lt], [at])
                op("dve", lambda e: e.tensor_tensor(out=at[:, 0:128], in0=at[:, 0:128], in1=mT[:, kb, :], op=ALU.mult), [at, maskT], [at])
                op("pe", lambda e: e.matmul(po[:, 0:128], lhsT=vA[:, Tk, kk * 128:(kk + 1) * 128], rhs=at[:, 0:128], start=False, stop=(kb == qb)),
                   [vA, at], [po])

            def far_group(kb0, n):
                pz = ob_["pz"]
                at = ob_["at"]
                for i in range(n):
                    kb = kb0 + i
                    Tk, kk = kb // 4, kb % 4
                    op("pe", (lambda i, Tk, kk: lambda e: e.matmul(pz[:, i * 128:(i + 1) * 128], lhsT=kTt[g][base:base + 64, Tk, kk * 128:(kk + 1) * 128],
                                                                   rhs=qT[c][base:base + 64, jj * 128:(jj + 1) * 128], start=True, stop=True))(i, Tk, kk),
                       [kTt[g], qT[c]], [pz])
                op("act", lambda e: e.activation(out=at[:, 0:n * 128], in_=pz[:, 0:n * 128], func=AF.Exp, bias=b31_sb[:, h:h + 1]), [pz, b31_sb], [at])
                op("dve", lambda e: e.tensor_tensor(out=at[:, 0:n * 128].rearrange("p (b q) -> p b q", q=128),
                                                    in0=at[:, 0:n * 128].rearrange("p (b q) -> p b q", q=128),
                                                    in1=mT[:, kb0:kb0 + n, :], op=ALU.mult), [at, maskT], [at])
                for i in range(n):
                    kb = kb0 + i
                    Tk, kk = kb // 4, kb % 4
                    op("pe", (lambda i, Tk, kk: lambda e: e.matmul(po[:, 0:128], lhsT=vA[:, Tk, kk * 128:(kk + 1) * 128], rhs=at[:, i * 128:(i + 1) * 128],
                                                                   start=False, stop=False))(i, Tk, kk), [vA, at], [po])

            if DBG.get("farbatch", 1):
                nfar = max(0, qb - 1)
                for kb0 in range(0, nfar, 4):
                    far_group(kb0, min(4, nfar - kb0))
                    yield
                for kb in range(nfar, qb + 1):
                    step(kb)
                    yield
            else:
                for kb in range(qb + 1):
                    step(kb)
                    yield
            ob = 64 - base
            op("dve", lambda e: e.reciprocal(out=zs_[base:base + 64, 0:128], in_=po[ob:ob + 64, 0:128]), [po], [zs_])
            op("dve", lambda e: e.tensor_tensor(out=t32[c][base:base + 64, jj * 128:(jj + 1) * 128], in0=po[base:base + 64, 0:128],
                                                in1=zs_[base:base + 64, 0:128], op=ALU.mult), [po, zs_], [t32[c]])

        def odd_prompt_tile(layer, j, T):
            W = wio_d[j]
            xTl, _ = make_xT(T)
            for c in range(4):
                blk = load_w(W, 128 * c, 128)
                ps = nxt("A")
                fm_mm(ps, blk, 128, xTl, 512)
                op("dve", (lambda c, ps: lambda e: e.tensor_scalar(out=qT[c][:, :], in0=ps[:, :], scalar1=0.125, scalar2=None,
                                                                   op0=ALU.mult))(c, ps), [ps], [qT[c]])
            for g in range(2):
                blk = load_w(W, 0, 0, pieces=[(512 + 64 * g, 64, 0), (512 + 64 * g, 64, 64)])
                ps = nxt("A")
                fm_mm(ps, blk, 128, xTl, 512)
                op("dve", (lambda g, ps: lambda e: e.tensor_copy(out=kTt[g][:, T, :], in_=ps[:, :]))(g, ps), [ps], [kTt[g]])
            blk = load_w(W, 0, 0, pieces=[(896, 32, 0), (896, 32, 32), (896, 32, 64), (896, 32, 96)])
            ps = nxt("A")
            fm_mm(ps, blk, 128, xTl, 512)
            op("dve", (lambda ps: lambda e: e.tensor_copy(out=kTt[2][:, T, :], in_=ps[:, :]))(ps), [ps], [kTt[2]])
            blk = load_w(W, 768, 128)
            ps = nxt("A")
            fm_mm(ps, blk, 128, xTl, 512)
            for ih in range(4):
                op("dve", (lambda ih, ps: lambda e: e.tensor_scalar(out=vpg[ih // 2][:, ih % 2, :], in0=ps[:, :],
                                                                    scalar1=C["c_rowm"][:, ih:ih + 1], scalar2=None, op0=ALU.mult))(ih, ps),
                   [ps, C["c_rowm"]], [vpg[ih // 2]])
            pstm = [psA[0], psA[1], psY[0], psY[1]]
            for (c0, n, pc) in ((512, 128, 0), (640, 128, 128), (896, 36, 256)):
                blk = load_w(W, c0, n)
                for jj in range(4):
                    tm_mm(pstm[jj], pc, blk, n, xTl, jj * 128, 128)
            rowbuf = t32[5]
            for jj in range(4):
                tok0 = T * 512 + jj * 128
                op("dve", (lambda jj: lambda e: e.tensor_copy(out=rowbuf[:, 0:292], in_=pstm[jj][:, 0:292]))(jj), [pstm[jj]], [rowbuf])
                dma("pool", (lambda tok0: lambda e: e.dma_start(out=ckvp_d[j, tok0:tok0 + 128, :], in_=rowbuf[:, 0:256]))(tok0), reads=[rowbuf])
                dma("pool", (lambda tok0: lambda e: e.dma_start(out=ckip_d[j, tok0:tok0 + 128, :], in_=rowbuf[:, 256:288]))(tok0), reads=[rowbuf])
                for g in range(2):
                    op("dve", (lambda g, jj: lambda e: e.tensor_copy(out=vbt[g][:, T, jj * 128:jj * 128 + 64],
                                                                    in_=rowbuf[:, 128 + 64 * g:192 + 64 * g]))(g, jj), [rowbuf], [vbt[g]])
                    op("pool", (lambda g, jj: lambda e: e.tensor_copy(out=vbt[2 + g][:, T, jj * 128 + 64:jj * 128 + 128],
                                                                     in_=rowbuf[:, 128 + 64 * g:192 + 64 * g]))(g, jj), [rowbuf], [vbt[2 + g]])
                op("dve", (lambda jj: lambda e: e.tensor_scalar(out=bsp_bc[:, jj, 0:4], in0=rowbuf[:, 288:292], scalar1=ISQ, scalar2=None,
                                                                op0=ALU.mult))(jj), [rowbuf], [bsp_bc])
            idx, wk = big32[0], big32[1]
            if DBG.get("ostage", 9) >= 1:
                interleave([odd_select(T, 0, idx, wk)])
                for jj in range(4):
                    mT = odd_finish_mask(T, jj, wk)
                    gens = [odd_attend(T, jj, mT)]
                    if jj < 3:
                        gens.append(odd_select(T, jj + 1, idx, wk))
                    interleave(gens)
            else:
                for c in range(4):
                    op("pool", (lambda c: lambda e: e.memset(t32[c][:, :], 0.0))(c), [], [t32[c]])
            for c in range(4):
                blk = load_w(W, 932 + 128 * c, 128)
                ps = nxt("A")
                fm_mm(ps, blk, 128, xTl, 512)
                sg = t32[4]
                silu_from_psum(sg, ps, 0, 128, 512)
                op("dve", (lambda c: lambda e: e.tensor_tensor(out=hT[c][:, :], in0=t32[c][:, :], in1=sg[:, :], op=ALU.mult))(c),
                   [t32[c], sg], [hT[c]])
            Pv = kpg[0][:, :, :].rearrange("p a b -> p (a b)")
            Av = kpg[1][:, :, :].rearrange("p a b -> p (a b)")
            Bv = big32[1]
            for g in range(4):
                wwin = 2 ** (g + 1)
                blk = load_w(W, 1444 + 128 * g, 128)
                ps = nxt("A")
                fm_mm(ps, blk, 128, xTl, 512)
                op("dve", (lambda g: lambda e: e.tensor_copy(out=Pv[:, 0:16], in_=wsp_f[:, g, 0:16]))(g), [wsp_f], [kpg[0]])
                op("dve", (lambda ps: lambda e: e.tensor_copy(out=Pv[:, 16:528], in_=ps[:, :]))(ps), [ps], [kpg[0]])
                op("pool", (lambda g: lambda e: e.tensor_copy(out=wsp_f[:, g, 0:16], in_=Pv[:, 512:528]))(g), [kpg[0]], [wsp_f])
                op("dve", lambda e: e.tensor_tensor(out=Av[:, 1:528], in0=Pv[:, 1:528], in1=Pv[:, 0:527], op=ALU.add), [kpg[0]], [kpg[1]])
                Sv, Sobj = Av, kpg[1]
                if g >= 1:
                    op("dve", lambda e: e.tensor_tensor(out=Bv[:, 3:528], in0=Av[:, 3:528], in1=Av[:, 1:526], op=ALU.add), [kpg[1]], [big32[1]])
                    Sv, Sobj = Bv, big32[1]
                if g >= 2:
                    op("dve", lambda e: e.tensor_tensor(out=Av[:, 7:528], in0=Bv[:, 7:528], in1=Bv[:, 3:524], op=ALU.add), [big32[1]], [kpg[1]])
                    Sv, Sobj = Av, kpg[1]
                if g >= 3:
                    op("dve", lambda e: e.tensor_tensor(out=Bv[:, 15:528], in0=Av[:, 15:528], in1=Av[:, 7:520], op=ALU.add), [kpg[1]], [big32[1]])
                    Sv, Sobj = Bv, big32[1]
                pl = att_b[0]
                op("dve", (lambda Sv, wwin: lambda e: e.scalar_tensor_tensor(out=pl[:, :], in0=Sv[:, 16:528], scalar=1.0 / wwin, in1=Pv[:, 16:528],
                                                                             op0=ALU.mult, op1=ALU.subtract))(Sv, wwin), [Sobj, kpg[0]], [pl])
                if T == 0:
                    op("dve", (lambda Sv, g: lambda e: e.tensor_tensor(out=zs[:, 0:16], in0=Sv[:, 16:32], in1=C["c_icnt"][:, g * 16:(g + 1) * 16],
                                                                       op=ALU.mult))(Sv, g), [Sobj, C["c_icnt"]], [zs])
                    op("dve", lambda e: e.tensor_tensor(out=pl[:, 0:16], in0=zs[:, 0:16], in1=Pv[:, 16:32], op=ALU.subtract), [zs, kpg[0]], [pl])
                pm = nxt("A")
                op("pe", (lambda g, pm: lambda e: e.matmul(pm[:, :], lhsT=wspT[:, g, :], rhs=pl[:, :], start=True, stop=True))(g, pm), [wspT, pl], [pm])
                blk = load_w(W, 1956 + 128 * g, 128)
                ps = nxt("A")
                fm_mm(ps, blk, 128, xTl, 512)
                sg = t32[4]
                silu_from_psum(sg, ps, 0, 128, 512)
                op("dve", (lambda g, pm: lambda e: e.scalar_tensor_tensor(out=hT[4 + g][:, :], in0=pm[:, :], scalar=wsp00[:, g:g + 1], in1=sg[:, :],
                                                                          op0=ALU.mult, op1=ALU.mult))(g, pm), [pm, wsp00, sg], [hT[4 + g]])
            if T == 3:
                pp = nxt("A")
                for cb in range(4):
                    blk = load_w(W, 1444 + 128 * cb, 128)
                    tm_mm(pp, cb * 128, blk, 128, xTl, 384, 128)
                op("dve", (lambda pp: lambda e: e.tensor_copy(out=e32[0][:, :], in_=pp[:, :]))(pp), [pp], [e32[0]])
                dma("pool", lambda e: e.dma_start(out=dbp_d[j], in_=e32[0][113:128, :]), reads=[e32[0]])
            if dbgh_d is not None and T == DBG.get("dbgT", 0):
                for f in range(8):
                    dma("pool", (lambda f: lambda e: e.dma_start(out=dbgh_d[f], in_=hT[f][:, :]))(f), reads=[hT[f]])
            layer_norm_tile(layer, [(4 * T + jj, jj * 128, 128) for jj in range(4)], hT)

        def odd_sample_tile(layer, j):
            W = wio_d[j]
            xTl, _ = make_xT(4)
            qs, rows, sgc_unused, pp_, sgd = t32[0], t32[1], t32[2], t32[3], t32[4]
            ps = nxt("A")
            sample_inproj(W, 0, 512, xTl, ps)
            op("dve", (lambda ps: lambda e: e.tensor_scalar(out=qs[0:NS, :], in0=ps[0:NS, :], scalar1=0.125, scalar2=None, op0=ALU.mult))(ps), [ps], [qs])
            ps = nxt("A")
            sample_inproj(W, 512, 420, xTl, ps)
            op("dve", (lambda ps: lambda e: e.tensor_copy(out=rows[0:NS, 0:420], in_=ps[0:NS, 0:420]))(ps), [ps], [rows])
            ps = nxt("A")
            sample_inproj(W, 1444, 512, xTl, ps)
            op("dve", (lambda ps: lambda e: e.tensor_copy(out=pp_[0:NS, :], in_=ps[0:NS, :]))(ps), [ps], [pp_])
            ps = nxt("A")
            sample_inproj(W, 1956, 512, xTl, ps)
            silu_from_psum(sgd, ps, 0, NS, 512)
            for c in range(4):
                blk = load_w(W, 932 + 128 * c, 128)
                ps = nxt("A")
                fm_mm(ps, blk, 128, xTl, NS)
                op("act", (lambda c, ps: lambda e: e.activation(out=gbTs[:, c, :], in_=ps[:, 0:NS], func=AF.Sigmoid))(c, ps), [ps], [gbTs])
                op("dve", (lambda c, ps: lambda e: e.tensor_tensor(out=gbTs[:, c, :], in0=gbTs[:, c, :], in1=ps[:, 0:NS], op=ALU.mult))(c, ps),
                   [gbTs, ps], [gbTs])
            dma("pool", lambda e: e.dma_start(out=ckvs_d[j], in_=rows[0:NS, 0:256]), reads=[rows])
            dma("pool", lambda e: e.dma_start(out=ckis_d[j], in_=rows[0:NS, 384:416]), reads=[rows])
            dma("pool", lambda e: e.dma_start(out=dbs_d[j, :, 0:14, :], in_=db_d[j, :, 1:15, :]))
            dma("pool", lambda e: e.dma_start(out=dbs_d[j, :, 14, :], in_=pp_[0:NS, :]), reads=[pp_])
            ev1 = dma("pool", lambda e: e.dma_start(out=scr_kv[:, 0:256], in_=rows[0:NS, 0:256]), reads=[rows], writes=[scrkv_b])
            ev2 = dma("pool", lambda e: e.dma_start(out=scr_kv[:, 256:288], in_=rows[0:NS, 384:416]), reads=[rows], writes=[scrkv_b2])
            G = big32[1]
            offs = [0, 128, 512, 1408]
            rs = t32[5]
            for g in range(4):
                wwin = 2 ** (g + 1)
                nr = wwin - 1
                Gb = big32[1] if g < 3 else big32[0]
                o0 = offs[g] if g < 3 else 0
                dma("sp", (lambda g, nr, Gb, o0: lambda e: e.dma_start(
                    out=Gb[0:NS, o0:o0 + nr * 128].rearrange("p (r c) -> p r c", c=128),
                    in_=db_d[j, :, 15 - nr:15, g * 128:(g + 1) * 128]))(g, nr, Gb, o0), writes=[Gb])
                op("dve", (lambda g, nr, Gb, o0: lambda e: e.tensor_reduce(
                    out=rs[0:NS, g * 128:(g + 1) * 128], in_=Gb[0:NS, o0:o0 + nr * 128].rearrange("p (r c) -> p c r", c=128),
                    axis=AX.X, op=ALU.add))(g, nr, Gb, o0), [Gb], [rs])
                op("dve", (lambda g: lambda e: e.tensor_tensor(out=rs[0:NS, g * 128:(g + 1) * 128], in0=rs[0:NS, g * 128:(g + 1) * 128],
                                                               in1=pp_[0:NS, g * 128:(g + 1) * 128], op=ALU.add))(g), [rs, pp_], [rs])
                op("dve", (lambda g, wwin: lambda e: e.scalar_tensor_tensor(
                    out=rs[0:NS, g * 128:(g + 1) * 128], in0=rs[0:NS, g * 128:(g + 1) * 128], scalar=1.0 / wwin,
                    in1=pp_[0:NS, g * 128:(g + 1) * 128], op0=ALU.mult, op1=ALU.subtract))(g, wwin), [rs, pp_], [rs])
            pt_ = nxt("A")
            for g in range(4):
                op("pe", (lambda g, pt_: lambda e: e.transpose(out=pt_[:, g * 128:g * 128 + NS], in_=rs[0:NS, g * 128:(g + 1) * 128],
                                                               identity=C["c_ident"][0:NS, 0:NS]))(g, pt_), [rs, C["c_ident"]], [pt_])
            plT = att_b[1]
            op("dve", (lambda pt_: lambda e: e.tensor_copy(out=plT[:, :], in_=pt_[:, :]))(pt_), [pt_], [plT])
            pm = nxt("A")
            for g in range(4):
                op("pe", (lambda g, pm: lambda e: e.matmul(pm[0:NS, g * 128:(g + 1) * 128], lhsT=plT[:, g * 128:g * 128 + NS], rhs=wspT[:, g, :],
                                                           start=True, stop=True))(g, pm), [plT, wspT], [pm])
            dscb = t32[2]
            dma("sp", lambda e: e.dma_start(out=dscb[0:NS, :], in_=dsc_d[j:j + 1, :].to_broadcast([NS, 512])), writes=[dscb])
            od = rs
            op("dve", (lambda pm: lambda e: e.tensor_tensor(out=od[0:NS, :], in0=pm[0:NS, :], in1=dscb[0:NS, :], op=ALU.mult))(pm), [pm, dscb], [od])
            op("dve", lambda e: e.tensor_tensor(out=od[0:NS, :], in0=od[0:NS, :], in1=sgd[0:NS, :], op=ALU.mult), [od, sgd], [od])
            pt_ = nxt("A")
            for g in range(4):
                op("pe", (lambda g, pt_: lambda e: e.transpose(out=pt_[:, g * 128:g * 128 + NS], in_=od[0:NS, g * 128:(g + 1) * 128],
                                                               identity=C["c_ident"][0:NS, 0:NS]))(g, pt_), [od, C["c_ident"]], [pt_])
            for g in range(4):
                op("dve", (lambda g, pt_: lambda e: e.tensor_copy(out=hTs[4 + g][:, :], in_=pt_[:, g * 128:g * 128 + NS]))(g, pt_), [pt_], [hTs[4 + g]])
            bsT = sps
            dma("sp", lambda e: e.dma_start(out=bsT[:, :], in_=bs_d[:, 0:128]), writes=[bsT])
            dma("sp", lambda e: e.dma_start(out=tts[:, 0:8], in_=bs_d[:, 128:136]), writes=[tts])
            idxall = S32
            KI, KIs = e32[0], sp32[0]
            qib = t32[2]
            op("pool", lambda e: e.memset(KIs[:, 0:32], 0.0), [], [KIs])

            def bcast_rows(b, src, c0, n, dst):
                qm = t32[5]
                op("dve", lambda e: e.tensor_scalar(out=qm[0:NS, 0:n], in0=src[0:NS, c0:c0 + n], scalar1=C["c_ident"][0:NS, b:b + 1],
                                                    scalar2=-1.0, op0=ALU.mult, op1=ALU.mult), [src, C["c_ident"]], [qm])
                pq = nxt("A")
                op("pe", lambda e: e.matmul(pq[:, 0:n], lhsT=C["c_nones"][0:NS, :], rhs=qm[0:NS, 0:n], start=True, stop=True),
                   [C["c_nones"], qm], [pq])
                op("dve", lambda e: e.tensor_copy(out=dst[:, 0:n], in_=pq[:, 0:n]), [pq], [dst])

            idx, wk = big32[0], big32[1]
            ckh = ck_d.rearrange("(n r) e -> n (r e)", r=64)
            qexp = t32[2]
            idxp = zs
            sc_ = t32[5]
            op("dve", lambda e: e.tensor_scalar(out=rows[0:NS, 416:420], in0=rows[0:NS, 416:420], scalar1=ISQ, scalar2=None, op0=ALU.mult),
               [rows], [rows])
            for bt in range(2):
                dma("sp", (lambda bt: lambda e: e.dma_start(out=idx2[:, bt:bt + 1],
                                                            in_=pt_d.rearrange("o (p c) -> (o p) c", c=1)[128 * bt:128 * (bt + 1), :]))(bt),
                    writes=[idx2])
            op("dve", lambda e: e.tensor_scalar(out=idx2[:, 0:2], in0=idx2[:, 0:2], scalar1=2, scalar2=None, op0=ALU.mult), [idx2], [idx2])
            for bt in range(2):
                pq = nxt("A")
                op("pe", (lambda bt, pq: lambda e: e.matmul(pq[:, 0:164], lhsT=C["c_esel"][:, 128 * bt:128 * (bt + 1)], rhs=rows[0:NS, 256:420],
                                                            start=True, stop=True))(bt, pq), [C["c_esel"], rows], [pq])
                op("dve", (lambda pq: lambda e: e.tensor_copy(out=qexp[:, 0:164], in_=pq[:, 0:164]))(pq), [pq], [qexp])
                for half in range(2):
                    dma("pool", (lambda bt, half: lambda e: e.indirect_dma_start(
                        out=idx[:, 0:2048], out_offset=None, in_=ckh[:, 0:2048], element_offset=j * pool_rows * 32 + half * 2048,
                        in_offset=bass.IndirectOffsetOnAxis(ap=idx2[:, bt:bt + 1], axis=0)))(bt, half), reads=[idx2], writes=[idx])
                    for ih in range(4):
                        op("dve", (lambda ih: lambda e: e.tensor_tensor(
                            out=wk[:, 0:2048].rearrange("p (k e) -> p k e", e=32), in0=idx[:, 0:2048].rearrange("p (k e) -> p k e", e=32),
                            in1=qexp[:, ih * 32:(ih + 1) * 32].unsqueeze(1).to_broadcast([128, 64, 32]), op=ALU.mult))(ih), [idx, qexp], [wk])
                        op("dve", lambda e: e.tensor_reduce(out=sc_[:, 0:64], in_=wk[:, 0:2048].rearrange("p (k e) -> p k e", e=32), axis=AX.X,
                                                            op=ALU.add), [wk], [sc_])
                        op("dve", lambda e: e.tensor_scalar(out=sc_[:, 0:64], in0=sc_[:, 0:64], scalar1=0.0, scalar2=None, op0=ALU.max), [sc_], [sc_])
                        if ih == 0:
                            op("dve", (lambda half: lambda e: e.tensor_scalar(out=idxp[:, half * 64:(half + 1) * 64], in0=sc_[:, 0:64],
                                                                              scalar1=qexp[:, 160:161], scalar2=None, op0=ALU.mult))(half),
                               [sc_, qexp], [idxp])
                        else:
                            op("dve", (lambda half, ih: lambda e: e.scalar_tensor_tensor(
                                out=idxp[:, half * 64:(half + 1) * 64], in0=sc_[:, 0:64], scalar=qexp[:, 160 + ih:161 + ih],
                                in1=idxp[:, half * 64:(half + 1) * 64], op0=ALU.mult, op1=ALU.add))(half, ih), [sc_, qexp, idxp], [idxp])
                for b8 in range(8):
                    bb = 8 * bt + b8
                    dma("pool", (lambda b8, bb: lambda e: e.dma_start(
                        out=scr_idx[bb:bb + 1, 0:2048].rearrange("o (g k) -> (o g) k", k=128), in_=idxp[b8 * 16:(b8 + 1) * 16, :]))(b8, bb),
                        reads=[idxp], writes=[scri_b])
            sfp = t32[5]
            op("dve", lambda e: e.tensor_tensor(out=sfp[0:NS, 0:128].rearrange("p (i e) -> p i e", e=32),
                                                in0=rows[0:NS, 256:384].rearrange("p (i e) -> p i e", e=32),
                                                in1=rows[0:NS, 384:416].unsqueeze(1).to_broadcast([NS, 4, 32]), op=ALU.mult), [rows], [sfp])
            op("dve", lambda e: e.tensor_reduce(out=sfp[0:NS, 128:132], in_=sfp[0:NS, 0:128].rearrange("p (i e) -> p i e", e=32), axis=AX.X,
                                                op=ALU.add), [sfp], [sfp])
            op("dve", lambda e: e.tensor_scalar(out=sfp[0:NS, 128:132], in0=sfp[0:NS, 128:132], scalar1=0.0, scalar2=None, op0=ALU.max), [sfp], [sfp])
            op("dve", lambda e: e.tensor_tensor(out=sfp[0:NS, 128:132], in0=sfp[0:NS, 128:132], in1=rows[0:NS, 416:420], op=ALU.mult), [sfp, rows], [sfp])
            op("pool", lambda e: e.memset(sfp[0:NS, 256:384], NEG), [], [sfp])
            op("dve", lambda e: e.tensor_reduce(out=sfp[0:NS, 256:257], in_=sfp[0:NS, 128:132], axis=AX.X, op=ALU.add), [sfp], [sfp])
            dma("pool", lambda e: e.dma_start(out=scr_idx[:, 2048:2176], in_=sfp[0:NS, 256:384]), reads=[sfp], writes=[scri_b])
            dma("sp", lambda e: e.dma_start(out=idx[0:NS, 0:2176], in_=scr_idx), reads=[scri_b], writes=[idx])
            op("pool", lambda e: e.tensor_copy(out=wk[0:NS, 0:2176], in_=idx[0:NS, 0:2176]), [idx], [wk])
            mx = small[1]
            for r in range(32):
                op("dve", lambda e: e.max(out=mx[0:NS, 0:8], in_=wk[0:NS, 0:2176]), [wk], [mx])
                op("dve", lambda e: e.match_replace(out=wk[0:NS, 0:2176], in_to_replace=mx[0:NS, 0:8], in_values=wk[0:NS, 0:2176],
                                                    imm_value=2.0 * NEG), [wk, mx], [wk])
            op("dve", lambda e: e.tensor_tensor(out=wk[0:NS, 0:2176], in0=wk[0:NS, 0:2176], in1=idx[0:NS, 0:2176], op=ALU.not_equal), [wk, idx], [wk])
            dma("pool", lambda e: e.dma_start(out=scr_msk, in_=wk[0:NS, 0:2176]), reads=[wk], writes=[scrm_b])
            mS = idxall
            for (r0, nr) in ((0, 128), (128, 128), (256, 16)):
                tb = t32[5]
                dma("sp", (lambda r0, nr: lambda e: e.dma_start(out=tb[0:nr, 0:128],
                                                                in_=scr_msk.rearrange("b (g k) -> (b g) k", k=128)[r0:r0 + nr, :]))(r0, nr),
                    reads=[scrm_b], writes=[tb])
                pt_ = nxt("A")
                op("pe", (lambda nr, pt_: lambda e: e.transpose(out=pt_[:, 0:nr], in_=tb[0:nr, 0:128], identity=C["c_ident"][0:nr, 0:nr]))(nr, pt_),
                   [tb, C["c_ident"]], [pt_])
                op("dve", (lambda r0, nr, pt_: lambda e: e.tensor_copy(out=mS[:, r0:r0 + nr], in_=pt_[:, 0:nr]))(r0, nr, pt_), [pt_], [mS])
            pn, pd = psO[0], psO[1]
            op("pe", lambda e: e.matmul(pn[:, 0:128], lhsT=zero_b[:, :], rhs=xT[0][:, 0:128], start=True, stop=False), [zero_b, xT[0]], [pn])
            op("pe", lambda e: e.matmul(pd[:, 0:128], lhsT=zero_b[:, :], rhs=xT[0][:, 0:128], start=True, stop=False), [zero_b, xT[0]], [pd])
            Kself, Vself = qb_sb, atts
            op("pool", lambda e: e.memset(Kself[:, 0:128], 0.0), [], [Kself])
            op("pool", lambda e: e.memset(Vself[:, :], 0.0), [], [Vself])
            ones64 = vbt[0][:, 0, 64:128]

            def attend_sample(b):
                qbq = t32[2]
                bcast_rows(b, qs, 0, 512, qbq)
                Kb = [kpg[0][:, :, :].rearrange("p a b -> p (a b)"), kpg[1][:, :, :].rearrange("p a b -> p (a b)")]
                Vb = [vpg[0][:, :, :].rearrange("p a b -> p (a b)"), vpg[1][:, :, :].rearrange("p a b -> p (a b)")]
                for pg in range(NPG):
                    hb, i = pg // 8, pg % 8
                    stg_ = t32[3] if pg % 2 == 0 else t32[4]
                    dma("pool", (lambda pg, stg_: lambda e: e.indirect_dma_start(
                        out=stg_[:, 0:256], out_offset=None, in_=cc_d[:, 0:256], element_offset=j * pool_rows * 256,
                        in_offset=bass.IndirectOffsetOnAxis(ap=idxC[:, b * NPG + pg:b * NPG + pg + 1], axis=0)))(pg, stg_),
                        reads=[idxC], writes=[stg_])
                    op("dve", (lambda hb, i, stg_: lambda e: e.tensor_copy(out=Kb[hb][:, i * 128:(i + 1) * 128], in_=stg_[:, 0:128]))(hb, i, stg_),
                       [stg_], [kpg[hb]])
                    op("dve", (lambda hb, i, stg_: lambda e: e.tensor_copy(out=Vb[hb][:, i * 128:(i + 1) * 128], in_=stg_[:, 128:256]))(hb, i, stg_),
                       [stg_], [vpg[hb]])
                dma("sp", lambda e: e.dma_start(out=Kself[0:1, 0:128], in_=scr_kv[b:b + 1, 0:128]), reads=[scrkv_b], writes=[Kself])
                vst = small[0]
                dma("sp", lambda e: e.dma_start(out=sp32[0][0:1, 128:256], in_=scr_kv[b:b + 1, 128:256]), reads=[scrkv_b], writes=[sp32[0]])
                op("dve", lambda e: e.tensor_copy(out=Vself[0:1, :], in_=sp32[0][0:1, 128:256]), [sp32[0]], [Vself])
                L = zs
                Ls = small[2]
                prod = big32[1]
                qv = qbq[:, 0:512].rearrange("p (g r d) -> p g r d", g=2, r=4)
                for pq4 in range(4):
                    hb, i0 = pq4 // 2, (pq4 % 2) * 4
                    Kv = Kb[hb][:, i0 * 128:(i0 + 4) * 128].rearrange("p (a g d) -> p a g d", g=2, d=64)
                    for g in range(2):
                        op("dve", (lambda Kv, g: lambda e: e.tensor_tensor(
                            out=prod[:, g * 1024:(g + 1) * 1024].rearrange("p (a r d) -> p a r d", r=4, d=64),
                            in0=Kv[:, :, g, :].unsqueeze(2).to_broadcast([128, 4, 4, 64]),
                            in1=qv[:, g, :, :].unsqueeze(1).to_broadcast([128, 4, 4, 64]), op=ALU.mult))(Kv, g), [kpg[hb], qbq], [prod])
                    for g in range(2):
                        op("dve", (lambda pq4, g: lambda e: e.tensor_reduce(
                            out=L[:, pq4 * 32:(pq4 + 1) * 32].rearrange("p (a h) -> p a h", h=8)[:, :, g * 4:(g + 1) * 4],
                            in_=prod[:, g * 1024:(g + 1) * 1024].rearrange("p (a r d) -> p a r d", r=4, d=64), axis=AX.X, op=ALU.add))(pq4, g),
                           [prod], [L])
                Ksv = Kself[:, 0:128].rearrange("p (g d) -> p g d", d=64)
                op("dve", lambda e: e.tensor_tensor(out=prod[:, 0:512].rearrange("p (g r d) -> p g r d", g=2, r=4),
                                                    in0=Ksv.unsqueeze(2).to_broadcast([128, 2, 4, 64]), in1=qv, op=ALU.mult), [Kself, qbq], [prod])
                op("dve", lambda e: e.tensor_reduce(out=Ls[:, 0:8], in_=prod[:, 0:512].rearrange("p (h d) -> p h d", d=64), axis=AX.X, op=ALU.add),
                   [prod], [Ls])
                op("dve", lambda e: e.tensor_tensor(out=L[:, :], in0=L[:, :], in1=bsT[:, :], op=ALU.add), [L, bsT], [L])
                op("dve", lambda e: e.tensor_tensor(out=Ls[:, 0:8], in0=Ls[:, 0:8], in1=tts[:, 0:8], op=ALU.add), [Ls, tts], [Ls])
                op("act", lambda e: e.activation(out=L[:, :], in_=L[:, :], func=AF.Exp), [L], [L])
                op("act", lambda e: e.activation(out=Ls[:, 0:8], in_=Ls[:, 0:8], func=AF.Exp), [Ls], [Ls])
                pb = att_b[0]
                op("dve", lambda e: e.tensor_tensor(out=pb[:, 0:128].rearrange("p (a h) -> p a h", h=8), in0=L[:, :].rearrange("p (a h) -> p a h", h=8),
                                                    in1=mS[:, b * 17:b * 17 + 16].unsqueeze(2).to_broadcast([128, 16, 8]), op=ALU.mult), [L, mS], [pb])
                op("dve", lambda e: e.tensor_scalar(out=pb[:, 128:136], in0=Ls[:, 0:8], scalar1=mS[:, b * 17 + 16:b * 17 + 17], scalar2=None,
                                                    op0=ALU.mult), [Ls, mS], [pb])
                for pg in range(NPG):
                    hb, i = pg // 8, pg % 8
                    op("pe", (lambda pg, hb, i: lambda e: e.matmul(pn[:, b * 8:(b + 1) * 8], lhsT=Vb[hb][:, i * 128:(i + 1) * 128],
                                                                   rhs=pb[:, pg * 8:(pg + 1) * 8], start=False, stop=False))(pg, hb, i), [vpg[hb], pb], [pn])
                    op("pe", (lambda pg: lambda e: e.matmul(pd[0:64, b * 8:(b + 1) * 8], lhsT=ones64, rhs=pb[:, pg * 8:(pg + 1) * 8],
                                                            start=False, stop=False))(pg), [vbt[0], pb], [pd])
                op("pe", lambda e: e.matmul(pn[:, b * 8:(b + 1) * 8], lhsT=Vself[:, :], rhs=pb[:, 128:136], start=False, stop=True), [Vself, pb], [pn])
                op("pe", lambda e: e.matmul(pd[0:64, b * 8:(b + 1) * 8], lhsT=ones64, rhs=pb[:, 128:136], start=False, stop=True), [vbt[0], pb], [pd])

            for b in range(NS):
                attend_sample(b)
            rd = zs
            op("dve", lambda e: e.reciprocal(out=rd[0:64, 0:128], in_=pd[0:64, 0:128]), [pd], [rd])
            op("dve", lambda e: e.reciprocal(out=rd[64:128, 0:128], in_=pd[0:64, 0:128]), [pd], [rd])
            oT = t32[5]
            op("dve", lambda e: e.tensor_tensor(out=oT[:, 0:128], in0=pn[:, 0:128], in1=rd[:, 0:128], op=ALU.mult), [pn, rd], [oT])
            for h in range(8):
                g, c, hb_ = h // 4, h // 2, 64 * (h % 2)
                tmpo = t32[4]
                op("dve", (lambda h, g, c, hb_: lambda e: e.tensor_copy(
                    out=tmpo[hb_:hb_ + 64, 0:NS], in_=oT[64 * g:64 * g + 64, 0:128].rearrange("p (b h) -> p b h", h=8)[:, :, h]))(h, g, c, hb_),
                   [oT], [tmpo])
                op("dve", (lambda h, g, c, hb_: lambda e: e.tensor_tensor(
                    out=hTs[c][hb_:hb_ + 64, :], in0=tmpo[hb_:hb_ + 64, 0:NS],
                    in1=gbTs[hb_:hb_ + 64, c, :], op=ALU.mult))(h, g, c, hb_), [tmpo, gbTs], [hTs[c]])
            layer_norm_block(layer, 16, hTs, 0, NS)

        for layer in range(n_layers):
            j = layer // 2
            layer_setup(layer)
            if layer % 2 == 0:
                even_setup(j)
                for T in range(DBG.get("ntiles", 4)):
                    even_prompt_tile(layer, j, T)
                if DBG.get("sample", 1):
                    even_sample_tile(layer, j)
            else:
                odd_setup(j)
                for T in range(DBG.get("ntiles", 4)):
                    odd_prompt_tile(layer, j, T)
                if DBG.get("sample", 1):
                    odd_sample_tile(layer, j)

        for jb in range(16):
            dma("pool", (lambda jb: lambda e: e.dma_start(out=yp_d[jb * 128:(jb + 1) * 128, :], in_=xtok[jb][:, :]))(jb), reads=[xtok[jb]])
        dma("pool", lambda e: e.dma_start(out=ys_d, in_=xtok[16][0:NS, :]), reads=[xtok[16]])
        fw.emit()
    return nc


def _bias_tables(rel_bias):
    k = np.arange(128)
    d0 = np.maximum(k[None, :] - k[:, None], 0)
    d1 = 128 + k[None, :] - k[:, None]
    t = np.zeros((128, 2, 8, 128), np.float32)
    for h in range(8):
        t[:, 0, h, :] = rel_bias[_t5_bucket(d0), h]
        t[:, 1, h, :] = rel_bias[_t5_bucket(d1), h]
    toep = t.reshape(128, 16 * 128)
    b31 = np.ascontiguousarray(rel_bias[31:32, :]).astype(np.float32)
    bs = np.zeros((128, 17, 8), np.float32)
    for pg in range(16):
        dist = 2048 - (pg * 128 + k)
        bs[:, pg, :] = rel_bias[_t5_bucket(dist), :]
    bs[:, 16, :] = rel_bias[0:1, :]
    return toep, b31, bs.reshape(128, 17 * 8)


def _core_inputs(inp, c, consts, tabs, pool_views, pt_rows):
    d = {
        "xp": np.ascontiguousarray(inp["x_prompt"][c]),
        "xs": np.ascontiguousarray(inp["x_sample"][c * NS:(c + 1) * NS, 0, :]),
        "pt": np.ascontiguousarray(pt_rows.reshape(1, NS * NPG)).astype(np.int32),
        "cb": pool_views[0], "cc": pool_views[1], "ck": pool_views[2],
        "dbuf": np.ascontiguousarray(inp["state_d_buf"][:, c * NS:(c + 1) * NS]),
        "w_in_even": inp["w_in_even"], "w_in_odd": inp["w_in_odd"], "w_out": inp["w_out"],
        "ln_g": inp["ln_g"], "ln_b": inp["ln_b"], "a_w_sp": inp["a_w_sp"], "a_b_sp": inp["a_b_sp"],
        "d_w_grp": inp["d_w_grp"], "d_scale": inp["d_scale"],
        "t_toep": tabs[0], "t_b31": tabs[1], "t_bsamp": tabs[2],
    }
    d.update(consts)
    return d


def kernel(**inputs):
    inp = {k: np.asarray(v) for k, v in inputs.items()}
    n_pool = inp["cache_b_kv"].shape[1]
    nc = build(4, n_pool * 128)
    consts = _consts()
    tabs = _bias_tables(inp["rel_bias"].astype(np.float32))
    views = (inp["cache_b_kv"].reshape(2 * n_pool * 128, 1024), inp["cache_c_kv"].reshape(2 * n_pool * 128, 256),
             inp["cache_c_kidx"].reshape(2 * n_pool * 128, 32))
    in_maps = [_core_inputs(inp, c, consts, tabs, views, inp["page_table"][c * NS:(c + 1) * NS]) for c in range(NCORES)]
    res = run_bass_kernel_spmd(nc, in_maps, core_ids=list(range(NCORES))).results
    return _assemble(res)


def _assemble(res):
    n = len(res)
    f = np.float32
    y_p = np.stack([res[c]["y_p"] for c in range(n)]).astype(f)
    y_s = np.concatenate([res[c]["y_s"] for c in range(n)])[:, None, :].astype(f)
    bkv_p = np.stack([res[c]["bkv_p"] for c in range(n)], axis=1).reshape(2, n, S, 2, 8, 64)
    bkv_s = np.concatenate([res[c]["bkv_s"] for c in range(n)], axis=1).reshape(2, n * NS, 1, 2, 8, 64)
    av_s = np.concatenate([res[c]["av_s"] for c in range(n)], axis=1).reshape(2, n * NS, 1, 512)
    ckv_p = np.stack([res[c]["ckv_p"] for c in range(n)], axis=1).reshape(2, n, S, 2, 2, 64)
    ckv_s = np.concatenate([res[c]["ckv_s"] for c in range(n)], axis=1).reshape(2, n * NS, 1, 2, 2, 64)
    cki_p = np.stack([res[c]["cki_p"] for c in range(n)], axis=1).reshape(2, n, S, 32)
    cki_s = np.concatenate([res[c]["cki_s"] for c in range(n)], axis=1).reshape(2, n * NS, 1, 32)
    db_p = np.stack([res[c]["dbuf_p"] for c in range(n)], axis=1).reshape(2, n, 15, 512)
    db_s = np.concatenate([res[c]["dbuf_s"] for c in range(n)], axis=1).reshape(2, n * NS, 15, 512)
    return tuple(np.ascontiguousarray(a, dtype=f) for a in
                 (y_p, y_s, bkv_p, bkv_s, av_s, ckv_p, ckv_s, cki_p, cki_s, db_p, db_s))
```

```python
import math
import numpy as np
from contextlib import ExitStack
import concourse.bass as bass
import concourse.mybir as mybir
from concourse.bass_utils import run_bass_kernel_spmd

F32 = mybir.dt.float32
F32R = mybir.dt.float32r
BF16 = mybir.dt.bfloat16
I32 = mybir.dt.int32
AF = mybir.ActivationFunctionType
ALU = mybir.AluOpType
AX = mybir.AxisListType

ENGS = ("pe", "act", "dve", "pool", "sp")
DBG = {}
NCORES = 8
S = 2048
DM = 1024
NS = 16
NPG = 16
EVEN_IN = 3584
ODD_IN = 2468
DN_ALPHA = (2.0 * 4) ** 0.25
LN_EPS = 1e-5
NEG = -1.0e30


class Buf:
    __slots__ = ("t", "name", "last_w", "readers")

    def __init__(self, t, name):
        self.t = t
        self.name = name
        self.last_w = None
        self.readers = {}

    def __getitem__(self, idx):
        return self.t[idx]


class View:
    def __init__(self, parent, ap):
        self.parent = parent
        self.ap = ap
        self.name = parent.name + "_v"

    def __getitem__(self, idx):
        return self.ap[idx]

    @property
    def last_w(self):
        return self.parent.last_w

    @last_w.setter
    def last_w(self, v):
        self.parent.last_w = v

    @property
    def readers(self):
        return self.parent.readers

    @readers.setter
    def readers(self, v):
        self.parent.readers = v


def interleave(gens):
    gens = list(gens)
    while gens:
        for g in list(gens):
            try:
                next(g)
            except StopIteration:
                gens.remove(g)


class FW:
    def __init__(self, nc, stack, n_dma_sems=32):
        self.nc = nc
        self.stack = stack
        self.ops = {e: [] for e in ENGS}
        self.cnt = {e: 0 for e in ENGS}
        self.sems = {e: stack.enter_context(nc.semaphore("sem_" + e)) for e in ENGS}
        self.dma_sems = [stack.enter_context(nc.semaphore("sem_dma%d" % i)) for i in range(n_dma_sems)]
        self.dma_cnt = [0] * n_dma_sems
        self.dma_rr = 0
        self.seen = {e: {} for e in ENGS}
        self.nbuf = 0

    def sb(self, shape, dt, name=None):
        self.nbuf += 1
        name = (name or "sb") + "_%d" % self.nbuf
        t = self.stack.enter_context(self.nc.sbuf_tensor(name, list(shape), dt))
        return Buf(t, name)

    def ps(self, shape, dt=F32, name=None):
        self.nbuf += 1
        name = (name or "ps") + "_%d" % self.nbuf
        t = self.stack.enter_context(self.nc.psum_tensor(name, list(shape), dt))
        return Buf(t, name)

    def _collect(self, eng, reads, writes):
        waits = {}

        def add(ev):
            if ev is None:
                return
            k, v = ev
            if k == "pe" and eng == "pe":
                return
            if waits.get(k, 0) < v:
                waits[k] = v

        for b in reads:
            add(b.last_w)
        for b in writes:
            add(b.last_w)
            for k, v in b.readers.items():
                add((k, v))
        seen = self.seen[eng]
        out = []
        for k, v in waits.items():
            if seen.get(k, 0) >= v:
                continue
            seen[k] = v
            out.append((k, v))
        return out

    def _commit(self, ev, reads, writes):
        k, v = ev
        for b in reads:
            if b.readers.get(k, 0) < v:
                b.readers[k] = v
        for b in writes:
            b.last_w = ev
            b.readers = {}

    def op(self, eng, fn, reads=(), writes=()):
        waits = self._collect(eng, reads, writes)
        self.cnt[eng] += 1
        ev = (eng, self.cnt[eng])
        self.ops[eng].append((waits, fn, (eng, 1)))
        self._commit(ev, reads, writes)
        return ev

    def dma(self, eng, fn, reads=(), writes=()):
        i = self.dma_rr
        self.dma_rr = (self.dma_rr + 1) % len(self.dma_sems)
        key = "dma%d" % i
        waits = self._collect(eng, reads, writes)
        prev = self.dma_cnt[i] * 16
        if prev > 0 and self.seen[eng].get(key, 0) < prev:
            self.seen[eng][key] = prev
            waits.append((key, prev))
        self.dma_cnt[i] += 1
        ev = (key, self.dma_cnt[i] * 16)
        self.ops[eng].append((waits, fn, (key, 16)))
        self._commit(ev, reads, writes)
        return ev

    def _sem(self, key):
        if key in self.sems:
            return self.sems[key]
        return self.dma_sems[int(key[3:])]

    def emit(self):
        nc = self.nc
        with nc.Block() as block:
            def run(engname, e):
                for waits, fn, (ik, iv) in self.ops[engname]:
                    for k, v in waits:
                        e.wait_ge(self._sem(k), v)
                    fn(e).then_inc(self._sem(ik), iv)

            @block.tensor
            def _(e):
                run("pe", e)

            @block.scalar
            def _(e):
                run("act", e)

            @block.vector
            def _(e):
                run("dve", e)

            @block.gpsimd
            def _(e):
                run("pool", e)

            @block.sync
            def _(e):
                run("sp", e)
                for i, c in enumerate(self.dma_cnt):
                    if c:
                        e.wait_ge(self.dma_sems[i], c * 16)


def _consts():
    c = {}
    k = np.arange(128)
    c["c_ident"] = np.eye(128, dtype=np.float32)
    c["c_ntri"] = -(k[:, None] >= k[None, :]).astype(np.float32)
    c["c_nones"] = -np.ones((128, 128), np.float32)
    c["c_negm"] = np.where(k[:, None] >= k[None, :], -30000.0, 0.0).astype(np.float32)
    pg = k // 8
    h = k % 8
    c["c_m2"] = ((h[:, None] == h[None, :]) & (pg[:, None] > pg[None, :])).astype(np.float32)
    c["c_dsac"] = (k[None, :] >= k[:, None]).astype(np.float32)
    c["c_negq"] = np.where(k[None, :] > k[:, None], NEG, 0.0).astype(np.float32)
    esel = np.zeros((16, 2, 8, 16), np.float32)
    for b in range(16):
        esel[b, b // 8, b % 8, :] = 1.0
    c["c_esel"] = esel.reshape(16, 256)
    rowm = np.zeros((128, 4), np.float32)
    for ih in range(4):
        rowm[32 * ih:32 * ih + 32, ih] = 1.0
    c["c_rowm"] = rowm
    icnt = np.zeros((4, 16), np.float32)
    for g, w in enumerate((2, 4, 8, 16)):
        icnt[g] = 1.0 / np.minimum(np.arange(16) + 1, w)
    c["c_icnt"] = np.broadcast_to(icnt.reshape(1, 64), (128, 64)).copy()
    negrow = np.zeros((128, 1), np.float32)
    negrow[1:] = NEG
    c["c_negrow"] = negrow
    return c


def _t5_bucket(n):
    n = np.asarray(n)
    nf = np.maximum(n, 1).astype(np.float32)
    large = 16 + (np.log(nf / 16) / math.log(128 / 16) * 16).astype(np.int32)
    large = np.minimum(large, 31)
    return np.where(n < 16, n, large)


CONST_SHAPES = {"c_ident": [128, 128], "c_ntri": [128, 128], "c_nones": [128, 128], "c_negm": [128, 128],
                "c_m2": [128, 128], "c_dsac": [128, 128], "c_negq": [128, 128],
                "c_rowm": [128, 4], "c_esel": [16, 256], "c_icnt": [128, 64], "c_negrow": [128, 1]}


def build(n_layers=4, pool_rows=2560 * 128, debug=False):
    nc = bass.Bass("TRN2", target_bir_lowering=False)

    def din(name, shape, dt=F32):
        return nc.dram_tensor(name, list(shape), dt, kind="ExternalInput").ap()

    def dout(name, shape, dt=F32):
        return nc.dram_tensor(name, list(shape), dt, kind="ExternalOutput").ap()

    xp_d = din("xp", [S, DM])
    xs_d = din("xs", [NS, DM])
    pt_d = din("pt", [1, NS * NPG], I32)
    cb_d = din("cb", [2 * pool_rows, 1024])
    cc_d = din("cc", [2 * pool_rows, 256])
    ck_d = din("ck", [2 * pool_rows, 32])
    db_d = din("dbuf", [2, NS, 15, 512])
    wie_d = din("w_in_even", [2, DM, EVEN_IN])
    wio_d = din("w_in_odd", [2, DM, ODD_IN])
    wo_d = din("w_out", [4, DM, DM])
    lng_d = din("ln_g", [4, DM])
    lnb_d = din("ln_b", [4, DM])
    wsp_d = din("a_w_sp", [2, 4, 128, 128])
    bsp_d = din("a_b_sp", [2, 4, 128])
    wgrp_d = din("d_w_grp", [2, 4, 128, 128])
    dsc_d = din("d_scale", [2, 512])
    tz_d = din("t_toep", [128, 16 * 128])
    b31_d = din("t_b31", [1, 8])
    bs_d = din("t_bsamp", [128, 17 * 8])
    cst = {n: din(n, s) for n, s in CONST_SHAPES.items()}

    yp_d = dout("y_p", [S, DM])
    ys_d = dout("y_s", [NS, DM])
    bkvp_d = dout("bkv_p", [2, S, 1024])
    bkvs_d = dout("bkv_s", [2, NS, 1024])
    avs_d = dout("av_s", [2, NS, 512])
    ckvp_d = dout("ckv_p", [2, S, 256])
    ckvs_d = dout("ckv_s", [2, NS, 256])
    ckip_d = dout("cki_p", [2, S, 32])
    ckis_d = dout("cki_s", [2, NS, 32])
    dbp_d = dout("dbuf_p", [2, 15, 512])
    dbs_d = dout("dbuf_s", [2, NS, 15, 512])
    dbgh_d = dout("dbg_h", [8, 128, 512], BF16) if DBG.get("dbgh") else None
    scr_idx = nc.dram_tensor("scr_idx", [NS, 17 * 128], F32, kind="Internal").ap()
    scr_msk = nc.dram_tensor("scr_msk", [NS, 17 * 128], F32, kind="Internal").ap()
    scr_kv = nc.dram_tensor("scr_kv", [NS, 288], F32, kind="Internal").ap()

    with ExitStack() as st:
        fw = FW(nc, st)
        st.enter_context(nc.allow_non_contiguous_dma(reason="small strided parameter loads"))
        op, dma = fw.op, fw.dma

        C = {}
        for n, s in CONST_SHAPES.items():
            C[n] = fw.sb(s, F32, n)
            dma("sp", (lambda b, a: lambda e: e.dma_start(out=b[:], in_=a))(C[n], cst[n]), writes=[C[n]])
        ident_b = fw.sb([128, 128], BF16, "ident_b")
        negm_b = fw.sb([128, 128], BF16, "negm_b")
        op("dve", lambda e: e.tensor_copy(out=ident_b[:], in_=C["c_ident"][:]), [C["c_ident"]], [ident_b])
        op("dve", lambda e: e.tensor_copy(out=negm_b[:], in_=C["c_negm"][:]), [C["c_negm"]], [negm_b])
        ntri_r = fw.sb([128, 128], F32R, "ntri_r")
        nones_r = fw.sb([128, 128], F32R, "nones_r")
        op("dve", lambda e: e.tensor_copy(out=ntri_r[:], in_=C["c_ntri"][:]), [C["c_ntri"]], [ntri_r])
        op("dve", lambda e: e.tensor_copy(out=nones_r[:], in_=C["c_nones"][:]), [C["c_nones"]], [nones_r])
        zero_b = fw.sb([128, 128], BF16, "zero_b")
        op("pool", lambda e: e.memset(zero_b[:], 0.0), [], [zero_b])
        pts = fw.sb([128, NS * NPG], I32, "pts")
        dma("sp", lambda e: e.dma_start(out=pts[:], in_=pt_d.to_broadcast([128, NS * NPG])), writes=[pts])
        idx2 = fw.sb([128, 2], I32, "idx2")
        iota_p = fw.sb([128, 1], I32, "iota_p")
        op("pool", lambda e: e.iota(out=iota_p[:], pattern=[[0, 1]], base=0, channel_multiplier=1), [], [iota_p])
        idxC = pts
        idxB = fw.sb([128, NS * NPG], I32, "idxB")
        op("dve", lambda e: e.tensor_scalar(out=idxC[:], in0=pts[:], scalar1=128, scalar2=iota_p[:, 0:1], op0=ALU.mult, op1=ALU.add),
           [pts, iota_p], [idxC])
        op("dve", lambda e: e.tensor_scalar(out=idxB[:], in0=idxC[:], scalar1=2, scalar2=None, op0=ALU.mult), [idxC], [idxB])

        scrkv_b, scrkv_b2, scri_b, scrm_b = Buf(None, "scrkv"), Buf(None, "scrkv2"), Buf(None, "scri"), Buf(None, "scrm")
        xtok = [fw.sb([128, DM], F32, "xtok%d" % j) for j in range(17)]
        for j in range(16):
            dma("sp", (lambda j: lambda e: e.dma_start(out=xtok[j][:], in_=xp_d[j * 128:(j + 1) * 128, :]))(j),
                writes=[xtok[j]])
        op("pool", lambda e: e.memset(xtok[16][:], 0.0), [], [xtok[16]])
        dma("sp", lambda e: e.dma_start(out=xtok[16][0:NS, :], in_=xs_d), writes=[xtok[16]])

        xT = [fw.sb([128, 512], BF16, "xT%d" % c) for c in range(8)]
        hT = [fw.sb([128, 512], BF16, "hT%d" % c) for c in range(8)]
        lng_sb = fw.sb([128, DM], F32, "lng_sb")
        lnb_sb = fw.sb([128, DM], F32, "lnb_sb")
        wst = [fw.sb([128, 4, 128], F32, "wst%d" % i) for i in range(2)]
        wbf = [fw.sb([128, 8, 128], BF16, "wbf%d" % i) for i in range(3)]
        ring = {"st": 0, "bf": 0}
        psA = [fw.ps([128, 512], F32, "psA%d" % i) for i in range(2)]
        psZ = [fw.ps([128, 512], F32, "psZ%d" % i) for i in range(2)]
        psO = [fw.ps([128, 512], F32, "psO%d" % i) for i in range(2)]
        psY = [fw.ps([128, 512], F32, "psY%d" % i) for i in range(2)]
        rr = {"A": 0, "Z": 0, "O": 0}

        def nxt(kind):
            lst = {"A": psA, "Z": psZ, "O": psO}[kind]
            rr[kind] = (rr[kind] + 1) % len(lst)
            return lst[rr[kind]]

        kTt = [fw.sb([128, 4, 512], BF16, "kTt%d" % t) for t in range(4)]
        vbt = [fw.sb([128, 4, 512], BF16, "vbt%d" % t) for t in range(4)]
        qT = [fw.sb([128, 512], BF16, "qT%d" % c) for c in range(4)]
        t32 = [fw.sb([128, 512], F32, "t32_%d" % i) for i in range(6)]
        big32 = [fw.sb([128, 2304], F32, "big32_%d" % i) for i in range(2)]
        att_b = [fw.sb([128, 512], BF16, "att%d" % i) for i in range(2)]
        e32 = [fw.sb([128, 512], F32, "e32_%d" % i) for i in range(1)] * 2
        sp32 = [fw.sb([128, 512], F32, "sp32_%d" % i) for i in range(1)] * 2
        S32 = fw.sb([128, 512], F32, "S32")
        small = [fw.sb([128, 8], F32, "small%d" % i) for i in range(4)]
        if DBG.get("wstx", 0):
            for xb_ in (e32[0], sp32[0], S32):
                for hv in range(2):
                    wst.append(View(xb_, xb_[:, hv * 256:(hv + 1) * 256].rearrange("p (a b) -> p a b", a=2)))
        wspT = fw.sb([128, 4, 128], BF16, "wspT")
        wsp_f = fw.sb([128, 4, 128], F32, "wsp_f")
        bsp_bc = fw.sb([128, 4, 128], F32, "bsp_bc")
        wsp00 = fw.sb([128, 4], F32, "wsp00")
        xTs = [fw.sb([128, NS], BF16, "xTs%d" % c) for c in range(8)]
        hTs = [fw.sb([128, NS], BF16, "hTs%d" % c) for c in range(8)]
        kpg = [fw.sb([128, 2, 512], F32, "kpg%d" % i) for i in range(2)]
        vpg = [fw.sb([128, 2, 512], BF16, "vpg%d" % i) for i in range(2)]
        qb_sb = fw.sb([128, 512], F32, "qb_sb")
        zs = fw.sb([128, 128], F32, "zs")
        sps = fw.sb([128, 128], F32, "sps")
        tts = fw.sb([128, 128], F32, "tts")
        atts = fw.sb([128, 128], BF16, "atts")
        oBT = fw.sb([128, 4, NS], F32, "oBT")
        gbTs = fw.sb([128, 4, NS], F32, "gbTs")
        s16 = t32

        def load_w(wap, col0, ncols, pieces=None):
            blk = wbf[ring["bf"] % 3]
            ring["bf"] += 1
            if pieces is None:
                pieces = [(col0, ncols, 0)]
            n_tot = max(d + n for (_, n, d) in pieces)
            if DBG.get("wdma", 0):
                for (sc, n, dc) in pieces:
                    src = wap[:, sc:sc + n].rearrange("(k p) c -> p k c", p=128)
                    for kh in range(2):
                        dma("pool", (lambda kh, src, n, dc: lambda e: e.dma_start(out=blk[:, 4 * kh:4 * kh + 4, dc:dc + n],
                                                                                  in_=src[:, 4 * kh:4 * kh + 4, :]))(kh, src, n, dc),
                            writes=[blk])
                return blk
            for kq in range(2):
                stg = wst[ring["st"] % len(wst)]
                ring["st"] += 1
                for (sc, n, dc) in pieces:
                    src = wap[:, sc:sc + n].rearrange("(k p) c -> p k c", p=128)
                    dma("sp", (lambda kq, stg, src, n, dc: lambda e: e.dma_start(out=stg[:, :, dc:dc + n],
                                                                                 in_=src[:, 4 * kq:4 * kq + 4, :]))(kq, stg, src, n, dc),
                        writes=[stg])
                if DBG.get("castact") and kq % 2 == 1:
                    op("act", (lambda kq, stg: lambda e: e.activation(out=blk[:, 2 * kq:2 * kq + 2, 0:n_tot], in_=stg[:, :, 0:n_tot],
                                                                      func=AF.Copy))(kq, stg), [stg], [blk])
                else:
                    op("dve", (lambda kq, stg: lambda e: e.tensor_copy(out=blk[:, 4 * kq:4 * kq + 4, 0:n_tot], in_=stg[:, :, 0:n_tot]))(kq, stg),
                       [stg], [blk])
            return blk

        def fm_mm(ps, blk, ncols, xTl, ntok):
            for k in range(8):
                op("pe", (lambda k: lambda e: e.matmul(ps[0:ncols, 0:ntok], lhsT=blk[:, k, 0:ncols],
                                                       rhs=xTl[k][:, 0:ntok], start=(k == 0), stop=(k == 7)))(k),
                   [blk, xTl[k]], [ps])

        def tm_mm(ps, pcol, blk, ncols, xTl, tok0, ntok):
            for k in range(8):
                op("pe", (lambda k: lambda e: e.matmul(ps[0:ntok, pcol:pcol + ncols], lhsT=xTl[k][:, tok0:tok0 + ntok],
                                                       rhs=blk[:, k, 0:ncols], start=(k == 0), stop=(k == 7)))(k),
                   [blk, xTl[k]], [ps])

        def make_xT(T):
            if T < 4:
                for jj in range(4):
                    src = xtok[4 * T + jj]
                    for half in range(2):
                        ps = nxt("A")
                        for cc in range(4):
                            c = half * 4 + cc
                            op("pe", (lambda c, cc, ps, src: lambda e: e.transpose(
                                out=ps[:, cc * 128:(cc + 1) * 128], in_=src[:, c * 128:(c + 1) * 128],
                                identity=C["c_ident"][:]))(c, cc, ps, src), [src, C["c_ident"]], [ps])
                        for cc in range(4):
                            c = half * 4 + cc
                            if DBG.get("xv", 0) == 2:
                                continue
                            ev_eng = "dve" if (cc % 2 == 0 or True) else "act"
                            op(ev_eng,
                               (lambda c, cc, ps, jj, ev_eng: lambda e: (e.tensor_copy(out=xT[c][:, jj * 128:(jj + 1) * 128],
                                                                               in_=ps[:, cc * 128:(cc + 1) * 128])
                                                                 if ev_eng == "dve" else
                                                                 e.activation(out=xT[c][:, jj * 128:(jj + 1) * 128],
                                                                              in_=ps[:, cc * 128:(cc + 1) * 128],
                                                                              func=AF.Copy)))(c, cc, ps, jj, ev_eng),
                               [ps], [xT[c]])
                return xT, 512
            src = xtok[16]
            for half in range(2):
                ps = nxt("A")
                for cc in range(4):
                    c = half * 4 + cc
                    op("pe", (lambda c, cc, ps: lambda e: e.transpose(
                        out=ps[:, cc * 128:cc * 128 + NS], in_=src[0:NS, c * 128:(c + 1) * 128],
                        identity=C["c_ident"][0:NS, 0:NS]))(c, cc, ps), [src, C["c_ident"]], [ps])
                for cc in range(4):
                    c = half * 4 + cc
                    op("dve", (lambda c, cc, ps: lambda e: e.tensor_copy(out=xTs[c][:, :], in_=ps[:, cc * 128:cc * 128 + NS]))(c, cc, ps),
                       [ps], [xTs[c]])
            return xTs, NS

        def gelu_inplace(x, n, p=128):
            tmp = t32[5] if n <= 512 else big32[1]
            op("dve", lambda e: e.tensor_tensor(out=tmp[0:p, 0:n], in0=x[0:p, 0:n], in1=x[0:p, 0:n], op=ALU.mult), [x], [tmp])
            op("dve", lambda e: e.tensor_scalar(out=tmp[0:p, 0:n], in0=tmp[0:p, 0:n], scalar1=0.044715, scalar2=1.0,
                                                op0=ALU.mult, op1=ALU.add), [tmp], [tmp])
            op("dve", lambda e: e.tensor_tensor(out=tmp[0:p, 0:n], in0=tmp[0:p, 0:n], in1=x[0:p, 0:n], op=ALU.mult), [tmp, x], [tmp])
            op("act", lambda e: e.activation(out=tmp[0:p, 0:n], in_=tmp[0:p, 0:n], func=AF.Sigmoid, scale=1.5957691216057308),
               [tmp], [tmp])
            op("dve", lambda e: e.tensor_tensor(out=x[0:p, 0:n], in0=x[0:p, 0:n], in1=tmp[0:p, 0:n], op=ALU.mult), [x, tmp], [x])

        def silu_from_psum(dst, ps, p0, p1, n):
            op("act", lambda e: e.activation(out=dst[p0:p1, 0:n], in_=ps[p0:p1, 0:n], func=AF.Sigmoid), [ps], [dst])
            op("dve", lambda e: e.tensor_tensor(out=dst[p0:p1, 0:n], in0=dst[p0:p1, 0:n], in1=ps[p0:p1, 0:n], op=ALU.mult),
               [dst, ps], [dst])

        def layer_norm_block(layer, j, hTl, tok0, ntok):
            layer_norm_tile(layer, [(j, tok0, ntok)], hTl)

        def layer_norm_tile(layer, blocks, hTl):
            banks = [psA[0], psA[1], psY[0], psY[1]]
            rbuf = [(big32[bi // 2], (bi % 2) * 1024) for bi in range(4)]
            for half in range(2):
                for cbq in range(4):
                    cb = half * 4 + cbq
                    blk = load_w(wo_d[layer], cb * 128, 128)
                    for bi, (j, tok0, ntok) in enumerate(blocks):
                        py = banks[bi]
                        for f in range(8):
                            op("pe", (lambda f, py, cbq, blk, tok0, ntok: lambda e: e.matmul(
                                py[0:ntok, cbq * 128:(cbq + 1) * 128], lhsT=hTl[f][:, tok0:tok0 + ntok], rhs=blk[:, f, :],
                                start=(f == 0), stop=(f == 7)))(f, py, cbq, blk, tok0, ntok), [hTl[f], blk], [py])
                for bi, (j, tok0, ntok) in enumerate(blocks):
                    py = banks[bi]
                    xb = xtok[j]
                    rb, ro = rbuf[bi]
                    op("dve", (lambda py, half, xb, rb, ro, ntok: lambda e: e.scalar_tensor_tensor(
                        out=rb[0:ntok, ro + half * 512:ro + (half + 1) * 512], in0=xb[0:ntok, half * 512:(half + 1) * 512],
                        scalar=DN_ALPHA, in1=py[0:ntok, :], op0=ALU.mult, op1=ALU.add))(py, half, xb, rb, ro, ntok), [xb, py], [rb])
            for bi, (j, tok0, ntok) in enumerate(blocks):
                ln_finish(xtok[j], rbuf[bi][0], rbuf[bi][1], ntok)

        def ln_finish(xb, rb, ro, ntok):
            st_ = small[0]
            sq = t32[0]
            op("dve", lambda e: e.tensor_reduce(out=st_[0:ntok, 0:1], in_=rb[0:ntok, ro:ro + DM], axis=AX.X, op=ALU.add), [rb], [st_])
            for hf in range(2):
                op("pool", (lambda hf: lambda e: e.tensor_tensor(out=sq[0:ntok, 0:512], in0=rb[0:ntok, ro + hf * 512:ro + (hf + 1) * 512],
                                                                 in1=rb[0:ntok, ro + hf * 512:ro + (hf + 1) * 512], op=ALU.mult))(hf), [rb], [sq])
                op("dve", (lambda hf: lambda e: e.tensor_reduce(out=st_[0:ntok, 5 + hf:6 + hf], in_=sq[0:ntok, 0:512], axis=AX.X, op=ALU.add))(hf),
                   [sq], [st_])
            op("dve", lambda e: e.tensor_tensor(out=st_[0:ntok, 1:2], in0=st_[0:ntok, 5:6], in1=st_[0:ntok, 6:7], op=ALU.add), [st_], [st_])
            op("dve", lambda e: e.tensor_scalar(out=st_[0:ntok, 0:2], in0=st_[0:ntok, 0:2], scalar1=1.0 / DM, scalar2=None,
                                                op0=ALU.mult), [st_], [st_])
            op("dve", lambda e: e.tensor_tensor(out=st_[0:ntok, 2:3], in0=st_[0:ntok, 0:1], in1=st_[0:ntok, 0:1], op=ALU.mult), [st_], [st_])
            op("dve", lambda e: e.tensor_tensor(out=st_[0:ntok, 3:4], in0=st_[0:ntok, 1:2], in1=st_[0:ntok, 2:3], op=ALU.subtract), [st_], [st_])
            op("dve", lambda e: e.tensor_scalar(out=st_[0:ntok, 4:5], in0=st_[0:ntok, 3:4], scalar1=LN_EPS, scalar2=None, op0=ALU.add), [st_], [st_])
            op("act", lambda e: e.activation(out=st_[0:ntok, 4:5], in_=st_[0:ntok, 4:5], func=AF.Sqrt), [st_], [st_])
            op("dve", lambda e: e.reciprocal(out=st_[0:ntok, 4:5], in_=st_[0:ntok, 4:5]), [st_], [st_])
            op("dve", lambda e: e.tensor_scalar(out=rb[0:ntok, ro:ro + DM], in0=rb[0:ntok, ro:ro + DM], scalar1=st_[0:ntok, 0:1],
                                                scalar2=st_[0:ntok, 4:5], op0=ALU.subtract, op1=ALU.mult), [rb, st_], [rb])
            op("pool", lambda e: e.tensor_tensor(out=rb[0:ntok, ro:ro + DM], in0=rb[0:ntok, ro:ro + DM], in1=lng_sb[0:ntok, :], op=ALU.mult),
               [rb, lng_sb], [rb])
            op("dve", lambda e: e.tensor_tensor(out=xb[0:ntok, :], in0=rb[0:ntok, ro:ro + DM], in1=lnb_sb[0:ntok, :], op=ALU.add),
               [rb, lnb_sb], [xb])

        def layer_setup(layer):
            dma("sp", lambda e: e.dma_start(out=lng_sb[:], in_=lng_d[layer:layer + 1, :].to_broadcast([128, DM])), writes=[lng_sb])
            dma("sp", lambda e: e.dma_start(out=lnb_sb[:], in_=lnb_d[layer:layer + 1, :].to_broadcast([128, DM])), writes=[lnb_sb])

        mpos = C["c_m2"]
        def even_setup(j):
            dma("sp", lambda e: e.dma_start(out=wsp_f[:], in_=wsp_d[j].rearrange("g i k -> i g k")), writes=[wsp_f])
            for g in range(4):
                op("pool", (lambda g: lambda e: e.affine_select(out=wsp_f[:, g, :], in_=wsp_f[:, g, :], pattern=[[-1, 128]],
                                                                compare_op=ALU.is_ge, fill=0.0, base=0,
                                                                channel_multiplier=1))(g), [wsp_f], [wsp_f])
            ps = nxt("A")
            for g in range(4):
                op("pe", (lambda g: lambda e: e.transpose(out=ps[:, g * 128:(g + 1) * 128], in_=wsp_f[:, g, :],
                                                          identity=C["c_ident"][:]))(g), [wsp_f, C["c_ident"]], [ps])
            op("dve", lambda e: e.tensor_copy(out=wspT[:].rearrange("p g k -> p (g k)"), in_=ps[:, :]), [ps], [wspT])
            dma("sp", lambda e: e.dma_start(out=bsp_bc[:].rearrange("p g k -> p (g k)"),
                                            in_=bsp_d[j:j + 1].rearrange("o g k -> o (g k)").to_broadcast([128, 512])),
                writes=[bsp_bc])
            dma("sp", lambda e: e.dma_start(out=wsp00[:], in_=wsp_d[j, :, 0:1, 0:1].rearrange("g a b -> (a b) g").to_broadcast([128, 4])),
                writes=[wsp00])

        RR = (lambda ap: ap.bitcast(F32R)) if DBG.get("f32r", 0) else (lambda ap: ap)

        def stick_break_head(T, c, hh, po, hb):
            base = 64 * hh
            kbs = list(range(4 * T + 3, -1, -1))
            op("pe", lambda e: e.matmul(po[:, 0:512], lhsT=zero_b[:, :], rhs=xT[0][:, 0:512], start=True, stop=False), [zero_b, xT[0]], [po])
            def step(bi, kb):
                m = kb - 4 * T
                c0 = max(0, m) * 128
                Tk, kk = kb // 4, kb % 4
                pz, ee, sp, at, S32 = hb["pz"], hb["ee"], hb["sp"], hb["at"], hb["S"]
                ksrc = kTt[Tk]
                op("pe", lambda e: e.matmul(pz[:, c0:512], lhsT=ksrc[base:base + 64, c, kk * 128:(kk + 1) * 128],
                                            rhs=qT[c][base:base + 64, c0:512], start=True, stop=True), [ksrc, qT[c]], [pz])
                if m >= 0:
                    op("pe", lambda e: e.matmul(pz[:, c0:c0 + 128], lhsT=ident_b[:], rhs=negm_b[:], start=False, stop=True),
                       [ident_b, negm_b], [pz])
                op("act", lambda e: e.activation(out=ee[:, c0:512], in_=pz[:, c0:512], func=AF.Exp), [pz], [ee])
                op("act", lambda e: e.activation(out=RR(sp[:, c0:512]), in_=ee[:, c0:512], func=AF.Ln, bias=1.0), [ee], [sp])
                if DBG.get("f32r", 0):
                    op("pe", lambda e: e.matmul(pz[:, c0:512], lhsT=ntri_r[:], rhs=sp[:, c0:512].bitcast(F32R),
                                                start=False, stop=True), [ntri_r, sp], [pz])
                    if bi > 0:
                        op("pe", lambda e: e.matmul(pz[:, c0:512], lhsT=nones_r[:], rhs=S32[:, c0:512].bitcast(F32R),
                                                    start=False, stop=True), [nones_r, S32], [pz])
                else:
                    op("pe", lambda e: e.matmul(pz[:, c0:512], lhsT=C["c_ntri"][:], rhs=sp[:, c0:512], start=False, stop=True),
                       [C["c_ntri"], sp], [pz])
                    if bi > 0:
                        op("pe", lambda e: e.matmul(pz[:, c0:512], lhsT=C["c_nones"][:], rhs=S32[:, c0:512], start=False, stop=True),
                           [C["c_nones"], S32], [pz])
                op("act", lambda e: e.activation(out=at[:, c0:512], in_=pz[:, c0:512], func=AF.Exp), [pz], [at])
                if bi == 0:
                    if c0 > 0:
                        op("pool", lambda e: e.memset(S32[:, 0:c0], 0.0), [], [S32])
                    op("pool", lambda e: e.tensor_copy(out=RR(S32[:, c0:512]), in_=sp[:, c0:512]), [sp], [S32])
                elif bi < len(kbs) - 1:
                    op("pool", lambda e: e.tensor_tensor(out=RR(S32[:, c0:512]), in0=S32[:, c0:512], in1=sp[:, c0:512], op=ALU.add),
                       [S32, sp], [S32])
                vsrc = vbt[Tk]
                last = (kb == 0)
                if m >= 0:
                    op("pe", lambda e: e.matmul(po[:, c0:c0 + 128], lhsT=vsrc[:, kk, c * 128:(c + 1) * 128], rhs=at[:, c0:c0 + 128],
                                                start=False, stop=last), [vsrc, at], [po])
                    if c0 + 128 < 512:
                        op("pe", lambda e: e.matmul(po[:, c0 + 128:512], lhsT=vsrc[:, kk, c * 128:(c + 1) * 128],
                                                    rhs=at[:, c0 + 128:512], start=False, stop=last), [vsrc, at], [po])
                else:
                    op("pe", lambda e: e.matmul(po[:, 0:512], lhsT=vsrc[:, kk, c * 128:(c + 1) * 128], rhs=at[:, 0:512],
                                                start=False, stop=last), [vsrc, at], [po])

            for bi, kb in enumerate(kbs):
                step(bi, kb)
                yield

        def even_prompt_tile(layer, j, T):
            W = wie_d[j]
            t0 = T * 512
            if DBG.get("stage", 9) < 1:
                return
            xTl, ntok = make_xT(T)
            sub = DBG.get("sub", 9)
            if sub < 1:
                return
            for c in range(4):
                blk = load_w(W, 2048 + 128 * c, 128)
                if sub < 2:
                    continue
                ps = nxt("A")
                fm_mm(ps, blk, 128, xTl, 512)
                if sub < 3:
                    continue
                op("dve", (lambda c, ps: lambda e: e.tensor_copy(out=kTt[T][:, c, :], in_=ps[:, :]))(c, ps), [ps], [kTt[T]])
                if sub < 4:
                    continue
                ps2 = nxt("A")
                for jj in range(4):
                    tm_mm(ps2, jj * 128, blk, 128, xTl, jj * 128, 128)
                for hf in range(2):
                    op("dve", (lambda c, ps2, hf: lambda e: e.tensor_copy(
                        out=big32[hf][:, 0:2048].rearrange("p (j f) -> p j f", j=2)[:, :, 128 * c:128 * (c + 1)],
                        in_=ps2[:, hf * 256:(hf + 1) * 256].rearrange("p (j f) -> p j f", j=2)))(c, ps2, hf), [ps2], [big32[hf]])
            if sub < 5:
                return
            for c in range(4):
                blk = load_w(W, 2560 + 128 * c, 128)
                ps2 = nxt("A")
                for jj in range(4):
                    tm_mm(ps2, jj * 128, blk, 128, xTl, jj * 128, 128)
                for hf in range(2):
                    op("dve", (lambda c, ps2, hf: lambda e: e.tensor_copy(
                        out=big32[hf][:, 0:2048].rearrange("p (j f) -> p j f", j=2)[:, :, 512 + 128 * c:512 + 128 * (c + 1)],
                        in_=ps2[:, hf * 256:(hf + 1) * 256].rearrange("p (j f) -> p j f", j=2)))(c, ps2, hf), [ps2], [big32[hf]])
                op("dve", (lambda c, ps2: lambda e: e.tensor_copy(out=vbt[T][:, :, 128 * c:128 * (c + 1)],
                                                                  in_=ps2[:, :].rearrange("p (j f) -> p j f", j=4)))(c, ps2), [ps2], [vbt[T]])
            for hf in range(2):
                dma("pool", (lambda hf: lambda e: e.dma_start(
                    out=bkvp_d[j, t0 + hf * 256:t0 + (hf + 1) * 256, :].rearrange("(j p) f -> p j f", p=128),
                    in_=big32[hf][:, 0:2048].rearrange("p (j f) -> p j f", j=2)))(hf), reads=[big32[hf]])
            if DBG.get("stage", 9) < 2:
                return
            for c in range(4):
                blk = load_w(W, 1536 + 128 * c, 128)
                ps = nxt("A")
                fm_mm(ps, blk, 128, xTl, 512)
                op("dve", (lambda c, ps: lambda e: e.tensor_scalar(out=qT[c][:, :], in0=ps[:, :], scalar1=0.125, scalar2=None,
                                                                   op0=ALU.mult))(c, ps), [ps], [qT[c]])
            if DBG.get("stage", 9) < 3:
                return
            vraw = big32[0]
            for c in range(4):
                blk = load_w(W, 512 + 128 * c, 128)
                ps2 = nxt("A")
                for jj in range(4):
                    tm_mm(ps2, jj * 128, blk, 128, xTl, jj * 128, 128)
                op("dve", (lambda c, ps2: lambda e: e.tensor_copy(
                    out=vraw[:, 0:2048].rearrange("p (j f) -> p j f", j=4)[:, :, 128 * c:128 * (c + 1)],
                    in_=ps2[:, :].rearrange("p (j f) -> p j f", j=4)))(c, ps2), [ps2], [vraw])
            gelu_inplace(vraw, 2048)
            standardize(vraw, 4, 128)
            for jj in range(4):
                op("pool", (lambda jj: lambda e: e.tensor_copy(out=hT[4 + jj][:, :], in_=vraw[:, jj * 512:(jj + 1) * 512]))(jj),
                   [vraw], [hT[4 + jj]])
            for g in range(4):
                blk = load_w(W, 128 * g, 128)
                ps = nxt("A")
                fm_mm(ps, blk, 128, xTl, 512)
                ug = t32[0]
                op("dve", (lambda ps: lambda e: e.tensor_copy(out=ug[:, :], in_=ps[:, :]))(ps), [ps], [ug])
                gelu_inplace(ug, 512)
                blk = load_w(W, 1024 + 128 * g, 128)
                ps = nxt("A")
                fm_mm(ps, blk, 128, xTl, 512)
                sg = t32[1]
                silu_from_psum(sg, ps, 0, 128, 512)
                ps = nxt("A")
                for jj in range(4):
                    op("pe", (lambda g, jj, ps: lambda e: e.matmul(ps[:, jj * 128:(jj + 1) * 128], lhsT=hT[4 + jj][:, 128 * g:128 * (g + 1)],
                                                                   rhs=wspT[:, g, :], start=True, stop=True))(g, jj, ps),
                       [hT[4 + jj], wspT], [ps])
                sb_ = t32[2]
                op("dve", (lambda g, ps: lambda e: e.tensor_tensor(
                    out=sb_[:, :].rearrange("p (j i) -> p j i", j=4), in0=ps[:, :].rearrange("p (j i) -> p j i", j=4),
                    in1=bsp_bc[:, g:g + 1, :].to_broadcast([128, 4, 128]), op=ALU.add))(g, ps), [ps, bsp_bc], [sb_])
                op("pool", lambda e: e.tensor_tensor(out=sb_[:, :], in0=sb_[:, :], in1=ug[:, :], op=ALU.mult), [sb_, ug], [sb_])
                op("dve", (lambda g: lambda e: e.tensor_tensor(out=hT[g][:, :], in0=sb_[:, :], in1=sg[:, :], op=ALU.mult))(g),
                   [sb_, sg], [hT[g]])
            if DBG.get("stage", 9) < 4:
                return
            for c in range(4):
                blk = load_w(W, 3072 + 128 * c, 128)
                ps = nxt("A")
                fm_mm(ps, blk, 128, xTl, 512)
                sgb = t32[3]
                silu_from_psum(sgb, ps, 0, 128, 512)
                hbufs = [dict(pz=psZ[0], ee=e32[0], sp=sp32[0], at=att_b[0], S=S32),
                         dict(pz=psZ[1], ee=View(kpg[0], kpg[0][:, 0, :]), sp=View(kpg[1], kpg[1][:, 0, :]), at=att_b[1], S=qb_sb)]
                interleave([stick_break_head(T, c, hh, psO[hh], hbufs[hh]) for hh in range(2)])
                for hh in range(2):
                    po = psO[hh]
                    b0 = 64 * hh
                    op("dve", (lambda c, po, b0: lambda e: e.tensor_tensor(out=hT[4 + c][b0:b0 + 64, :], in0=po[b0:b0 + 64, :],
                                                                           in1=sgb[b0:b0 + 64, :], op=ALU.mult))(c, po, b0),
                       [po, sgb], [hT[4 + c]])
            if dbgh_d is not None and T == DBG.get("dbgT", 0) and DBG.get("dbgL", 0) == 0:
                for f in range(8):
                    dma("pool", (lambda f: lambda e: e.dma_start(out=dbgh_d[f], in_=hT[f][:, :]))(f), reads=[hT[f]])
            if DBG.get("stage", 9) < 5:
                return
            layer_norm_tile(layer, [(4 * T + jj, jj * 128, 128) for jj in range(4)], hT)

        def standardize(x, nj, p):
            sq = big32[1]
            n = nj * 512
            st_ = small[1]
            xv = x[0:p, 0:n].rearrange("p (j f) -> p j f", j=nj)
            op("dve", lambda e: e.tensor_reduce(out=st_[0:p, 0:nj], in_=xv, axis=AX.X, op=ALU.add), [x], [st_])
            op("pool", lambda e: e.tensor_tensor(out=sq[0:p, 0:n], in0=x[0:p, 0:n], in1=x[0:p, 0:n], op=ALU.mult), [x], [sq])
            st2 = small[2]
            op("dve", lambda e: e.tensor_reduce(out=st2[0:p, 0:nj], in_=sq[0:p, 0:n].rearrange("p (j f) -> p j f", j=nj),
                                                axis=AX.X, op=ALU.add), [sq], [st2])
            op("dve", lambda e: e.tensor_scalar(out=st_[0:p, 0:nj], in0=st_[0:p, 0:nj], scalar1=1.0 / 512, scalar2=None, op0=ALU.mult),
               [st_], [st_])
            op("dve", lambda e: e.tensor_scalar(out=st2[0:p, 0:nj], in0=st2[0:p, 0:nj], scalar1=1.0 / 512, scalar2=None, op0=ALU.mult),
               [st2], [st2])
            st3 = small[3]
            op("dve", lambda e: e.tensor_tensor(out=st3[0:p, 0:nj], in0=st_[0:p, 0:nj], in1=st_[0:p, 0:nj], op=ALU.mult), [st_], [st3])
            op("dve", lambda e: e.tensor_tensor(out=st2[0:p, 0:nj], in0=st2[0:p, 0:nj], in1=st3[0:p, 0:nj], op=ALU.subtract),
               [st2, st3], [st2])
            op("dve", lambda e: e.tensor_scalar(out=st2[0:p, 0:nj], in0=st2[0:p, 0:nj], scalar1=LN_EPS, scalar2=None, op0=ALU.add), [st2], [st2])
            op("act", lambda e: e.activation(out=st2[0:p, 0:nj], in_=st2[0:p, 0:nj], func=AF.Sqrt), [st2], [st2])
            op("dve", lambda e: e.reciprocal(out=st2[0:p, 0:nj], in_=st2[0:p, 0:nj]), [st2], [st2])
            for jx in range(nj):
                op("dve", (lambda jx: lambda e: e.tensor_scalar(out=x[0:p, jx * 512:(jx + 1) * 512], in0=x[0:p, jx * 512:(jx + 1) * 512],
                                                                scalar1=st_[0:p, jx:jx + 1], scalar2=st2[0:p, jx:jx + 1],
                                                                op0=ALU.subtract, op1=ALU.mult))(jx), [x, st_, st2], [x])

        def sample_inproj(W, col0, ncols, xTl, ps):
            c0 = 0
            while c0 < ncols:
                n = min(128, ncols - c0)
                blk = load_w(W, col0 + c0, n)
                tm_mm(ps, c0, blk, n, xTl, 0, NS)
                c0 += n

        def page_val(e, b, pg):
            return e.value_load(pts[0:1, b * NPG + pg:b * NPG + pg + 1])

        def even_sample_tile(layer, j):
            W = wie_d[j]
            xTl, ntok = make_xT(4)
            for c in range(4):
                blk = load_w(W, 3072 + 128 * c, 128)
                ps = nxt("A")
                fm_mm(ps, blk, 128, xTl, NS)
                op("act", (lambda c, ps: lambda e: e.activation(out=gbTs[:, c, :], in_=ps[:, 0:NS], func=AF.Sigmoid))(c, ps), [ps], [gbTs])
                op("dve", (lambda c, ps: lambda e: e.tensor_tensor(out=gbTs[:, c, :], in0=gbTs[:, c, :], in1=ps[:, 0:NS], op=ALU.mult))(c, ps),
                   [gbTs, ps], [gbTs])
            for hf in range(2):
                psk = nxt("A")
                sample_inproj(W, 2048 + 512 * hf, 512, xTl, psk)
                op("dve", (lambda hf, psk: lambda e: e.tensor_copy(out=big32[0][0:NS, hf * 512:(hf + 1) * 512], in_=psk[0:NS, :]))(hf, psk),
                   [psk], [big32[0]])
            dma("pool", lambda e: e.dma_start(out=bkvs_d[j], in_=big32[0][0:NS, 0:1024]), reads=[big32[0]])
            v = s16[0]
            psv = nxt("A")
            sample_inproj(W, 512, 512, xTl, psv)
            op("dve", lambda e: e.tensor_copy(out=v[0:NS, :], in_=psv[0:NS, :]), [psv], [v])
            gelu_inplace(v, 512, NS)
            standardize(v, 1, NS)
            dma("pool", lambda e: e.dma_start(out=avs_d[j], in_=v[0:NS, :]), reads=[v])
            u = s16[1]
            psu = nxt("A")
            sample_inproj(W, 0, 512, xTl, psu)
            op("dve", lambda e: e.tensor_copy(out=u[0:NS, :], in_=psu[0:NS, :]), [psu], [u])
            gelu_inplace(u, 512, NS)
            ga = s16[2]
            psg = nxt("A")
            sample_inproj(W, 1024, 512, xTl, psg)
            silu_from_psum(ga, psg, 0, NS, 512)
            sv = s16[3]
            op("dve", lambda e: e.tensor_tensor(out=sv[0:NS, :].rearrange("p (g f) -> p g f", g=4),
                                                in0=v[0:NS, :].rearrange("p (g f) -> p g f", g=4),
                                                in1=wsp00[0:NS, :].unsqueeze(2).to_broadcast([NS, 4, 128]), op=ALU.mult), [v, wsp00], [sv])
            op("dve", lambda e: e.tensor_tensor(out=sv[0:NS, :].rearrange("p (g f) -> p g f", g=4),
                                                in0=sv[0:NS, :].rearrange("p (g f) -> p g f", g=4),
                                                in1=bsp_bc[0:NS, :, 0:1].to_broadcast([NS, 4, 128]), op=ALU.add), [sv, bsp_bc], [sv])
            op("dve", lambda e: e.tensor_tensor(out=sv[0:NS, :], in0=sv[0:NS, :], in1=u[0:NS, :], op=ALU.mult), [sv, u], [sv])
            op("dve", lambda e: e.tensor_tensor(out=sv[0:NS, :], in0=sv[0:NS, :], in1=ga[0:NS, :], op=ALU.mult), [sv, ga], [sv])
            ps = nxt("A")
            for g in range(4):
                op("pe", (lambda g, ps: lambda e: e.transpose(out=ps[:, g * 128:g * 128 + NS], in_=sv[0:NS, g * 128:(g + 1) * 128],
                                                              identity=C["c_ident"][0:NS, 0:NS]))(g, ps), [sv, C["c_ident"]], [ps])
            for g in range(4):
                op("dve", (lambda g, ps: lambda e: e.tensor_copy(out=hTs[g][:, :], in_=ps[:, g * 128:g * 128 + NS]))(g, ps), [ps], [hTs[g]])
            qsc = s16[4]
            psq = nxt("A")
            sample_inproj(W, 1536, 512, xTl, psq)
            op("dve", lambda e: e.tensor_scalar(out=qsc[0:NS, :], in0=psq[0:NS, :], scalar1=0.125, scalar2=None, op0=ALU.mult), [psq], [qsc])
            VB = [View(big32[i], big32[i][:, 0:2304].bitcast(BF16)) for i in range(2)]
            pso = psO[0]
            op("pe", lambda e: e.matmul(pso[:, 0:512], lhsT=zero_b[:, :], rhs=xT[0][:, 0:512], start=True, stop=False), [zero_b, xT[0]], [pso])
            for b in range(NS):
                pq = nxt("A")
                qm = t32[5]
                op("dve", (lambda b: lambda e: e.tensor_scalar(out=qm[0:NS, :], in0=qsc[0:NS, :], scalar1=C["c_ident"][0:NS, b:b + 1],
                                                               scalar2=-1.0, op0=ALU.mult, op1=ALU.mult))(b), [qsc, C["c_ident"]], [qm])
                op("pe", (lambda b, pq: lambda e: e.matmul(pq[:, :], lhsT=C["c_nones"][0:NS, :], rhs=qm[0:NS, :],
                                                           start=True, stop=True))(b, pq), [C["c_nones"], qm], [pq])
                op("dve", (lambda pq: lambda e: e.tensor_copy(out=qb_sb[:, :], in_=pq[:, :]))(pq), [pq], [qb_sb])
                for pg in range(NPG):
                    kb_ = kpg[pg % 2]
                    dma("pool", (lambda b, pg, kb_: lambda e: e.indirect_dma_start(
                        out=kb_[:, :, :].rearrange("p a f -> p (a f)"), out_offset=None, in_=cb_d[:, 0:1024], element_offset=j * pool_rows * 1024,
                        in_offset=bass.IndirectOffsetOnAxis(ap=idxC[:, b * NPG + pg:b * NPG + pg + 1], axis=0)))(b, pg, kb_),
                        reads=[idxC], writes=[kb_])
                    prod = t32[0]
                    op("dve", (lambda kb_: lambda e: e.tensor_tensor(out=prod[:, 0:512], in0=kb_[:, 0, :], in1=qb_sb[:, :], op=ALU.mult))(kb_),
                       [kb_, qb_sb], [prod])
                    op("dve", (lambda pg: lambda e: e.tensor_reduce(
                        out=zs[:, pg * 8:(pg + 1) * 8], in_=prod[:, 0:512].rearrange("p (a d) -> p a d", d=64), axis=AX.X,
                        op=ALU.add))(pg), [prod], [zs])
                    vdst = VB[0 if pg < 9 else 1]
                    vo = (pg if pg < 9 else pg - 9) * 512
                    op("dve", (lambda kb_, vdst, vo: lambda e: e.tensor_copy(out=vdst[:, vo:vo + 512], in_=kb_[:, 1, :]))(kb_, vdst, vo),
                       [kb_], [vdst])
                op("act", lambda e: e.activation(out=sps[:, :], in_=zs[:, :], func=AF.Exp), [zs], [sps])
                op("act", lambda e: e.activation(out=sps[:, :], in_=sps[:, :], func=AF.Ln, bias=1.0), [sps], [sps])
                pt_ = nxt("Z")
                op("pe", (lambda pt_: lambda e: e.matmul(pt_[:, 0:128], lhsT=sps[:, :], rhs=C["c_nones"][:, :], start=True, stop=True))(pt_),
                   [sps, C["c_nones"]], [pt_])
                op("dve", (lambda pt_: lambda e: e.tensor_copy(out=tts[:, :], in_=pt_[:, 0:128]))(pt_), [pt_], [tts])
                pz = nxt("Z")
                op("pe", (lambda pz: lambda e: e.matmul(pz[:, 0:128], lhsT=C["c_ident"][:, :], rhs=zs[:, :], start=True, stop=True))(pz),
                   [C["c_ident"], zs], [pz])
                op("pe", (lambda pz: lambda e: e.matmul(pz[:, 0:128], lhsT=C["c_ntri"][:, :], rhs=sps[:, :], start=False, stop=True))(pz),
                   [C["c_ntri"], sps], [pz])
                op("pe", (lambda pz: lambda e: e.matmul(pz[:, 0:128], lhsT=tts[:, :], rhs=mpos[:, :], start=False, stop=True))(pz),
                   [tts, mpos], [pz])
                op("act", (lambda pz: lambda e: e.activation(out=atts[:, :], in_=pz[:, 0:128], func=AF.Exp))(pz), [pz], [atts])
                for pg in range(NPG):
                    vsrc_ = VB[0 if pg < 9 else 1]
                    vo = (pg if pg < 9 else pg - 9) * 512
                    for c in range(4):
                        op("pe", (lambda b, pg, c, vsrc_, vo: lambda e: e.matmul(
                            pso[:, c * 128 + b * 8:c * 128 + b * 8 + 8], lhsT=vsrc_[:, vo + c * 128:vo + (c + 1) * 128],
                            rhs=atts[:, pg * 8:(pg + 1) * 8], start=False, stop=(pg == 15)))(b, pg, c, vsrc_, vo),
                           [vsrc_, atts], [pso])
            for c in range(4):
                src = pso[:, c * 128:(c + 1) * 128].rearrange("p (b h) -> p b h", h=8)
                op("dve", (lambda c, src: lambda e: e.tensor_copy(out=oBT[0:64, c, :], in_=src[0:64, :, 2 * c]))(c, src), [pso], [oBT])
                op("dve", (lambda c, src: lambda e: e.tensor_copy(out=oBT[64:128, c, :], in_=src[64:128, :, 2 * c + 1]))(c, src), [pso], [oBT])
            for c in range(4):
                op("dve", (lambda c: lambda e: e.tensor_tensor(out=hTs[4 + c][:, :], in0=oBT[:, c, :], in1=gbTs[:, c, :], op=ALU.mult))(c),
                   [oBT, gbTs], [hTs[4 + c]])
            layer_norm_block(layer, 16, hTs, 0, NS)

        mpos = C["c_m2"]


        ISQ = 1.0 / math.sqrt(32.0)
        maskT = kTt[3]
        b31_sb = small[3]

        def odd_setup(j):
            for i in range(4):
                op("pool", (lambda i: lambda e: e.memset(vbt[i][:, :, :], 1.0))(i), [], [vbt[i]])
            dma("sp", lambda e: e.dma_start(out=qb_sb[:, :].rearrange("p (g d) -> p g d", g=4), in_=wgrp_d[j].rearrange("g c d -> c g d")),
                writes=[qb_sb])
            op("dve", lambda e: e.tensor_copy(out=wspT[:].rearrange("p g d -> p (g d)"), in_=qb_sb[:, :]), [qb_sb], [wspT])
            dma("sp", lambda e: e.dma_start(out=wsp00[:, :], in_=dsc_d[j:j + 1, :].rearrange("o (g p) -> p (o g)", p=128)), writes=[wsp00])
            dma("sp", lambda e: e.dma_start(out=b31_sb[:, 0:8], in_=b31_d.to_broadcast([128, 8])), writes=[b31_sb])
            op("pool", lambda e: e.memset(wsp_f[:, :, :], 0.0), [], [wsp_f])

        def odd_select(T, jj, idx, wk):
            qb = 4 * T + jj
            Wk = (qb + 1) * 128
            wq = bsp_bc
            if qb < 2:
                return
            nkc = (Wk + 511) // 512
            for kc in range(nkc):
                n = min(512, Wk - kc * 512)
                for ih in range(4):
                    pz = nxt("A")
                    op("pe", (lambda kc, n, ih, pz: lambda e: e.matmul(pz[:, 0:n], lhsT=vpg[ih // 2][:, ih % 2, jj * 128:(jj + 1) * 128],
                                                                       rhs=kTt[2][:, kc, 0:n], start=True, stop=True))(kc, n, ih, pz),
                       [vpg[ih // 2], kTt[2]], [pz])
                    rl = e32[0]
                    op("act", (lambda n, pz: lambda e: e.activation(out=rl[:, 0:n], in_=pz[:, 0:n], func=AF.Relu))(n, pz), [pz], [rl])
                    if ih == 0:
                        op("dve", (lambda kc, n: lambda e: e.tensor_scalar(out=idx[:, kc * 512:kc * 512 + n], in0=rl[:, 0:n],
                                                                           scalar1=wq[:, jj, 0:1], scalar2=None, op0=ALU.mult))(kc, n),
                           [rl, wq], [idx])
                    else:
                        op("dve", (lambda kc, n, ih: lambda e: e.scalar_tensor_tensor(
                            out=idx[:, kc * 512:kc * 512 + n], in0=rl[:, 0:n], scalar=wq[:, jj, ih:ih + 1],
                            in1=idx[:, kc * 512:kc * 512 + n], op0=ALU.mult, op1=ALU.add))(kc, n, ih), [rl, wq, idx], [idx])
                yield
            op("dve", lambda e: e.tensor_tensor(out=idx[:, qb * 128:Wk], in0=idx[:, qb * 128:Wk], in1=C["c_negq"][:, :], op=ALU.add),
               [idx, C["c_negq"]], [idx])
            op("pool", lambda e: e.tensor_copy(out=wk[:, 0:Wk], in_=idx[:, 0:Wk]), [idx], [wk])
            mx = small[1]
            for r in range(32):
                op("dve", lambda e: e.max(out=mx[:, 0:8], in_=wk[:, 0:Wk]), [wk], [mx])
                op("dve", lambda e: e.match_replace(out=wk[:, 0:Wk], in_to_replace=mx[:, 0:8], in_values=wk[:, 0:Wk], imm_value=2.0 * NEG),
                   [wk, mx], [wk])
                yield
            op("dve", lambda e: e.tensor_tensor(out=wk[:, 0:Wk], in0=wk[:, 0:Wk], in1=idx[:, 0:Wk], op=ALU.not_equal), [wk, idx], [wk])

        def odd_finish_mask(T, jj, wk):
            qb = 4 * T + jj
            mT = maskT[:, :, :].rearrange("p a (b q) -> p (a b) q", q=128)
            if qb < 2:
                for kb in range(qb):
                    op("pool", (lambda kb: lambda e: e.memset(mT[:, kb, :], 1.0))(kb), [], [maskT])
                op("dve", lambda e: e.tensor_copy(out=mT[:, qb, :], in_=C["c_dsac"][:, :]), [C["c_dsac"]], [maskT])
            else:
                for k4 in range(0, qb + 1, 4):
                    nb = min(4, qb + 1 - k4)
                    pt_ = nxt("A")
                    for i in range(nb):
                        op("pe", (lambda k4, i, pt_: lambda e: e.transpose(out=pt_[:, i * 128:(i + 1) * 128],
                                                                           in_=wk[:, (k4 + i) * 128:(k4 + i + 1) * 128],
                                                                           identity=C["c_ident"][:]))(k4, i, pt_), [wk, C["c_ident"]], [pt_])
                    op("dve", (lambda k4, nb, pt_: lambda e: e.tensor_copy(out=mT[:, k4:k4 + nb, :],
                                                                           in_=pt_[:, 0:nb * 128].rearrange("p (b q) -> p b q", q=128)))(k4, nb, pt_),
                       [pt_], [maskT])
            return mT

        def odd_attend(T, jj, mT):
            qb = 4 * T + jj
            obufs = [dict(po=psO[0], pz=psZ[0], at=att_b[0], tz=[sps, tts], lt=sp32[0], rz=zs),
                     dict(po=psO[1], pz=psZ[1], at=att_b[1], tz=[View(kpg[0], kpg[0][:, 0, 0:128]), View(kpg[0], kpg[0][:, 0, 128:256])],
                          lt=View(kpg[1], kpg[1][:, 0, 0:128]), rz=View(qb_sb, qb_sb[:, 0:128]))]
            for h2 in range(0, 8, 2):
                gens = [odd_head(T, jj, qb, h2 + i, mT, obufs[i]) for i in range(2)]
                while gens:
                    for g_ in list(gens):
                        try:
                            next(g_)
                        except StopIteration:
                            gens.remove(g_)
                    yield

        def odd_head(T, jj, qb, h, mT, ob_):
            g, c, base = h // 4, h // 2, 64 * (h % 2)
            po = ob_["po"]
            op("pe", lambda e: e.matmul(po[:, 0:128], lhsT=zero_b[:, :], rhs=xT[0][:, 0:128], start=True, stop=False), [zero_b, xT[0]], [po])
            vA = vbt[g] if base == 0 else vbt[2 + g]
            tz = ob_["tz"]
            zs_ = ob_["rz"]
            if True:
                for dl in range(2):
                    dma("sp", (lambda dl: lambda e: e.dma_start(out=tz[dl][:, 0:128], in_=tz_d[:, (dl * 8 + h) * 128:(dl * 8 + h + 1) * 128]))(dl),
                        writes=[tz[dl]])

            def step(kb):
                Tk, kk = kb // 4, kb % 4
                pz = ob_["pz"]
                at = ob_["at"]
                op("pe", lambda e: e.matmul(pz[:, 0:128], lhsT=kTt[g][base:base + 64, Tk, kk * 128:(kk + 1) * 128],
                                            rhs=qT[c][base:base + 64, jj * 128:(jj + 1) * 128], start=True, stop=True), [kTt[g], qT[c]], [pz])
                delta = qb - kb
                if delta >= 2:
                    op("act", lambda e: e.activation(out=at[:, 0:128], in_=pz[:, 0:128], func=AF.Exp, bias=b31_sb[:, h:h + 1]), [pz, b31_sb], [at])
                else:
                    lt = ob_["lt"]
                    op("dve", lambda e: e.tensor_tensor(out=lt[:, 0:128], in0=pz[:, 0:128], in1=tz[delta][:, 0:128], op=ALU.add), [pz, tz[delta]], [lt])
                    op("act", lambda e: e.activation(out=at[:, 0:128], in_=lt[:, 0:128], func=AF.Exp), [lt], [at])
                op("dve", lambda e: e.tensor_tensor(out=at[:, 0:128], in0=at[:, 0:128], in1=mT[:, kb, :], op=ALU.mult), [at, maskT], [at])
                op("pe", lambda e: e.matmul(po[:, 0:128], lhsT=vA[:, Tk, kk * 128:(kk + 1) * 128], rhs=at[:, 0:128], start=False, stop=(kb == qb)),
                   [vA, at], [po])

            def far_group(kb0, n):
                pz = ob_["pz"]
                at = ob_["at"]
                for i in range(n):
                    kb = kb0 + i
                    Tk, kk = kb // 4, kb % 4
                    op("pe", (lambda i, Tk, kk: lambda e: e.matmul(pz[:, i * 128:(i + 1) * 128], lhsT=kTt[g][base:base + 64, Tk, kk * 128:(kk + 1) * 128],
                                                                   rhs=qT[c][base:base + 64, jj * 128:(jj + 1) * 128], start=True, stop=True))(i, Tk, kk),
                       [kTt[g], qT[c]], [pz])
                op("act", lambda e: e.activation(out=at[:, 0:n * 128], in_=pz[:, 0:n * 128], func=AF.Exp, bias=b31_sb[:, h:h + 1]), [pz, b31_sb], [at])
                op("dve", lambda e: e.tensor_tensor(out=at[:, 0:n * 128].rearrange("p (b q) -> p b q", q=128),
                                                    in0=at[:, 0:n * 128].rearrange("p (b q) -> p b q", q=128),
                                                    in1=mT[:, kb0:kb0 + n, :], op=ALU.mult), [at, maskT], [at])
                for i in range(n):
                    kb = kb0 + i
                    Tk, kk = kb // 4, kb % 4
                    op("pe", (lambda i, Tk, kk: lambda e: e.matmul(po[:, 0:128], lhsT=vA[:, Tk, kk * 128:(kk + 1) * 128], rhs=at[:, i * 128:(i + 1) * 128],
                                                                   start=False, stop=False))(i, Tk, kk), [vA, at], [po])

            if DBG.get("farbatch", 1):
                nfar = max(0, qb - 1)
                for kb0 in range(0, nfar, 4):
                    far_group(kb0, min(4, nfar - kb0))
                    yield
                for kb in range(nfar, qb + 1):
                    step(kb)
                    yield
            else:
                for kb in range(qb + 1):
                    step(kb)
                    yield
            ob = 64 - base
            op("dve", lambda e: e.reciprocal(out=zs_[base:base + 64, 0:128], in_=po[ob:ob + 64, 0:128]), [po], [zs_])
            op("dve", lambda e: e.tensor_tensor(out=t32[c][base:base + 64, jj * 128:(jj + 1) * 128], in0=po[base:base + 64, 0:128],
                                                in1=zs_[base:base + 64, 0:128], op=ALU.mult), [po, zs_], [t32[c]])

        def odd_prompt_tile(layer, j, T):
            W = wio_d[j]
            xTl, _ = make_xT(T)
            for c in range(4):
                blk = load_w(W, 128 * c, 128)
                ps = nxt("A")
                fm_mm(ps, blk, 128, xTl, 512)
                op("dve", (lambda c, ps: lambda e: e.tensor_scalar(out=qT[c][:, :], in0=ps[:, :], scalar1=0.125, scalar2=None,
                                                                   op0=ALU.mult))(c, ps), [ps], [qT[c]])
            for g in range(2):
                blk = load_w(W, 0, 0, pieces=[(512 + 64 * g, 64, 0), (512 + 64 * g, 64, 64)])
                ps = nxt("A")
                fm_mm(ps, blk, 128, xTl, 512)
                op("dve", (lambda g, ps: lambda e: e.tensor_copy(out=kTt[g][:, T, :], in_=ps[:, :]))(g, ps), [ps], [kTt[g]])
            blk = load_w(W, 0, 0, pieces=[(896, 32, 0), (896, 32, 32), (896, 32, 64), (896, 32, 96)])
            ps = nxt("A")
            fm_mm(ps, blk, 128, xTl, 512)
            op("dve", (lambda ps: lambda e: e.tensor_copy(out=kTt[2][:, T, :], in_=ps[:, :]))(ps), [ps], [kTt[2]])
            blk = load_w(W, 768, 128)
            ps = nxt("A")
            fm_mm(ps, blk, 128, xTl, 512)
            for ih in range(4):
                op("dve", (lambda ih, ps: lambda e: e.tensor_scalar(out=vpg[ih // 2][:, ih % 2, :], in0=ps[:, :],
                                                                    scalar1=C["c_rowm"][:, ih:ih + 1], scalar2=None, op0=ALU.mult))(ih, ps),
                   [ps, C["c_rowm"]], [vpg[ih // 2]])
            pstm = [psA[0], psA[1], psY[0], psY[1]]
            for (c0, n, pc) in ((512, 128, 0), (640, 128, 128), (896, 36, 256)):
                blk = load_w(W, c0, n)
                for jj in range(4):
                    tm_mm(pstm[jj], pc, blk, n, xTl, jj * 128, 128)
            rowbuf = t32[5]
            for jj in range(4):
                tok0 = T * 512 + jj * 128
                op("dve", (lambda jj: lambda e: e.tensor_copy(out=rowbuf[:, 0:292], in_=pstm[jj][:, 0:292]))(jj), [pstm[jj]], [rowbuf])
                dma("pool", (lambda tok0: lambda e: e.dma_start(out=ckvp_d[j, tok0:tok0 + 128, :], in_=rowbuf[:, 0:256]))(tok0), reads=[rowbuf])
                dma("pool", (lambda tok0: lambda e: e.dma_start(out=ckip_d[j, tok0:tok0 + 128, :], in_=rowbuf[:, 256:288]))(tok0), reads=[rowbuf])
                for g in range(2):
                    op("dve", (lambda g, jj: lambda e: e.tensor_copy(out=vbt[g][:, T, jj * 128:jj * 128 + 64],
                                                                    in_=rowbuf[:, 128 + 64 * g:192 + 64 * g]))(g, jj), [rowbuf], [vbt[g]])
                    op("pool", (lambda g, jj: lambda e: e.tensor_copy(out=vbt[2 + g][:, T, jj * 128 + 64:jj * 128 + 128],
                                                                     in_=rowbuf[:, 128 + 64 * g:192 + 64 * g]))(g, jj), [rowbuf], [vbt[2 + g]])
                op("dve", (lambda jj: lambda e: e.tensor_scalar(out=bsp_bc[:, jj, 0:4], in0=rowbuf[:, 288:292], scalar1=ISQ, scalar2=None,
                                                                op0=ALU.mult))(jj), [rowbuf], [bsp_bc])
            idx, wk = big32[0], big32[1]
            if DBG.get("ostage", 9) >= 1:
                interleave([odd_select(T, 0, idx, wk)])
                for jj in range(4):
                    mT = odd_finish_mask(T, jj, wk)
                    gens = [odd_attend(T, jj, mT)]
                    if jj < 3:
                        gens.append(odd_select(T, jj + 1, idx, wk))
                    interleave(gens)
            else:
                for c in range(4):
                    op("pool", (lambda c: lambda e: e.memset(t32[c][:, :], 0.0))(c), [], [t32[c]])
            for c in range(4):
                blk = load_w(W, 932 + 128 * c, 128)
                ps = nxt("A")
                fm_mm(ps, blk, 128, xTl, 512)
                sg = t32[4]
                silu_from_psum(sg, ps, 0, 128, 512)
                op("dve", (lambda c: lambda e: e.tensor_tensor(out=hT[c][:, :], in0=t32[c][:, :], in1=sg[:, :], op=ALU.mult))(c),
                   [t32[c], sg], [hT[c]])
            Pv = kpg[0][:, :, :].rearrange("p a b -> p (a b)")
            Av = kpg[1][:, :, :].rearrange("p a b -> p (a b)")
            Bv = big32[1]
            for g in range(4):
                wwin = 2 ** (g + 1)
                blk = load_w(W, 1444 + 128 * g, 128)
                ps = nxt("A")
                fm_mm(ps, blk, 128, xTl, 512)
                op("dve", (lambda g: lambda e: e.tensor_copy(out=Pv[:, 0:16], in_=wsp_f[:, g, 0:16]))(g), [wsp_f], [kpg[0]])
                op("dve", (lambda ps: lambda e: e.tensor_copy(out=Pv[:, 16:528], in_=ps[:, :]))(ps), [ps], [kpg[0]])
                op("pool", (lambda g: lambda e: e.tensor_copy(out=wsp_f[:, g, 0:16], in_=Pv[:, 512:528]))(g), [kpg[0]], [wsp_f])
                op("dve", lambda e: e.tensor_tensor(out=Av[:, 1:528], in0=Pv[:, 1:528], in1=Pv[:, 0:527], op=ALU.add), [kpg[0]], [kpg[1]])
                Sv, Sobj = Av, kpg[1]
                if g >= 1:
                    op("dve", lambda e: e.tensor_tensor(out=Bv[:, 3:528], in0=Av[:, 3:528], in1=Av[:, 1:526], op=ALU.add), [kpg[1]], [big32[1]])
                    Sv, Sobj = Bv, big32[1]
                if g >= 2:
                    op("dve", lambda e: e.tensor_tensor(out=Av[:, 7:528], in0=Bv[:, 7:528], in1=Bv[:, 3:524], op=ALU.add), [big32[1]], [kpg[1]])
                    Sv, Sobj = Av, kpg[1]
                if g >= 3:
                    op("dve", lambda e: e.tensor_tensor(out=Bv[:, 15:528], in0=Av[:, 15:528], in1=Av[:, 7:520], op=ALU.add), [kpg[1]], [big32[1]])
                    Sv, Sobj = Bv, big32[1]
                pl = att_b[0]
                op("dve", (lambda Sv, wwin: lambda e: e.scalar_tensor_tensor(out=pl[:, :], in0=Sv[:, 16:528], scalar=1.0 / wwin, in1=Pv[:, 16:528],
                                                                             op0=ALU.mult, op1=ALU.subtract))(Sv, wwin), [Sobj, kpg[0]], [pl])
                if T == 0:
                    op("dve", (lambda Sv, g: lambda e: e.tensor_tensor(out=zs[:, 0:16], in0=Sv[:, 16:32], in1=C["c_icnt"][:, g * 16:(g + 1) * 16],
                                                                       op=ALU.mult))(Sv, g), [Sobj, C["c_icnt"]], [zs])
                    op("dve", lambda e: e.tensor_tensor(out=pl[:, 0:16], in0=zs[:, 0:16], in1=Pv[:, 16:32], op=ALU.subtract), [zs, kpg[0]], [pl])
                pm = nxt("A")
                op("pe", (lambda g, pm: lambda e: e.matmul(pm[:, :], lhsT=wspT[:, g, :], rhs=pl[:, :], start=True, stop=True))(g, pm), [wspT, pl], [pm])
                blk = load_w(W, 1956 + 128 * g, 128)
                ps = nxt("A")
                fm_mm(ps, blk, 128, xTl, 512)
                sg = t32[4]
                silu_from_psum(sg, ps, 0, 128, 512)
                op("dve", (lambda g, pm: lambda e: e.scalar_tensor_tensor(out=hT[4 + g][:, :], in0=pm[:, :], scalar=wsp00[:, g:g + 1], in1=sg[:, :],
                                                                          op0=ALU.mult, op1=ALU.mult))(g, pm), [pm, wsp00, sg], [hT[4 + g]])
            if T == 3:
                pp = nxt("A")
                for cb in range(4):
                    blk = load_w(W, 1444 + 128 * cb, 128)
                    tm_mm(pp, cb * 128, blk, 128, xTl, 384, 128)
                op("dve", (lambda pp: lambda e: e.tensor_copy(out=e32[0][:, :], in_=pp[:, :]))(pp), [pp], [e32[0]])
                dma("pool", lambda e: e.dma_start(out=dbp_d[j], in_=e32[0][113:128, :]), reads=[e32[0]])
            if dbgh_d is not None and T == DBG.get("dbgT", 0):
                for f in range(8):
                    dma("pool", (lambda f: lambda e: e.dma_start(out=dbgh_d[f], in_=hT[f][:, :]))(f), reads=[hT[f]])
            layer_norm_tile(layer, [(4 * T + jj, jj * 128, 128) for jj in range(4)], hT)

        def odd_sample_tile(layer, j):
            W = wio_d[j]
            xTl, _ = make_xT(4)
            qs, rows, sgc_unused, pp_, sgd = t32[0], t32[1], t32[2], t32[3], t32[4]
            ps = nxt("A")
            sample_inproj(W, 0, 512, xTl, ps)
            op("dve", (lambda ps: lambda e: e.tensor_scalar(out=qs[0:NS, :], in0=ps[0:NS, :], scalar1=0.125, scalar2=None, op0=ALU.mult))(ps), [ps], [qs])
            ps = nxt("A")
            sample_inproj(W, 512, 420, xTl, ps)
            op("dve", (lambda ps: lambda e: e.tensor_copy(out=rows[0:NS, 0:420], in_=ps[0:NS, 0:420]))(ps), [ps], [rows])
            ps = nxt("A")
            sample_inproj(W, 1444, 512, xTl, ps)
            op("dve", (lambda ps: lambda e: e.tensor_copy(out=pp_[0:NS, :], in_=ps[0:NS, :]))(ps), [ps], [pp_])
            ps = nxt("A")
            sample_inproj(W, 1956, 512, xTl, ps)
            silu_from_psum(sgd, ps, 0, NS, 512)
            for c in range(4):
                blk = load_w(W, 932 + 128 * c, 128)
                ps = nxt("A")
                fm_mm(ps, blk, 128, xTl, NS)
                op("act", (lambda c, ps: lambda e: e.activation(out=gbTs[:, c, :], in_=ps[:, 0:NS], func=AF.Sigmoid))(c, ps), [ps], [gbTs])
                op("dve", (lambda c, ps: lambda e: e.tensor_tensor(out=gbTs[:, c, :], in0=gbTs[:, c, :], in1=ps[:, 0:NS], op=ALU.mult))(c, ps),
                   [gbTs, ps], [gbTs])
            dma("pool", lambda e: e.dma_start(out=ckvs_d[j], in_=rows[0:NS, 0:256]), reads=[rows])
            dma("pool", lambda e: e.dma_start(out=ckis_d[j], in_=rows[0:NS, 384:416]), reads=[rows])
            dma("pool", lambda e: e.dma_start(out=dbs_d[j, :, 0:14, :], in_=db_d[j, :, 1:15, :]))
            dma("pool", lambda e: e.dma_start(out=dbs_d[j, :, 14, :], in_=pp_[0:NS, :]), reads=[pp_])
            ev1 = dma("pool", lambda e: e.dma_start(out=scr_kv[:, 0:256], in_=rows[0:NS, 0:256]), reads=[rows], writes=[scrkv_b])
            ev2 = dma("pool", lambda e: e.dma_start(out=scr_kv[:, 256:288], in_=rows[0:NS, 384:416]), reads=[rows], writes=[scrkv_b2])
            G = big32[1]
            offs = [0, 128, 512, 1408]
            rs = t32[5]
            for g in range(4):
                wwin = 2 ** (g + 1)
                nr = wwin - 1
                Gb = big32[1] if g < 3 else big32[0]
                o0 = offs[g] if g < 3 else 0
                dma("sp", (lambda g, nr, Gb, o0: lambda e: e.dma_start(
                    out=Gb[0:NS, o0:o0 + nr * 128].rearrange("p (r c) -> p r c", c=128),
                    in_=db_d[j, :, 15 - nr:15, g * 128:(g + 1) * 128]))(g, nr, Gb, o0), writes=[Gb])
                op("dve", (lambda g, nr, Gb, o0: lambda e: e.tensor_reduce(
                    out=rs[0:NS, g * 128:(g + 1) * 128], in_=Gb[0:NS, o0:o0 + nr * 128].rearrange("p (r c) -> p c r", c=128),
                    axis=AX.X, op=ALU.add))(g, nr, Gb, o0), [Gb], [rs])
                op("dve", (lambda g: lambda e: e.tensor_tensor(out=rs[0:NS, g * 128:(g + 1) * 128], in0=rs[0:NS, g * 128:(g + 1) * 128],
                                                               in1=pp_[0:NS, g * 128:(g + 1) * 128], op=ALU.add))(g), [rs, pp_], [rs])
                op("dve", (lambda g, wwin: lambda e: e.scalar_tensor_tensor(
                    out=rs[0:NS, g * 128:(g + 1) * 128], in0=rs[0:NS, g * 128:(g + 1) * 128], scalar=1.0 / wwin,
                    in1=pp_[0:NS, g * 128:(g + 1) * 128], op0=ALU.mult, op1=ALU.subtract))(g, wwin), [rs, pp_], [rs])
            pt_ = nxt("A")
            for g in range(4):
                op("pe", (lambda g, pt_: lambda e: e.transpose(out=pt_[:, g * 128:g * 128 + NS], in_=rs[0:NS, g * 128:(g + 1) * 128],
                                                               identity=C["c_ident"][0:NS, 0:NS]))(g, pt_), [rs, C["c_ident"]], [pt_])
            plT = att_b[1]
            op("dve", (lambda pt_: lambda e: e.tensor_copy(out=plT[:, :], in_=pt_[:, :]))(pt_), [pt_], [plT])
            pm = nxt("A")
            for g in range(4):
                op("pe", (lambda g, pm: lambda e: e.matmul(pm[0:NS, g * 128:(g + 1) * 128], lhsT=plT[:, g * 128:g * 128 + NS], rhs=wspT[:, g, :],
                                                           start=True, stop=True))(g, pm), [plT, wspT], [pm])
            dscb = t32[2]
            dma("sp", lambda e: e.dma_start(out=dscb[0:NS, :], in_=dsc_d[j:j + 1, :].to_broadcast([NS, 512])), writes=[dscb])
            od = rs
            op("dve", (lambda pm: lambda e: e.tensor_tensor(out=od[0:NS, :], in0=pm[0:NS, :], in1=dscb[0:NS, :], op=ALU.mult))(pm), [pm, dscb], [od])
            op("dve", lambda e: e.tensor_tensor(out=od[0:NS, :], in0=od[0:NS, :], in1=sgd[0:NS, :], op=ALU.mult), [od, sgd], [od])
            pt_ = nxt("A")
            for g in range(4):
                op("pe", (lambda g, pt_: lambda e: e.transpose(out=pt_[:, g * 128:g * 128 + NS], in_=od[0:NS, g * 128:(g + 1) * 128],
                                                               identity=C["c_ident"][0:NS, 0:NS]))(g, pt_), [od, C["c_ident"]], [pt_])
            for g in range(4):
                op("dve", (lambda g, pt_: lambda e: e.tensor_copy(out=hTs[4 + g][:, :], in_=pt_[:, g * 128:g * 128 + NS]))(g, pt_), [pt_], [hTs[4 + g]])
            bsT = sps
            dma("sp", lambda e: e.dma_start(out=bsT[:, :], in_=bs_d[:, 0:128]), writes=[bsT])
            dma("sp", lambda e: e.dma_start(out=tts[:, 0:8], in_=bs_d[:, 128:136]), writes=[tts])
            idxall = S32
            KI, KIs = e32[0], sp32[0]
            qib = t32[2]
            op("pool", lambda e: e.memset(KIs[:, 0:32], 0.0), [], [KIs])

            def bcast_rows(b, src, c0, n, dst):
                qm = t32[5]
                op("dve", lambda e: e.tensor_scalar(out=qm[0:NS, 0:n], in0=src[0:NS, c0:c0 + n], scalar1=C["c_ident"][0:NS, b:b + 1],
                                                    scalar2=-1.0, op0=ALU.mult, op1=ALU.mult), [src, C["c_ident"]], [qm])
                pq = nxt("A")
                op("pe", lambda e: e.matmul(pq[:, 0:n], lhsT=C["c_nones"][0:NS, :], rhs=qm[0:NS, 0:n], start=True, stop=True),
                   [C["c_nones"], qm], [pq])
                op("dve", lambda e: e.tensor_copy(out=dst[:, 0:n], in_=pq[:, 0:n]), [pq], [dst])

            idx, wk = big32[0], big32[1]
            ckh = ck_d.rearrange("(n r) e -> n (r e)", r=64)
            qexp = t32[2]
            idxp = zs
            sc_ = t32[5]
            op("dve", lambda e: e.tensor_scalar(out=rows[0:NS, 416:420], in0=rows[0:NS, 416:420], scalar1=ISQ, scalar2=None, op0=ALU.mult),
               [rows], [rows])
            for bt in range(2):
                dma("sp", (lambda bt: lambda e: e.dma_start(out=idx2[:, bt:bt + 1],
                                                            in_=pt_d.rearrange("o (p c) -> (o p) c", c=1)[128 * bt:128 * (bt + 1), :]))(bt),
                    writes=[idx2])
            op("dve", lambda e: e.tensor_scalar(out=idx2[:, 0:2], in0=idx2[:, 0:2], scalar1=2, scalar2=None, op0=ALU.mult), [idx2], [idx2])
            for bt in range(2):
                pq = nxt("A")
                op("pe", (lambda bt, pq: lambda e: e.matmul(pq[:, 0:164], lhsT=C["c_esel"][:, 128 * bt:128 * (bt + 1)], rhs=rows[0:NS, 256:420],
                                                            start=True, stop=True))(bt, pq), [C["c_esel"], rows], [pq])
                op("dve", (lambda pq: lambda e: e.tensor_copy(out=qexp[:, 0:164], in_=pq[:, 0:164]))(pq), [pq], [qexp])
                for half in range(2):
                    dma("pool", (lambda bt, half: lambda e: e.indirect_dma_start(
                        out=idx[:, 0:2048], out_offset=None, in_=ckh[:, 0:2048], element_offset=j * pool_rows * 32 + half * 2048,
                        in_offset=bass.IndirectOffsetOnAxis(ap=idx2[:, bt:bt + 1], axis=0)))(bt, half), reads=[idx2], writes=[idx])
                    for ih in range(4):
                        op("dve", (lambda ih: lambda e: e.tensor_tensor(
                            out=wk[:, 0:2048].rearrange("p (k e) -> p k e", e=32), in0=idx[:, 0:2048].rearrange("p (k e) -> p k e", e=32),
                            in1=qexp[:, ih * 32:(ih + 1) * 32].unsqueeze(1).to_broadcast([128, 64, 32]), op=ALU.mult))(ih), [idx, qexp], [wk])
                        op("dve", lambda e: e.tensor_reduce(out=sc_[:, 0:64], in_=wk[:, 0:2048].rearrange("p (k e) -> p k e", e=32), axis=AX.X,
                                                            op=ALU.add), [wk], [sc_])
                        op("dve", lambda e: e.tensor_scalar(out=sc_[:, 0:64], in0=sc_[:, 0:64], scalar1=0.0, scalar2=None, op0=ALU.max), [sc_], [sc_])
                        if ih == 0:
                            op("dve", (lambda half: lambda e: e.tensor_scalar(out=idxp[:, half * 64:(half + 1) * 64], in0=sc_[:, 0:64],
                                                                              scalar1=qexp[:, 160:161], scalar2=None, op0=ALU.mult))(half),
                               [sc_, qexp], [idxp])
                        else:
                            op("dve", (lambda half, ih: lambda e: e.scalar_tensor_tensor(
                                out=idxp[:, half * 64:(half + 1) * 64], in0=sc_[:, 0:64], scalar=qexp[:, 160 + ih:161 + ih],
                                in1=idxp[:, half * 64:(half + 1) * 64], op0=ALU.mult, op1=ALU.add))(half, ih), [sc_, qexp, idxp], [idxp])
                for b8 in range(8):
                    bb = 8 * bt + b8
                    dma("pool", (lambda b8, bb: lambda e: e.dma_start(
                        out=scr_idx[bb:bb + 1, 0:2048].rearrange("o (g k) -> (o g) k", k=128), in_=idxp[b8 * 16:(b8 + 1) * 16, :]))(b8, bb),
                        reads=[idxp], writes=[scri_b])
            sfp = t32[5]
            op("dve", lambda e: e.tensor_tensor(out=sfp[0:NS, 0:128].rearrange("p (i e) -> p i e", e=32),
                                                in0=rows[0:NS, 256:384].rearrange("p (i e) -> p i e", e=32),
                                                in1=rows[0:NS, 384:416].unsqueeze(1).to_broadcast([NS, 4, 32]), op=ALU.mult), [rows], [sfp])
            op("dve", lambda e: e.tensor_reduce(out=sfp[0:NS, 128:132], in_=sfp[0:NS, 0:128].rearrange("p (i e) -> p i e", e=32), axis=AX.X,
                                                op=ALU.add), [sfp], [sfp])
            op("dve", lambda e: e.tensor_scalar(out=sfp[0:NS, 128:132], in0=sfp[0:NS, 128:132], scalar1=0.0, scalar2=None, op0=ALU.max), [sfp], [sfp])
            op("dve", lambda e: e.tensor_tensor(out=sfp[0:NS, 128:132], in0=sfp[0:NS, 128:132], in1=rows[0:NS, 416:420], op=ALU.mult), [sfp, rows], [sfp])
            op("pool", lambda e: e.memset(sfp[0:NS, 256:384], NEG), [], [sfp])
            op("dve", lambda e: e.tensor_reduce(out=sfp[0:NS, 256:257], in_=sfp[0:NS, 128:132], axis=AX.X, op=ALU.add), [sfp], [sfp])
            dma("pool", lambda e: e.dma_start(out=scr_idx[:, 2048:2176], in_=sfp[0:NS, 256:384]), reads=[sfp], writes=[scri_b])
            dma("sp", lambda e: e.dma_start(out=idx[0:NS, 0:2176], in_=scr_idx), reads=[scri_b], writes=[idx])
            op("pool", lambda e: e.tensor_copy(out=wk[0:NS, 0:2176], in_=idx[0:NS, 0:2176]), [idx], [wk])
            mx = small[1]
            for r in range(32):
                op("dve", lambda e: e.max(out=mx[0:NS, 0:8], in_=wk[0:NS, 0:2176]), [wk], [mx])
                op("dve", lambda e: e.match_replace(out=wk[0:NS, 0:2176], in_to_replace=mx[0:NS, 0:8], in_values=wk[0:NS, 0:2176],
                                                    imm_value=2.0 * NEG), [wk, mx], [wk])
            op("dve", lambda e: e.tensor_tensor(out=wk[0:NS, 0:2176], in0=wk[0:NS, 0:2176], in1=idx[0:NS, 0:2176], op=ALU.not_equal), [wk, idx], [wk])
            dma("pool", lambda e: e.dma_start(out=scr_msk, in_=wk[0:NS, 0:2176]), reads=[wk], writes=[scrm_b])
            mS = idxall
            for (r0, nr) in ((0, 128), (128, 128), (256, 16)):
                tb = t32[5]
                dma("sp", (lambda r0, nr: lambda e: e.dma_start(out=tb[0:nr, 0:128],
                                                                in_=scr_msk.rearrange("b (g k) -> (b g) k", k=128)[r0:r0 + nr, :]))(r0, nr),
                    reads=[scrm_b], writes=[tb])
                pt_ = nxt("A")
                op("pe", (lambda nr, pt_: lambda e: e.transpose(out=pt_[:, 0:nr], in_=tb[0:nr, 0:128], identity=C["c_ident"][0:nr, 0:nr]))(nr, pt_),
                   [tb, C["c_ident"]], [pt_])
                op("dve", (lambda r0, nr, pt_: lambda e: e.tensor_copy(out=mS[:, r0:r0 + nr], in_=pt_[:, 0:nr]))(r0, nr, pt_), [pt_], [mS])
            pn, pd = psO[0], psO[1]
            op("pe", lambda e: e.matmul(pn[:, 0:128], lhsT=zero_b[:, :], rhs=xT[0][:, 0:128], start=True, stop=False), [zero_b, xT[0]], [pn])
            op("pe", lambda e: e.matmul(pd[:, 0:128], lhsT=zero_b[:, :], rhs=xT[0][:, 0:128], start=True, stop=False), [zero_b, xT[0]], [pd])
            Kself, Vself = qb_sb, atts
            op("pool", lambda e: e.memset(Kself[:, 0:128], 0.0), [], [Kself])
            op("pool", lambda e: e.memset(Vself[:, :], 0.0), [], [Vself])
            ones64 = vbt[0][:, 0, 64:128]

            def attend_sample(b):
                qbq = t32[2]
                bcast_rows(b, qs, 0, 512, qbq)
                Kb = [kpg[0][:, :, :].rearrange("p a b -> p (a b)"), kpg[1][:, :, :].rearrange("p a b -> p (a b)")]
                Vb = [vpg[0][:, :, :].rearrange("p a b -> p (a b)"), vpg[1][:, :, :].rearrange("p a b -> p (a b)")]
                for pg in range(NPG):
                    hb, i = pg // 8, pg % 8
                    stg_ = t32[3] if pg % 2 == 0 else t32[4]
                    dma("pool", (lambda pg, stg_: lambda e: e.indirect_dma_start(
                        out=stg_[:, 0:256], out_offset=None, in_=cc_d[:, 0:256], element_offset=j * pool_rows * 256,
                        in_offset=bass.IndirectOffsetOnAxis(ap=idxC[:, b * NPG + pg:b * NPG + pg + 1], axis=0)))(pg, stg_),
                        reads=[idxC], writes=[stg_])
                    op("dve", (lambda hb, i, stg_: lambda e: e.tensor_copy(out=Kb[hb][:, i * 128:(i + 1) * 128], in_=stg_[:, 0:128]))(hb, i, stg_),
                       [stg_], [kpg[hb]])
                    op("dve", (lambda hb, i, stg_: lambda e: e.tensor_copy(out=Vb[hb][:, i * 128:(i + 1) * 128], in_=stg_[:, 128:256]))(hb, i, stg_),
                       [stg_], [vpg[hb]])
                dma("sp", lambda e: e.dma_start(out=Kself[0:1, 0:128], in_=scr_kv[b:b + 1, 0:128]), reads=[scrkv_b], writes=[Kself])
                vst = small[0]
                dma("sp", lambda e: e.dma_start(out=sp32[0][0:1, 128:256], in_=scr_kv[b:b + 1, 128:256]), reads=[scrkv_b], writes=[sp32[0]])
                op("dve", lambda e: e.tensor_copy(out=Vself[0:1, :], in_=sp32[0][0:1, 128:256]), [sp32[0]], [Vself])
                L = zs
                Ls = small[2]
                prod = big32[1]
                qv = qbq[:, 0:512].rearrange("p (g r d) -> p g r d", g=2, r=4)
                for pq4 in range(4):
                    hb, i0 = pq4 // 2, (pq4 % 2) * 4
                    Kv = Kb[hb][:, i0 * 128:(i0 + 4) * 128].rearrange("p (a g d) -> p a g d", g=2, d=64)
                    for g in range(2):
                        op("dve", (lambda Kv, g: lambda e: e.tensor_tensor(
                            out=prod[:, g * 1024:(g + 1) * 1024].rearrange("p (a r d) -> p a r d", r=4, d=64),
                            in0=Kv[:, :, g, :].unsqueeze(2).to_broadcast([128, 4, 4, 64]),
                            in1=qv[:, g, :, :].unsqueeze(1).to_broadcast([128, 4, 4, 64]), op=ALU.mult))(Kv, g), [kpg[hb], qbq], [prod])
                    for g in range(2):
                        op("dve", (lambda pq4, g: lambda e: e.tensor_reduce(
                            out=L[:, pq4 * 32:(pq4 + 1) * 32].rearrange("p (a h) -> p a h", h=8)[:, :, g * 4:(g + 1) * 4],
                            in_=prod[:, g * 1024:(g + 1) * 1024].rearrange("p (a r d) -> p a r d", r=4, d=64), axis=AX.X, op=ALU.add))(pq4, g),
                           [prod], [L])
                Ksv = Kself[:, 0:128].rearrange("p (g d) -> p g d", d=64)
                op("dve", lambda e: e.tensor_tensor(out=prod[:, 0:512].rearrange("p (g r d) -> p g r d", g=2, r=4),
                                                    in0=Ksv.unsqueeze(2).to_broadcast([128, 2, 4, 64]), in1=qv, op=ALU.mult), [Kself, qbq], [prod])
                op("dve", lambda e: e.tensor_reduce(out=Ls[:, 0:8], in_=prod[:, 0:512].rearrange("p (h d) -> p h d", d=64), axis=AX.X, op=ALU.add),
                   [prod], [Ls])
                op("dve", lambda e: e.tensor_tensor(out=L[:, :], in0=L[:, :], in1=bsT[:, :], op=ALU.add), [L, bsT], [L])
                op("dve", lambda e: e.tensor_tensor(out=Ls[:, 0:8], in0=Ls[:, 0:8], in1=tts[:, 0:8], op=ALU.add), [Ls, tts], [Ls])
                op("act", lambda e: e.activation(out=L[:, :], in_=L[:, :], func=AF.Exp), [L], [L])
                op("act", lambda e: e.activation(out=Ls[:, 0:8], in_=Ls[:, 0:8], func=AF.Exp), [Ls], [Ls])
                pb = att_b[0]
                op("dve", lambda e: e.tensor_tensor(out=pb[:, 0:128].rearrange("p (a h) -> p a h", h=8), in0=L[:, :].rearrange("p (a h) -> p a h", h=8),
                                                    in1=mS[:, b * 17:b * 17 + 16].unsqueeze(2).to_broadcast([128, 16, 8]), op=ALU.mult), [L, mS], [pb])
                op("dve", lambda e: e.tensor_scalar(out=pb[:, 128:136], in0=Ls[:, 0:8], scalar1=mS[:, b * 17 + 16:b * 17 + 17], scalar2=None,
                                                    op0=ALU.mult), [Ls, mS], [pb])
                for pg in range(NPG):
                    hb, i = pg // 8, pg % 8
                    op("pe", (lambda pg, hb, i: lambda e: e.matmul(pn[:, b * 8:(b + 1) * 8], lhsT=Vb[hb][:, i * 128:(i + 1) * 128],
                                                                   rhs=pb[:, pg * 8:(pg + 1) * 8], start=False, stop=False))(pg, hb, i), [vpg[hb], pb], [pn])
                    op("pe", (lambda pg: lambda e: e.matmul(pd[0:64, b * 8:(b + 1) * 8], lhsT=ones64, rhs=pb[:, pg * 8:(pg + 1) * 8],
                                                            start=False, stop=False))(pg), [vbt[0], pb], [pd])
                op("pe", lambda e: e.matmul(pn[:, b * 8:(b + 1) * 8], lhsT=Vself[:, :], rhs=pb[:, 128:136], start=False, stop=True), [Vself, pb], [pn])
                op("pe", lambda e: e.matmul(pd[0:64, b * 8:(b + 1) * 8], lhsT=ones64, rhs=pb[:, 128:136], start=False, stop=True), [vbt[0], pb], [pd])

            for b in range(NS):
                attend_sample(b)
            rd = zs
            op("dve", lambda e: e.reciprocal(out=rd[0:64, 0:128], in_=pd[0:64, 0:128]), [pd], [rd])
            op("dve", lambda e: e.reciprocal(out=rd[64:128, 0:128], in_=pd[0:64, 0:128]), [pd], [rd])
            oT = t32[5]
            op("dve", lambda e: e.tensor_tensor(out=oT[:, 0:128], in0=pn[:, 0:128], in1=rd[:, 0:128], op=ALU.mult), [pn, rd], [oT])
            for h in range(8):
                g, c, hb_ = h // 4, h // 2, 64 * (h % 2)
                tmpo = t32[4]
                op("dve", (lambda h, g, c, hb_: lambda e: e.tensor_copy(
                    out=tmpo[hb_:hb_ + 64, 0:NS], in_=oT[64 * g:64 * g + 64, 0:128].rearrange("p (b h) -> p b h", h=8)[:, :, h]))(h, g, c, hb_),
                   [oT], [tmpo])
                op("dve", (lambda h, g, c, hb_: lambda e: e.tensor_tensor(
                    out=hTs[c][hb_:hb_ + 64, :], in0=tmpo[hb_:hb_ + 64, 0:NS],
                    in1=gbTs[hb_:hb_ + 64, c, :], op=ALU.mult))(h, g, c, hb_), [tmpo, gbTs], [hTs[c]])
            layer_norm_block(layer, 16, hTs, 0, NS)

        for layer in range(n_layers):
            j = layer // 2
            layer_setup(layer)
            if layer % 2 == 0:
                even_setup(j)
                for T in range(DBG.get("ntiles", 4)):
                    even_prompt_tile(layer, j, T)
                if DBG.get("sample", 1):
                    even_sample_tile(layer, j)
            else:
                odd_setup(j)
                for T in range(DBG.get("ntiles", 4)):
                    odd_prompt_tile(layer, j, T)
                if DBG.get("sample", 1):
                    odd_sample_tile(layer, j)

        for jb in range(16):
            dma("pool", (lambda jb: lambda e: e.dma_start(out=yp_d[jb * 128:(jb + 1) * 128, :], in_=xtok[jb][:, :]))(jb), reads=[xtok[jb]])
        dma("pool", lambda e: e.dma_start(out=ys_d, in_=xtok[16][0:NS, :]), reads=[xtok[16]])
        fw.emit()
    return nc


def _bias_tables(rel_bias):
    k = np.arange(128)
    d0 = np.maximum(k[None, :] - k[:, None], 0)
    d1 = 128 + k[None, :] - k[:, None]
    t = np.zeros((128, 2, 8, 128), np.float32)
    for h in range(8):
        t[:, 0, h, :] = rel_bias[_t5_bucket(d0), h]
        t[:, 1, h, :] = rel_bias[_t5_bucket(d1), h]
    toep = t.reshape(128, 16 * 128)
    b31 = np.ascontiguousarray(rel_bias[31:32, :]).astype(np.float32)
    bs = np.zeros((128, 17, 8), np.float32)
    for pg in range(16):
        dist = 2048 - (pg * 128 + k)
        bs[:, pg, :] = rel_bias[_t5_bucket(dist), :]
    bs[:, 16, :] = rel_bias[0:1, :]
    return toep, b31, bs.reshape(128, 17 * 8)


def _core_inputs(inp, c, consts, tabs, pool_views, pt_rows):
    d = {
        "xp": np.ascontiguousarray(inp["x_prompt"][c]),
        "xs": np.ascontiguousarray(inp["x_sample"][c * NS:(c + 1) * NS, 0, :]),
        "pt": np.ascontiguousarray(pt_rows.reshape(1, NS * NPG)).astype(np.int32),
        "cb": pool_views[0], "cc": pool_views[1], "ck": pool_views[2],
        "dbuf": np.ascontiguousarray(inp["state_d_buf"][:, c * NS:(c + 1) * NS]),
        "w_in_even": inp["w_in_even"], "w_in_odd": inp["w_in_odd"], "w_out": inp["w_out"],
        "ln_g": inp["ln_g"], "ln_b": inp["ln_b"], "a_w_sp": inp["a_w_sp"], "a_b_sp": inp["a_b_sp"],
        "d_w_grp": inp["d_w_grp"], "d_scale": inp["d_scale"],
        "t_toep": tabs[0], "t_b31": tabs[1], "t_bsamp": tabs[2],
    }
    d.update(consts)
    return d


def kernel(**inputs):
    inp = {k: np.asarray(v) for k, v in inputs.items()}
    n_pool = inp["cache_b_kv"].shape[1]
    nc = build(4, n_pool * 128)
    consts = _consts()
    tabs = _bias_tables(inp["rel_bias"].astype(np.float32))
    views = (inp["cache_b_kv"].reshape(2 * n_pool * 128, 1024), inp["cache_c_kv"].reshape(2 * n_pool * 128, 256),
             inp["cache_c_kidx"].reshape(2 * n_pool * 128, 32))
    in_maps = [_core_inputs(inp, c, consts, tabs, views, inp["page_table"][c * NS:(c + 1) * NS]) for c in range(NCORES)]
    res = run_bass_kernel_spmd(nc, in_maps, core_ids=list(range(NCORES))).results
    return _assemble(res)


def _assemble(res):
    n = len(res)
    f = np.float32
    y_p = np.stack([res[c]["y_p"] for c in range(n)]).astype(f)
    y_s = np.concatenate([res[c]["y_s"] for c in range(n)])[:, None, :].astype(f)
    bkv_p = np.stack([res[c]["bkv_p"] for c in range(n)], axis=1).reshape(2, n, S, 2, 8, 64)
    bkv_s = np.concatenate([res[c]["bkv_s"] for c in range(n)], axis=1).reshape(2, n * NS, 1, 2, 8, 64)
    av_s = np.concatenate([res[c]["av_s"] for c in range(n)], axis=1).reshape(2, n * NS, 1, 512)
    ckv_p = np.stack([res[c]["ckv_p"] for c in range(n)], axis=1).reshape(2, n, S, 2, 2, 64)
    ckv_s = np.concatenate([res[c]["ckv_s"] for c in range(n)], axis=1).reshape(2, n * NS, 1, 2, 2, 64)
    cki_p = np.stack([res[c]["cki_p"] for c in range(n)], axis=1).reshape(2, n, S, 32)
    cki_s = np.concatenate([res[c]["cki_s"] for c in range(n)], axis=1).reshape(2, n * NS, 1, 32)
    db_p = np.stack([res[c]["dbuf_p"] for c in range(n)], axis=1).reshape(2, n, 15, 512)
    db_s = np.concatenate([res[c]["dbuf_s"] for c in range(n)], axis=1).reshape(2, n * NS, 15, 512)
    return tuple(np.ascontiguousarray(a, dtype=f) for a in
                 (y_p, y_s, bkv_p, bkv_s, av_s, ckv_p, ckv_s, cki_p, cki_s, db_p, db_s))
```
